# Optimizing a Trainium2 kernel written in Bass

```python
import math
import jax, jax.numpy as jnp
from jax import lax
import numpy as np

D_MODEL = 1024
BATCH = 2
SEQ = 8192
DEPTH = 4
DEC_BATCH = 8
DEC_SEQ = 32
PAST_LEN = 1024

CHUNK = 64
D_MIX = D_MODEL
HEAD_DIM = 64
D_A = D_MIX // 4
N_BLK_A = D_A // HEAD_DIM
CONV_W = 4
LRU_C = 8.0
D_B = 3 * D_MIX // 8
N_HEAD_B = D_B // HEAD_DIM
LORA_W = 64
LORA_A = 64
LORA_G = 128
D_B_IN = 3 * D_B + LORA_W + LORA_A + LORA_G
D_C = D_MIX - D_A - D_B
N_HEAD_C = D_C // HEAD_DIM
D_IN = 2 * D_A + D_B_IN + 3 * D_C
D_FF = -(-8 * D_MODEL // (3 * 256)) * 256
RMS_EPS = 1e-6
GN_EPS_B = 64e-5
GN_EPS_C = 1e-6

kernel_name = 'hybrid_rglru_rwkv7_mlstm_stream_step'


def rmsnorm(x, g, eps=RMS_EPS):
    xf = x.astype(jnp.float32)
    y = xf * lax.rsqrt(jnp.mean(xf * xf, axis=-1, keepdims=True) + eps)
    return (y * g.astype(jnp.float32)).astype(x.dtype)


def head_norm(y, eps):
    yc = y - jnp.mean(y, axis=-1, keepdims=True)
    return yc * lax.rsqrt(jnp.mean(yc * yc, axis=-1, keepdims=True) + eps)


def causal_conv(x, buf, w, b):
    T = x.shape[1]
    xp = jnp.concatenate([buf, x], axis=1)
    y = b
    for j in range(CONV_W):
        y = y + w[j] * xp[:, j:j + T]
    return y, xp[:, -(CONV_W - 1):]


def token_shift(x, buf):
    xp = jnp.concatenate([buf, x], axis=1)
    return xp[:, :-1], xp[:, -1:]


def rg_lru(x, h0, w_r, b_r, w_i, b_i, lam):
    Bn, T, _ = x.shape
    xb = x.reshape(Bn, T, N_BLK_A, HEAD_DIM)
    gate_r = jax.nn.sigmoid(jnp.einsum('btnd,nde->btne', xb, w_r).reshape(Bn, T, D_A) + b_r)
    gate_i = jax.nn.sigmoid(jnp.einsum('btnd,nde->btne', xb, w_i).reshape(Bn, T, D_A) + b_i)
    log_a = -LRU_C * gate_r * jax.nn.softplus(-lam)
    a = jnp.exp(log_a)
    u = jnp.sqrt(-jnp.expm1(2.0 * log_a)) * (gate_i * x)

    def combine(left, right):
        a1, b1 = left
        a2, b2 = right
        return a1 * a2, a2 * b1 + b2

    a_cum, h = lax.associative_scan(combine, (a, u), axis=1)
    h = h + a_cum * h0[:, None, :]
    return h, h[:, -1]


def rwkv7_mix(pb, shift_buf, S0, mu, w0, w2, a0, a2, g2, k_k, k_a, r_k, ln_w, ln_b):
    Bn, T, _ = pb.shape
    prev, shift_new = token_shift(pb, shift_buf)
    xs = pb + (prev - pb) * mu
    o1, o2, o3 = D_B, 2 * D_B, 3 * D_B
    o4 = o3 + LORA_W
    o5 = o4 + LORA_A
    r, k, v = xs[..., :o1], xs[..., o1:o2], xs[..., o2:o3]
    wd, ad, gd = xs[..., o3:o4], xs[..., o4:o5], xs[..., o5:]
    w_log = -jax.nn.softplus(-(w0 + jnp.tanh(wd) @ w2)) - 0.5
    decay = jnp.exp(-jnp.exp(w_log))
    a = jax.nn.sigmoid(a0 + ad @ a2)
    g = jax.nn.sigmoid(gd) @ g2
    heads = lambda z: z.reshape(Bn, T, N_HEAD_B, HEAD_DIM)
    kk = heads(k * k_k)
    kk = kk / jnp.maximum(jnp.linalg.norm(kk, axis=-1, keepdims=True), 1e-12)
    k = k * (1.0 + (a - 1.0) * k_a)
    r_h, k_h, v_h, w_h, a_h = heads(r), heads(k), heads(v), heads(decay), heads(a)

    def step(S, inp):
        w_t, kk_t, kka_t, k_t, v_t, r_t = inp
        S = (S * w_t[:, :, None, :]
             - jnp.einsum('bhvk,bhk->bhv', S, kk_t)[..., None] * kka_t[:, :, None, :]
             + v_t[..., None] * k_t[:, :, None, :])
        return S, jnp.einsum('bhvk,bhk->bhv', S, r_t)

    seq_first = lambda z: jnp.moveaxis(z, 1, 0)
    S_new, y = lax.scan(step, S0, (seq_first(w_h), seq_first(kk), seq_first(kk * a_h),
                                   seq_first(k_h), seq_first(v_h), seq_first(r_h)))
    y = jnp.moveaxis(y, 0, 1)
    y = head_norm(y, GN_EPS_B) * ln_w.reshape(N_HEAD_B, HEAD_DIM) + ln_b.reshape(N_HEAD_B, HEAD_DIM)
    bonus = jnp.sum(r_h * k_h * r_k, axis=-1, keepdims=True) * v_h
    out = (y + bonus).reshape(Bn, T, D_B) * g
    return out, shift_new, S_new


def mlstm_chunk(q, k, v, i_pre, logf, C0, n0, m0):
    L = q.shape[2]
    b = jnp.cumsum(logf, axis=-1)
    causal = jnp.tril(jnp.ones((L, L), dtype=bool))
    log_w = jnp.where(causal, b[..., :, None] - b[..., None, :] + i_pre[..., None, :], -jnp.inf)
    log_inter = b + m0[..., None]
    m = jnp.maximum(log_inter, jnp.max(log_w, axis=-1))
    w_inter = jnp.exp(log_inter - m)
    scores = jnp.einsum('bhtn,bhsn->bhts', q, k) * jnp.exp(log_w - m[..., None])
    num = (w_inter[..., None] * jnp.einsum('bhtn,bhnv->bhtv', q, C0)
           + jnp.einsum('bhts,bhsv->bhtv', scores, v))
    den = w_inter * jnp.einsum('bhtn,bhn->bht', q, n0) + jnp.sum(scores, axis=-1)
    h = num / jnp.maximum(jnp.abs(den), jnp.exp(-m))[..., None]
    log_end = b[..., -1:] - b + i_pre
    log_inter_end = b[..., -1] + m0
    m_new = jnp.maximum(log_inter_end, jnp.max(log_end, axis=-1))
    w_end = jnp.exp(log_end - m_new[..., None])
    w0_end = jnp.exp(log_inter_end - m_new)
    C_new = w0_end[..., None, None] * C0 + jnp.einsum('bhs,bhsn,bhsv->bhnv', w_end, k, v)
    n_new = w0_end[..., None] * n0 + jnp.einsum('bhs,bhsn->bhn', w_end, k)
    return h, (C_new, n_new, m_new)


def mlstm_scan(q, k, v, i_pre, logf, C0, n0, m0):
    Bn, H, T, N = q.shape
    L = CHUNK if T % CHUNK == 0 else T
    nc = T // L
    chunks = lambda z: jnp.moveaxis(z.reshape(Bn, H, nc, L, *z.shape[3:]), 2, 0)

    def step(carry, inp):
        h, carry = mlstm_chunk(*inp, *carry)
        return carry, h

    state, hs = lax.scan(step, (C0, n0, m0),
                         (chunks(q), chunks(k), chunks(v), chunks(i_pre), chunks(logf)))
    return jnp.moveaxis(hs, 0, 2).reshape(Bn, H, T, N), state


def mlstm_mix(pc, conv_buf, C0, n0, m0, conv_w, conv_b, wq, wk, w_if, b_if, gn_w):
    Bn, T, _ = pc.shape
    xc, vc, zc = pc[..., :D_C], pc[..., D_C:2 * D_C], pc[..., 2 * D_C:]
    xconv, conv_new = causal_conv(xc, conv_buf, conv_w, conv_b)
    xact = jax.nn.silu(xconv).reshape(Bn, T, N_HEAD_C, HEAD_DIM)
    q = jnp.einsum('bthd,hde->bthe', xact, wq)
    k = jnp.einsum('bthd,hde->bthe', xact, wk)
    v = vc.reshape(Bn, T, N_HEAD_C, HEAD_DIM)
    gates = jnp.concatenate([q.reshape(Bn, T, D_C), k.reshape(Bn, T, D_C), vc], axis=-1) @ w_if + b_if
    i_pre = gates[..., :N_HEAD_C]
    logf = jax.nn.log_sigmoid(gates[..., N_HEAD_C:])
    bh = lambda z: jnp.moveaxis(z, 1, 2)
    h, (C_new, n_new, m_new) = mlstm_scan(bh(q), bh(k) * HEAD_DIM ** -0.5, bh(v),
                                          bh(i_pre), bh(logf), C0, n0, m0)
    h = jnp.moveaxis(h, 2, 1)
    h = head_norm(h, GN_EPS_C) * gn_w.reshape(N_HEAD_C, HEAD_DIM)
    out = jax.nn.sigmoid(zc) * h.reshape(Bn, T, D_C)
    return out, conv_new, C_new, n_new, m_new


def layer(x, conv_a, h_lru, shift_b, wkv, conv_c, mem_c, mem_n, mem_m,
          norm1, w_in, conv_a_w, conv_a_b, lru_wr, lru_br, lru_wi, lru_bi, lru_lambda, norm_a,
          rwkv_mu, rwkv_w0, rwkv_w2, rwkv_a0, rwkv_a2, rwkv_g2, rwkv_kk, rwkv_ka, rwkv_rk,
          rwkv_lnw, rwkv_lnb,
          conv_c_w, conv_c_b, mlstm_wq, mlstm_wk, mlstm_wif, mlstm_bif, mlstm_gn,
          w_out, norm2, w_ffn_in, w_ffn_out):
    f32 = jnp.float32
    hn = rmsnorm(x, norm1)
    proj = jnp.einsum('btd,de->bte', hn, w_in).astype(f32)
    pa_x, pa_g = proj[..., :D_A], proj[..., D_A:2 * D_A]
    pb = proj[..., 2 * D_A:2 * D_A + D_B_IN]
    pc = proj[..., 2 * D_A + D_B_IN:]
    xa, conv_a_new = causal_conv(pa_x, conv_a.astype(f32), conv_a_w, conv_a_b)
    h, h_last = rg_lru(xa, h_lru.astype(f32), lru_wr, lru_br, lru_wi, lru_bi, lru_lambda)
    ya = rmsnorm(h, norm_a) * jax.nn.gelu(pa_g)
    yb, shift_new, S_new = rwkv7_mix(pb, shift_b.astype(f32), wkv.astype(f32), rwkv_mu, rwkv_w0,
                                     rwkv_w2, rwkv_a0, rwkv_a2, rwkv_g2, rwkv_kk, rwkv_ka, rwkv_rk,
                                     rwkv_lnw, rwkv_lnb)
    yc, conv_c_new, C_new, n_new, m_new = mlstm_mix(pc, conv_c.astype(f32), mem_c.astype(f32),
                                                    mem_n.astype(f32), mem_m.astype(f32),
                                                    conv_c_w, conv_c_b, mlstm_wq, mlstm_wk,
                                                    mlstm_wif, mlstm_bif, mlstm_gn)
    y_mix = jnp.concatenate([ya, yb, yc], axis=-1).astype(x.dtype)
    x = x + jnp.einsum('bte,ed->btd', y_mix, w_out)
    hn2 = rmsnorm(x, norm2)
    gu = jnp.einsum('btd,df->btf', hn2, w_ffn_in)
    x = x + jnp.einsum('btf,fd->btd', jax.nn.silu(gu[..., :D_FF]) * gu[..., D_FF:], w_ffn_out)
    return x, (conv_a_new, h_last, shift_new, S_new, conv_c_new, C_new, n_new, m_new)


def setup_inputs(seed: int = 0) -> dict:
    key = jax.random.key(seed)
    ks = iter(jax.random.split(key, 64))
    nrm = lambda shape, scale: scale * jax.random.normal(next(ks), shape, jnp.float32)
    uni = lambda shape, lo, hi: jax.random.uniform(next(ks), shape, jnp.float32, lo, hi)
    P = DEPTH
    lru_s = uni((P, D_A), 0.9, 0.999) ** (1.0 / LRU_C)
    f_bias = jnp.linspace(3.0, 6.0, N_HEAD_C, dtype=jnp.float32)
    return {
        'x_prompt': nrm((BATCH, SEQ, D_MODEL), 1.0),
        'x_sample': nrm((DEC_BATCH, DEC_SEQ, D_MODEL), 1.0),
        'state_conv_a': nrm((P, DEC_BATCH, CONV_W - 1, D_A), 1.0),
        'state_lru': nrm((P, DEC_BATCH, D_A), 0.5),
        'state_shift_b': nrm((P, DEC_BATCH, 1, D_B_IN), 1.0),
        'state_wkv': nrm((P, DEC_BATCH, N_HEAD_B, HEAD_DIM, HEAD_DIM), 0.3),
        'state_conv_c': nrm((P, DEC_BATCH, CONV_W - 1, D_C), 1.0),
        'state_mem_c': nrm((P, DEC_BATCH, N_HEAD_C, HEAD_DIM, HEAD_DIM), 0.1),
        'state_mem_n': nrm((P, DEC_BATCH, N_HEAD_C, HEAD_DIM), 0.1),
        'state_mem_m': nrm((P, DEC_BATCH, N_HEAD_C), 1.0),
        'norm1': 1.0 + nrm((P, D_MODEL), 0.02),
        'w_in': nrm((P, D_MODEL, D_IN), D_MODEL ** -0.5),
        'conv_a_w': nrm((P, CONV_W, D_A), CONV_W ** -0.5),
        'conv_a_b': nrm((P, D_A), 0.01),
        'lru_wr': nrm((P, N_BLK_A, HEAD_DIM, HEAD_DIM), HEAD_DIM ** -0.5),
        'lru_br': nrm((P, D_A), 0.01),
        'lru_wi': nrm((P, N_BLK_A, HEAD_DIM, HEAD_DIM), HEAD_DIM ** -0.5),
        'lru_bi': nrm((P, D_A), 0.01),
        'lru_lambda': jnp.log(lru_s) - jnp.log1p(-lru_s),
        'norm_a': 1.0 + nrm((P, D_A), 0.02),
        'rwkv_mu': uni((P, D_B_IN), 0.0, 1.0),
        'rwkv_w0': uni((P, D_B), -6.0, 1.0),
        'rwkv_w2': nrm((P, LORA_W, D_B), 0.1 * LORA_W ** -0.5),
        'rwkv_a0': nrm((P, D_B), 0.1),
        'rwkv_a2': nrm((P, LORA_A, D_B), 0.1 * LORA_A ** -0.5),
        'rwkv_g2': nrm((P, LORA_G, D_B), LORA_G ** -0.5),
        'rwkv_kk': 1.0 + nrm((P, D_B), 0.1),
        'rwkv_ka': 1.0 + nrm((P, D_B), 0.1),
        'rwkv_rk': nrm((P, N_HEAD_B, HEAD_DIM), 0.1),
        'rwkv_lnw': 1.0 + nrm((P, D_B), 0.02),
        'rwkv_lnb': nrm((P, D_B), 0.01),
        'conv_c_w': nrm((P, CONV_W, D_C), CONV_W ** -0.5),
        'conv_c_b': nrm((P, D_C), 0.01),
        'mlstm_wq': nrm((P, N_HEAD_C, HEAD_DIM, HEAD_DIM), HEAD_DIM ** -0.5),
        'mlstm_wk': nrm((P, N_HEAD_C, HEAD_DIM, HEAD_DIM), HEAD_DIM ** -0.5),
        'mlstm_wif': nrm((P, 3 * D_C, 2 * N_HEAD_C), (3 * D_C) ** -0.5),
        'mlstm_bif': jnp.concatenate([nrm((P, N_HEAD_C), 0.1), f_bias + nrm((P, N_HEAD_C), 0.1)], axis=-1),
        'mlstm_gn': 1.0 + nrm((P, D_C), 0.02),
        'w_out': nrm((P, D_MIX, D_MODEL), D_MIX ** -0.5),
        'norm2': 1.0 + nrm((P, D_MODEL), 0.02),
        'w_ffn_in': nrm((P, D_MODEL, 2 * D_FF), D_MODEL ** -0.5),
        'w_ffn_out': nrm((P, D_FF, D_MODEL), D_FF ** -0.5),
        'norm_f': 1.0 + nrm((D_MODEL,), 0.02),
    }


def reference(x_prompt, x_sample, state_conv_a, state_lru, state_shift_b, state_wkv, state_conv_c,
              state_mem_c, state_mem_n, state_mem_m,
              norm1, w_in, conv_a_w, conv_a_b, lru_wr, lru_br, lru_wi, lru_bi, lru_lambda, norm_a,
              rwkv_mu, rwkv_w0, rwkv_w2, rwkv_a0, rwkv_a2, rwkv_g2, rwkv_kk, rwkv_ka, rwkv_rk,
              rwkv_lnw, rwkv_lnb,
              conv_c_w, conv_c_b, mlstm_wq, mlstm_wk, mlstm_wif, mlstm_bif, mlstm_gn,
              w_out, norm2, w_ffn_in, w_ffn_out, norm_f):
    layer_w = (norm1, w_in, conv_a_w, conv_a_b, lru_wr, lru_br, lru_wi, lru_bi, lru_lambda, norm_a,
               rwkv_mu, rwkv_w0, rwkv_w2, rwkv_a0, rwkv_a2, rwkv_g2, rwkv_kk, rwkv_ka, rwkv_rk,
               rwkv_lnw, rwkv_lnb,
               conv_c_w, conv_c_b, mlstm_wq, mlstm_wk, mlstm_wif, mlstm_bif, mlstm_gn,
               w_out, norm2, w_ffn_in, w_ffn_out)

    def run(x, states):
        per_layer = []
        for l in range(DEPTH):
            x, st = layer(x, *[s[l] for s in states], *[w[l] for w in layer_w])
            per_layer.append(st)
        stacked = tuple(jnp.stack([st[j] for st in per_layer]) for j in range(len(states)))
        return rmsnorm(x, norm_f), stacked

    f32 = jnp.float32
    Bp = x_prompt.shape[0]
    zero_states = (jnp.zeros((DEPTH, Bp, CONV_W - 1, D_A), f32),
                   jnp.zeros((DEPTH, Bp, D_A), f32),
                   jnp.zeros((DEPTH, Bp, 1, D_B_IN), f32),
                   jnp.zeros((DEPTH, Bp, N_HEAD_B, HEAD_DIM, HEAD_DIM), f32),
                   jnp.zeros((DEPTH, Bp, CONV_W - 1, D_C), f32),
                   jnp.zeros((DEPTH, Bp, N_HEAD_C, HEAD_DIM, HEAD_DIM), f32),
                   jnp.zeros((DEPTH, Bp, N_HEAD_C, HEAD_DIM), f32),
                   jnp.zeros((DEPTH, Bp, N_HEAD_C), f32))
    y_prompt, (p_conv_a, p_lru, p_shift_b, p_wkv, p_conv_c, p_mem_c, p_mem_n, p_mem_m) = run(
        x_prompt, zero_states)
    y_sample, (s_conv_a, s_lru, s_shift_b, s_wkv, s_conv_c, s_mem_c, s_mem_n, s_mem_m) = run(
        x_sample, (state_conv_a, state_lru, state_shift_b, state_wkv, state_conv_c,
                   state_mem_c, state_mem_n, state_mem_m))
    return (y_prompt, y_sample,
            p_conv_a, p_lru, p_shift_b, p_wkv, p_conv_c, p_mem_c, p_mem_n, p_mem_m,
            s_conv_a, s_lru, s_shift_b, s_wkv, s_conv_c, s_mem_c, s_mem_n, s_mem_m)
```

```python
import math
from contextlib import ExitStack
import numpy as np
import concourse.bass as bass
import concourse.mybir as mybir
from concourse.bass_utils import run_bass_kernel_spmd
from concourse.alu_op_type import AluOpType as ALU

AF = mybir.ActivationFunctionType
F32 = mybir.dt.float32
BF16 = mybir.dt.bfloat16
AX = mybir.AxisListType

EPOCH = 16000
NDMA_SLOTS = 10
SAME_ENGINE_SYNC = True

D = 1024
DIN = 3072
DA = 256
DB = 384
DBIN = 1408
DC = 384
DFF = 2816
NCORES = 8
TSMP = 32
C_DEC = math.exp(-0.5)
RMS_EPS = 1e-6
GN_EPS_B = 64e-5
GN_EPS_C = 1e-6


class Prog:
    ENG = ['pe', 'act', 'dve', 'pool', 'sp']

    def __init__(self):
        self.stream = {e: [] for e in self.ENG}
        self.n = {e: 0 for e in self.ENG}
        self.lastw = {}
        self.readers = {}
        self.seen = {e: {} for e in self.ENG}
        self.dma_slot_next = {e: 0 for e in self.ENG}
        self.dma_slot_val = {}
        self.out_tokens = []

    def _deps(self, eng, reads, writes, extra=()):
        toks = list(extra)
        for k in reads:
            t = self.lastw.get(k)
            if t:
                toks.append(t)
        for k in writes:
            t = self.lastw.get(k)
            if t:
                toks.append(t)
            toks.extend(self.readers.get(k, {}).values())
        need = {}
        for (src, val) in toks:
            if need.get(src, 0) < val:
                need[src] = val
        out = []
        for src, val in need.items():
            if src == ('e', eng) and not SAME_ENGINE_SYNC:
                continue
            if self.seen[eng].get(src, 0) >= val:
                continue
            self.seen[eng][src] = val
            out.append((src, val))
        return out

    def op(self, eng, fn, w=(), r=()):
        w = list(w) + [k for k in r if k.startswith('PS') and k[2:].isdigit()]
        waits = self._deps(eng, r, w)
        self.n[eng] += 1
        tok = (('e', eng), self.n[eng])
        for k in w:
            self.lastw[k] = tok
            self.readers[k] = {}
        for k in r:
            self.readers.setdefault(k, {})[('e', eng)] = tok
        self.stream[eng].append((waits, fn, tok))
        return tok

    def dma(self, q, fn, w=(), r=(), is_output=False):
        slot = (q, self.dma_slot_next[q] % NDMA_SLOTS)
        self.dma_slot_next[q] += 1
        src = ('d', slot)
        prev = self.dma_slot_val.get(slot, 0)
        extra = [(src, prev)] if prev else []
        waits = self._deps(q, r, w, extra)
        val = prev + 16
        self.dma_slot_val[slot] = val
        tok = (src, val)
        for k in w:
            self.lastw[k] = tok
            self.readers[k] = {}
        for k in r:
            self.readers.setdefault(k, {})[src] = tok
        self.stream[q].append((waits, fn, tok))
        if is_output:
            self.out_tokens.append(tok)
        return tok

    def finish(self):
        need = {}
        for (src, val) in self.out_tokens:
            need[src] = max(need.get(src, 0), val)
        self.stream['sp'].append((list(need.items()), None, None))

    def emit(self, nc, stack):
        sems = {}

        def getsem(src, val):
            if src[0] == 'e':
                ep = (val - 1) // EPOCH
                key = (src, ep)
                v = val - ep * EPOCH
            else:
                key = (src, 0)
                v = val
            if key not in sems:
                sems[key] = stack.enter_context(nc.semaphore("s%d" % len(sems)))
            return sems[key], v

        for e in self.ENG:
            for (waits, fn, tok) in self.stream[e]:
                for (src, val) in waits:
                    getsem(src, val)
                if tok is not None:
                    getsem(*tok)
        block = stack.enter_context(nc.Block())
        names = {'pe': 'tensor', 'act': 'scalar', 'dve': 'vector', 'pool': 'gpsimd', 'sp': 'sync'}
        for e in self.ENG:
            items = self.stream[e]
            if not items:
                continue

            def body(engh, items=items):
                for (waits, fn, tok) in items:
                    for (src, val) in waits:
                        s, v = getsem(src, val)
                        engh.wait_ge(s, v)
                    if fn is None:
                        continue
                    ins = fn(engh)
                    s, v = getsem(*tok)
                    ins.then_inc(s, 16 if tok[0][0] == 'd' else 1)
            getattr(block, names[e])(body)
        self.nsems = len(sems)


def _consts(TS):
    cw = {}
    cols = []

    def add(name, arr):
        a = np.zeros((128, arr.shape[1]), np.float32)
        a[:arr.shape[0]] = arr
        cw[name] = (sum(c.shape[1] for c in cols), arr.shape[1])
        cols.append(a)
    add('ident', np.eye(128, dtype=np.float32))
    bd = np.zeros((128, 128), np.float32)
    bd[:64, :64] = 1
    bd[64:, 64:] = 1
    add('onesbd', bd)
    add('ident2', np.concatenate([np.eye(64, dtype=np.float32)] * 2, 0))
    r = np.arange(128)[:, None] % 64
    c = np.arange(64)[None, :]
    su = (c > r).astype(np.float32)
    sl = (c < r).astype(np.float32)
    ui = (c >= r).astype(np.float32)
    add('mask5', np.concatenate([-su, -sl, ui, su, ui], 1))
    negm = np.where(np.arange(64)[:, None] > c, -30000.0, 0.0).astype(np.float32)
    add('negm6', np.tile(negm, (1, 6)))
    cm = np.ones((128, TS), np.float32)
    cm[:, ::64] = 0
    add('cmask', cm)
    sel6 = np.zeros((6, 6, 64), np.float32)
    for h in range(6):
        sel6[h, h, :] = 1
    add('sel6', sel6.reshape(6, 384))
    selb = np.zeros((6, 128), np.float32)
    for k in range(6):
        selb[k, (k % 2) * 64:(k % 2) * 64 + 64] = 1
    add('selb', selb)
    ps_ = np.zeros((6, 3), np.float32)
    for k in range(6):
        ps_[k, k // 2] = 1
    add('pairsel', ps_)
    return np.concatenate(cols, 1), cw


WNAMES = ['norm1', 'w_in', 'conv_a_w', 'conv_a_b', 'lru_wr', 'lru_br', 'lru_wi', 'lru_bi', 'lru_lambda',
          'norm_a', 'rwkv_mu', 'rwkv_w0', 'rwkv_w2', 'rwkv_a0', 'rwkv_a2', 'rwkv_g2', 'rwkv_kk', 'rwkv_ka',
          'rwkv_rk', 'rwkv_lnw', 'rwkv_lnb', 'conv_c_w', 'conv_c_b', 'mlstm_wq', 'mlstm_wk', 'mlstm_wif',
          'mlstm_bif', 'mlstm_gn', 'w_out', 'norm2', 'w_ffn_in', 'w_ffn_out', 'norm_f']


def build(DEPTH, TP, TS, SMDT=F32, jobs=('p', 's')):
    assert TP % TS == 0 and TS % 128 == 0
    nc = bass.Bass("TRN2", target_bir_lowering=False)
    cnp, cw = _consts(TS)
    CW = cnp.shape[1]

    def din(name, shape):
        return nc.dram_tensor(name, list(shape), F32, kind="ExternalInput").ap()

    def dout(name, shape):
        return nc.dram_tensor(name, list(shape), F32, kind="ExternalOutput").ap()

    I = {}
    I['xp'] = din('xp', [TP, D])
    I['xs'] = din('xs', [TSMP, D])
    I['consts'] = din('consts', [128, CW])
    st_shapes = {'conv_a': [DEPTH, 3, DA], 'lru': [DEPTH, DA], 'shift': [DEPTH, DBIN], 'wkv': [DEPTH, 6, 64, 64],
                 'conv_c': [DEPTH, 3, DC], 'mem_c': [DEPTH, 6, 64, 64], 'mem_n': [DEPTH, 6, 64], 'mem_m': [DEPTH, 6]}
    for k, s in st_shapes.items():
        I['s_' + k] = din('s_' + k, s)
    wshapes = {'norm1': [DEPTH, D], 'w_in': [DEPTH, D, DIN], 'conv_a_w': [DEPTH, 4, DA], 'conv_a_b': [DEPTH, DA],
               'lru_wr': [DEPTH, 4, 64, 64], 'lru_br': [DEPTH, DA], 'lru_wi': [DEPTH, 4, 64, 64], 'lru_bi': [DEPTH, DA],
               'lru_lambda': [DEPTH, DA], 'norm_a': [DEPTH, DA], 'rwkv_mu': [DEPTH, DBIN], 'rwkv_w0': [DEPTH, DB],
               'rwkv_w2': [DEPTH, 64, DB], 'rwkv_a0': [DEPTH, DB], 'rwkv_a2': [DEPTH, 64, DB], 'rwkv_g2': [DEPTH, 128, DB],
               'rwkv_kk': [DEPTH, DB], 'rwkv_ka': [DEPTH, DB], 'rwkv_rk': [DEPTH, 6, 64], 'rwkv_lnw': [DEPTH, DB],
               'rwkv_lnb': [DEPTH, DB], 'conv_c_w': [DEPTH, 4, DC], 'conv_c_b': [DEPTH, DC], 'mlstm_wq': [DEPTH, 6, 64, 64],
               'mlstm_wk': [DEPTH, 6, 64, 64], 'mlstm_wif': [DEPTH, 3 * DC, 12], 'mlstm_bif': [DEPTH, 12],
               'mlstm_gn': [DEPTH, DC], 'w_out': [DEPTH, D, D], 'norm2': [DEPTH, D], 'w_ffn_in': [DEPTH, D, 2 * DFF],
               'w_ffn_out': [DEPTH, DFF, D], 'norm_f': [D]}
    for k in WNAMES:
        I[k] = din(k, wshapes[k])
    O = {}
    O['yp'] = dout('yp', [TP, D])
    O['ys'] = dout('ys', [TSMP, D])
    for jb in ('p', 's'):
        for k, s in st_shapes.items():
            O[jb + '_' + k] = dout('o' + jb + '_' + k, s)

    P = Prog()
    with ExitStack() as st:
        def sb(name, shape, dt=F32):
            return st.enter_context(nc.sbuf_tensor(name, list(shape), dt))

        def psum(name, shape, dt=F32):
            return st.enter_context(nc.psum_tensor(name, list(shape), dt))

        def tt(eng, out, in0, in1, op, w, r):
            P.op(eng, lambda e: e.tensor_tensor(out=out, in0=in0, in1=in1, op=op), w, r)

        def ts(eng, out, in0, s1, s2, op0, op1, w, r):
            if op1 is None:
                P.op(eng, lambda e: e.tensor_scalar(out=out, in0=in0, scalar1=s1, scalar2=None, op0=op0), w, r)
            else:
                P.op(eng, lambda e: e.tensor_scalar(out=out, in0=in0, scalar1=s1, scalar2=s2, op0=op0, op1=op1), w, r)

        def stt(out, in0, scalar, in1, op0, op1, w, r):
            P.op('dve', lambda e: e.scalar_tensor_tensor(out=out, in0=in0, scalar=scalar, in1=in1, op0=op0, op1=op1), w, r)

        def act(out, in_, func, w, r, scale=1.0, bias=0.0):
            P.op('act', lambda e: e.activation(out=out, in_=in_, func=func, scale=scale, bias=bias), w, r)

        def cpy(eng, out, in_, w, r):
            if eng == 'act':
                P.op('act', lambda e: e.activation(out=out, in_=in_, func=AF.Copy), w, r)
            else:
                P.op(eng, lambda e: e.tensor_copy(out=out, in_=in_), w, r)

        def mm(out, lhsT, rhs, start, stop, w, r):
            P.op('pe', lambda e: e.matmul(out, lhsT=lhsT, rhs=rhs, start=start, stop=stop), w, r)

        def tp(out, in_, ident, w, r):
            P.op('pe', lambda e: e.transpose(out, in_, ident), w, r)

        def recip(out, in_, w, r):
            P.op('dve', lambda e: e.reciprocal(out=out, in_=in_), w, r)

        def scan(out, d0, d1, init, op0, op1, w, r):
            P.op('dve', lambda e: e.tensor_tensor_scan(out=out, data0=d0, data1=d1, initial=init, op0=op0, op1=op1), w, r)

        def red(out, in_, w, r):
            P.op('dve', lambda e: e.tensor_reduce(out=out, in_=in_, axis=AX.X, op=ALU.add), w, r)

        def memset(eng, ap, val, w):
            P.op(eng, lambda e: e.memset(ap, val), w, ())

        def dma(q, out, in_, w=(), r=(), slow=False, is_output=False):
            if slow:
                P.dma(q, lambda e: e.dma_start(out=out, in_=in_, allow_slow_non_contiguous=True), w, r, is_output)
            else:
                P.dma(q, lambda e: e.dma_start(out=out, in_=in_), w, r, is_output)

        CT = sb('CT', [128, CW])
        dma('sp', CT[:], I['consts'], w=['CT'])

        def cst(name, rows=128):
            o, n = cw[name]
            return CT[0:rows, o:o + n]
        ident = cst('ident')
        ident2 = cst('ident2')
        onesbd_f = cst('onesbd')
        cmask = cst('cmask')
        onesb = sb('onesb', [128, 128], BF16)
        memset('dve', onesb[:], 1.0, ['onesb'])
        onesbd_b = sb('onesbd_b', [128, 128], BF16)
        cpy('dve', onesbd_b[:], onesbd_f, ['onesbd_b'], ['CT'])
        ones_c = sb('ones_c', [128, 1])
        memset('dve', ones_c[:], 1.0, ['ones_f'])
        if SMDT == F32:
            mask5 = cst('mask5').rearrange("p (b l) -> p b l", b=5)
            ident_s = ident
            mk5key = 'CT'
        else:
            mask5t = sb('mask5t', [128, 5, 64], SMDT)
            cpy('dve', mask5t[:], cst('mask5').rearrange("p (b l) -> p b l", b=5), ['mask5t'], ['CT'])
            mask5 = mask5t[:]
            ident_st = sb('ident_st', [128, 128], SMDT)
            cpy('dve', ident_st[:], ident, ['ident_st'], ['CT'])
            ident_s = ident_st[:]
            mk5key = 'mask5t'
        negm6 = cst('negm6', 64).rearrange("p (h l) -> p h l", h=6)
        sel6 = cst('sel6', 6).rearrange("p (h l) -> p h l", h=6)
        selb = cst('selb', 6)
        pairsel = cst('pairsel', 6)

        NV = 88
        PV = sb('PV', [128, DEPTH, NV])
        NF = sb('NF', [128, 8])
        BI = sb('BI', [6, DEPTH])
        NBF = sb('NBF', [6, DEPTH])

        def pvload(name, col, n):
            for l in range(DEPTH):
                dma('sp', PV[:, l, col:col + n], I[name][l].rearrange("(c p) -> p c", p=128), w=['PV'], slow=True)
        pvload('norm1', 0, 8)
        pvload('norm2', 8, 16 - 8)
        for l in range(DEPTH):
            for j in range(4):
                dma('sp', PV[:, l, 16 + 2 * j:18 + 2 * j], I['conv_a_w'][l, j].rearrange("(c p) -> p c", p=128), w=['PV'], slow=True)
                dma('sp', PV[:, l, 66 + 3 * j:69 + 3 * j], I['conv_c_w'][l, j].rearrange("(c p) -> p c", p=128), w=['PV'], slow=True)
        pvload('conv_a_b', 24, 2)
        pvload('lru_br', 26, 2)
        pvload('lru_bi', 28, 2)
        pvload('lru_lambda', 30, 2)
        pvload('norm_a', 32, 2)
        pvload('rwkv_mu', 34, 11)
        pvload('rwkv_w0', 45, 3)
        pvload('rwkv_a0', 48, 3)
        pvload('rwkv_kk', 51, 3)
        pvload('rwkv_ka', 54, 3)
        for l in range(DEPTH):
            dma('sp', PV[:, l, 57:60], I['rwkv_rk'][l].rearrange("(c hh) n -> (hh n) c", hh=2), w=['PV'], slow=True)
        pvload('rwkv_lnw', 60, 3)
        pvload('rwkv_lnb', 63, 3)
        pvload('conv_c_b', 78, 3)
        pvload('mlstm_gn', 81, 3)
        dma('sp', NF[:], I['norm_f'].rearrange("(c p) -> p c", p=128), w=['NF'], slow=True)
        dma('sp', BI[:], I['mlstm_bif'][:, 0:6].rearrange("l h -> h l"), w=['BI'], slow=True)
        dma('sp', NBF[:], I['mlstm_bif'][:, 6:12].rearrange("l h -> h l"), w=['NBF'], slow=True)
        ts('dve', NBF[:], NBF[:], -1.0, None, ALU.mult, None, ['NBF'], ['NBF'])
        SPT = sb('SPT', [128, DEPTH, 2])
        act(SPT[:], PV[:, :, 30:32], AF.Exp, ['SPT'], ['PV'], scale=-1.0)
        act(SPT[:], SPT[:], AF.Ln, ['SPT'], ['SPT'], bias=1.0)
        ts('dve', PV[:, :, 84:86], SPT[:], -8.0, None, ALU.mult, None, ['PV'], ['SPT'])
        ts('dve', PV[:, :, 86:88], SPT[:], -16.0, None, ALU.mult, None, ['PV'], ['SPT'])

        WRbd = sb('WRbd', [128, DEPTH, 2, 128])
        WIbd = sb('WIbd', [128, DEPTH, 2, 128])
        memset('dve', WRbd[:], 0.0, ['WRbd'])
        memset('dve', WIbd[:], 0.0, ['WIbd'])
        WQbd = sb('WQbd', [128, DEPTH, 3, 128], BF16)
        WKbd = sb('WKbd', [128, DEPTH, 3, 128], BF16)
        memset('dve', WQbd[:], 0.0, ['WQbd'])
        memset('dve', WKbd[:], 0.0, ['WKbd'])
        W2 = sb('W2', [128, DEPTH, DB], BF16)
        A2 = sb('A2', [128, DEPTH, DB], BF16)
        G2 = sb('G2', [128, DEPTH, DB], BF16)
        WIF = sb('WIF', [128, DEPTH, 9, 12], SMDT)
        for l in range(DEPTH):
            for n in range(4):
                hb, c = n % 2, n // 2
                dma('sp', WRbd[64 * hb:64 * hb + 64, l, c, 64 * hb:64 * hb + 64], I['lru_wr'][l, n], w=['WRbd'])
                dma('sp', WIbd[64 * hb:64 * hb + 64, l, c, 64 * hb:64 * hb + 64], I['lru_wi'][l, n], w=['WIbd'])
            for h in range(6):
                hb, c = h % 2, h // 2
                dma('pool', WQbd[64 * hb:64 * hb + 64, l, c, 64 * hb:64 * hb + 64], I['mlstm_wq'][l, h], w=['WQbd'])
                dma('pool', WKbd[64 * hb:64 * hb + 64, l, c, 64 * hb:64 * hb + 64], I['mlstm_wk'][l, h], w=['WKbd'])
            dma('pool', W2[0:64, l, :], I['rwkv_w2'][l], w=['W2'])
            dma('pool', A2[64:128, l, :], I['rwkv_a2'][l], w=['A2'])
            dma('pool', G2[:, l, :], I['rwkv_g2'][l], w=['G2'])
            dma('pool' if SMDT != F32 else 'sp', WIF[:, l, :, :], I['mlstm_wif'][l].rearrange("(kc p) n -> p kc n", p=128), w=['WIF'])
        ts('dve', WIF[:, :, 3:6, :], WIF[:, :, 3:6, :], 8.0, None, ALU.mult, None, ['WIF'], ['WIF'])

        HISTA = sb('HISTA', [128, DEPTH, 2, 3])
        HISTB = sb('HISTB', [128, DEPTH, 11, 1])
        HISTC = sb('HISTC', [128, DEPTH, 3, 3])
        HL = sb('HL', [128, DEPTH, 2])
        HS = sb('HS', [128, DEPTH, 3, 64])
        CA = sb('CA', [128, DEPTH, 3, 65])
        MS = sb('MS', [6, DEPTH])
        WS = sb('WS', [64, 6, 64])

        X = sb('X', [128, 8, TS])
        HN = sb('HN', [128, 8, TS], BF16)
        NSLOT = 11
        PJ = sb('PJ', [128, NSLOT, 3 + TS])
        YM = sb('YM', [128, 8, TS], BF16)
        assert NSLOT * (3 + TS) * 4 >= 22 * TS * 2
        ACTT = PJ[:].rearrange("p s c -> p (s c)").bitcast(BF16)[:, 0:22 * TS].rearrange("p (f t) -> p f t", f=22)
        def actk(f):
            b0, b1 = f * TS * 2, (f + 1) * TS * 2 - 1
            sl_ = (3 + TS) * 4
            return ['PJ%d' % s_ for s_ in range(b0 // sl_, b1 // sl_ + 1)]
        NWB = 2
        WB = [sb('WB%d' % i, [128, 4096], BF16) for i in range(NWB)]
        NTF = 14
        TF = [sb('TF%d' % i, [128, TS]) for i in range(NTF)]
        NTB = 5
        TB = [sb('TB%d' % i, [128, TS], BF16) for i in range(NTB)]
        KR = sb('KR', [128, TS // 64, 2, 64], SMDT)
        KT_ = sb('KT_', [128, TS], SMDT)
        BT_ = sb('BT_', [128, TS], SMDT)
        VS_ = sb('VS_', [128, TS], SMDT)
        W5 = sb('W5', [64, 2, 5, 64], SMDT)
        QMa = sb('QMa', [64, 2, 2, 64], SMDT)
        QMb = sb('QMb', [64, 2, 2, 64], SMDT)
        TTa = sb('TTa', [64, 2, 2, 64], SMDT)
        TTb = sb('TTb', [64, 2, 2, 64], SMDT)
        TM = sb('TM', [64, 3, 128], SMDT)
        XN = sb('XN', [64, 2, 64], SMDT)
        UU = sb('UU', [64, 2, 64], SMDT)
        HG = sb('HG', [128, 64])
        assert SMDT == F32
        HSs = None
        VA = sb('VA', [64, 6, 65], SMDT)
        KTm = sb('KTm', [64, 6, 64], SMDT)
        WE = sb('WE', [64, 2, 6])
        DD = sb('DD', [64, 6, 64])
        STt = sb('STt', [64, 6, 64], SMDT)
        T1 = sb('T1', [64, 6, 65])
        NUM = sb('NUM', [64, 6, 65])
        DEN = sb('DEN', [64, 6])
        HTt = sb('HTt', [64, 6, 64])
        HC = sb('HC', [64, 6, 64])
        SQ = sb('SQ', [64, 6, 64])
        MEAN = sb('MEAN', [64, 6])
        VAR = sb('VAR', [64, 6])
        VW = sb('VW', [64, 6, 65], SMDT)
        WLE = sb('WLE', [6, 3])
        W0 = sb('W0', [128, 3])
        CT2 = sb('CT2', [128, 3, 65])
        RS = TF[13]
        if TS >= 512:
            STG = [(TF[11], 'TF11'), (TF[12], 'TF12')]
        else:
            STG = [(sb('STG0', [128, 512]), 'STG0'), (sb('STG1', [128, 512]), 'STG1')]
        PS = [psum('PS%d' % i, [128, 512]) for i in range(8)]

        memset('dve', VA[:], 1.0, ['VA'])

        wstate = {'i': 0}

        def wload(parts):
            i = wstate['i'] % NWB
            wstate['i'] += 1
            key = 'WB%d' % i
            for (c0, KC, ncol, src) in parts:
                dst = WB[i][:, c0:c0 + KC * ncol].rearrange("p (k n) -> p k n", k=KC)
                dma('pool', dst, src, w=[key])
            return WB[i], key

        def win_src(l, c0, ncol):
            return I['w_in'][l][:, c0:c0 + ncol].rearrange("(k p) n -> p k n", p=128)

        def rmsnorm_to(T, gcol_base, l, out_fn):
            for k in range(8):
                act(TB[0][:, 0:T] if k % 2 == 0 else TB[1][:, 0:T], X[:, k, 0:T], AF.Square,
                    ['TB%d' % (k % 2)], ['X%d' % k])
                mm(PS[0][:, 0:T], onesb[:], TB[k % 2][:, 0:T], k == 0, k == 7, ['PS0'], ['onesb', 'TB%d' % (k % 2)])
            act(RS[:, 0:T], PS[0][:, 0:T], AF.Sqrt, ['TF13'], ['PS0'], scale=1.0 / D, bias=RMS_EPS)
            recip(RS[:, 0:T], RS[:, 0:T], ['TF13'], ['TF13'])
            for k in range(8):
                out_fn(k)

        def inproj(l, T, wt, wkey, ncols_tile, chunk_list):
            for ci, (cit, slot) in enumerate(chunk_list):
                bank = ci % 4
                for k in range(8):
                    mm(PS[bank][:, 0:T], wt[:, k * ncols_tile + cit * 128: k * ncols_tile + cit * 128 + 128],
                       HN[:, k, 0:T], k == 0, k == 7, ['PS%d' % bank], [wkey] + ['HN%d' % k])
                cpy('act' if ci % 2 == 0 else 'dve', PJ[:, slot, 3:3 + T], PS[bank][:, 0:T], ['PJ%d' % slot], ['PS%d' % bank])

        def pv(l, c):
            return PV[:, l, c:c + 1]

        def task(job, l, T, L, tok0, first_seg, last_seg, xin, yout, last_layer):
            NCH = T // L
            if l == 0:
                nblk = (T + 127) // 128
                for tb in range(nblk):
                    n = min(128, T - tb * 128)
                    for half in range(2):
                        xt, xk = STG[half]
                        bank = 4 + half
                        dma('sp', xt[0:n, 0:512], xin[tok0 + tb * 128: tok0 + tb * 128 + n, half * 512:(half + 1) * 512], w=[xk])
                        for mmi in range(4):
                            tp(PS[bank][:, mmi * 128: mmi * 128 + n], xt[0:n, mmi * 128:(mmi + 1) * 128], ident[0:n, 0:n],
                               ['PS%d' % bank], [xk, 'CT'])
                        for mmi in range(4):
                            m = half * 4 + mmi
                            cpy('act' if mmi % 2 else 'dve', X[:, m, tb * 128: tb * 128 + n],
                                PS[bank][:, mmi * 128: mmi * 128 + n], ['X%d' % m], ['PS%d' % bank])
            def o1(k):
                stt(HN[:, k, 0:T], X[:, k, 0:T], pv(l, k), RS[:, 0:T], ALU.mult, ALU.mult, ['HN%d' % k], ['X%d' % k, 'PV', 'TF13'])
            rmsnorm_to(T, 0, l, o1)

            wt, wkey = wload([(0, 8, 512, win_src(l, 0, 512))])
            cpy('dve', PJ[:, 0:2, 0:3], HISTA[:, l, :, :], ['PJ0', 'PJ1'], ['HISTA'])
            inproj(l, T, wt, wkey, 512, [(0, 0), (1, 1), (2, 2), (3, 3)])
            cpy('dve', HISTA[:, l, :, :], PJ[:, 0:2, T:T + 3], ['HISTA'], ['PJ0', 'PJ1'])
            for c in range(2):
                xa, gr, gi, aa, a2t, mu_, uu, hh = TF[0], TF[1], TF[2], TF[3], TF[4], TF[5], TF[6], TF[7 + c]
                act(xa[:, 0:T], PJ[:, c, 3:3 + T], AF.Identity, ['TF0'], ['PJ%d' % c, 'PV'],
                    scale=pv(l, 16 + 2 * 3 + c), bias=pv(l, 24 + c))
                for j in range(3):
                    stt(xa[:, 0:T], PJ[:, c, j:j + T], pv(l, 16 + 2 * j + c), xa[:, 0:T], ALU.mult, ALU.add,
                        ['TF0'], ['TF0', 'PJ%d' % c, 'PV'])
                mm(PS[0][:, 0:T], WRbd[:, l, c, :], xa[:, 0:T], True, True, ['PS0'], ['WRbd', 'TF0'])
                mm(PS[1][:, 0:T], WIbd[:, l, c, :], xa[:, 0:T], True, True, ['PS1'], ['WIbd', 'TF0'])
                act(gr[:, 0:T], PS[0][:, 0:T], AF.Sigmoid, ['TF1'], ['PS0', 'PV'], bias=pv(l, 26 + c))
                act(gi[:, 0:T], PS[1][:, 0:T], AF.Sigmoid, ['TF2'], ['PS1', 'PV'], bias=pv(l, 28 + c))
                act(aa[:, 0:T], gr[:, 0:T], AF.Exp, ['TF3'], ['TF1', 'PV'], scale=pv(l, 84 + c))
                act(a2t[:, 0:T], gr[:, 0:T], AF.Exp, ['TF4'], ['TF1', 'PV'], scale=pv(l, 86 + c))
                act(mu_[:, 0:T], a2t[:, 0:T], AF.Sqrt, ['TF5'], ['TF4'], scale=-1.0, bias=1.0)
                tt('dve', uu[:, 0:T], gi[:, 0:T], xa[:, 0:T], ALU.mult, ['TF6'], ['TF2', 'TF0'])
                tt('dve', uu[:, 0:T], uu[:, 0:T], mu_[:, 0:T], ALU.mult, ['TF6'], ['TF6', 'TF5'])
                scan(hh[:, 0:T], aa[:, 0:T], uu[:, 0:T], HL[:, l, c:c + 1], ALU.mult, ALU.add,
                     ['TF%d' % (7 + c)], ['TF3', 'TF6', 'HL'])
                cpy('dve', HL[:, l, c:c + 1], hh[:, T - 1:T], ['HL'], ['TF%d' % (7 + c)])
                act(TB[c][:, 0:T], hh[:, 0:T], AF.Square, ['TB%d' % c], ['TF%d' % (7 + c)])
            for c in range(2):
                mm(PS[2][:, 0:T], onesb[:], TB[c][:, 0:T], c == 0, c == 1, ['PS2'], ['onesb', 'TB%d' % c])
            act(TF[9][:, 0:T], PS[2][:, 0:T], AF.Sqrt, ['TF9'], ['PS2'], scale=1.0 / DA, bias=RMS_EPS)
            recip(TF[9][:, 0:T], TF[9][:, 0:T], ['TF9'], ['TF9'])
            for c in range(2):
                g = PJ[:, 2 + c, 3:3 + T]
                gk = 'PJ%d' % (2 + c)
                t1, t2 = TF[0], TF[1]
                act(t1[:, 0:T], g, AF.Square, ['TF0'], [gk])
                ts('dve', t1[:, 0:T], t1[:, 0:T], 0.044715, 1.0, ALU.mult, ALU.add, ['TF0'], ['TF0'])
                tt('dve', t1[:, 0:T], t1[:, 0:T], g, ALU.mult, ['TF0'], ['TF0', gk])
                act(t2[:, 0:T], t1[:, 0:T], AF.Sigmoid, ['TF1'], ['TF0'], scale=2.0 * math.sqrt(2.0 / math.pi))
                tt('dve', t2[:, 0:T], t2[:, 0:T], g, ALU.mult, ['TF1'], ['TF1', gk])
                stt(t1[:, 0:T], TF[7 + c][:, 0:T], pv(l, 32 + c), TF[9][:, 0:T], ALU.mult, ALU.mult,
                    ['TF0'], ['TF%d' % (7 + c), 'PV', 'TF9'])
                tt('dve', YM[:, c, 0:T], t1[:, 0:T], t2[:, 0:T], ALU.mult, ['YM%d' % c], ['TF0', 'TF1'])

            cpy('dve', PJ[:, 0:11, 2:3], HISTB[:, l, :, :], ['PJ%d' % s for s in range(0, 11)], ['HISTB'])
            for (c0, nch_, s0) in ((512, 4, 0), (1024, 4, 4), (1536, 3, 8)):
                wt, wkey = wload([(0, 8, nch_ * 128, win_src(l, c0, nch_ * 128))])
                inproj(l, T, wt, wkey, nch_ * 128, [(i, s0 + i) for i in range(nch_)])
            cpy('dve', HISTB[:, l, :, :], PJ[:, 0:11, T + 2:T + 3], ['HISTB'], ['PJ%d' % s for s in range(0, 11)])

            def mix(out, slot, mcol, w):
                tt('dve', out, PJ[:, slot, 2:2 + T], PJ[:, slot, 3:3 + T], ALU.subtract, w, ['PJ%d' % slot])
                stt(out, out, pv(l, 34 + mcol), PJ[:, slot, 3:3 + T], ALU.mult, ALU.add, w, w + ['PJ%d' % slot, 'PV'])
            LW, SG = TB[2], TB[3]
            mix(TF[0][:, 0:T], 9, 9, ['TF0'])
            act(LW[0:64, 0:T], TF[0][0:64, 0:T], AF.Tanh, ['TB2'], ['TF0'])
            act(LW[64:128, 0:T], TF[0][64:128, 0:T], AF.Copy, ['TB2'], ['TF0'])
            mix(TF[0][:, 0:T], 10, 10, ['TF0'])
            act(SG[:, 0:T], TF[0][:, 0:T], AF.Sigmoid, ['TB3'], ['TF0'])
            for j in range(3):
                R, K, V, SGW, A, G, KK, RN, K2, BV, BON, CL = [TF[i] for i in range(12)]
                kR, kK, kV, kSGW, kA, kG, kKK, kRN, kK2, kBV, kBON, kCL = ['TF%d' % i for i in range(12)]
                Y = TF[12]
                mix(R[:, 0:T], j, j, [kR])
                mix(K[:, 0:T], 3 + j, 3 + j, [kK])
                mix(V[:, 0:T], 6 + j, 6 + j, [kV])
                jc = slice(j * 128, (j + 1) * 128)
                mm(PS[0][:, 0:T], W2[0:64, l, jc], LW[0:64, 0:T], True, True, ['PS0'], ['W2', 'TB2'])
                act(SGW[:, 0:T], PS[0][:, 0:T], AF.Sigmoid, [kSGW], ['PS0', 'PV'], bias=pv(l, 45 + j))
                mm(PS[1][:, 0:T], A2[64:128, l, jc], LW[64:128, 0:T], True, True, ['PS1'], ['A2', 'TB2'])
                act(A[:, 0:T], PS[1][:, 0:T], AF.Sigmoid, [kA], ['PS1', 'PV'], bias=pv(l, 48 + j))
                mm(PS[2][:, 0:T], G2[:, l, jc], SG[:, 0:T], True, True, ['PS2'], ['G2', 'TB3'])
                cpy('act', G[:, 0:T], PS[2][:, 0:T], [kG], ['PS2'])
                ts('dve', KK[:, 0:T], K[:, 0:T], pv(l, 51 + j), None, ALU.mult, None, [kKK], [kK, 'PV'])
                act(TB[4][:, 0:T], KK[:, 0:T], AF.Square, ['TB4'], [kKK])
                mm(PS[3][:, 0:T], onesbd_b[:], TB[4][:, 0:T], True, True, ['PS3'], ['onesbd_b', 'TB4'])
                act(RN[:, 0:T], PS[3][:, 0:T], AF.Sqrt, [kRN], ['PS3'])
                ts('dve', RN[:, 0:T], RN[:, 0:T], 1e-12, None, ALU.max, None, [kRN], [kRN])
                recip(RN[:, 0:T], RN[:, 0:T], [kRN], [kRN])
                tt('dve', KK[:, 0:T], KK[:, 0:T], RN[:, 0:T], ALU.mult, [kKK], [kKK, kRN])
                ts('dve', K2[:, 0:T], A[:, 0:T], -1.0, pv(l, 54 + j), ALU.add, ALU.mult, [kK2], [kA, 'PV'])
                stt(K2[:, 0:T], K2[:, 0:T], 1.0, K[:, 0:T], ALU.add, ALU.mult, [kK2], [kK2, kK])
                tt('dve', BV[:, 0:T], KK[:, 0:T], A[:, 0:T], ALU.mult, [kBV], [kKK, kA])
                tt('dve', RN[:, 0:T], R[:, 0:T], K2[:, 0:T], ALU.mult, [kRN], [kR, kK2])
                ts('dve', TB[4][:, 0:T], RN[:, 0:T], pv(l, 57 + j), None, ALU.mult, None, ['TB4'], [kRN, 'PV'])
                mm(PS[3][:, 0:T], onesbd_b[:], TB[4][:, 0:T], True, True, ['PS3'], ['onesbd_b', 'TB4'])
                tt('dve', BON[:, 0:T], PS[3][:, 0:T], V[:, 0:T], ALU.mult, [kBON], ['PS3', kV])
                scan(CL[:, 0:T], cmask[:, 0:T], SGW[:, 0:T], 0.0, ALU.mult, ALU.add, [kCL], ['CT', kSGW])
                EG, EGI, EGX = TF[13], RN, K
                kEG, kEGI = 'TF13', kRN
                act(EG[:, 0:T], CL[:, 0:T], AF.Exp, [kEG], [kCL], scale=-C_DEC)
                act(EGI[:, 0:T], CL[:, 0:T], AF.Exp, [kEGI], [kCL], scale=C_DEC)
                tt('dve', SGW[:, 0:T], CL[:, 0:T], SGW[:, 0:T], ALU.subtract, [kSGW], [kCL, kSGW])
                act(SGW[:, 0:T], SGW[:, 0:T], AF.Exp, [kSGW], [kSGW], scale=-C_DEC)
                v3 = lambda ap: ap.rearrange("p (n l) -> p n l", l=L)
                tt('dve', KR[:, 0:NCH, 1, 0:L], v3(R[:, 0:T]), v3(EG[:, 0:T]), ALU.mult, ['KR'], [kR, kEG])
                tt('dve', KR[:, 0:NCH, 0, 0:L], v3(KK[:, 0:T]), v3(SGW[:, 0:T]), ALU.mult, ['KR'], [kKK, kSGW])
                tt('dve', KT_[:, 0:T], K2[:, 0:T], EGI[:, 0:T], ALU.mult, ['KT_'], [kK2, kEGI])
                tt('dve', BT_[:, 0:T], BV[:, 0:T], EGI[:, 0:T], ALU.mult, ['BT_'], [kBV, kEGI])
                cpy('act', VS_[:, 0:T], V[:, 0:T], ['VS_'], [kV])
                Hm = HS[:, l, j, :]
                if SMDT != F32:
                    cpy('dve', HSs[:, j, :], Hm, ['HSs'], ['HS'])
                    Hs = HSs[:, j, :]
                    hsk = 'HSs'
                else:
                    Hs = Hm
                    hsk = 'HS'
                for n in range(NCH):
                    cs = slice(n * L, (n + 1) * L)
                    for hh in range(2):
                        rw = slice(64 * hh, 64 * hh + 64)
                        pg = PS[4 + hh]
                        pk = 'PS%d' % (4 + hh)
                        mm(pg[0:L, 0:L], BT_[rw, cs], KR[rw, n, 0, 0:L], True, True, [pk], ['BT_', 'KR'])
                        mm(pg[0:L, 64:64 + L], KR[rw, n, 0, 0:L], BT_[rw, cs], True, True, [pk], ['BT_', 'KR'])
                        mm(pg[0:L, 128:128 + L], BT_[rw, cs], KR[rw, n, 1, 0:L], True, True, [pk], ['BT_', 'KR'])
                        mm(pg[0:L, 192:192 + L], KT_[rw, cs], KR[rw, n, 0, 0:L], True, True, [pk], ['KT_', 'KR'])
                        mm(pg[0:L, 256:256 + L], KT_[rw, cs], KR[rw, n, 1, 0:L], True, True, [pk], ['KT_', 'KR'])
                        tt('dve', W5[:, hh, :, :], pg[0:64, 0:320].rearrange("p (b l) -> p b l", b=5),
                           mask5[0:64, :, :], ALU.mult, ['W5'], [pk, mk5key])
                    for bi, (src, sk) in enumerate(((KT_, 'KT_'), (BT_, 'BT_'), (VS_, 'VS_'))):
                        tp(PS[7][0:L, bi * 128:(bi + 1) * 128], src[:, cs], ident_s, ['PS7'], [sk, 'CT'])
                    cpy('act', TM[:, :, :], PS[7][0:64, 0:384].rearrange("p (b l) -> p b l", b=3), ['TM'], ['PS7'])
                    for hh in range(2):
                        tt('dve', TTa[:, hh, :, :], W5[:, hh, 0:2, :],
                           ident[0:64, 0:64].rearrange("p (o l) -> p o l", o=1).to_broadcast([64, 2, 64]),
                           ALU.add, ['TTa'], ['W5', 'CT'])
                    qm_cur, qk, qoff = W5, 'W5', 0
                    tt_cur, tk = TTa, 'TTa'
                    nlev = int(math.log2(L)) - 1
                    for lev in range(nlev):
                        lastl = (lev == nlev - 1)
                        qm_nx, qnk = (QMa, 'QMa') if lev % 2 == 0 else (QMb, 'QMb')
                        tt_nx, tnk = (TTb, 'TTb') if lev % 2 == 0 else (TTa, 'TTa')
                        for hh in range(2):
                            mm(PS[6][0:L, hh * 128:hh * 128 + L], qm_cur[0:L, hh, 1, 0:L], qm_cur[0:L, hh, 0, 0:L], True, True, ['PS6'], [qk])
                            if not lastl:
                                mm(PS[6][0:L, hh * 128 + 64:hh * 128 + 64 + L], qm_cur[0:L, hh, 0, 0:L], qm_cur[0:L, hh, 1, 0:L], True, True, ['PS6'], [qk])
                        cpy('act', qm_nx[:].rearrange("p a b l -> p (a b l)"), PS[6][0:64, 0:256], [qnk], ['PS6'])
                        for hh in range(2):
                            mm(PS[6][0:L, 256 + hh * 128:256 + hh * 128 + L], tt_cur[0:L, hh, 1, 0:L], qm_nx[0:L, hh, 0, 0:L], True, True, ['PS6'], [tk, qnk])
                            if not lastl:
                                mm(PS[6][0:L, 256 + hh * 128 + 64:256 + hh * 128 + 64 + L], qm_nx[0:L, hh, 0, 0:L], tt_cur[0:L, hh, 1, 0:L], True, True, ['PS6'], [tk, qnk])
                        tt('dve', tt_nx[:].rearrange("p a b l -> p (a b l)"), PS[6][0:64, 256:512],
                           tt_cur[:].rearrange("p a b l -> p (a b l)"), ALU.add, [tnk], ['PS6', tk])
                        qm_cur, qk = qm_nx, qnk
                        tt_cur, tk = tt_nx, tnk
                    for hh in range(2):
                        rw = slice(64 * hh, 64 * hh + 64)
                        fo = slice(64 * hh, 64 * hh + 64)
                        mm(PS[5][0:L, 384 + 64 * hh:448 + 64 * hh], KR[rw, n, 0, 0:L], Hs[rw, :], True, False, ['PS5'], ['KR', hsk])
                        mm(PS[5][0:L, 384 + 64 * hh:448 + 64 * hh], W5[0:L, hh, 3, 0:L], TM[0:L, 2, fo], False, True, ['PS5'], ['W5', 'TM'])
                    act(XN[:].rearrange("p a v -> p (a v)"), PS[5][0:64, 384:512], AF.Copy, ['XN'], ['PS5'], scale=-1.0)
                    for hh in range(2):
                        mm(PS[7][0:L, 384 + 64 * hh:448 + 64 * hh], tt_cur[0:L, hh, 0, 0:L], XN[0:L, hh, :], True, True, ['PS7'], [tk, 'XN'])
                    cpy('dve', UU[:].rearrange("p a v -> p (a v)"), PS[7][0:64, 384:512], ['UU'], ['PS7'])
                    for hh in range(2):
                        rw = slice(64 * hh, 64 * hh + 64)
                        fo = slice(64 * hh, 64 * hh + 64)
                        mm(PS[4][rw, 384:384 + L], Hs[rw, :], KR[rw, n, 1, 0:L], True, False, ['PS4'], [hsk, 'KR'])
                        mm(PS[4][rw, 384:384 + L], UU[0:L, hh, :], W5[0:L, hh, 2, 0:L], False, False, ['PS4'], ['UU', 'W5'])
                        mm(PS[4][rw, 384:384 + L], TM[0:L, 2, fo], W5[0:L, hh, 4, 0:L], False, True, ['PS4'], ['TM', 'W5'])
                    cpy('act', Y[:, cs], PS[4][:, 384:384 + L], ['TF12'], ['PS4'])
                    for hh in range(2):
                        rw = slice(64 * hh, 64 * hh + 64)
                        fo = slice(64 * hh, 64 * hh + 64)
                        mm(PS[3][rw, 0:64], TM[0:L, 1, fo], UU[0:L, hh, :], True, False, ['PS3'], ['TM', 'UU'])
                        mm(PS[3][rw, 0:64], TM[0:L, 0, fo], TM[0:L, 2, fo], False, True, ['PS3'], ['TM'])
                    gl = EG[:, n * L + L - 1: n * L + L]
                    act(HG[:, :], Hm, AF.Identity, ['HG'], ['HS', kEG], scale=gl)
                    stt(Hm, PS[3][:, 0:64], gl, HG[:, :], ALU.mult, ALU.add, ['HS'], ['PS3', kEG, 'HG'])

                mm(PS[0][:, 0:T], onesbd_f, Y[:, 0:T], True, True, ['PS0'], ['CT', 'TF12'])
                stt(Y[:, 0:T], PS[0][:, 0:T], -1.0 / 64, Y[:, 0:T], ALU.mult, ALU.add, ['TF12'], ['PS0', 'TF12'])
                act(CL[:, 0:T], Y[:, 0:T], AF.Square, [kCL], ['TF12'])
                mm(PS[1][:, 0:T], onesbd_f, CL[:, 0:T], True, True, ['PS1'], ['CT', kCL])
                act(CL[:, 0:T], PS[1][:, 0:T], AF.Sqrt, [kCL], ['PS1'], scale=1.0 / 64, bias=GN_EPS_B)
                recip(CL[:, 0:T], CL[:, 0:T], [kCL], [kCL])
                tt('dve', Y[:, 0:T], Y[:, 0:T], CL[:, 0:T], ALU.mult, ['TF12'], ['TF12', kCL])
                ts('dve', Y[:, 0:T], Y[:, 0:T], pv(l, 60 + j), pv(l, 63 + j), ALU.mult, ALU.add, ['TF12'], ['TF12', 'PV'])
                tt('dve', Y[:, 0:T], Y[:, 0:T], BON[:, 0:T], ALU.add, ['TF12'], ['TF12', kBON])
                tt('dve', YM[:, 2 + j, 0:T], Y[:, 0:T], G[:, 0:T], ALU.mult, ['YM%d' % (2 + j)], ['TF12', kG])

            cpy('dve', PJ[:, 0:3, 0:3], HISTC[:, l, :, :], ['PJ0', 'PJ1', 'PJ2'], ['HISTC'])
            for (c0, nch_, s0) in ((1920, 4, 0), (2432, 4, 4), (2944, 1, 8)):
                wt, wkey = wload([(0, 8, nch_ * 128, win_src(l, c0, nch_ * 128))])
                inproj(l, T, wt, wkey, nch_ * 128, [(i, s0 + i) for i in range(nch_)])
            cpy('dve', HISTC[:, l, :, :], PJ[:, 0:3, T:T + 3], ['HISTC'], ['PJ0', 'PJ1', 'PJ2'])
            KRf = KR[:].rearrange("p n a l -> p (n a l)")
            QB = [TF[12], TF[13], KT_]
            KS = [BT_, VS_, KRf]
            kQB = ['TF12', 'TF13', 'KT_']
            kKS = ['BT_', 'VS_', 'KR']
            for j in range(3):
                xc = TF[0]
                act(xc[:, 0:T], PJ[:, j, 3:3 + T], AF.Identity, ['TF0'], ['PJ%d' % j, 'PV'],
                    scale=pv(l, 66 + 3 * 3 + j), bias=pv(l, 78 + j))
                for t_ in range(3):
                    stt(xc[:, 0:T], PJ[:, j, t_:t_ + T], pv(l, 66 + 3 * t_ + j), xc[:, 0:T], ALU.mult, ALU.add,
                        ['TF0'], ['TF0', 'PJ%d' % j, 'PV'])
                act(TB[2][:, 0:T], xc[:, 0:T], AF.Silu, ['TB2'], ['TF0'])
                mm(PS[0][:, 0:T], WQbd[:, l, j, :], TB[2][:, 0:T], True, True, ['PS0'], ['WQbd', 'TB2'])
                mm(PS[1][:, 0:T], WKbd[:, l, j, :], TB[2][:, 0:T], True, True, ['PS1'], ['WKbd', 'TB2'])
                cpy('act', QB[j][:, 0:T], PS[0][:, 0:T], [kQB[j]], ['PS0'])
                act(KS[j][:, 0:T], PS[1][:, 0:T], AF.Copy, [kKS[j]], ['PS1'], scale=0.125)

            def vsrc(j, cs_):
                return PJ[:, 3 + j, 3 + cs_.start:3 + cs_.stop], 'PJ%d' % (3 + j)
            allT = slice(0, T)
            srcs = [(QB[j][:, 0:T], kQB[j]) for j in range(3)] + [(KS[j][:, 0:T], kKS[j]) for j in range(3)] + \
                   [vsrc(j, allT) for j in range(3)]
            for kc, (s_, sk) in enumerate(srcs):
                mm(PS[2][0:6, 0:T], WIF[:, l, kc, 0:6], s_, kc == 0, kc == 8, ['PS2'], ['WIF', sk])
            for kc, (s_, sk) in enumerate(srcs):
                mm(PS[3][0:6, 0:T], WIF[:, l, kc, 6:12], s_, kc == 0, kc == 8, ['PS3'], ['WIF', sk])
            G6 = [TF[i][0:6, :] for i in range(4, 12)]
            IP, L1, CS, AH, MX, MT, WL, NMX = G6
            kIP, kL1, kCS, kAH, kMX, kMT, kWL, kNMX = ['TF%d' % i for i in range(4, 12)]
            act(IP[:, 0:T], PS[2][0:6, 0:T], AF.Identity, [kIP], ['PS2', 'BI'], bias=BI[:, l:l + 1])
            act(L1[:, 0:T], PS[3][0:6, 0:T], AF.Exp, [kL1], ['PS3', 'NBF'], scale=-1.0, bias=NBF[:, l:l + 1])
            act(L1[:, 0:T], L1[:, 0:T], AF.Ln, [kL1], [kL1], bias=1.0)
            scan(CS[:, 0:T], ones_c[0:6, 0:1].to_broadcast([6, T]), L1[:, 0:T], 0.0, ALU.mult, ALU.add, [kCS], ['ones_f', kL1])
            tt('dve', AH[:, 0:T], IP[:, 0:T], CS[:, 0:T], ALU.add, [kAH], [kIP, kCS])
            scan(MX[:, 0:T], ones_c[0:6, 0:1].to_broadcast([6, T]), AH[:, 0:T], MS[:, l:l + 1], ALU.mult, ALU.max, [kMX], ['ones_f', kAH, 'MS'])
            tt('dve', MT[:, 0:T], MX[:, 0:T], CS[:, 0:T], ALU.subtract, [kMT], [kMX, kCS])
            ts('dve', NMX[:, 0:T], MX[:, 0:T], -1.0, None, ALU.mult, None, [kNMX], [kMX])
            for n in range(NCH):
                cs = slice(n * L, (n + 1) * L)
                prev = MS[:, l:l + 1] if n == 0 else MX[:, n * L - 1:n * L]
                ts('dve', WL[:, cs], MX[:, cs], prev, None, ALU.subtract, None, [kWL], [kMX, 'MS'])
            cpy('dve', MS[:, l:l + 1], MT[:, T - 1:T], ['MS'], [kMT])
            YC = [TF[1], TF[2], TF[3]]
            Cm = CA[:, l, :, :]
            Cs = Cm
            csk = 'CA'
            for n in range(NCH):
                cs = slice(n * L, (n + 1) * L)
                for j in range(3):
                    vs_, vk_ = vsrc(j, cs)
                    tp(PS[7][0:L, j * 128:(j + 1) * 128], vs_, ident, ['PS7'], [vk_, 'CT'])
                cpy('act', VA[0:L, :, 0:64], PS[7][0:L, 0:384].rearrange("p (h v) -> p h v", h=6), ['VA'], ['PS7'])
                for j in range(3):
                    tp(PS[4][0:L, j * 128:(j + 1) * 128], KS[j][:, cs], ident, ['PS4'], [kKS[j], 'CT'])
                cpy('dve', KTm[0:L, :, :], PS[4][0:L, 0:384].rearrange("p (h v) -> p h v", h=6), ['KTm'], ['PS4'])
                tp(PS[6][0:L, 0:6], WL[:, cs], ident[0:6, 0:6], ['PS6'], [kWL, 'CT'])
                tp(PS[6][0:L, 6:12], MT[:, cs], ident[0:6, 0:6], ['PS6'], [kMT, 'CT'])
                act(WE[0:L, :, :], PS[6][0:L, 0:12].rearrange("p (a h) -> p a h", a=2), AF.Exp, ['WE'], ['PS6'], scale=-1.0)
                mm(PS[2][0:L, 0:384].rearrange("p (h l) -> p h l", h=6)[:, :, 0:L], ident[0:L, 0:L], negm6[0:L, :, 0:L],
                   True, False, ['PS2'], ['CT'])
                for h in range(6):
                    o_ = PS[2][0:L, h * 64:h * 64 + L]
                    mm(o_, sel6[:, h, 0:L], NMX[:, cs], False, False, ['PS2'], ['CT', kNMX])
                    mm(o_, AH[:, cs], sel6[:, h, 0:L], False, h == 5, ['PS2'], ['CT', kAH])
                act(DD[0:L, :, 0:L], PS[2][0:L, 0:384].rearrange("p (h l) -> p h l", h=6)[:, :, 0:L], AF.Exp, ['DD'], ['PS2'])
                for h in range(6):
                    j, hh = h // 2, h % 2
                    rw = slice(64 * hh, 64 * hh + 64)
                    mm(PS[5][0:L, h * 64:h * 64 + L], KS[j][rw, cs], QB[j][rw, cs], True, True, ['PS5'], [kKS[j], kQB[j]])
                tt('dve', STt[0:L, :, 0:L], PS[5][0:L, 0:384].rearrange("p (h l) -> p h l", h=6)[:, :, 0:L],
                   DD[0:L, :, 0:L], ALU.mult, ['STt'], ['PS5', 'DD'])
                for h in range(6):
                    j, hh = h // 2, h % 2
                    rw = slice(64 * hh, 64 * hh + 64)
                    mm(PS[1][0:L, h * 65:h * 65 + 65], STt[0:L, h, 0:L], VA[0:L, h, :], True, True, ['PS1'], ['STt', 'VA'])
                    mm(PS[6][0:L, 64 + h * 65:64 + h * 65 + 65], QB[j][rw, cs], Cs[rw, j, :], True, True, ['PS6'], [kQB[j], csk])
                tt('dve', T1[0:L], PS[6][0:L, 64:64 + 390].rearrange("p (h v) -> p h v", h=6),
                   WE[0:L, 0, :].rearrange("p (h o) -> p h o", o=1).to_broadcast([L, 6, 65]), ALU.mult, ['T1'], ['PS6', 'WE'])
                tt('dve', NUM[0:L], PS[1][0:L, 0:390].rearrange("p (h v) -> p h v", h=6), T1[0:L], ALU.add, ['NUM'], ['PS1', 'T1'])
                act(DEN[0:L, :], NUM[0:L, :, 64], AF.Abs, ['DEN'], ['NUM'])
                tt('dve', DEN[0:L, :], DEN[0:L, :], WE[0:L, 1, :], ALU.max, ['DEN'], ['DEN', 'WE'])
                recip(DEN[0:L, :], DEN[0:L, :], ['DEN'], ['DEN'])
                tt('dve', HTt[0:L], NUM[0:L, :, 0:64], DEN[0:L, :].rearrange("p (h o) -> p h o", o=1).to_broadcast([L, 6, 64]),
                   ALU.mult, ['HTt'], ['NUM', 'DEN'])
                red(MEAN[0:L, :], HTt[0:L], ['MEAN'], ['HTt'])
                stt(HC[0:L], MEAN[0:L, :].rearrange("p (h o) -> p h o", o=1).to_broadcast([L, 6, 64]), -1.0 / 64, HTt[0:L],
                    ALU.mult, ALU.add, ['HC'], ['MEAN', 'HTt'])
                act(SQ[0:L], HC[0:L], AF.Square, ['SQ'], ['HC'])
                red(VAR[0:L, :], SQ[0:L], ['VAR'], ['SQ'])
                act(VAR[0:L, :], VAR[0:L, :], AF.Sqrt, ['VAR'], ['VAR'], scale=1.0 / 64, bias=GN_EPS_C)
                recip(VAR[0:L, :], VAR[0:L, :], ['VAR'], ['VAR'])
                tt('dve', HC[0:L], HC[0:L], VAR[0:L, :].rearrange("p (h o) -> p h o", o=1).to_broadcast([L, 6, 64]),
                   ALU.mult, ['HC'], ['HC', 'VAR'])
                for j in range(3):
                    tp(PS[0][:, j * 64:j * 64 + L], HC[0:L, 2 * j:2 * j + 2, :].rearrange("p h v -> p (h v)"), ident[0:L, 0:L],
                       ['PS0'], ['HC', 'CT'])
                for j in range(3):
                    act(YC[j][:, cs], PS[0][:, j * 64:j * 64 + L], AF.Identity, ['TF%d' % (1 + j)], ['PS0', 'PV'], scale=pv(l, 81 + j))
                tt('dve', VW[0:L], VA[0:L], DD[0:L, :, L - 1:L].to_broadcast([L, 6, 65]), ALU.mult, ['VW'], ['VA', 'DD'])
                for h in range(6):
                    j, hh = h // 2, h % 2
                    rw = slice(64 * hh, 64 * hh + 64)
                    mm(PS[3][rw, j * 65:j * 65 + 65], KTm[0:L, h, :], VW[0:L, h, :], True, True, ['PS3'], ['KTm', 'VW'])
                ts('dve', WLE[:, :], pairsel, WL[:, n * L + L - 1:n * L + L], None, ALU.mult, None, ['WLE'], ['CT', kWL])
                mm(PS[6][:, 480:483], selb, WLE[:, :], True, True, ['PS6'], ['CT', 'WLE'])
                act(W0[:, :], PS[6][:, 480:483], AF.Exp, ['W0'], ['PS6'], scale=-1.0)
                tt('dve', CT2[:], Cm, W0[:, :].rearrange("p (h o) -> p h o", o=1).to_broadcast([128, 3, 65]), ALU.mult, ['CT2'], ['CA', 'W0'])
                tt('dve', Cm, PS[3][:, 0:195].rearrange("p (h v) -> p h v", h=3), CT2[:], ALU.add, ['CA'], ['PS3', 'CT2'])
            for j in range(3):
                act(TF[0][:, 0:T], PJ[:, 6 + j, 3:3 + T], AF.Sigmoid, ['TF0'], ['PJ%d' % (6 + j)])
                tt('dve', YM[:, 5 + j, 0:T], TF[0][:, 0:T], YC[j][:, 0:T], ALU.mult, ['YM%d' % (5 + j)], ['TF0', 'TF%d' % (1 + j)])

            for half in range(2):
                wt, wkey = wload([(0, 8, 512, I['w_out'][l][:, half * 512:(half + 1) * 512].rearrange("(k p) n -> p k n", p=128))])
                for mi in range(4):
                    m = half * 4 + mi
                    bank = mi % 4
                    for k in range(8):
                        mm(PS[bank][:, 0:T], wt[:, k * 512 + mi * 128:k * 512 + mi * 128 + 128], YM[:, k, 0:T], k == 0, k == 7,
                           ['PS%d' % bank], [wkey, 'YM%d' % k])
                    tt('dve', X[:, m, 0:T], PS[bank][:, 0:T], X[:, m, 0:T], ALU.add, ['X%d' % m], ['PS%d' % bank, 'X%d' % m])
            def o2(k):
                stt(HN[:, k, 0:T], X[:, k, 0:T], pv(l, 8 + k), RS[:, 0:T], ALU.mult, ALU.mult, ['HN%d' % k], ['X%d' % k, 'PV', 'TF13'])
            rmsnorm_to(T, 8, l, o2)
            for i in range(11):
                wf = I['w_ffn_in'][l]
                wt, wkey = wload([(0, 8, 256, wf[:, 256 * i:256 * i + 256].rearrange("(k p) n -> p k n", p=128)),
                                  (2048, 8, 256, wf[:, DFF + 256 * i:DFF + 256 * i + 256].rearrange("(k p) n -> p k n", p=128))])
                for q in range(2):
                    f = 2 * i + q
                    bg, bu = (0, 1) if q == 0 else (2, 3)
                    for k in range(8):
                        mm(PS[bg][:, 0:T], wt[:, k * 256 + q * 128:k * 256 + q * 128 + 128], HN[:, k, 0:T], k == 0, k == 7,
                           ['PS%d' % bg], [wkey, 'HN%d' % k])
                    for k in range(8):
                        mm(PS[bu][:, 0:T], wt[:, 2048 + k * 256 + q * 128:2048 + k * 256 + q * 128 + 128], HN[:, k, 0:T], k == 0, k == 7,
                           ['PS%d' % bu], [wkey, 'HN%d' % k])
                    tk_ = TF[q]
                    act(tk_[:, 0:T], PS[bg][:, 0:T], AF.Silu, ['TF%d' % q], ['PS%d' % bg])
                    tt('dve', ACTT[:, f, 0:T], tk_[:, 0:T], PS[bu][:, 0:T], ALU.mult, actk(f), ['TF%d' % q, 'PS%d' % bu])
            for m in range(8):
                wo = I['w_ffn_out'][l]
                wt, wkey = wload([(0, 22, 128, wo[:, 128 * m:128 * m + 128].rearrange("(k p) n -> p k n", p=128))])
                bank = m % 4
                for k in range(22):
                    mm(PS[bank][:, 0:T], wt[:, k * 128:k * 128 + 128], ACTT[:, k, 0:T], k == 0, k == 21,
                       ['PS%d' % bank], [wkey] + actk(k))
                tt('dve', X[:, m, 0:T], PS[bank][:, 0:T], X[:, m, 0:T], ALU.add, ['X%d' % m], ['PS%d' % bank, 'X%d' % m])
            if last_layer:
                def o3(k):
                    stt(TF[k][:, 0:T], X[:, k, 0:T], NF[:, k:k + 1], RS[:, 0:T], ALU.mult, ALU.mult, ['TF%d' % k], ['X%d' % k, 'NF', 'TF13'])
                rmsnorm_to(T, 0, l, o3)
                nblk = (T + 127) // 128
                for tb in range(nblk):
                    n = min(128, T - tb * 128)
                    for half in range(2):
                        xt, xk = STG[half]
                        bank = 4 + half
                        for mmi in range(4):
                            m = half * 4 + mmi
                            tp(PS[bank][0:n, mmi * 128:(mmi + 1) * 128], TF[m][:, tb * 128:tb * 128 + n], ident,
                               ['PS%d' % bank], ['TF%d' % m, 'CT'])
                        cpy('act' if half else 'dve', xt[0:n, 0:512], PS[bank][0:n, 0:512], [xk], ['PS%d' % bank])
                        dma('sp', yout[tok0 + tb * 128:tok0 + tb * 128 + n, half * 512:(half + 1) * 512], xt[0:n, 0:512], r=[xk], is_output=True)

        for job in jobs:
            if job == 'p':
                T, L, nseg, xin, yout = TS, 64, TP // TS, I['xp'], O['yp']
                for t_, k_ in ((HISTA, 'HISTA'), (HISTB, 'HISTB'), (HISTC, 'HISTC'), (HL, 'HL'), (HS, 'HS'), (CA, 'CA'), (MS, 'MS')):
                    memset('dve', t_[:], 0.0, [k_])
            else:
                T, L, nseg, xin, yout = TSMP, 32, 1, I['xs'], O['ys']
                for l in range(DEPTH):
                    for j_ in range(3):
                        dma('sp', HISTA[:, l, :, j_], I['s_conv_a'][l, j_].rearrange("(c p) -> p c", p=128), w=['HISTA'], slow=True)
                        dma('sp', HISTC[:, l, :, j_], I['s_conv_c'][l, j_].rearrange("(c p) -> p c", p=128), w=['HISTC'], slow=True)
                    dma('sp', HISTB[:, l, :, 0], I['s_shift'][l].rearrange("(c p) -> p c", p=128), w=['HISTB'], slow=True)
                    dma('sp', HL[:, l, :], I['s_lru'][l].rearrange("(c p) -> p c", p=128), w=['HL'], slow=True)
                    dma('sp', CA[:, l, :, 0:64], I['s_mem_c'][l].rearrange("(jp hh) n v -> (hh n) jp v", hh=2), w=['CA'])
                    dma('sp', CA[:, l, :, 64], I['s_mem_n'][l].rearrange("(jp hh) n -> (hh n) jp", hh=2), w=['CA'], slow=True)
                    dma('sp', WS[:], I['s_wkv'][l].rearrange("h v k -> v h k"), w=['WS'])
                    for j in range(3):
                        tp(PS[4][:, j * 64:j * 64 + 64], WS[:, 2 * j:2 * j + 2, :].rearrange("p h k -> p (h k)"), ident[0:64, 0:64],
                           ['PS4'], ['WS', 'CT'])
                    cpy('dve', HS[:, l, :, :], PS[4][:, 0:192].rearrange("p (j v) -> p j v", j=3), ['HS'], ['PS4'])
                dma('sp', MS[:], I['s_mem_m'].rearrange("l h -> h l"), w=['MS'], slow=True)
            for seg in range(nseg):
                for l in range(DEPTH):
                    task(job, l, T, L, seg * T, seg == 0, seg == nseg - 1, xin, yout, l == DEPTH - 1)
            pre = job + '_'
            for l in range(DEPTH):
                for j_ in range(3):
                    dma('sp', O[pre + 'conv_a'][l, j_].rearrange("(c p) -> p c", p=128), HISTA[:, l, :, j_], r=['HISTA'], slow=True, is_output=True)
                    dma('sp', O[pre + 'conv_c'][l, j_].rearrange("(c p) -> p c", p=128), HISTC[:, l, :, j_], r=['HISTC'], slow=True, is_output=True)
                dma('sp', O[pre + 'shift'][l].rearrange("(c p) -> p c", p=128), HISTB[:, l, :, 0], r=['HISTB'], slow=True, is_output=True)
                dma('sp', O[pre + 'lru'][l].rearrange("(c p) -> p c", p=128), HL[:, l, :], r=['HL'], slow=True, is_output=True)
                dma('sp', O[pre + 'mem_c'][l].rearrange("(jp hh) n v -> (hh n) jp v", hh=2), CA[:, l, :, 0:64], r=['CA'], is_output=True)
                dma('sp', O[pre + 'mem_n'][l].rearrange("(jp hh) n -> (hh n) jp", hh=2), CA[:, l, :, 64], r=['CA'], slow=True, is_output=True)
                for j in range(3):
                    tp(PS[4][0:64, j * 128:(j + 1) * 128], HS[:, l, j, :], ident, ['PS4'], ['HS', 'CT'])
                cpy('dve', WS[:], PS[4][0:64, 0:384].rearrange("p (h k) -> p h k", h=6), ['WS'], ['PS4'])
                dma('sp', O[pre + 'wkv'][l].rearrange("h v k -> v h k"), WS[:], r=['WS'], is_output=True)
            dma('sp', O[pre + 'mem_m'].rearrange("l h -> h l"), MS[:], r=['MS'], slow=True, is_output=True)
        P.finish()
        P.emit(nc, st)
    return nc, cnp


_CACHE = {}
TS_DEFAULT = 512


def kernel(**inputs):
    inputs = {k: np.asarray(v) for k, v in inputs.items()}
    xp = inputs['x_prompt']
    xs = inputs['x_sample']
    DEPTH = inputs['norm1'].shape[0]
    B, TP, _ = xp.shape
    TS = min(TS_DEFAULT, TP)
    key = (DEPTH, TP, TS)
    if key not in _CACHE:
        _CACHE[key] = build(DEPTH, TP, TS)
    nc, cnp = _CACHE[key]
    f32 = lambda a: np.ascontiguousarray(a, dtype=np.float32)
    in_maps = []
    for c in range(NCORES):
        m = {'xp': f32(xp[c % B]), 'xs': f32(xs[c]), 'consts': cnp}
        m['s_conv_a'] = f32(inputs['state_conv_a'][:, c])
        m['s_lru'] = f32(inputs['state_lru'][:, c])
        m['s_shift'] = f32(inputs['state_shift_b'][:, c, 0])
        m['s_wkv'] = f32(inputs['state_wkv'][:, c])
        m['s_conv_c'] = f32(inputs['state_conv_c'][:, c])
        m['s_mem_c'] = f32(inputs['state_mem_c'][:, c])
        m['s_mem_n'] = f32(inputs['state_mem_n'][:, c])
        m['s_mem_m'] = f32(inputs['state_mem_m'][:, c])
        for k in WNAMES:
            m[k] = f32(inputs[k])
        in_maps.append(m)
    res = run_bass_kernel_spmd(nc, in_maps, core_ids=list(range(NCORES)))
    R = res.results
    y_prompt = np.stack([R[b]['yp'] for b in range(B)], 0)
    y_sample = np.stack([R[c]['ys'] for c in range(NCORES)], 0)
    outs = [y_prompt, y_sample]
    names = ['conv_a', 'lru', 'shift', 'wkv', 'conv_c', 'mem_c', 'mem_n', 'mem_m']
    for jb, n in (('p', B), ('s', NCORES)):
        for nm in names:
            a = np.stack([R[c]['o' + jb + '_' + nm] for c in range(n)], 1)
            if nm == 'shift':
                a = a[:, :, None, :]
            outs.append(np.ascontiguousarray(a.astype(np.float32)))
    return tuple(outs)
```

```python
import math
from contextlib import ExitStack
import numpy as np
import concourse.bass as bass
import concourse.mybir as mybir
from concourse.bass_utils import run_bass_kernel_spmd
from concourse.alu_op_type import AluOpType as ALU

AF = mybir.ActivationFunctionType
F32 = mybir.dt.float32
BF16 = mybir.dt.bfloat16
AX = mybir.AxisListType

EPOCH = 16000
NDMA_SLOTS = 10
SAME_ENGINE_SYNC = True
NOSYNC_ENGINES = ('pe',)

D = 1024
DIN = 3072
DA = 256
DB = 384
DBIN = 1408
DC = 384
DFF = 2816
NCORES = 8
TSMP = 32
C_DEC = math.exp(-0.5)
RMS_EPS = 1e-6
GN_EPS_B = 64e-5
GN_EPS_C = 1e-6


class Prog:
    ENG = ['pe', 'act', 'dve', 'pool', 'sp']

    def __init__(self):
        self.stream = {e: [] for e in self.ENG}
        self.n = {e: 0 for e in self.ENG}
        self.lastw = {}
        self.readers = {}
        self.seen = {e: {} for e in self.ENG}
        self.dma_slot_next = {e: 0 for e in self.ENG}
        self.dma_slot_val = {}
        self.out_tokens = []
        self.pe_rg = {}

    def _deps(self, eng, reads, writes, extra=(), force=()):
        toks = list(extra) + list(force)
        forced_src = set(t[0] for t in force)
        for k in reads:
            t = self.lastw.get(k)
            if t:
                toks.append(t)
        for k in writes:
            t = self.lastw.get(k)
            if t:
                toks.append(t)
            toks.extend(self.readers.get(k, {}).values())
        need = {}
        for (src, val) in toks:
            if need.get(src, 0) < val:
                need[src] = val
        out = []
        for src, val in need.items():
            if src == ('e', eng) and (not SAME_ENGINE_SYNC or eng in NOSYNC_ENGINES) and src not in forced_src:
                continue
            if self.seen[eng].get(src, 0) >= val:
                continue
            self.seen[eng][src] = val
            out.append((src, val))
        return out

    def op(self, eng, fn, w=(), r=(), rg=None):
        w = list(w) + [k for k in r if k.startswith('PS') and k[2:].isdigit()]
        force = []
        if eng == 'pe':
            for k in w:
                if k.startswith('PS'):
                    prev = self.pe_rg.get(k)
                    if prev is not None and prev[0] != rg and self.lastw.get(k) == prev[1]:
                        force.append(prev[1])
        waits = self._deps(eng, r, w, force=force)
        self.n[eng] += 1
        tok = (('e', eng), self.n[eng])
        for k in w:
            self.lastw[k] = tok
            self.readers[k] = {}
        for k in r:
            self.readers.setdefault(k, {})[('e', eng)] = tok
        self.stream[eng].append((waits, fn, tok))
        if eng == 'pe':
            for k in w:
                if k.startswith('PS'):
                    self.pe_rg[k] = (rg, tok)
        return tok

    def dma(self, q, fn, w=(), r=(), is_output=False):
        slot = (q, self.dma_slot_next[q] % NDMA_SLOTS)
        self.dma_slot_next[q] += 1
        src = ('d', slot)
        prev = self.dma_slot_val.get(slot, 0)
        extra = [(src, prev)] if prev else []
        waits = self._deps(q, r, w, extra)
        val = prev + 16
        self.dma_slot_val[slot] = val
        tok = (src, val)
        for k in w:
            self.lastw[k] = tok
            self.readers[k] = {}
        for k in r:
            self.readers.setdefault(k, {})[src] = tok
        self.stream[q].append((waits, fn, tok))
        if is_output:
            self.out_tokens.append(tok)
        return tok

    def finish(self):
        need = {}
        for (src, val) in self.out_tokens:
            need[src] = max(need.get(src, 0), val)
        self.stream['sp'].append((list(need.items()), None, None))

    def emit(self, nc, stack):
        sems = {}

        def getsem(src, val):
            if src[0] == 'e':
                ep = (val - 1) // EPOCH
                key = (src, ep)
                v = val - ep * EPOCH
            else:
                key = (src, 0)
                v = val
            if key not in sems:
                sems[key] = stack.enter_context(nc.semaphore("s%d" % len(sems)))
            return sems[key], v

        for e in self.ENG:
            for (waits, fn, tok) in self.stream[e]:
                for (src, val) in waits:
                    getsem(src, val)
                if tok is not None:
                    getsem(*tok)
        block = stack.enter_context(nc.Block())
        names = {'pe': 'tensor', 'act': 'scalar', 'dve': 'vector', 'pool': 'gpsimd', 'sp': 'sync'}
        for e in self.ENG:
            items = self.stream[e]
            if not items:
                continue

            def body(engh, items=items):
                for (waits, fn, tok) in items:
                    for (src, val) in waits:
                        s, v = getsem(src, val)
                        engh.wait_ge(s, v)
                    if fn is None:
                        continue
                    ins = fn(engh)
                    s, v = getsem(*tok)
                    ins.then_inc(s, 16 if tok[0][0] == 'd' else 1)
            getattr(block, names[e])(body)
        self.nsems = len(sems)


def _consts(TS):
    cw = {}
    cols = []

    def add(name, arr):
        a = np.zeros((128, arr.shape[1]), np.float32)
        a[:arr.shape[0]] = arr
        cw[name] = (sum(c.shape[1] for c in cols), arr.shape[1])
        cols.append(a)
    add('ident', np.eye(128, dtype=np.float32))
    bd = np.zeros((128, 128), np.float32)
    bd[:64, :64] = 1
    bd[64:, 64:] = 1
    add('onesbd', bd)
    add('ident2', np.concatenate([np.eye(64, dtype=np.float32)] * 2, 0))
    r = np.arange(128)[:, None] % 64
    c = np.arange(64)[None, :]
    su = (c > r).astype(np.float32)
    sl = (c < r).astype(np.float32)
    ui = (c >= r).astype(np.float32)
    add('mask5', np.concatenate([-su, -sl, ui, su, ui], 1))
    negm = np.where(np.arange(64)[:, None] > c, -30000.0, 0.0).astype(np.float32)
    add('negm6', np.tile(negm, (1, 6)))
    cm = np.ones((128, TS), np.float32)
    cm[:, ::64] = 0
    add('cmask', cm)
    sel6 = np.zeros((6, 6, 64), np.float32)
    for h in range(6):
        sel6[h, h, :] = 1
    add('sel6', sel6.reshape(6, 384))
    selb = np.zeros((6, 128), np.float32)
    for k in range(6):
        selb[k, (k % 2) * 64:(k % 2) * 64 + 64] = 1
    add('selb', selb)
    ps_ = np.zeros((6, 3), np.float32)
    for k in range(6):
        ps_[k, k // 2] = 1
    add('pairsel', ps_)
    return np.concatenate(cols, 1), cw


WNAMES = ['norm1', 'w_in', 'conv_a_w', 'conv_a_b', 'lru_wr', 'lru_br', 'lru_wi', 'lru_bi', 'lru_lambda',
          'norm_a', 'rwkv_mu', 'rwkv_w0', 'rwkv_w2', 'rwkv_a0', 'rwkv_a2', 'rwkv_g2', 'rwkv_kk', 'rwkv_ka',
          'rwkv_rk', 'rwkv_lnw', 'rwkv_lnb', 'conv_c_w', 'conv_c_b', 'mlstm_wq', 'mlstm_wk', 'mlstm_wif',
          'mlstm_bif', 'mlstm_gn', 'w_out', 'norm2', 'w_ffn_in', 'w_ffn_out', 'norm_f']


def build(DEPTH, TP, TS, SMDT=F32, jobs=('p', 's')):
    assert TP % TS == 0 and TS % 128 == 0
    nc = bass.Bass("TRN2", target_bir_lowering=False)
    cnp, cw = _consts(TS)
    CW = cnp.shape[1]

    def din(name, shape):
        return nc.dram_tensor(name, list(shape), F32, kind="ExternalInput").ap()

    def dout(name, shape):
        return nc.dram_tensor(name, list(shape), F32, kind="ExternalOutput").ap()

    I = {}
    I['xp'] = din('xp', [TP, D])
    I['xs'] = din('xs', [TSMP, D])
    I['consts'] = din('consts', [128, CW])
    st_shapes = {'conv_a': [DEPTH, 3, DA], 'lru': [DEPTH, DA], 'shift': [DEPTH, DBIN], 'wkv': [DEPTH, 6, 64, 64],
                 'conv_c': [DEPTH, 3, DC], 'mem_c': [DEPTH, 6, 64, 64], 'mem_n': [DEPTH, 6, 64], 'mem_m': [DEPTH, 6]}
    for k, s in st_shapes.items():
        I['s_' + k] = din('s_' + k, s)
    wshapes = {'norm1': [DEPTH, D], 'w_in': [DEPTH, D, DIN], 'conv_a_w': [DEPTH, 4, DA], 'conv_a_b': [DEPTH, DA],
               'lru_wr': [DEPTH, 4, 64, 64], 'lru_br': [DEPTH, DA], 'lru_wi': [DEPTH, 4, 64, 64], 'lru_bi': [DEPTH, DA],
               'lru_lambda': [DEPTH, DA], 'norm_a': [DEPTH, DA], 'rwkv_mu': [DEPTH, DBIN], 'rwkv_w0': [DEPTH, DB],
               'rwkv_w2': [DEPTH, 64, DB], 'rwkv_a0': [DEPTH, DB], 'rwkv_a2': [DEPTH, 64, DB], 'rwkv_g2': [DEPTH, 128, DB],
               'rwkv_kk': [DEPTH, DB], 'rwkv_ka': [DEPTH, DB], 'rwkv_rk': [DEPTH, 6, 64], 'rwkv_lnw': [DEPTH, DB],
               'rwkv_lnb': [DEPTH, DB], 'conv_c_w': [DEPTH, 4, DC], 'conv_c_b': [DEPTH, DC], 'mlstm_wq': [DEPTH, 6, 64, 64],
               'mlstm_wk': [DEPTH, 6, 64, 64], 'mlstm_wif': [DEPTH, 3 * DC, 12], 'mlstm_bif': [DEPTH, 12],
               'mlstm_gn': [DEPTH, DC], 'w_out': [DEPTH, D, D], 'norm2': [DEPTH, D], 'w_ffn_in': [DEPTH, D, 2 * DFF],
               'w_ffn_out': [DEPTH, DFF, D], 'norm_f': [D]}
    for k in WNAMES:
        I[k] = din(k, wshapes[k])
    O = {}
    O['yp'] = dout('yp', [TP, D])
    O['ys'] = dout('ys', [TSMP, D])
    for jb in ('p', 's'):
        for k, s in st_shapes.items():
            O[jb + '_' + k] = dout('o' + jb + '_' + k, s)

    P = Prog()
    with ExitStack() as st:
        def sb(name, shape, dt=F32):
            return st.enter_context(nc.sbuf_tensor(name, list(shape), dt))

        def psum(name, shape, dt=F32):
            return st.enter_context(nc.psum_tensor(name, list(shape), dt))

        def tt(eng, out, in0, in1, op, w, r):
            P.op(eng, lambda e: e.tensor_tensor(out=out, in0=in0, in1=in1, op=op), w, r)

        def ts(eng, out, in0, s1, s2, op0, op1, w, r):
            if op1 is None:
                P.op(eng, lambda e: e.tensor_scalar(out=out, in0=in0, scalar1=s1, scalar2=None, op0=op0), w, r)
            else:
                P.op(eng, lambda e: e.tensor_scalar(out=out, in0=in0, scalar1=s1, scalar2=s2, op0=op0, op1=op1), w, r)

        def stt(out, in0, scalar, in1, op0, op1, w, r):
            P.op('dve', lambda e: e.scalar_tensor_tensor(out=out, in0=in0, scalar=scalar, in1=in1, op0=op0, op1=op1), w, r)

        def act(out, in_, func, w, r, scale=1.0, bias=0.0):
            P.op('act', lambda e: e.activation(out=out, in_=in_, func=func, scale=scale, bias=bias), w, r)

        def cpy(eng, out, in_, w, r):
            if eng == 'act':
                P.op('act', lambda e: e.activation(out=out, in_=in_, func=AF.Copy), w, r)
            else:
                P.op(eng, lambda e: e.tensor_copy(out=out, in_=in_), w, r)

        def _rg(ap):
            n = ap.partition_size()
            return (ap.base_partition(), 32 if n <= 32 else (64 if n <= 64 else 128))

        def mm(out, lhsT, rhs, start, stop, w, r):
            P.op('pe', lambda e: e.matmul(out, lhsT=lhsT, rhs=rhs, start=start, stop=stop), w, r, rg=_rg(lhsT))

        def tp(out, in_, ident, w, r):
            P.op('pe', lambda e: e.transpose(out, in_, ident), w, r, rg=_rg(in_))

        def recip(out, in_, w, r):
            P.op('dve', lambda e: e.reciprocal(out=out, in_=in_), w, r)

        def scan(out, d0, d1, init, op0, op1, w, r):
            P.op('dve', lambda e: e.tensor_tensor_scan(out=out, data0=d0, data1=d1, initial=init, op0=op0, op1=op1), w, r)

        def red(out, in_, w, r):
            P.op('dve', lambda e: e.tensor_reduce(out=out, in_=in_, axis=AX.X, op=ALU.add), w, r)

        def memset(eng, ap, val, w):
            P.op(eng, lambda e: e.memset(ap, val), w, ())

        def dma(q, out, in_, w=(), r=(), slow=False, is_output=False):
            if slow:
                P.dma(q, lambda e: e.dma_start(out=out, in_=in_, allow_slow_non_contiguous=True), w, r, is_output)
            else:
                P.dma(q, lambda e: e.dma_start(out=out, in_=in_), w, r, is_output)

        CT = sb('CT', [128, CW])
        dma('sp', CT[:], I['consts'], w=['CT'])

        def cst(name, rows=128):
            o, n = cw[name]
            return CT[0:rows, o:o + n]
        ident = cst('ident')
        ident2 = cst('ident2')
        onesbd_f = cst('onesbd')
        cmask = cst('cmask')
        onesb = sb('onesb', [128, 128], BF16)
        memset('dve', onesb[:], 1.0, ['onesb'])
        onesbd_b = sb('onesbd_b', [128, 128], BF16)
        cpy('dve', onesbd_b[:], onesbd_f, ['onesbd_b'], ['CT'])
        ones_c = sb('ones_c', [128, 1])
        memset('dve', ones_c[:], 1.0, ['ones_f'])
        if SMDT == F32:
            mask5 = cst('mask5').rearrange("p (b l) -> p b l", b=5)
            ident_s = ident
            mk5key = 'CT'
        else:
            mask5t = sb('mask5t', [128, 5, 64], SMDT)
            cpy('dve', mask5t[:], cst('mask5').rearrange("p (b l) -> p b l", b=5), ['mask5t'], ['CT'])
            mask5 = mask5t[:]
            ident_st = sb('ident_st', [128, 128], SMDT)
            cpy('dve', ident_st[:], ident, ['ident_st'], ['CT'])
            ident_s = ident_st[:]
            mk5key = 'mask5t'
        negm6 = cst('negm6', 64).rearrange("p (h l) -> p h l", h=6)
        sel6 = cst('sel6', 6).rearrange("p (h l) -> p h l", h=6)
        selb = cst('selb', 6)
        pairsel = cst('pairsel', 6)

        NV = 88
        PV = sb('PV', [128, DEPTH, NV])
        NF = sb('NF', [128, 8])
        BI = sb('BI', [6, DEPTH])
        NBF = sb('NBF', [6, DEPTH])

        def pvload(name, col, n):
            for l in range(DEPTH):
                dma('sp', PV[:, l, col:col + n], I[name][l].rearrange("(c p) -> p c", p=128), w=['PV'], slow=True)
        pvload('norm1', 0, 8)
        pvload('norm2', 8, 16 - 8)
        for l in range(DEPTH):
            for j in range(4):
                dma('sp', PV[:, l, 16 + 2 * j:18 + 2 * j], I['conv_a_w'][l, j].rearrange("(c p) -> p c", p=128), w=['PV'], slow=True)
                dma('sp', PV[:, l, 66 + 3 * j:69 + 3 * j], I['conv_c_w'][l, j].rearrange("(c p) -> p c", p=128), w=['PV'], slow=True)
        pvload('conv_a_b', 24, 2)
        pvload('lru_br', 26, 2)
        pvload('lru_bi', 28, 2)
        pvload('lru_lambda', 30, 2)
        pvload('norm_a', 32, 2)
        pvload('rwkv_mu', 34, 11)
        pvload('rwkv_w0', 45, 3)
        pvload('rwkv_a0', 48, 3)
        pvload('rwkv_kk', 51, 3)
        pvload('rwkv_ka', 54, 3)
        for l in range(DEPTH):
            dma('sp', PV[:, l, 57:60], I['rwkv_rk'][l].rearrange("(c hh) n -> (hh n) c", hh=2), w=['PV'], slow=True)
        pvload('rwkv_lnw', 60, 3)
        pvload('rwkv_lnb', 63, 3)
        pvload('conv_c_b', 78, 3)
        pvload('mlstm_gn', 81, 3)
        dma('sp', NF[:], I['norm_f'].rearrange("(c p) -> p c", p=128), w=['NF'], slow=True)
        dma('sp', BI[:], I['mlstm_bif'][:, 0:6].rearrange("l h -> h l"), w=['BI'], slow=True)
        dma('sp', NBF[:], I['mlstm_bif'][:, 6:12].rearrange("l h -> h l"), w=['NBF'], slow=True)
        ts('dve', NBF[:], NBF[:], -1.0, None, ALU.mult, None, ['NBF'], ['NBF'])
        SPT = sb('SPT', [128, DEPTH, 2])
        act(SPT[:], PV[:, :, 30:32], AF.Exp, ['SPT'], ['PV'], scale=-1.0)
        act(SPT[:], SPT[:], AF.Ln, ['SPT'], ['SPT'], bias=1.0)
        ts('dve', PV[:, :, 84:86], SPT[:], -8.0, None, ALU.mult, None, ['PV'], ['SPT'])
        ts('dve', PV[:, :, 86:88], SPT[:], -16.0, None, ALU.mult, None, ['PV'], ['SPT'])

        WRbd = sb('WRbd', [128, DEPTH, 2, 128])
        WIbd = sb('WIbd', [128, DEPTH, 2, 128])
        memset('dve', WRbd[:], 0.0, ['WRbd'])
        memset('dve', WIbd[:], 0.0, ['WIbd'])
        WQbd = sb('WQbd', [128, DEPTH, 3, 128], BF16)
        WKbd = sb('WKbd', [128, DEPTH, 3, 128], BF16)
        memset('dve', WQbd[:], 0.0, ['WQbd'])
        memset('dve', WKbd[:], 0.0, ['WKbd'])
        W2 = sb('W2', [128, DEPTH, DB], BF16)
        A2 = sb('A2', [128, DEPTH, DB], BF16)
        G2 = sb('G2', [128, DEPTH, DB], BF16)
        WIF = sb('WIF', [128, DEPTH, 9, 12], SMDT)
        for l in range(DEPTH):
            for n in range(4):
                hb, c = n % 2, n // 2
                dma('sp', WRbd[64 * hb:64 * hb + 64, l, c, 64 * hb:64 * hb + 64], I['lru_wr'][l, n], w=['WRbd'])
                dma('sp', WIbd[64 * hb:64 * hb + 64, l, c, 64 * hb:64 * hb + 64], I['lru_wi'][l, n], w=['WIbd'])
            for h in range(6):
                hb, c = h % 2, h // 2
                dma('pool', WQbd[64 * hb:64 * hb + 64, l, c, 64 * hb:64 * hb + 64], I['mlstm_wq'][l, h], w=['WQbd'])
                dma('pool', WKbd[64 * hb:64 * hb + 64, l, c, 64 * hb:64 * hb + 64], I['mlstm_wk'][l, h], w=['WKbd'])
            dma('pool', W2[0:64, l, :], I['rwkv_w2'][l], w=['W2'])
            dma('pool', A2[64:128, l, :], I['rwkv_a2'][l], w=['A2'])
            dma('pool', G2[:, l, :], I['rwkv_g2'][l], w=['G2'])
            dma('pool' if SMDT != F32 else 'sp', WIF[:, l, :, :], I['mlstm_wif'][l].rearrange("(kc p) n -> p kc n", p=128), w=['WIF'])
        ts('dve', WIF[:, :, 3:6, :], WIF[:, :, 3:6, :], 8.0, None, ALU.mult, None, ['WIF'], ['WIF'])

        HISTA = sb('HISTA', [128, DEPTH, 2, 3])
        HISTB = sb('HISTB', [128, DEPTH, 11, 1])
        HISTC = sb('HISTC', [128, DEPTH, 3, 3])
        HL = sb('HL', [128, DEPTH, 2])
        HS = sb('HS', [128, DEPTH, 3, 64])
        CA = sb('CA', [128, DEPTH, 3, 65])
        MS = sb('MS', [6, DEPTH])
        WS = sb('WS', [64, 6, 64])

        X = sb('X', [128, 8, TS])
        HN = sb('HN', [128, 8, TS], BF16)
        NSLOT = 11
        PJ = sb('PJ', [128, NSLOT, 3 + TS])
        YM = sb('YM', [128, 8, TS], BF16)
        assert NSLOT * (3 + TS) * 4 >= 22 * TS * 2
        ACTT = PJ[:].rearrange("p s c -> p (s c)").bitcast(BF16)[:, 0:22 * TS].rearrange("p (f t) -> p f t", f=22)
        def actk(f):
            b0, b1 = f * TS * 2, (f + 1) * TS * 2 - 1
            sl_ = (3 + TS) * 4
            return ['PJ%d' % s_ for s_ in range(b0 // sl_, b1 // sl_ + 1)]
        NWB = 2
        WB = [sb('WB%d' % i, [128, 4096], BF16) for i in range(NWB)]
        NTF = 14
        TF = [sb('TF%d' % i, [128, TS]) for i in range(NTF)]
        NTB = 5
        TB = [sb('TB%d' % i, [128, TS], BF16) for i in range(NTB)]
        KR = sb('KR', [128, TS // 64, 2, 64], SMDT)
        KT_ = sb('KT_', [128, TS], SMDT)
        BT_ = sb('BT_', [128, TS], SMDT)
        VS_ = sb('VS_', [128, TS], SMDT)
        W5 = sb('W5', [64, 2, 5, 64], SMDT)
        QMa = sb('QMa', [64, 2, 2, 64], SMDT)
        QMb = sb('QMb', [64, 2, 2, 64], SMDT)
        TTa = sb('TTa', [64, 2, 2, 64], SMDT)
        TTb = sb('TTb', [64, 2, 2, 64], SMDT)
        TM = sb('TM', [64, 3, 128], SMDT)
        XN = sb('XN', [64, 2, 64], SMDT)
        UU = sb('UU', [64, 2, 64], SMDT)
        HG = sb('HG', [128, 64])
        assert SMDT == F32
        HSs = None
        VA = sb('VA', [64, 6, 65], SMDT)
        KTm = sb('KTm', [64, 6, 64], SMDT)
        WE = sb('WE', [64, 2, 6])
        DD = sb('DD', [64, 6, 64])
        STt = sb('STt', [64, 6, 64], SMDT)
        T1 = sb('T1', [64, 6, 65])
        NUM = sb('NUM', [64, 6, 65])
        DEN = sb('DEN', [64, 6])
        HTt = sb('HTt', [64, 6, 64])
        HC = sb('HC', [64, 6, 64])
        SQ = sb('SQ', [64, 6, 64])
        MEAN = sb('MEAN', [64, 6])
        VAR = sb('VAR', [64, 6])
        VW = sb('VW', [64, 6, 65], SMDT)
        WLE = sb('WLE', [6, 3])
        W0 = sb('W0', [128, 3])
        CT2 = sb('CT2', [128, 3, 65])
        RS = TF[13]
        if TS >= 512:
            STG = [(TF[11], 'TF11'), (TF[12], 'TF12')]
        else:
            STG = [(sb('STG0', [128, 512]), 'STG0'), (sb('STG1', [128, 512]), 'STG1')]
        PS = [psum('PS%d' % i, [128, 512]) for i in range(8)]

        memset('dve', VA[:], 1.0, ['VA'])

        wstate = {'i': 0}

        def wload(parts):
            i = wstate['i'] % NWB
            wstate['i'] += 1
            key = 'WB%d' % i
            for (c0, KC, ncol, src) in parts:
                dst = WB[i][:, c0:c0 + KC * ncol].rearrange("p (k n) -> p k n", k=KC)
                dma('pool', dst, src, w=[key])
            return WB[i], key

        def win_src(l, c0, ncol):
            return I['w_in'][l][:, c0:c0 + ncol].rearrange("(k p) n -> p k n", p=128)

        def rmsnorm_to(T, gcol_base, l, out_fn):
            for k in range(8):
                act(TB[0][:, 0:T] if k % 2 == 0 else TB[1][:, 0:T], X[:, k, 0:T], AF.Square,
                    ['TB%d' % (k % 2)], ['X%d' % k])
                mm(PS[0][:, 0:T], onesb[:], TB[k % 2][:, 0:T], k == 0, k == 7, ['PS0'], ['onesb', 'TB%d' % (k % 2)])
            act(RS[:, 0:T], PS[0][:, 0:T], AF.Sqrt, ['TF13'], ['PS0'], scale=1.0 / D, bias=RMS_EPS)
            recip(RS[:, 0:T], RS[:, 0:T], ['TF13'], ['TF13'])
            for k in range(8):
                out_fn(k)

        def inproj(l, T, wt, wkey, ncols_tile, chunk_list):
            for ci, (cit, slot) in enumerate(chunk_list):
                bank = ci % 4
                for k in range(8):
                    mm(PS[bank][:, 0:T], wt[:, k * ncols_tile + cit * 128: k * ncols_tile + cit * 128 + 128],
                       HN[:, k, 0:T], k == 0, k == 7, ['PS%d' % bank], [wkey] + ['HN%d' % k])
                cpy('act' if ci % 2 == 0 else 'dve', PJ[:, slot, 3:3 + T], PS[bank][:, 0:T], ['PJ%d' % slot], ['PS%d' % bank])

        def pv(l, c):
            return PV[:, l, c:c + 1]

        def task(job, l, T, L, tok0, first_seg, last_seg, xin, yout, last_layer):
            NCH = T // L
            if l == 0:
                nblk = (T + 127) // 128
                for tb in range(nblk):
                    n = min(128, T - tb * 128)
                    for half in range(2):
                        xt, xk = STG[half]
                        bank = 4 + half
                        dma('sp', xt[0:n, 0:512], xin[tok0 + tb * 128: tok0 + tb * 128 + n, half * 512:(half + 1) * 512], w=[xk])
                        for mmi in range(4):
                            tp(PS[bank][:, mmi * 128: mmi * 128 + n], xt[0:n, mmi * 128:(mmi + 1) * 128], ident[0:n, 0:n],
                               ['PS%d' % bank], [xk, 'CT'])
                        for mmi in range(4):
                            m = half * 4 + mmi
                            cpy('act' if mmi % 2 else 'dve', X[:, m, tb * 128: tb * 128 + n],
                                PS[bank][:, mmi * 128: mmi * 128 + n], ['X%d' % m], ['PS%d' % bank])
            def o1(k):
                stt(HN[:, k, 0:T], X[:, k, 0:T], pv(l, k), RS[:, 0:T], ALU.mult, ALU.mult, ['HN%d' % k], ['X%d' % k, 'PV', 'TF13'])
            rmsnorm_to(T, 0, l, o1)

            wt, wkey = wload([(0, 8, 512, win_src(l, 0, 512))])
            cpy('dve', PJ[:, 0:2, 0:3], HISTA[:, l, :, :], ['PJ0', 'PJ1'], ['HISTA'])
            inproj(l, T, wt, wkey, 512, [(0, 0), (1, 1), (2, 2), (3, 3)])
            cpy('dve', HISTA[:, l, :, :], PJ[:, 0:2, T:T + 3], ['HISTA'], ['PJ0', 'PJ1'])
            for c in range(2):
                xa, gr, gi, aa, a2t, mu_, uu, hh = TF[0], TF[1], TF[2], TF[3], TF[4], TF[5], TF[6], TF[7 + c]
                act(xa[:, 0:T], PJ[:, c, 3:3 + T], AF.Identity, ['TF0'], ['PJ%d' % c, 'PV'],
                    scale=pv(l, 16 + 2 * 3 + c), bias=pv(l, 24 + c))
                for j in range(3):
                    stt(xa[:, 0:T], PJ[:, c, j:j + T], pv(l, 16 + 2 * j + c), xa[:, 0:T], ALU.mult, ALU.add,
                        ['TF0'], ['TF0', 'PJ%d' % c, 'PV'])
                mm(PS[0][:, 0:T], WRbd[:, l, c, :], xa[:, 0:T], True, True, ['PS0'], ['WRbd', 'TF0'])
                mm(PS[1][:, 0:T], WIbd[:, l, c, :], xa[:, 0:T], True, True, ['PS1'], ['WIbd', 'TF0'])
                act(gr[:, 0:T], PS[0][:, 0:T], AF.Sigmoid, ['TF1'], ['PS0', 'PV'], bias=pv(l, 26 + c))
                act(gi[:, 0:T], PS[1][:, 0:T], AF.Sigmoid, ['TF2'], ['PS1', 'PV'], bias=pv(l, 28 + c))
                act(aa[:, 0:T], gr[:, 0:T], AF.Exp, ['TF3'], ['TF1', 'PV'], scale=pv(l, 84 + c))
                act(a2t[:, 0:T], gr[:, 0:T], AF.Exp, ['TF4'], ['TF1', 'PV'], scale=pv(l, 86 + c))
                act(mu_[:, 0:T], a2t[:, 0:T], AF.Sqrt, ['TF5'], ['TF4'], scale=-1.0, bias=1.0)
                tt('dve', uu[:, 0:T], gi[:, 0:T], xa[:, 0:T], ALU.mult, ['TF6'], ['TF2', 'TF0'])
                tt('dve', uu[:, 0:T], uu[:, 0:T], mu_[:, 0:T], ALU.mult, ['TF6'], ['TF6', 'TF5'])
                scan(hh[:, 0:T], aa[:, 0:T], uu[:, 0:T], HL[:, l, c:c + 1], ALU.mult, ALU.add,
                     ['TF%d' % (7 + c)], ['TF3', 'TF6', 'HL'])
                cpy('dve', HL[:, l, c:c + 1], hh[:, T - 1:T], ['HL'], ['TF%d' % (7 + c)])
                act(TB[c][:, 0:T], hh[:, 0:T], AF.Square, ['TB%d' % c], ['TF%d' % (7 + c)])
            for c in range(2):
                mm(PS[2][:, 0:T], onesb[:], TB[c][:, 0:T], c == 0, c == 1, ['PS2'], ['onesb', 'TB%d' % c])
            act(TF[9][:, 0:T], PS[2][:, 0:T], AF.Sqrt, ['TF9'], ['PS2'], scale=1.0 / DA, bias=RMS_EPS)
            recip(TF[9][:, 0:T], TF[9][:, 0:T], ['TF9'], ['TF9'])
            for c in range(2):
                g = PJ[:, 2 + c, 3:3 + T]
                gk = 'PJ%d' % (2 + c)
                t1, t2 = TF[0], TF[1]
                act(t1[:, 0:T], g, AF.Square, ['TF0'], [gk])
                ts('dve', t1[:, 0:T], t1[:, 0:T], 0.044715, 1.0, ALU.mult, ALU.add, ['TF0'], ['TF0'])
                tt('dve', t1[:, 0:T], t1[:, 0:T], g, ALU.mult, ['TF0'], ['TF0', gk])
                act(t2[:, 0:T], t1[:, 0:T], AF.Sigmoid, ['TF1'], ['TF0'], scale=2.0 * math.sqrt(2.0 / math.pi))
                tt('dve', t2[:, 0:T], t2[:, 0:T], g, ALU.mult, ['TF1'], ['TF1', gk])
                stt(t1[:, 0:T], TF[7 + c][:, 0:T], pv(l, 32 + c), TF[9][:, 0:T], ALU.mult, ALU.mult,
                    ['TF0'], ['TF%d' % (7 + c), 'PV', 'TF9'])
                tt('dve', YM[:, c, 0:T], t1[:, 0:T], t2[:, 0:T], ALU.mult, ['YM%d' % c], ['TF0', 'TF1'])

            cpy('dve', PJ[:, 0:11, 2:3], HISTB[:, l, :, :], ['PJ%d' % s for s in range(0, 11)], ['HISTB'])
            for (c0, nch_, s0) in ((512, 4, 0), (1024, 4, 4), (1536, 3, 8)):
                wt, wkey = wload([(0, 8, nch_ * 128, win_src(l, c0, nch_ * 128))])
                inproj(l, T, wt, wkey, nch_ * 128, [(i, s0 + i) for i in range(nch_)])
            cpy('dve', HISTB[:, l, :, :], PJ[:, 0:11, T + 2:T + 3], ['HISTB'], ['PJ%d' % s for s in range(0, 11)])

            def mix(out, slot, mcol, w):
                tt('dve', out, PJ[:, slot, 2:2 + T], PJ[:, slot, 3:3 + T], ALU.subtract, w, ['PJ%d' % slot])
                stt(out, out, pv(l, 34 + mcol), PJ[:, slot, 3:3 + T], ALU.mult, ALU.add, w, w + ['PJ%d' % slot, 'PV'])
            LW, SG = TB[2], TB[3]
            mix(TF[0][:, 0:T], 9, 9, ['TF0'])
            act(LW[0:64, 0:T], TF[0][0:64, 0:T], AF.Tanh, ['TB2'], ['TF0'])
            act(LW[64:128, 0:T], TF[0][64:128, 0:T], AF.Copy, ['TB2'], ['TF0'])
            mix(TF[0][:, 0:T], 10, 10, ['TF0'])
            act(SG[:, 0:T], TF[0][:, 0:T], AF.Sigmoid, ['TB3'], ['TF0'])
            for j in range(3):
                R, K, V, SGW, A, G, KK, RN, K2, BV, BON, CL = [TF[i] for i in range(12)]
                kR, kK, kV, kSGW, kA, kG, kKK, kRN, kK2, kBV, kBON, kCL = ['TF%d' % i for i in range(12)]
                Y = TF[12]
                mix(R[:, 0:T], j, j, [kR])
                mix(K[:, 0:T], 3 + j, 3 + j, [kK])
                mix(V[:, 0:T], 6 + j, 6 + j, [kV])
                jc = slice(j * 128, (j + 1) * 128)
                mm(PS[0][:, 0:T], W2[0:64, l, jc], LW[0:64, 0:T], True, True, ['PS0'], ['W2', 'TB2'])
                act(SGW[:, 0:T], PS[0][:, 0:T], AF.Sigmoid, [kSGW], ['PS0', 'PV'], bias=pv(l, 45 + j))
                mm(PS[1][:, 0:T], A2[64:128, l, jc], LW[64:128, 0:T], True, True, ['PS1'], ['A2', 'TB2'])
                act(A[:, 0:T], PS[1][:, 0:T], AF.Sigmoid, [kA], ['PS1', 'PV'], bias=pv(l, 48 + j))
                mm(PS[2][:, 0:T], G2[:, l, jc], SG[:, 0:T], True, True, ['PS2'], ['G2', 'TB3'])
                cpy('act', G[:, 0:T], PS[2][:, 0:T], [kG], ['PS2'])
                ts('dve', KK[:, 0:T], K[:, 0:T], pv(l, 51 + j), None, ALU.mult, None, [kKK], [kK, 'PV'])
                act(TB[4][:, 0:T], KK[:, 0:T], AF.Square, ['TB4'], [kKK])
                mm(PS[3][:, 0:T], onesbd_b[:], TB[4][:, 0:T], True, True, ['PS3'], ['onesbd_b', 'TB4'])
                act(RN[:, 0:T], PS[3][:, 0:T], AF.Sqrt, [kRN], ['PS3'])
                ts('dve', RN[:, 0:T], RN[:, 0:T], 1e-12, None, ALU.max, None, [kRN], [kRN])
                recip(RN[:, 0:T], RN[:, 0:T], [kRN], [kRN])
                tt('dve', KK[:, 0:T], KK[:, 0:T], RN[:, 0:T], ALU.mult, [kKK], [kKK, kRN])
                ts('dve', K2[:, 0:T], A[:, 0:T], -1.0, pv(l, 54 + j), ALU.add, ALU.mult, [kK2], [kA, 'PV'])
                stt(K2[:, 0:T], K2[:, 0:T], 1.0, K[:, 0:T], ALU.add, ALU.mult, [kK2], [kK2, kK])
                tt('dve', BV[:, 0:T], KK[:, 0:T], A[:, 0:T], ALU.mult, [kBV], [kKK, kA])
                tt('dve', RN[:, 0:T], R[:, 0:T], K2[:, 0:T], ALU.mult, [kRN], [kR, kK2])
                ts('dve', TB[4][:, 0:T], RN[:, 0:T], pv(l, 57 + j), None, ALU.mult, None, ['TB4'], [kRN, 'PV'])
                mm(PS[3][:, 0:T], onesbd_b[:], TB[4][:, 0:T], True, True, ['PS3'], ['onesbd_b', 'TB4'])
                tt('dve', BON[:, 0:T], PS[3][:, 0:T], V[:, 0:T], ALU.mult, [kBON], ['PS3', kV])
                scan(CL[:, 0:T], cmask[:, 0:T], SGW[:, 0:T], 0.0, ALU.mult, ALU.add, [kCL], ['CT', kSGW])
                EG, EGI, EGX = TF[13], RN, K
                kEG, kEGI = 'TF13', kRN
                act(EG[:, 0:T], CL[:, 0:T], AF.Exp, [kEG], [kCL], scale=-C_DEC)
                act(EGI[:, 0:T], CL[:, 0:T], AF.Exp, [kEGI], [kCL], scale=C_DEC)
                tt('dve', SGW[:, 0:T], CL[:, 0:T], SGW[:, 0:T], ALU.subtract, [kSGW], [kCL, kSGW])
                act(SGW[:, 0:T], SGW[:, 0:T], AF.Exp, [kSGW], [kSGW], scale=-C_DEC)
                v3 = lambda ap: ap.rearrange("p (n l) -> p n l", l=L)
                tt('dve', KR[:, 0:NCH, 1, 0:L], v3(R[:, 0:T]), v3(EG[:, 0:T]), ALU.mult, ['KR'], [kR, kEG])
                tt('dve', KR[:, 0:NCH, 0, 0:L], v3(KK[:, 0:T]), v3(SGW[:, 0:T]), ALU.mult, ['KR'], [kKK, kSGW])
                tt('dve', KT_[:, 0:T], K2[:, 0:T], EGI[:, 0:T], ALU.mult, ['KT_'], [kK2, kEGI])
                tt('dve', BT_[:, 0:T], BV[:, 0:T], EGI[:, 0:T], ALU.mult, ['BT_'], [kBV, kEGI])
                cpy('act', VS_[:, 0:T], V[:, 0:T], ['VS_'], [kV])
                Hm = HS[:, l, j, :]
                if SMDT != F32:
                    cpy('dve', HSs[:, j, :], Hm, ['HSs'], ['HS'])
                    Hs = HSs[:, j, :]
                    hsk = 'HSs'
                else:
                    Hs = Hm
                    hsk = 'HS'
                for n in range(NCH):
                    cs = slice(n * L, (n + 1) * L)
                    for hh in range(2):
                        rw = slice(64 * hh, 64 * hh + 64)
                        pg = PS[4 + hh]
                        pk = 'PS%d' % (4 + hh)
                        mm(pg[0:L, 0:L], BT_[rw, cs], KR[rw, n, 0, 0:L], True, True, [pk], ['BT_', 'KR'])
                        mm(pg[0:L, 64:64 + L], KR[rw, n, 0, 0:L], BT_[rw, cs], True, True, [pk], ['BT_', 'KR'])
                        mm(pg[0:L, 128:128 + L], BT_[rw, cs], KR[rw, n, 1, 0:L], True, True, [pk], ['BT_', 'KR'])
                        mm(pg[0:L, 192:192 + L], KT_[rw, cs], KR[rw, n, 0, 0:L], True, True, [pk], ['KT_', 'KR'])
                        mm(pg[0:L, 256:256 + L], KT_[rw, cs], KR[rw, n, 1, 0:L], True, True, [pk], ['KT_', 'KR'])
                        tt('dve', W5[:, hh, :, :], pg[0:64, 0:320].rearrange("p (b l) -> p b l", b=5),
                           mask5[0:64, :, :], ALU.mult, ['W5'], [pk, mk5key])
                    for bi, (src, sk) in enumerate(((KT_, 'KT_'), (BT_, 'BT_'), (VS_, 'VS_'))):
                        tp(PS[7][0:L, bi * 128:(bi + 1) * 128], src[:, cs], ident_s, ['PS7'], [sk, 'CT'])
                    cpy('act', TM[:, :, :], PS[7][0:64, 0:384].rearrange("p (b l) -> p b l", b=3), ['TM'], ['PS7'])
                    for hh in range(2):
                        tt('dve', TTa[:, hh, :, :], W5[:, hh, 0:2, :],
                           ident[0:64, 0:64].rearrange("p (o l) -> p o l", o=1).to_broadcast([64, 2, 64]),
                           ALU.add, ['TTa'], ['W5', 'CT'])
                    qm_cur, qk, qoff = W5, 'W5', 0
                    tt_cur, tk = TTa, 'TTa'
                    nlev = int(math.log2(L)) - 1
                    for lev in range(nlev):
                        lastl = (lev == nlev - 1)
                        qm_nx, qnk = (QMa, 'QMa') if lev % 2 == 0 else (QMb, 'QMb')
                        tt_nx, tnk = (TTb, 'TTb') if lev % 2 == 0 else (TTa, 'TTa')
                        for hh in range(2):
                            mm(PS[6][0:L, hh * 128:hh * 128 + L], qm_cur[0:L, hh, 1, 0:L], qm_cur[0:L, hh, 0, 0:L], True, True, ['PS6'], [qk])
                            if not lastl:
                                mm(PS[6][0:L, hh * 128 + 64:hh * 128 + 64 + L], qm_cur[0:L, hh, 0, 0:L], qm_cur[0:L, hh, 1, 0:L], True, True, ['PS6'], [qk])
                        cpy('act', qm_nx[:].rearrange("p a b l -> p (a b l)"), PS[6][0:64, 0:256], [qnk], ['PS6'])
                        for hh in range(2):
                            mm(PS[6][0:L, 256 + hh * 128:256 + hh * 128 + L], tt_cur[0:L, hh, 1, 0:L], qm_nx[0:L, hh, 0, 0:L], True, True, ['PS6'], [tk, qnk])
                            if not lastl:
                                mm(PS[6][0:L, 256 + hh * 128 + 64:256 + hh * 128 + 64 + L], qm_nx[0:L, hh, 0, 0:L], tt_cur[0:L, hh, 1, 0:L], True, True, ['PS6'], [tk, qnk])
                        tt('dve', tt_nx[:].rearrange("p a b l -> p (a b l)"), PS[6][0:64, 256:512],
                           tt_cur[:].rearrange("p a b l -> p (a b l)"), ALU.add, [tnk], ['PS6', tk])
                        qm_cur, qk = qm_nx, qnk
                        tt_cur, tk = tt_nx, tnk
                    for hh in range(2):
                        rw = slice(64 * hh, 64 * hh + 64)
                        fo = slice(64 * hh, 64 * hh + 64)
                        mm(PS[5][0:L, 384 + 64 * hh:448 + 64 * hh], KR[rw, n, 0, 0:L], Hs[rw, :], True, False, ['PS5'], ['KR', hsk])
                        mm(PS[5][0:L, 384 + 64 * hh:448 + 64 * hh], W5[0:L, hh, 3, 0:L], TM[0:L, 2, fo], False, True, ['PS5'], ['W5', 'TM'])
                    act(XN[:].rearrange("p a v -> p (a v)"), PS[5][0:64, 384:512], AF.Copy, ['XN'], ['PS5'], scale=-1.0)
                    for hh in range(2):
                        mm(PS[7][0:L, 384 + 64 * hh:448 + 64 * hh], tt_cur[0:L, hh, 0, 0:L], XN[0:L, hh, :], True, True, ['PS7'], [tk, 'XN'])
                    cpy('dve', UU[:].rearrange("p a v -> p (a v)"), PS[7][0:64, 384:512], ['UU'], ['PS7'])
                    for hh in range(2):
                        rw = slice(64 * hh, 64 * hh + 64)
                        fo = slice(64 * hh, 64 * hh + 64)
                        mm(PS[4][rw, 384:384 + L], Hs[rw, :], KR[rw, n, 1, 0:L], True, False, ['PS4'], [hsk, 'KR'])
                        mm(PS[4][rw, 384:384 + L], UU[0:L, hh, :], W5[0:L, hh, 2, 0:L], False, False, ['PS4'], ['UU', 'W5'])
                        mm(PS[4][rw, 384:384 + L], TM[0:L, 2, fo], W5[0:L, hh, 4, 0:L], False, True, ['PS4'], ['TM', 'W5'])
                    cpy('act', Y[:, cs], PS[4][:, 384:384 + L], ['TF12'], ['PS4'])
                    for hh in range(2):
                        rw = slice(64 * hh, 64 * hh + 64)
                        fo = slice(64 * hh, 64 * hh + 64)
                        mm(PS[3][rw, 0:64], TM[0:L, 1, fo], UU[0:L, hh, :], True, False, ['PS3'], ['TM', 'UU'])
                        mm(PS[3][rw, 0:64], TM[0:L, 0, fo], TM[0:L, 2, fo], False, True, ['PS3'], ['TM'])
                    gl = EG[:, n * L + L - 1: n * L + L]
                    act(HG[:, :], Hm, AF.Identity, ['HG'], ['HS', kEG], scale=gl)
                    stt(Hm, PS[3][:, 0:64], gl, HG[:, :], ALU.mult, ALU.add, ['HS'], ['PS3', kEG, 'HG'])

                mm(PS[0][:, 0:T], onesbd_f, Y[:, 0:T], True, True, ['PS0'], ['CT', 'TF12'])
                stt(Y[:, 0:T], PS[0][:, 0:T], -1.0 / 64, Y[:, 0:T], ALU.mult, ALU.add, ['TF12'], ['PS0', 'TF12'])
                act(CL[:, 0:T], Y[:, 0:T], AF.Square, [kCL], ['TF12'])
                mm(PS[1][:, 0:T], onesbd_f, CL[:, 0:T], True, True, ['PS1'], ['CT', kCL])
                act(CL[:, 0:T], PS[1][:, 0:T], AF.Sqrt, [kCL], ['PS1'], scale=1.0 / 64, bias=GN_EPS_B)
                recip(CL[:, 0:T], CL[:, 0:T], [kCL], [kCL])
                tt('dve', Y[:, 0:T], Y[:, 0:T], CL[:, 0:T], ALU.mult, ['TF12'], ['TF12', kCL])
                ts('dve', Y[:, 0:T], Y[:, 0:T], pv(l, 60 + j), pv(l, 63 + j), ALU.mult, ALU.add, ['TF12'], ['TF12', 'PV'])
                tt('dve', Y[:, 0:T], Y[:, 0:T], BON[:, 0:T], ALU.add, ['TF12'], ['TF12', kBON])
                tt('dve', YM[:, 2 + j, 0:T], Y[:, 0:T], G[:, 0:T], ALU.mult, ['YM%d' % (2 + j)], ['TF12', kG])

            cpy('dve', PJ[:, 0:3, 0:3], HISTC[:, l, :, :], ['PJ0', 'PJ1', 'PJ2'], ['HISTC'])
            for (c0, nch_, s0) in ((1920, 4, 0), (2432, 4, 4), (2944, 1, 8)):
                wt, wkey = wload([(0, 8, nch_ * 128, win_src(l, c0, nch_ * 128))])
                inproj(l, T, wt, wkey, nch_ * 128, [(i, s0 + i) for i in range(nch_)])
            cpy('dve', HISTC[:, l, :, :], PJ[:, 0:3, T:T + 3], ['HISTC'], ['PJ0', 'PJ1', 'PJ2'])
            KRf = KR[:].rearrange("p n a l -> p (n a l)")
            QB = [TF[12], TF[13], KT_]
            KS = [BT_, VS_, KRf]
            kQB = ['TF12', 'TF13', 'KT_']
            kKS = ['BT_', 'VS_', 'KR']
            for j in range(3):
                xc = TF[0]
                act(xc[:, 0:T], PJ[:, j, 3:3 + T], AF.Identity, ['TF0'], ['PJ%d' % j, 'PV'],
                    scale=pv(l, 66 + 3 * 3 + j), bias=pv(l, 78 + j))
                for t_ in range(3):
                    stt(xc[:, 0:T], PJ[:, j, t_:t_ + T], pv(l, 66 + 3 * t_ + j), xc[:, 0:T], ALU.mult, ALU.add,
                        ['TF0'], ['TF0', 'PJ%d' % j, 'PV'])
                act(TB[2][:, 0:T], xc[:, 0:T], AF.Silu, ['TB2'], ['TF0'])
                mm(PS[0][:, 0:T], WQbd[:, l, j, :], TB[2][:, 0:T], True, True, ['PS0'], ['WQbd', 'TB2'])
                mm(PS[1][:, 0:T], WKbd[:, l, j, :], TB[2][:, 0:T], True, True, ['PS1'], ['WKbd', 'TB2'])
                cpy('act', QB[j][:, 0:T], PS[0][:, 0:T], [kQB[j]], ['PS0'])
                act(KS[j][:, 0:T], PS[1][:, 0:T], AF.Copy, [kKS[j]], ['PS1'], scale=0.125)

            def vsrc(j, cs_):
                return PJ[:, 3 + j, 3 + cs_.start:3 + cs_.stop], 'PJ%d' % (3 + j)
            allT = slice(0, T)
            srcs = [(QB[j][:, 0:T], kQB[j]) for j in range(3)] + [(KS[j][:, 0:T], kKS[j]) for j in range(3)] + \
                   [vsrc(j, allT) for j in range(3)]
            for kc, (s_, sk) in enumerate(srcs):
                mm(PS[2][0:6, 0:T], WIF[:, l, kc, 0:6], s_, kc == 0, kc == 8, ['PS2'], ['WIF', sk])
            for kc, (s_, sk) in enumerate(srcs):
                mm(PS[3][0:6, 0:T], WIF[:, l, kc, 6:12], s_, kc == 0, kc == 8, ['PS3'], ['WIF', sk])
            G6 = [TF[i][0:6, :] for i in range(4, 12)]
            IP, L1, CS, AH, MX, MT, WL, NMX = G6
            kIP, kL1, kCS, kAH, kMX, kMT, kWL, kNMX = ['TF%d' % i for i in range(4, 12)]
            act(IP[:, 0:T], PS[2][0:6, 0:T], AF.Identity, [kIP], ['PS2', 'BI'], bias=BI[:, l:l + 1])
            act(L1[:, 0:T], PS[3][0:6, 0:T], AF.Exp, [kL1], ['PS3', 'NBF'], scale=-1.0, bias=NBF[:, l:l + 1])
            act(L1[:, 0:T], L1[:, 0:T], AF.Ln, [kL1], [kL1], bias=1.0)
            scan(CS[:, 0:T], ones_c[0:6, 0:1].to_broadcast([6, T]), L1[:, 0:T], 0.0, ALU.mult, ALU.add, [kCS], ['ones_f', kL1])
            tt('dve', AH[:, 0:T], IP[:, 0:T], CS[:, 0:T], ALU.add, [kAH], [kIP, kCS])
            scan(MX[:, 0:T], ones_c[0:6, 0:1].to_broadcast([6, T]), AH[:, 0:T], MS[:, l:l + 1], ALU.mult, ALU.max, [kMX], ['ones_f', kAH, 'MS'])
            tt('dve', MT[:, 0:T], MX[:, 0:T], CS[:, 0:T], ALU.subtract, [kMT], [kMX, kCS])
            ts('dve', NMX[:, 0:T], MX[:, 0:T], -1.0, None, ALU.mult, None, [kNMX], [kMX])
            for n in range(NCH):
                cs = slice(n * L, (n + 1) * L)
                prev = MS[:, l:l + 1] if n == 0 else MX[:, n * L - 1:n * L]
                ts('dve', WL[:, cs], MX[:, cs], prev, None, ALU.subtract, None, [kWL], [kMX, 'MS'])
            cpy('dve', MS[:, l:l + 1], MT[:, T - 1:T], ['MS'], [kMT])
            YC = [TF[1], TF[2], TF[3]]
            Cm = CA[:, l, :, :]
            Cs = Cm
            csk = 'CA'
            for n in range(NCH):
                cs = slice(n * L, (n + 1) * L)
                for j in range(3):
                    vs_, vk_ = vsrc(j, cs)
                    tp(PS[7][0:L, j * 128:(j + 1) * 128], vs_, ident, ['PS7'], [vk_, 'CT'])
                cpy('act', VA[0:L, :, 0:64], PS[7][0:L, 0:384].rearrange("p (h v) -> p h v", h=6), ['VA'], ['PS7'])
                for j in range(3):
                    tp(PS[4][0:L, j * 128:(j + 1) * 128], KS[j][:, cs], ident, ['PS4'], [kKS[j], 'CT'])
                cpy('dve', KTm[0:L, :, :], PS[4][0:L, 0:384].rearrange("p (h v) -> p h v", h=6), ['KTm'], ['PS4'])
                tp(PS[6][0:L, 0:6], WL[:, cs], ident[0:6, 0:6], ['PS6'], [kWL, 'CT'])
                tp(PS[6][0:L, 6:12], MT[:, cs], ident[0:6, 0:6], ['PS6'], [kMT, 'CT'])
                act(WE[0:L, :, :], PS[6][0:L, 0:12].rearrange("p (a h) -> p a h", a=2), AF.Exp, ['WE'], ['PS6'], scale=-1.0)
                mm(PS[2][0:L, 0:384].rearrange("p (h l) -> p h l", h=6)[:, :, 0:L], ident[0:L, 0:L], negm6[0:L, :, 0:L],
                   True, False, ['PS2'], ['CT'])
                for h in range(6):
                    o_ = PS[2][0:L, h * 64:h * 64 + L]
                    mm(o_, sel6[:, h, 0:L], NMX[:, cs], False, False, ['PS2'], ['CT', kNMX])
                    mm(o_, AH[:, cs], sel6[:, h, 0:L], False, h == 5, ['PS2'], ['CT', kAH])
                act(DD[0:L, :, 0:L], PS[2][0:L, 0:384].rearrange("p (h l) -> p h l", h=6)[:, :, 0:L], AF.Exp, ['DD'], ['PS2'])
                for h in range(6):
                    j, hh = h // 2, h % 2
                    rw = slice(64 * hh, 64 * hh + 64)
                    mm(PS[5][0:L, h * 64:h * 64 + L], KS[j][rw, cs], QB[j][rw, cs], True, True, ['PS5'], [kKS[j], kQB[j]])
                tt('dve', STt[0:L, :, 0:L], PS[5][0:L, 0:384].rearrange("p (h l) -> p h l", h=6)[:, :, 0:L],
                   DD[0:L, :, 0:L], ALU.mult, ['STt'], ['PS5', 'DD'])
                for h in range(6):
                    j, hh = h // 2, h % 2
                    rw = slice(64 * hh, 64 * hh + 64)
                    mm(PS[1][0:L, h * 65:h * 65 + 65], STt[0:L, h, 0:L], VA[0:L, h, :], True, True, ['PS1'], ['STt', 'VA'])
                    mm(PS[6][0:L, 64 + h * 65:64 + h * 65 + 65], QB[j][rw, cs], Cs[rw, j, :], True, True, ['PS6'], [kQB[j], csk])
                tt('dve', T1[0:L], PS[6][0:L, 64:64 + 390].rearrange("p (h v) -> p h v", h=6),
                   WE[0:L, 0, :].rearrange("p (h o) -> p h o", o=1).to_broadcast([L, 6, 65]), ALU.mult, ['T1'], ['PS6', 'WE'])
                tt('dve', NUM[0:L], PS[1][0:L, 0:390].rearrange("p (h v) -> p h v", h=6), T1[0:L], ALU.add, ['NUM'], ['PS1', 'T1'])
                act(DEN[0:L, :], NUM[0:L, :, 64], AF.Abs, ['DEN'], ['NUM'])
                tt('dve', DEN[0:L, :], DEN[0:L, :], WE[0:L, 1, :], ALU.max, ['DEN'], ['DEN', 'WE'])
                recip(DEN[0:L, :], DEN[0:L, :], ['DEN'], ['DEN'])
                tt('dve', HTt[0:L], NUM[0:L, :, 0:64], DEN[0:L, :].rearrange("p (h o) -> p h o", o=1).to_broadcast([L, 6, 64]),
                   ALU.mult, ['HTt'], ['NUM', 'DEN'])
                red(MEAN[0:L, :], HTt[0:L], ['MEAN'], ['HTt'])
                stt(HC[0:L], MEAN[0:L, :].rearrange("p (h o) -> p h o", o=1).to_broadcast([L, 6, 64]), -1.0 / 64, HTt[0:L],
                    ALU.mult, ALU.add, ['HC'], ['MEAN', 'HTt'])
                act(SQ[0:L], HC[0:L], AF.Square, ['SQ'], ['HC'])
                red(VAR[0:L, :], SQ[0:L], ['VAR'], ['SQ'])
                act(VAR[0:L, :], VAR[0:L, :], AF.Sqrt, ['VAR'], ['VAR'], scale=1.0 / 64, bias=GN_EPS_C)
                recip(VAR[0:L, :], VAR[0:L, :], ['VAR'], ['VAR'])
                tt('dve', HC[0:L], HC[0:L], VAR[0:L, :].rearrange("p (h o) -> p h o", o=1).to_broadcast([L, 6, 64]),
                   ALU.mult, ['HC'], ['HC', 'VAR'])
                for j in range(3):
                    tp(PS[0][:, j * 64:j * 64 + L], HC[0:L, 2 * j:2 * j + 2, :].rearrange("p h v -> p (h v)"), ident[0:L, 0:L],
                       ['PS0'], ['HC', 'CT'])
                for j in range(3):
                    act(YC[j][:, cs], PS[0][:, j * 64:j * 64 + L], AF.Identity, ['TF%d' % (1 + j)], ['PS0', 'PV'], scale=pv(l, 81 + j))
                tt('dve', VW[0:L], VA[0:L], DD[0:L, :, L - 1:L].to_broadcast([L, 6, 65]), ALU.mult, ['VW'], ['VA', 'DD'])
                for h in range(6):
                    j, hh = h // 2, h % 2
                    rw = slice(64 * hh, 64 * hh + 64)
                    mm(PS[3][rw, j * 65:j * 65 + 65], KTm[0:L, h, :], VW[0:L, h, :], True, True, ['PS3'], ['KTm', 'VW'])
                ts('dve', WLE[:, :], pairsel, WL[:, n * L + L - 1:n * L + L], None, ALU.mult, None, ['WLE'], ['CT', kWL])
                mm(PS[6][:, 480:483], selb, WLE[:, :], True, True, ['PS6'], ['CT', 'WLE'])
                act(W0[:, :], PS[6][:, 480:483], AF.Exp, ['W0'], ['PS6'], scale=-1.0)
                tt('dve', CT2[:], Cm, W0[:, :].rearrange("p (h o) -> p h o", o=1).to_broadcast([128, 3, 65]), ALU.mult, ['CT2'], ['CA', 'W0'])
                tt('dve', Cm, PS[3][:, 0:195].rearrange("p (h v) -> p h v", h=3), CT2[:], ALU.add, ['CA'], ['PS3', 'CT2'])
            for j in range(3):
                act(TF[0][:, 0:T], PJ[:, 6 + j, 3:3 + T], AF.Sigmoid, ['TF0'], ['PJ%d' % (6 + j)])
                tt('dve', YM[:, 5 + j, 0:T], TF[0][:, 0:T], YC[j][:, 0:T], ALU.mult, ['YM%d' % (5 + j)], ['TF0', 'TF%d' % (1 + j)])

            for half in range(2):
                wt, wkey = wload([(0, 8, 512, I['w_out'][l][:, half * 512:(half + 1) * 512].rearrange("(k p) n -> p k n", p=128))])
                for mi in range(4):
                    m = half * 4 + mi
                    bank = mi % 4
                    for k in range(8):
                        mm(PS[bank][:, 0:T], wt[:, k * 512 + mi * 128:k * 512 + mi * 128 + 128], YM[:, k, 0:T], k == 0, k == 7,
                           ['PS%d' % bank], [wkey, 'YM%d' % k])
                    tt('dve', X[:, m, 0:T], PS[bank][:, 0:T], X[:, m, 0:T], ALU.add, ['X%d' % m], ['PS%d' % bank, 'X%d' % m])
            def o2(k):
                stt(HN[:, k, 0:T], X[:, k, 0:T], pv(l, 8 + k), RS[:, 0:T], ALU.mult, ALU.mult, ['HN%d' % k], ['X%d' % k, 'PV', 'TF13'])
            rmsnorm_to(T, 8, l, o2)
            for i in range(11):
                wf = I['w_ffn_in'][l]
                wt, wkey = wload([(0, 8, 256, wf[:, 256 * i:256 * i + 256].rearrange("(k p) n -> p k n", p=128)),
                                  (2048, 8, 256, wf[:, DFF + 256 * i:DFF + 256 * i + 256].rearrange("(k p) n -> p k n", p=128))])
                for q in range(2):
                    f = 2 * i + q
                    bg, bu = (0, 1) if q == 0 else (2, 3)
                    for k in range(8):
                        mm(PS[bg][:, 0:T], wt[:, k * 256 + q * 128:k * 256 + q * 128 + 128], HN[:, k, 0:T], k == 0, k == 7,
                           ['PS%d' % bg], [wkey, 'HN%d' % k])
                    for k in range(8):
                        mm(PS[bu][:, 0:T], wt[:, 2048 + k * 256 + q * 128:2048 + k * 256 + q * 128 + 128], HN[:, k, 0:T], k == 0, k == 7,
                           ['PS%d' % bu], [wkey, 'HN%d' % k])
                    tk_ = TF[q]
                    act(tk_[:, 0:T], PS[bg][:, 0:T], AF.Silu, ['TF%d' % q], ['PS%d' % bg])
                    tt('dve', ACTT[:, f, 0:T], tk_[:, 0:T], PS[bu][:, 0:T], ALU.mult, actk(f), ['TF%d' % q, 'PS%d' % bu])
            for m in range(8):
                wo = I['w_ffn_out'][l]
                wt, wkey = wload([(0, 22, 128, wo[:, 128 * m:128 * m + 128].rearrange("(k p) n -> p k n", p=128))])
                bank = m % 4
                for k in range(22):
                    mm(PS[bank][:, 0:T], wt[:, k * 128:k * 128 + 128], ACTT[:, k, 0:T], k == 0, k == 21,
                       ['PS%d' % bank], [wkey] + actk(k))
                tt('dve', X[:, m, 0:T], PS[bank][:, 0:T], X[:, m, 0:T], ALU.add, ['X%d' % m], ['PS%d' % bank, 'X%d' % m])
            if last_layer:
                def o3(k):
                    stt(TF[k][:, 0:T], X[:, k, 0:T], NF[:, k:k + 1], RS[:, 0:T], ALU.mult, ALU.mult, ['TF%d' % k], ['X%d' % k, 'NF', 'TF13'])
                rmsnorm_to(T, 0, l, o3)
                nblk = (T + 127) // 128
                for tb in range(nblk):
                    n = min(128, T - tb * 128)
                    for half in range(2):
                        xt, xk = STG[half]
                        bank = 4 + half
                        for mmi in range(4):
                            m = half * 4 + mmi
                            tp(PS[bank][0:n, mmi * 128:(mmi + 1) * 128], TF[m][:, tb * 128:tb * 128 + n], ident,
                               ['PS%d' % bank], ['TF%d' % m, 'CT'])
                        cpy('act' if half else 'dve', xt[0:n, 0:512], PS[bank][0:n, 0:512], [xk], ['PS%d' % bank])
                        dma('sp', yout[tok0 + tb * 128:tok0 + tb * 128 + n, half * 512:(half + 1) * 512], xt[0:n, 0:512], r=[xk], is_output=True)

        for job in jobs:
            if job == 'p':
                T, L, nseg, xin, yout = TS, 64, TP // TS, I['xp'], O['yp']
                for t_, k_ in ((HISTA, 'HISTA'), (HISTB, 'HISTB'), (HISTC, 'HISTC'), (HL, 'HL'), (HS, 'HS'), (CA, 'CA'), (MS, 'MS')):
                    memset('dve', t_[:], 0.0, [k_])
            else:
                T, L, nseg, xin, yout = TSMP, 32, 1, I['xs'], O['ys']
                for l in range(DEPTH):
                    for j_ in range(3):
                        dma('sp', HISTA[:, l, :, j_], I['s_conv_a'][l, j_].rearrange("(c p) -> p c", p=128), w=['HISTA'], slow=True)
                        dma('sp', HISTC[:, l, :, j_], I['s_conv_c'][l, j_].rearrange("(c p) -> p c", p=128), w=['HISTC'], slow=True)
                    dma('sp', HISTB[:, l, :, 0], I['s_shift'][l].rearrange("(c p) -> p c", p=128), w=['HISTB'], slow=True)
                    dma('sp', HL[:, l, :], I['s_lru'][l].rearrange("(c p) -> p c", p=128), w=['HL'], slow=True)
                    dma('sp', CA[:, l, :, 0:64], I['s_mem_c'][l].rearrange("(jp hh) n v -> (hh n) jp v", hh=2), w=['CA'])
                    dma('sp', CA[:, l, :, 64], I['s_mem_n'][l].rearrange("(jp hh) n -> (hh n) jp", hh=2), w=['CA'], slow=True)
                    dma('sp', WS[:], I['s_wkv'][l].rearrange("h v k -> v h k"), w=['WS'])
                    for j in range(3):
                        tp(PS[4][:, j * 64:j * 64 + 64], WS[:, 2 * j:2 * j + 2, :].rearrange("p h k -> p (h k)"), ident[0:64, 0:64],
                           ['PS4'], ['WS', 'CT'])
                    cpy('dve', HS[:, l, :, :], PS[4][:, 0:192].rearrange("p (j v) -> p j v", j=3), ['HS'], ['PS4'])
                dma('sp', MS[:], I['s_mem_m'].rearrange("l h -> h l"), w=['MS'], slow=True)
            for seg in range(nseg):
                for l in range(DEPTH):
                    task(job, l, T, L, seg * T, seg == 0, seg == nseg - 1, xin, yout, l == DEPTH - 1)
            pre = job + '_'
            for l in range(DEPTH):
                for j_ in range(3):
                    dma('sp', O[pre + 'conv_a'][l, j_].rearrange("(c p) -> p c", p=128), HISTA[:, l, :, j_], r=['HISTA'], slow=True, is_output=True)
                    dma('sp', O[pre + 'conv_c'][l, j_].rearrange("(c p) -> p c", p=128), HISTC[:, l, :, j_], r=['HISTC'], slow=True, is_output=True)
                dma('sp', O[pre + 'shift'][l].rearrange("(c p) -> p c", p=128), HISTB[:, l, :, 0], r=['HISTB'], slow=True, is_output=True)
                dma('sp', O[pre + 'lru'][l].rearrange("(c p) -> p c", p=128), HL[:, l, :], r=['HL'], slow=True, is_output=True)
                dma('sp', O[pre + 'mem_c'][l].rearrange("(jp hh) n v -> (hh n) jp v", hh=2), CA[:, l, :, 0:64], r=['CA'], is_output=True)
                dma('sp', O[pre + 'mem_n'][l].rearrange("(jp hh) n -> (hh n) jp", hh=2), CA[:, l, :, 64], r=['CA'], slow=True, is_output=True)
                for j in range(3):
                    tp(PS[4][0:64, j * 128:(j + 1) * 128], HS[:, l, j, :], ident, ['PS4'], ['HS', 'CT'])
                cpy('dve', WS[:], PS[4][0:64, 0:384].rearrange("p (h k) -> p h k", h=6), ['WS'], ['PS4'])
                dma('sp', O[pre + 'wkv'][l].rearrange("h v k -> v h k"), WS[:], r=['WS'], is_output=True)
            dma('sp', O[pre + 'mem_m'].rearrange("l h -> h l"), MS[:], r=['MS'], slow=True, is_output=True)
        P.finish()
        P.emit(nc, st)
    return nc, cnp


_CACHE = {}
TS_DEFAULT = 512


def kernel(**inputs):
    inputs = {k: np.asarray(v) for k, v in inputs.items()}
    xp = inputs['x_prompt']
    xs = inputs['x_sample']
    DEPTH = inputs['norm1'].shape[0]
    B, TP, _ = xp.shape
    TS = min(TS_DEFAULT, TP)
    key = (DEPTH, TP, TS)
    if key not in _CACHE:
        _CACHE[key] = build(DEPTH, TP, TS)
    nc, cnp = _CACHE[key]
    f32 = lambda a: np.ascontiguousarray(a, dtype=np.float32)
    in_maps = []
    for c in range(NCORES):
        m = {'xp': f32(xp[c % B]), 'xs': f32(xs[c]), 'consts': cnp}
        m['s_conv_a'] = f32(inputs['state_conv_a'][:, c])
        m['s_lru'] = f32(inputs['state_lru'][:, c])
        m['s_shift'] = f32(inputs['state_shift_b'][:, c, 0])
        m['s_wkv'] = f32(inputs['state_wkv'][:, c])
        m['s_conv_c'] = f32(inputs['state_conv_c'][:, c])
        m['s_mem_c'] = f32(inputs['state_mem_c'][:, c])
        m['s_mem_n'] = f32(inputs['state_mem_n'][:, c])
        m['s_mem_m'] = f32(inputs['state_mem_m'][:, c])
        for k in WNAMES:
            m[k] = f32(inputs[k])
        in_maps.append(m)
    res = run_bass_kernel_spmd(nc, in_maps, core_ids=list(range(NCORES)))
    R = res.results
    y_prompt = np.stack([R[b]['yp'] for b in range(B)], 0)
    y_sample = np.stack([R[c]['ys'] for c in range(NCORES)], 0)
    outs = [y_prompt, y_sample]
    names = ['conv_a', 'lru', 'shift', 'wkv', 'conv_c', 'mem_c', 'mem_n', 'mem_m']
    for jb, n in (('p', B), ('s', NCORES)):
        for nm in names:
            a = np.stack([R[c]['o' + jb + '_' + nm] for c in range(n)], 1)
            if nm == 'shift':
                a = a[:, :, None, :]
            outs.append(np.ascontiguousarray(a.astype(np.float32)))
    return tuple(outs)
```

```python
import math
from contextlib import ExitStack
import numpy as np
import concourse.bass as bass
import concourse.mybir as mybir
from concourse.bass_utils import run_bass_kernel_spmd
from concourse.alu_op_type import AluOpType as ALU

AF = mybir.ActivationFunctionType
F32 = mybir.dt.float32
BF16 = mybir.dt.bfloat16
AX = mybir.AxisListType

EPOCH = 16000
NDMA_SLOTS = 10
SAME_ENGINE_SYNC = True
NOSYNC_ENGINES = ('pe',)

D = 1024
DIN = 3072
DA = 256
DB = 384
DBIN = 1408
DC = 384
DFF = 2816
NCORES = 8
TSMP = 32
C_DEC = math.exp(-0.5)
RMS_EPS = 1e-6
GN_EPS_B = 64e-5
GN_EPS_C = 1e-6


class Prog:
    ENG = ['pe', 'act', 'dve', 'pool', 'sp']

    def __init__(self):
        self.stream = {e: [] for e in self.ENG}
        self.n = {e: 0 for e in self.ENG}
        self.lastw = {}
        self.readers = {}
        self.seen = {e: {} for e in self.ENG}
        self.dma_slot_next = {e: 0 for e in self.ENG}
        self.dma_slot_val = {}
        self.out_tokens = []
        self.pe_rg = {}

    def _deps(self, eng, reads, writes, extra=(), force=()):
        toks = list(extra) + list(force)
        forced_src = set(t[0] for t in force)
        for k in reads:
            t = self.lastw.get(k)
            if t:
                toks.append(t)
        for k in writes:
            t = self.lastw.get(k)
            if t:
                toks.append(t)
            toks.extend(self.readers.get(k, {}).values())
        need = {}
        for (src, val) in toks:
            if need.get(src, 0) < val:
                need[src] = val
        out = []
        for src, val in need.items():
            if src == ('e', eng) and (not SAME_ENGINE_SYNC or eng in NOSYNC_ENGINES) and src not in forced_src:
                continue
            if self.seen[eng].get(src, 0) >= val:
                continue
            self.seen[eng][src] = val
            out.append((src, val))
        return out

    def op(self, eng, fn, w=(), r=(), rg=None):
        w = list(w) + [k for k in r if k.startswith('PS') and k[2:].isdigit()]
        force = []
        if eng == 'pe':
            for k in w:
                if k.startswith('PS'):
                    prev = self.pe_rg.get(k)
                    if prev is not None and prev[0] != rg and self.lastw.get(k) == prev[1]:
                        force.append(prev[1])
        waits = self._deps(eng, r, w, force=force)
        self.n[eng] += 1
        tok = (('e', eng), self.n[eng])
        for k in w:
            self.lastw[k] = tok
            self.readers[k] = {}
        for k in r:
            self.readers.setdefault(k, {})[('e', eng)] = tok
        self.stream[eng].append((waits, fn, tok))
        if eng == 'pe':
            for k in w:
                if k.startswith('PS'):
                    self.pe_rg[k] = (rg, tok)
        return tok

    def dma(self, q, fn, w=(), r=(), is_output=False):
        slot = (q, self.dma_slot_next[q] % NDMA_SLOTS)
        self.dma_slot_next[q] += 1
        src = ('d', slot)
        prev = self.dma_slot_val.get(slot, 0)
        extra = [(src, prev)] if prev else []
        waits = self._deps(q, r, w, extra)
        val = prev + 16
        self.dma_slot_val[slot] = val
        tok = (src, val)
        for k in w:
            self.lastw[k] = tok
            self.readers[k] = {}
        for k in r:
            self.readers.setdefault(k, {})[src] = tok
        self.stream[q].append((waits, fn, tok))
        if is_output:
            self.out_tokens.append(tok)
        return tok

    def finish(self):
        need = {}
        for (src, val) in self.out_tokens:
            need[src] = max(need.get(src, 0), val)
        self.stream['sp'].append((list(need.items()), None, None))

    def emit(self, nc, stack):
        sems = {}

        def getsem(src, val):
            if src[0] == 'e':
                ep = (val - 1) // EPOCH
                key = (src, ep)
                v = val - ep * EPOCH
            else:
                key = (src, 0)
                v = val
            if key not in sems:
                sems[key] = stack.enter_context(nc.semaphore("s%d" % len(sems)))
            return sems[key], v

        for e in self.ENG:
            for (waits, fn, tok) in self.stream[e]:
                for (src, val) in waits:
                    getsem(src, val)
                if tok is not None:
                    getsem(*tok)
        block = stack.enter_context(nc.Block())
        names = {'pe': 'tensor', 'act': 'scalar', 'dve': 'vector', 'pool': 'gpsimd', 'sp': 'sync'}
        for e in self.ENG:
            items = self.stream[e]
            if not items:
                continue

            def body(engh, items=items):
                for (waits, fn, tok) in items:
                    for (src, val) in waits:
                        s, v = getsem(src, val)
                        engh.wait_ge(s, v)
                    if fn is None:
                        continue
                    ins = fn(engh)
                    s, v = getsem(*tok)
                    ins.then_inc(s, 16 if tok[0][0] == 'd' else 1)
            getattr(block, names[e])(body)
        self.nsems = len(sems)


def _consts(TS):
    cw = {}
    cols = []

    def add(name, arr):
        a = np.zeros((128, arr.shape[1]), np.float32)
        a[:arr.shape[0]] = arr
        cw[name] = (sum(c.shape[1] for c in cols), arr.shape[1])
        cols.append(a)
    add('ident', np.eye(128, dtype=np.float32))
    bd = np.zeros((128, 128), np.float32)
    bd[:64, :64] = 1
    bd[64:, 64:] = 1
    add('onesbd', bd)
    add('ident2', np.concatenate([np.eye(64, dtype=np.float32)] * 2, 0))
    r = np.arange(128)[:, None] % 64
    c = np.arange(64)[None, :]
    su = (c > r).astype(np.float32)
    sl = (c < r).astype(np.float32)
    ui = (c >= r).astype(np.float32)
    add('mask5', np.concatenate([-su, -sl, ui, su, ui], 1))
    negm = np.where(np.arange(64)[:, None] > c, -30000.0, 0.0).astype(np.float32)
    add('negm6', np.tile(negm, (1, 6)))
    cm = np.ones((128, TS), np.float32)
    cm[:, ::64] = 0
    add('cmask', cm)
    sel6 = np.zeros((6, 6, 64), np.float32)
    for h in range(6):
        sel6[h, h, :] = 1
    add('sel6', sel6.reshape(6, 384))
    selb = np.zeros((6, 128), np.float32)
    for k in range(6):
        selb[k, (k % 2) * 64:(k % 2) * 64 + 64] = 1
    add('selb', selb)
    ps_ = np.zeros((6, 3), np.float32)
    for k in range(6):
        ps_[k, k // 2] = 1
    add('pairsel', ps_)
    return np.concatenate(cols, 1), cw


WNAMES = ['norm1', 'w_in', 'conv_a_w', 'conv_a_b', 'lru_wr', 'lru_br', 'lru_wi', 'lru_bi', 'lru_lambda',
          'norm_a', 'rwkv_mu', 'rwkv_w0', 'rwkv_w2', 'rwkv_a0', 'rwkv_a2', 'rwkv_g2', 'rwkv_kk', 'rwkv_ka',
          'rwkv_rk', 'rwkv_lnw', 'rwkv_lnb', 'conv_c_w', 'conv_c_b', 'mlstm_wq', 'mlstm_wk', 'mlstm_wif',
          'mlstm_bif', 'mlstm_gn', 'w_out', 'norm2', 'w_ffn_in', 'w_ffn_out', 'norm_f']


def build(DEPTH, TP, TS, SMDT=F32, jobs=('p', 's')):
    assert TP % TS == 0 and TS % 128 == 0
    nc = bass.Bass("TRN2", target_bir_lowering=False)
    cnp, cw = _consts(TS)
    CW = cnp.shape[1]

    def din(name, shape):
        return nc.dram_tensor(name, list(shape), F32, kind="ExternalInput").ap()

    def dout(name, shape):
        return nc.dram_tensor(name, list(shape), F32, kind="ExternalOutput").ap()

    I = {}
    I['xp'] = din('xp', [TP, D])
    I['xs'] = din('xs', [TSMP, D])
    I['consts'] = din('consts', [128, CW])
    st_shapes = {'conv_a': [DEPTH, 3, DA], 'lru': [DEPTH, DA], 'shift': [DEPTH, DBIN], 'wkv': [DEPTH, 6, 64, 64],
                 'conv_c': [DEPTH, 3, DC], 'mem_c': [DEPTH, 6, 64, 64], 'mem_n': [DEPTH, 6, 64], 'mem_m': [DEPTH, 6]}
    for k, s in st_shapes.items():
        I['s_' + k] = din('s_' + k, s)
    wshapes = {'norm1': [DEPTH, D], 'w_in': [DEPTH, D, DIN], 'conv_a_w': [DEPTH, 4, DA], 'conv_a_b': [DEPTH, DA],
               'lru_wr': [DEPTH, 4, 64, 64], 'lru_br': [DEPTH, DA], 'lru_wi': [DEPTH, 4, 64, 64], 'lru_bi': [DEPTH, DA],
               'lru_lambda': [DEPTH, DA], 'norm_a': [DEPTH, DA], 'rwkv_mu': [DEPTH, DBIN], 'rwkv_w0': [DEPTH, DB],
               'rwkv_w2': [DEPTH, 64, DB], 'rwkv_a0': [DEPTH, DB], 'rwkv_a2': [DEPTH, 64, DB], 'rwkv_g2': [DEPTH, 128, DB],
               'rwkv_kk': [DEPTH, DB], 'rwkv_ka': [DEPTH, DB], 'rwkv_rk': [DEPTH, 6, 64], 'rwkv_lnw': [DEPTH, DB],
               'rwkv_lnb': [DEPTH, DB], 'conv_c_w': [DEPTH, 4, DC], 'conv_c_b': [DEPTH, DC], 'mlstm_wq': [DEPTH, 6, 64, 64],
               'mlstm_wk': [DEPTH, 6, 64, 64], 'mlstm_wif': [DEPTH, 3 * DC, 12], 'mlstm_bif': [DEPTH, 12],
               'mlstm_gn': [DEPTH, DC], 'w_out': [DEPTH, D, D], 'norm2': [DEPTH, D], 'w_ffn_in': [DEPTH, D, 2 * DFF],
               'w_ffn_out': [DEPTH, DFF, D], 'norm_f': [D]}
    for k in WNAMES:
        I[k] = din(k, wshapes[k])
    O = {}
    O['yp'] = dout('yp', [TP, D])
    O['ys'] = dout('ys', [TSMP, D])
    for jb in ('p', 's'):
        for k, s in st_shapes.items():
            O[jb + '_' + k] = dout('o' + jb + '_' + k, s)

    WSC = {'w_in': nc.dram_tensor('w_in_b', [DEPTH, D, DIN], BF16), 'w_out': nc.dram_tensor('w_out_b', [DEPTH, D, D], BF16),
           'w_ffn_in': nc.dram_tensor('w_ffn_in_b', [DEPTH, D, 2 * DFF], BF16), 'w_ffn_out': nc.dram_tensor('w_ffn_out_b', [DEPTH, DFF, D], BF16)}
    P = Prog()
    with ExitStack() as st:
        def sb(name, shape, dt=F32):
            return st.enter_context(nc.sbuf_tensor(name, list(shape), dt))

        def psum(name, shape, dt=F32):
            return st.enter_context(nc.psum_tensor(name, list(shape), dt))

        def tt(eng, out, in0, in1, op, w, r):
            P.op(eng, lambda e: e.tensor_tensor(out=out, in0=in0, in1=in1, op=op), w, r)

        def ts(eng, out, in0, s1, s2, op0, op1, w, r):
            if op1 is None:
                P.op(eng, lambda e: e.tensor_scalar(out=out, in0=in0, scalar1=s1, scalar2=None, op0=op0), w, r)
            else:
                P.op(eng, lambda e: e.tensor_scalar(out=out, in0=in0, scalar1=s1, scalar2=s2, op0=op0, op1=op1), w, r)

        def stt(out, in0, scalar, in1, op0, op1, w, r):
            P.op('dve', lambda e: e.scalar_tensor_tensor(out=out, in0=in0, scalar=scalar, in1=in1, op0=op0, op1=op1), w, r)

        def act(out, in_, func, w, r, scale=1.0, bias=0.0):
            P.op('act', lambda e: e.activation(out=out, in_=in_, func=func, scale=scale, bias=bias), w, r)

        def cpy(eng, out, in_, w, r):
            if eng == 'act':
                P.op('act', lambda e: e.activation(out=out, in_=in_, func=AF.Copy), w, r)
            else:
                P.op(eng, lambda e: e.tensor_copy(out=out, in_=in_), w, r)

        def _rg(ap):
            n = ap.partition_size()
            return (ap.base_partition(), 32 if n <= 32 else (64 if n <= 64 else 128))

        def mm(out, lhsT, rhs, start, stop, w, r):
            P.op('pe', lambda e: e.matmul(out, lhsT=lhsT, rhs=rhs, start=start, stop=stop), w, r, rg=_rg(lhsT))

        def tp(out, in_, ident, w, r):
            P.op('pe', lambda e: e.transpose(out, in_, ident), w, r, rg=_rg(in_))

        def recip(out, in_, w, r):
            P.op('dve', lambda e: e.reciprocal(out=out, in_=in_), w, r)

        def scan(out, d0, d1, init, op0, op1, w, r):
            P.op('dve', lambda e: e.tensor_tensor_scan(out=out, data0=d0, data1=d1, initial=init, op0=op0, op1=op1), w, r)

        def red(out, in_, w, r):
            P.op('dve', lambda e: e.tensor_reduce(out=out, in_=in_, axis=AX.X, op=ALU.add), w, r)

        def memset(eng, ap, val, w):
            P.op(eng, lambda e: e.memset(ap, val), w, ())

        def dma(q, out, in_, w=(), r=(), slow=False, is_output=False):
            if slow:
                P.dma(q, lambda e: e.dma_start(out=out, in_=in_, allow_slow_non_contiguous=True), w, r, is_output)
            else:
                P.dma(q, lambda e: e.dma_start(out=out, in_=in_), w, r, is_output)

        CT = sb('CT', [128, CW])
        dma('sp', CT[:], I['consts'], w=['CT'])

        def cst(name, rows=128):
            o, n = cw[name]
            return CT[0:rows, o:o + n]
        ident = cst('ident')
        ident2 = cst('ident2')
        onesbd_f = cst('onesbd')
        cmask = cst('cmask')
        onesb = sb('onesb', [128, 128], BF16)
        memset('dve', onesb[:], 1.0, ['onesb'])
        onesbd_b = sb('onesbd_b', [128, 128], BF16)
        cpy('dve', onesbd_b[:], onesbd_f, ['onesbd_b'], ['CT'])
        ones_c = sb('ones_c', [128, 1])
        memset('dve', ones_c[:], 1.0, ['ones_f'])
        if SMDT == F32:
            mask5 = cst('mask5').rearrange("p (b l) -> p b l", b=5)
            ident_s = ident
            mk5key = 'CT'
        else:
            mask5t = sb('mask5t', [128, 5, 64], SMDT)
            cpy('dve', mask5t[:], cst('mask5').rearrange("p (b l) -> p b l", b=5), ['mask5t'], ['CT'])
            mask5 = mask5t[:]
            ident_st = sb('ident_st', [128, 128], SMDT)
            cpy('dve', ident_st[:], ident, ['ident_st'], ['CT'])
            ident_s = ident_st[:]
            mk5key = 'mask5t'
        negm6 = cst('negm6', 64).rearrange("p (h l) -> p h l", h=6)
        sel6 = cst('sel6', 6).rearrange("p (h l) -> p h l", h=6)
        selb = cst('selb', 6)
        pairsel = cst('pairsel', 6)

        NV = 88
        PV = sb('PV', [128, DEPTH, NV])
        NF = sb('NF', [128, 8])
        BI = sb('BI', [6, DEPTH])
        NBF = sb('NBF', [6, DEPTH])

        def pvload(name, col, n):
            for l in range(DEPTH):
                dma('sp', PV[:, l, col:col + n], I[name][l].rearrange("(c p) -> p c", p=128), w=['PV'], slow=True)
        pvload('norm1', 0, 8)
        pvload('norm2', 8, 16 - 8)
        for l in range(DEPTH):
            for j in range(4):
                dma('sp', PV[:, l, 16 + 2 * j:18 + 2 * j], I['conv_a_w'][l, j].rearrange("(c p) -> p c", p=128), w=['PV'], slow=True)
                dma('sp', PV[:, l, 66 + 3 * j:69 + 3 * j], I['conv_c_w'][l, j].rearrange("(c p) -> p c", p=128), w=['PV'], slow=True)
        pvload('conv_a_b', 24, 2)
        pvload('lru_br', 26, 2)
        pvload('lru_bi', 28, 2)
        pvload('lru_lambda', 30, 2)
        pvload('norm_a', 32, 2)
        pvload('rwkv_mu', 34, 11)
        pvload('rwkv_w0', 45, 3)
        pvload('rwkv_a0', 48, 3)
        pvload('rwkv_kk', 51, 3)
        pvload('rwkv_ka', 54, 3)
        for l in range(DEPTH):
            dma('sp', PV[:, l, 57:60], I['rwkv_rk'][l].rearrange("(c hh) n -> (hh n) c", hh=2), w=['PV'], slow=True)
        pvload('rwkv_lnw', 60, 3)
        pvload('rwkv_lnb', 63, 3)
        pvload('conv_c_b', 78, 3)
        pvload('mlstm_gn', 81, 3)
        dma('sp', NF[:], I['norm_f'].rearrange("(c p) -> p c", p=128), w=['NF'], slow=True)
        dma('sp', BI[:], I['mlstm_bif'][:, 0:6].rearrange("l h -> h l"), w=['BI'], slow=True)
        dma('sp', NBF[:], I['mlstm_bif'][:, 6:12].rearrange("l h -> h l"), w=['NBF'], slow=True)
        ts('dve', NBF[:], NBF[:], -1.0, None, ALU.mult, None, ['NBF'], ['NBF'])
        SPT = sb('SPT', [128, DEPTH, 2])
        act(SPT[:], PV[:, :, 30:32], AF.Exp, ['SPT'], ['PV'], scale=-1.0)
        act(SPT[:], SPT[:], AF.Ln, ['SPT'], ['SPT'], bias=1.0)
        ts('dve', PV[:, :, 84:86], SPT[:], -8.0, None, ALU.mult, None, ['PV'], ['SPT'])
        ts('dve', PV[:, :, 86:88], SPT[:], -16.0, None, ALU.mult, None, ['PV'], ['SPT'])

        WRbd = sb('WRbd', [128, DEPTH, 2, 128])
        WIbd = sb('WIbd', [128, DEPTH, 2, 128])
        memset('dve', WRbd[:], 0.0, ['WRbd'])
        memset('dve', WIbd[:], 0.0, ['WIbd'])
        WQbd = sb('WQbd', [128, DEPTH, 3, 128], BF16)
        WKbd = sb('WKbd', [128, DEPTH, 3, 128], BF16)
        memset('dve', WQbd[:], 0.0, ['WQbd'])
        memset('dve', WKbd[:], 0.0, ['WKbd'])
        W2 = sb('W2', [128, DEPTH, DB], BF16)
        A2 = sb('A2', [128, DEPTH, DB], BF16)
        G2 = sb('G2', [128, DEPTH, DB], BF16)
        WIF = sb('WIF', [128, DEPTH, 9, 12], SMDT)
        for l in range(DEPTH):
            for n in range(4):
                hb, c = n % 2, n // 2
                dma('sp', WRbd[64 * hb:64 * hb + 64, l, c, 64 * hb:64 * hb + 64], I['lru_wr'][l, n], w=['WRbd'])
                dma('sp', WIbd[64 * hb:64 * hb + 64, l, c, 64 * hb:64 * hb + 64], I['lru_wi'][l, n], w=['WIbd'])
            for h in range(6):
                hb, c = h % 2, h // 2
                dma('pool', WQbd[64 * hb:64 * hb + 64, l, c, 64 * hb:64 * hb + 64], I['mlstm_wq'][l, h], w=['WQbd'])
                dma('pool', WKbd[64 * hb:64 * hb + 64, l, c, 64 * hb:64 * hb + 64], I['mlstm_wk'][l, h], w=['WKbd'])
            dma('pool', W2[0:64, l, :], I['rwkv_w2'][l], w=['W2'])
            dma('pool', A2[64:128, l, :], I['rwkv_a2'][l], w=['A2'])
            dma('pool', G2[:, l, :], I['rwkv_g2'][l], w=['G2'])
            dma('pool' if SMDT != F32 else 'sp', WIF[:, l, :, :], I['mlstm_wif'][l].rearrange("(kc p) n -> p kc n", p=128), w=['WIF'])
        ts('dve', WIF[:, :, 3:6, :], WIF[:, :, 3:6, :], 8.0, None, ALU.mult, None, ['WIF'], ['WIF'])

        HISTA = sb('HISTA', [128, DEPTH, 2, 3])
        HISTB = sb('HISTB', [128, DEPTH, 11, 1])
        HISTC = sb('HISTC', [128, DEPTH, 3, 3])
        HL = sb('HL', [128, DEPTH, 2])
        HS = sb('HS', [128, DEPTH, 3, 64])
        CA = sb('CA', [128, DEPTH, 3, 65])
        MS = sb('MS', [6, DEPTH])
        WS = sb('WS', [64, 6, 64])

        X = sb('X', [128, 8, TS])
        HN = sb('HN', [128, 8, TS], BF16)
        NSLOT = 11
        PJ = sb('PJ', [128, NSLOT, 3 + TS])
        YM = sb('YM', [128, 8, TS], BF16)
        assert NSLOT * (3 + TS) * 4 >= 22 * TS * 2
        ACTT = PJ[:].rearrange("p s c -> p (s c)").bitcast(BF16)[:, 0:22 * TS].rearrange("p (f t) -> p f t", f=22)
        def actk(f):
            b0, b1 = f * TS * 2, (f + 1) * TS * 2 - 1
            sl_ = (3 + TS) * 4
            return ['PJ%d' % s_ for s_ in range(b0 // sl_, b1 // sl_ + 1)]
        NWB = 3
        WB = [sb('WB%d' % i, [128, 4096], BF16) for i in range(NWB)]
        NTF = 14
        TF = [sb('TF%d' % i, [128, TS]) for i in range(NTF)]
        NTB = 5
        TB = [sb('TB%d' % i, [128, TS], BF16) for i in range(NTB)]
        KR = sb('KR', [128, TS // 64, 2, 64], SMDT)
        KT_ = sb('KT_', [128, TS], SMDT)
        BT_ = sb('BT_', [128, TS], SMDT)
        VS_ = sb('VS_', [128, TS], SMDT)
        W5 = sb('W5', [64, 2, 5, 64], SMDT)
        QMa = sb('QMa', [64, 2, 2, 64], SMDT)
        QMb = sb('QMb', [64, 2, 2, 64], SMDT)
        TTa = sb('TTa', [64, 2, 2, 64], SMDT)
        TTb = sb('TTb', [64, 2, 2, 64], SMDT)
        TM = sb('TM', [64, 3, 128], SMDT)
        XN = sb('XN', [64, 2, 64], SMDT)
        UU = sb('UU', [64, 2, 64], SMDT)
        HG = sb('HG', [128, 64])
        HSs = sb('HSs', [128, 3, 64], SMDT) if SMDT != F32 else None
        CAs = sb('CAs', [128, 3, 65], SMDT) if SMDT != F32 else None
        QKVb = [sb('QKV%d' % i, [128, TS], SMDT) for i in range(9)] if SMDT != F32 else None
        VA = sb('VA', [64, 6, 65], SMDT)
        KTm = sb('KTm', [64, 6, 64], SMDT)
        WE = sb('WE', [64, 2, 6])
        DD = sb('DD', [64, 6, 64])
        STt = sb('STt', [64, 6, 64], SMDT)
        T1 = sb('T1', [64, 6, 65])
        NUM = sb('NUM', [64, 6, 65])
        DEN = sb('DEN', [64, 6])
        HTt = sb('HTt', [64, 6, 64])
        HC = sb('HC', [64, 6, 64])
        SQ = sb('SQ', [64, 6, 64])
        MEAN = sb('MEAN', [64, 6])
        VAR = sb('VAR', [64, 6])
        VW = sb('VW', [64, 6, 65], SMDT)
        WLE = sb('WLE', [6, 3])
        W0 = sb('W0', [128, 3])
        CT2 = sb('CT2', [128, 3, 65])
        RS = TF[13]
        if TS >= 512:
            STG = [(TF[11], 'TF11'), (TF[12], 'TF12')]
        else:
            STG = [(sb('STG0', [128, 512]), 'STG0'), (sb('STG1', [128, 512]), 'STG1')]
        if SMDT == F32:
            PS = [psum('PS%d' % i, [128, 512]) for i in range(8)]
            PTA, kPTA, PTB, kPTB = PS[7], 'PS7', PS[4], 'PS4'
        else:
            PS = [psum('PS%d' % i, [128, 512]) for i in range(7)]
            PSB = psum('PSB', [128, 1024], SMDT)
            PTA, kPTA, PTB, kPTB = PSB[:, 0:512], 'PS7', PSB[:, 512:1024], 'PS7'

        memset('dve', VA[:], 1.0, ['VA'])

        def wconvert(l):
            for nm in ('w_in', 'w_out', 'w_ffn_in', 'w_ffn_out'):
                dma('pool', WSC[nm].ap()[l].rearrange("(k p) n -> p k n", p=128), I[nm][l].rearrange("(k p) n -> p k n", p=128),
                    w=['DW_%s%d' % (nm, l)])
        wconvert(0)
        wstate = {'i': 0}

        def wload(parts):
            i = wstate['i'] % NWB
            wstate['i'] += 1
            key = 'WB%d' % i
            for (c0, KC, ncol, src, rk) in parts:
                dst = WB[i][:, c0:c0 + KC * ncol].rearrange("p (k n) -> p k n", k=KC)
                dma('pool', dst, src, w=[key], r=[rk])
            return WB[i], key

        def win_src(l, c0, ncol):
            return WSC['w_in'].ap()[l][:, c0:c0 + ncol].rearrange("(k p) n -> p k n", p=128), 'DW_w_in%d' % l

        def rmsnorm_to(T, gcol_base, l, out_fn):
            for k in range(8):
                act(TB[0][:, 0:T] if k % 2 == 0 else TB[1][:, 0:T], X[:, k, 0:T], AF.Square,
                    ['TB%d' % (k % 2)], ['X%d' % k])
                mm(PS[0][:, 0:T], onesb[:], TB[k % 2][:, 0:T], k == 0, k == 7, ['PS0'], ['onesb', 'TB%d' % (k % 2)])
            act(RS[:, 0:T], PS[0][:, 0:T], AF.Sqrt, ['TF13'], ['PS0'], scale=1.0 / D, bias=RMS_EPS)
            recip(RS[:, 0:T], RS[:, 0:T], ['TF13'], ['TF13'])
            for k in range(8):
                out_fn(k)

        def inproj(l, T, wt, wkey, ncols_tile, chunk_list):
            for ci, (cit, slot) in enumerate(chunk_list):
                bank = ci % 4
                for k in range(8):
                    mm(PS[bank][:, 0:T], wt[:, k * ncols_tile + cit * 128: k * ncols_tile + cit * 128 + 128],
                       HN[:, k, 0:T], k == 0, k == 7, ['PS%d' % bank], [wkey] + ['HN%d' % k])
                cpy('act' if ci % 2 == 0 else 'dve', PJ[:, slot, 3:3 + T], PS[bank][:, 0:T], ['PJ%d' % slot], ['PS%d' % bank])

        def pv(l, c):
            return PV[:, l, c:c + 1]

        def task(job, l, T, L, tok0, first_seg, last_seg, xin, yout, last_layer):
            NCH = T // L
            if job == jobs[0] and first_seg and l + 1 < DEPTH:
                wconvert(l + 1)
            if l == 0:
                nblk = (T + 127) // 128
                for tb in range(nblk):
                    n = min(128, T - tb * 128)
                    for half in range(2):
                        xt, xk = STG[half]
                        bank = 4 + half
                        dma('sp', xt[0:n, 0:512], xin[tok0 + tb * 128: tok0 + tb * 128 + n, half * 512:(half + 1) * 512], w=[xk])
                        for mmi in range(4):
                            tp(PS[bank][:, mmi * 128: mmi * 128 + n], xt[0:n, mmi * 128:(mmi + 1) * 128], ident[0:n, 0:n],
                               ['PS%d' % bank], [xk, 'CT'])
                        for mmi in range(4):
                            m = half * 4 + mmi
                            cpy('act' if mmi % 2 else 'dve', X[:, m, tb * 128: tb * 128 + n],
                                PS[bank][:, mmi * 128: mmi * 128 + n], ['X%d' % m], ['PS%d' % bank])
            def o1(k):
                stt(HN[:, k, 0:T], X[:, k, 0:T], pv(l, k), RS[:, 0:T], ALU.mult, ALU.mult, ['HN%d' % k], ['X%d' % k, 'PV', 'TF13'])
            rmsnorm_to(T, 0, l, o1)

            wt, wkey = wload([(0, 8, 512) + win_src(l, 0, 512)])
            cpy('dve', PJ[:, 0:2, 0:3], HISTA[:, l, :, :], ['PJ0', 'PJ1'], ['HISTA'])
            inproj(l, T, wt, wkey, 512, [(0, 0), (1, 1), (2, 2), (3, 3)])
            cpy('dve', HISTA[:, l, :, :], PJ[:, 0:2, T:T + 3], ['HISTA'], ['PJ0', 'PJ1'])
            for c in range(2):
                xa, gr, gi, aa, a2t, mu_, uu, hh = TF[0], TF[1], TF[2], TF[3], TF[4], TF[5], TF[6], TF[7 + c]
                act(xa[:, 0:T], PJ[:, c, 3:3 + T], AF.Identity, ['TF0'], ['PJ%d' % c, 'PV'],
                    scale=pv(l, 16 + 2 * 3 + c), bias=pv(l, 24 + c))
                for j in range(3):
                    stt(xa[:, 0:T], PJ[:, c, j:j + T], pv(l, 16 + 2 * j + c), xa[:, 0:T], ALU.mult, ALU.add,
                        ['TF0'], ['TF0', 'PJ%d' % c, 'PV'])
                mm(PS[0][:, 0:T], WRbd[:, l, c, :], xa[:, 0:T], True, True, ['PS0'], ['WRbd', 'TF0'])
                mm(PS[1][:, 0:T], WIbd[:, l, c, :], xa[:, 0:T], True, True, ['PS1'], ['WIbd', 'TF0'])
                act(gr[:, 0:T], PS[0][:, 0:T], AF.Sigmoid, ['TF1'], ['PS0', 'PV'], bias=pv(l, 26 + c))
                act(gi[:, 0:T], PS[1][:, 0:T], AF.Sigmoid, ['TF2'], ['PS1', 'PV'], bias=pv(l, 28 + c))
                act(aa[:, 0:T], gr[:, 0:T], AF.Exp, ['TF3'], ['TF1', 'PV'], scale=pv(l, 84 + c))
                act(a2t[:, 0:T], gr[:, 0:T], AF.Exp, ['TF4'], ['TF1', 'PV'], scale=pv(l, 86 + c))
                act(mu_[:, 0:T], a2t[:, 0:T], AF.Sqrt, ['TF5'], ['TF4'], scale=-1.0, bias=1.0)
                tt('dve', uu[:, 0:T], gi[:, 0:T], xa[:, 0:T], ALU.mult, ['TF6'], ['TF2', 'TF0'])
                tt('dve', uu[:, 0:T], uu[:, 0:T], mu_[:, 0:T], ALU.mult, ['TF6'], ['TF6', 'TF5'])
                scan(hh[:, 0:T], aa[:, 0:T], uu[:, 0:T], HL[:, l, c:c + 1], ALU.mult, ALU.add,
                     ['TF%d' % (7 + c)], ['TF3', 'TF6', 'HL'])
                cpy('dve', HL[:, l, c:c + 1], hh[:, T - 1:T], ['HL'], ['TF%d' % (7 + c)])
                act(TB[c][:, 0:T], hh[:, 0:T], AF.Square, ['TB%d' % c], ['TF%d' % (7 + c)])
            for c in range(2):
                mm(PS[2][:, 0:T], onesb[:], TB[c][:, 0:T], c == 0, c == 1, ['PS2'], ['onesb', 'TB%d' % c])
            act(TF[9][:, 0:T], PS[2][:, 0:T], AF.Sqrt, ['TF9'], ['PS2'], scale=1.0 / DA, bias=RMS_EPS)
            recip(TF[9][:, 0:T], TF[9][:, 0:T], ['TF9'], ['TF9'])
            for c in range(2):
                g = PJ[:, 2 + c, 3:3 + T]
                gk = 'PJ%d' % (2 + c)
                t1, t2 = TF[0], TF[1]
                act(t1[:, 0:T], g, AF.Square, ['TF0'], [gk])
                ts('dve', t1[:, 0:T], t1[:, 0:T], 0.044715, 1.0, ALU.mult, ALU.add, ['TF0'], ['TF0'])
                tt('dve', t1[:, 0:T], t1[:, 0:T], g, ALU.mult, ['TF0'], ['TF0', gk])
                act(t2[:, 0:T], t1[:, 0:T], AF.Sigmoid, ['TF1'], ['TF0'], scale=2.0 * math.sqrt(2.0 / math.pi))
                tt('dve', t2[:, 0:T], t2[:, 0:T], g, ALU.mult, ['TF1'], ['TF1', gk])
                stt(t1[:, 0:T], TF[7 + c][:, 0:T], pv(l, 32 + c), TF[9][:, 0:T], ALU.mult, ALU.mult,
                    ['TF0'], ['TF%d' % (7 + c), 'PV', 'TF9'])
                tt('dve', YM[:, c, 0:T], t1[:, 0:T], t2[:, 0:T], ALU.mult, ['YM%d' % c], ['TF0', 'TF1'])

            cpy('dve', PJ[:, 0:11, 2:3], HISTB[:, l, :, :], ['PJ%d' % s for s in range(0, 11)], ['HISTB'])
            for (c0, nch_, s0) in ((512, 4, 0), (1024, 4, 4), (1536, 3, 8)):
                wt, wkey = wload([(0, 8, nch_ * 128) + win_src(l, c0, nch_ * 128)])
                inproj(l, T, wt, wkey, nch_ * 128, [(i, s0 + i) for i in range(nch_)])
            cpy('dve', HISTB[:, l, :, :], PJ[:, 0:11, T + 2:T + 3], ['HISTB'], ['PJ%d' % s for s in range(0, 11)])

            def mix(out, slot, mcol, w):
                tt('dve', out, PJ[:, slot, 2:2 + T], PJ[:, slot, 3:3 + T], ALU.subtract, w, ['PJ%d' % slot])
                stt(out, out, pv(l, 34 + mcol), PJ[:, slot, 3:3 + T], ALU.mult, ALU.add, w, w + ['PJ%d' % slot, 'PV'])
            LW, SG = TB[2], TB[3]
            mix(TF[0][:, 0:T], 9, 9, ['TF0'])
            act(LW[0:64, 0:T], TF[0][0:64, 0:T], AF.Tanh, ['TB2'], ['TF0'])
            act(LW[64:128, 0:T], TF[0][64:128, 0:T], AF.Copy, ['TB2'], ['TF0'])
            mix(TF[0][:, 0:T], 10, 10, ['TF0'])
            act(SG[:, 0:T], TF[0][:, 0:T], AF.Sigmoid, ['TB3'], ['TF0'])
            for j in range(3):
                R, K, V, SGW, A, G, KK, RN, K2, BV, BON, CL = [TF[i] for i in range(12)]
                kR, kK, kV, kSGW, kA, kG, kKK, kRN, kK2, kBV, kBON, kCL = ['TF%d' % i for i in range(12)]
                Y = TF[12]
                mix(R[:, 0:T], j, j, [kR])
                mix(K[:, 0:T], 3 + j, 3 + j, [kK])
                mix(V[:, 0:T], 6 + j, 6 + j, [kV])
                jc = slice(j * 128, (j + 1) * 128)
                mm(PS[0][:, 0:T], W2[0:64, l, jc], LW[0:64, 0:T], True, True, ['PS0'], ['W2', 'TB2'])
                act(SGW[:, 0:T], PS[0][:, 0:T], AF.Sigmoid, [kSGW], ['PS0', 'PV'], bias=pv(l, 45 + j))
                mm(PS[1][:, 0:T], A2[64:128, l, jc], LW[64:128, 0:T], True, True, ['PS1'], ['A2', 'TB2'])
                act(A[:, 0:T], PS[1][:, 0:T], AF.Sigmoid, [kA], ['PS1', 'PV'], bias=pv(l, 48 + j))
                mm(PS[2][:, 0:T], G2[:, l, jc], SG[:, 0:T], True, True, ['PS2'], ['G2', 'TB3'])
                cpy('act', G[:, 0:T], PS[2][:, 0:T], [kG], ['PS2'])
                ts('dve', KK[:, 0:T], K[:, 0:T], pv(l, 51 + j), None, ALU.mult, None, [kKK], [kK, 'PV'])
                act(TB[4][:, 0:T], KK[:, 0:T], AF.Square, ['TB4'], [kKK])
                mm(PS[3][:, 0:T], onesbd_b[:], TB[4][:, 0:T], True, True, ['PS3'], ['onesbd_b', 'TB4'])
                act(RN[:, 0:T], PS[3][:, 0:T], AF.Sqrt, [kRN], ['PS3'])
                ts('dve', RN[:, 0:T], RN[:, 0:T], 1e-12, None, ALU.max, None, [kRN], [kRN])
                recip(RN[:, 0:T], RN[:, 0:T], [kRN], [kRN])
                tt('dve', KK[:, 0:T], KK[:, 0:T], RN[:, 0:T], ALU.mult, [kKK], [kKK, kRN])
                ts('dve', K2[:, 0:T], A[:, 0:T], -1.0, pv(l, 54 + j), ALU.add, ALU.mult, [kK2], [kA, 'PV'])
                stt(K2[:, 0:T], K2[:, 0:T], 1.0, K[:, 0:T], ALU.add, ALU.mult, [kK2], [kK2, kK])
                tt('dve', BV[:, 0:T], KK[:, 0:T], A[:, 0:T], ALU.mult, [kBV], [kKK, kA])
                tt('dve', RN[:, 0:T], R[:, 0:T], K2[:, 0:T], ALU.mult, [kRN], [kR, kK2])
                ts('dve', TB[4][:, 0:T], RN[:, 0:T], pv(l, 57 + j), None, ALU.mult, None, ['TB4'], [kRN, 'PV'])
                mm(PS[3][:, 0:T], onesbd_b[:], TB[4][:, 0:T], True, True, ['PS3'], ['onesbd_b', 'TB4'])
                tt('dve', BON[:, 0:T], PS[3][:, 0:T], V[:, 0:T], ALU.mult, [kBON], ['PS3', kV])
                scan(CL[:, 0:T], cmask[:, 0:T], SGW[:, 0:T], 0.0, ALU.mult, ALU.add, [kCL], ['CT', kSGW])
                EG, EGI, EGX = TF[13], RN, K
                kEG, kEGI = 'TF13', kRN
                act(EG[:, 0:T], CL[:, 0:T], AF.Exp, [kEG], [kCL], scale=-C_DEC)
                act(EGI[:, 0:T], CL[:, 0:T], AF.Exp, [kEGI], [kCL], scale=C_DEC)
                tt('dve', SGW[:, 0:T], CL[:, 0:T], SGW[:, 0:T], ALU.subtract, [kSGW], [kCL, kSGW])
                act(SGW[:, 0:T], SGW[:, 0:T], AF.Exp, [kSGW], [kSGW], scale=-C_DEC)
                v3 = lambda ap: ap.rearrange("p (n l) -> p n l", l=L)
                tt('dve', KR[:, 0:NCH, 1, 0:L], v3(R[:, 0:T]), v3(EG[:, 0:T]), ALU.mult, ['KR'], [kR, kEG])
                tt('dve', KR[:, 0:NCH, 0, 0:L], v3(KK[:, 0:T]), v3(SGW[:, 0:T]), ALU.mult, ['KR'], [kKK, kSGW])
                tt('dve', KT_[:, 0:T], K2[:, 0:T], EGI[:, 0:T], ALU.mult, ['KT_'], [kK2, kEGI])
                tt('dve', BT_[:, 0:T], BV[:, 0:T], EGI[:, 0:T], ALU.mult, ['BT_'], [kBV, kEGI])
                cpy('act', VS_[:, 0:T], V[:, 0:T], ['VS_'], [kV])
                Hm = HS[:, l, j, :]
                if SMDT != F32:
                    cpy('dve', HSs[:, j, :], Hm, ['HSs'], ['HS'])
                    Hs = HSs[:, j, :]
                    hsk = 'HSs'
                else:
                    Hs = Hm
                    hsk = 'HS'
                for n in range(NCH):
                    cs = slice(n * L, (n + 1) * L)
                    for hh in range(2):
                        rw = slice(64 * hh, 64 * hh + 64)
                        pg = PS[4 + hh]
                        pk = 'PS%d' % (4 + hh)
                        mm(pg[0:L, 0:L], BT_[rw, cs], KR[rw, n, 0, 0:L], True, True, [pk], ['BT_', 'KR'])
                        mm(pg[0:L, 64:64 + L], KR[rw, n, 0, 0:L], BT_[rw, cs], True, True, [pk], ['BT_', 'KR'])
                        mm(pg[0:L, 128:128 + L], BT_[rw, cs], KR[rw, n, 1, 0:L], True, True, [pk], ['BT_', 'KR'])
                        mm(pg[0:L, 192:192 + L], KT_[rw, cs], KR[rw, n, 0, 0:L], True, True, [pk], ['KT_', 'KR'])
                        mm(pg[0:L, 256:256 + L], KT_[rw, cs], KR[rw, n, 1, 0:L], True, True, [pk], ['KT_', 'KR'])
                        tt('dve', W5[:, hh, :, :], pg[0:64, 0:320].rearrange("p (b l) -> p b l", b=5),
                           mask5[0:64, :, :], ALU.mult, ['W5'], [pk, mk5key])
                    for bi, (src, sk) in enumerate(((KT_, 'KT_'), (BT_, 'BT_'), (VS_, 'VS_'))):
                        tp(PTA[0:L, bi * 128:(bi + 1) * 128], src[:, cs], ident_s, [kPTA], [sk, 'CT', 'ident_st'])
                    cpy('act', TM[:, :, :], PTA[0:64, 0:384].rearrange("p (b l) -> p b l", b=3), ['TM'], [kPTA])
                    for hh in range(2):
                        tt('dve', TTa[:, hh, :, :], W5[:, hh, 0:2, :],
                           ident[0:64, 0:64].rearrange("p (o l) -> p o l", o=1).to_broadcast([64, 2, 64]),
                           ALU.add, ['TTa'], ['W5', 'CT'])
                    qm_cur, qk, qoff = W5, 'W5', 0
                    tt_cur, tk = TTa, 'TTa'
                    nlev = int(math.log2(L)) - 1
                    for lev in range(nlev):
                        lastl = (lev == nlev - 1)
                        qm_nx, qnk = (QMa, 'QMa') if lev % 2 == 0 else (QMb, 'QMb')
                        tt_nx, tnk = (TTb, 'TTb') if lev % 2 == 0 else (TTa, 'TTa')
                        for hh in range(2):
                            mm(PS[6][0:L, hh * 128:hh * 128 + L], qm_cur[0:L, hh, 1, 0:L], qm_cur[0:L, hh, 0, 0:L], True, True, ['PS6'], [qk])
                            if not lastl:
                                mm(PS[6][0:L, hh * 128 + 64:hh * 128 + 64 + L], qm_cur[0:L, hh, 0, 0:L], qm_cur[0:L, hh, 1, 0:L], True, True, ['PS6'], [qk])
                        cpy('act', qm_nx[:].rearrange("p a b l -> p (a b l)"), PS[6][0:64, 0:256], [qnk], ['PS6'])
                        for hh in range(2):
                            mm(PS[6][0:L, 256 + hh * 128:256 + hh * 128 + L], tt_cur[0:L, hh, 1, 0:L], qm_nx[0:L, hh, 0, 0:L], True, True, ['PS6'], [tk, qnk])
                            if not lastl:
                                mm(PS[6][0:L, 256 + hh * 128 + 64:256 + hh * 128 + 64 + L], qm_nx[0:L, hh, 0, 0:L], tt_cur[0:L, hh, 1, 0:L], True, True, ['PS6'], [tk, qnk])
                        tt('dve', tt_nx[:].rearrange("p a b l -> p (a b l)"), PS[6][0:64, 256:512],
                           tt_cur[:].rearrange("p a b l -> p (a b l)"), ALU.add, [tnk], ['PS6', tk])
                        qm_cur, qk = qm_nx, qnk
                        tt_cur, tk = tt_nx, tnk
                    for hh in range(2):
                        rw = slice(64 * hh, 64 * hh + 64)
                        fo = slice(64 * hh, 64 * hh + 64)
                        mm(PS[5][0:L, 384 + 64 * hh:448 + 64 * hh], KR[rw, n, 0, 0:L], Hs[rw, :], True, False, ['PS5'], ['KR', hsk])
                        mm(PS[5][0:L, 384 + 64 * hh:448 + 64 * hh], W5[0:L, hh, 3, 0:L], TM[0:L, 2, fo], False, True, ['PS5'], ['W5', 'TM'])
                    act(XN[:].rearrange("p a v -> p (a v)"), PS[5][0:64, 384:512], AF.Copy, ['XN'], ['PS5'], scale=-1.0)
                    for hh in range(2):
                        mm(PS[3][0:L, 128 + 64 * hh:192 + 64 * hh], tt_cur[0:L, hh, 0, 0:L], XN[0:L, hh, :], True, True, ['PS3'], [tk, 'XN'])
                    cpy('dve', UU[:].rearrange("p a v -> p (a v)"), PS[3][0:64, 128:256], ['UU'], ['PS3'])
                    for hh in range(2):
                        rw = slice(64 * hh, 64 * hh + 64)
                        fo = slice(64 * hh, 64 * hh + 64)
                        mm(PS[4][rw, 384:384 + L], Hs[rw, :], KR[rw, n, 1, 0:L], True, False, ['PS4'], [hsk, 'KR'])
                        mm(PS[4][rw, 384:384 + L], UU[0:L, hh, :], W5[0:L, hh, 2, 0:L], False, False, ['PS4'], ['UU', 'W5'])
                        mm(PS[4][rw, 384:384 + L], TM[0:L, 2, fo], W5[0:L, hh, 4, 0:L], False, True, ['PS4'], ['TM', 'W5'])
                    cpy('act', Y[:, cs], PS[4][:, 384:384 + L], ['TF12'], ['PS4'])
                    for hh in range(2):
                        rw = slice(64 * hh, 64 * hh + 64)
                        fo = slice(64 * hh, 64 * hh + 64)
                        mm(PS[3][rw, 0:64], TM[0:L, 1, fo], UU[0:L, hh, :], True, False, ['PS3'], ['TM', 'UU'])
                        mm(PS[3][rw, 0:64], TM[0:L, 0, fo], TM[0:L, 2, fo], False, True, ['PS3'], ['TM'])
                    gl = EG[:, n * L + L - 1: n * L + L]
                    act(HG[:, :], Hm, AF.Identity, ['HG'], ['HS', kEG], scale=gl)
                    stt(Hm, PS[3][:, 0:64], gl, HG[:, :], ALU.mult, ALU.add, ['HS'], ['PS3', kEG, 'HG'])
                    if SMDT != F32:
                        cpy('act', HSs[:, j, :], Hm, ['HSs'], ['HS'])

                mm(PS[0][:, 0:T], onesbd_f, Y[:, 0:T], True, True, ['PS0'], ['CT', 'TF12'])
                stt(Y[:, 0:T], PS[0][:, 0:T], -1.0 / 64, Y[:, 0:T], ALU.mult, ALU.add, ['TF12'], ['PS0', 'TF12'])
                act(CL[:, 0:T], Y[:, 0:T], AF.Square, [kCL], ['TF12'])
                mm(PS[1][:, 0:T], onesbd_f, CL[:, 0:T], True, True, ['PS1'], ['CT', kCL])
                act(CL[:, 0:T], PS[1][:, 0:T], AF.Sqrt, [kCL], ['PS1'], scale=1.0 / 64, bias=GN_EPS_B)
                recip(CL[:, 0:T], CL[:, 0:T], [kCL], [kCL])
                tt('dve', Y[:, 0:T], Y[:, 0:T], CL[:, 0:T], ALU.mult, ['TF12'], ['TF12', kCL])
                ts('dve', Y[:, 0:T], Y[:, 0:T], pv(l, 60 + j), pv(l, 63 + j), ALU.mult, ALU.add, ['TF12'], ['TF12', 'PV'])
                tt('dve', Y[:, 0:T], Y[:, 0:T], BON[:, 0:T], ALU.add, ['TF12'], ['TF12', kBON])
                tt('dve', YM[:, 2 + j, 0:T], Y[:, 0:T], G[:, 0:T], ALU.mult, ['YM%d' % (2 + j)], ['TF12', kG])

            cpy('dve', PJ[:, 0:3, 0:3], HISTC[:, l, :, :], ['PJ0', 'PJ1', 'PJ2'], ['HISTC'])
            for (c0, nch_, s0) in ((1920, 4, 0), (2432, 4, 4), (2944, 1, 8)):
                wt, wkey = wload([(0, 8, nch_ * 128) + win_src(l, c0, nch_ * 128)])
                inproj(l, T, wt, wkey, nch_ * 128, [(i, s0 + i) for i in range(nch_)])
            cpy('dve', HISTC[:, l, :, :], PJ[:, 0:3, T:T + 3], ['HISTC'], ['PJ0', 'PJ1', 'PJ2'])
            if SMDT == F32:
                KRf = KR[:].rearrange("p n a l -> p (n a l)")
                QB = [TF[12], TF[13], KT_]
                KS = [BT_, VS_, KRf]
                kQB = ['TF12', 'TF13', 'KT_']
                kKS = ['BT_', 'VS_', 'KR']
            else:
                QB, KS, VBt = QKVb[0:3], QKVb[3:6], QKVb[6:9]
                kQB = ['QKV%d' % i for i in range(3)]
                kKS = ['QKV%d' % i for i in range(3, 6)]
            for j in range(3):
                xc = TF[0]
                act(xc[:, 0:T], PJ[:, j, 3:3 + T], AF.Identity, ['TF0'], ['PJ%d' % j, 'PV'],
                    scale=pv(l, 66 + 3 * 3 + j), bias=pv(l, 78 + j))
                for t_ in range(3):
                    stt(xc[:, 0:T], PJ[:, j, t_:t_ + T], pv(l, 66 + 3 * t_ + j), xc[:, 0:T], ALU.mult, ALU.add,
                        ['TF0'], ['TF0', 'PJ%d' % j, 'PV'])
                act(TB[2][:, 0:T], xc[:, 0:T], AF.Silu, ['TB2'], ['TF0'])
                mm(PS[0][:, 0:T], WQbd[:, l, j, :], TB[2][:, 0:T], True, True, ['PS0'], ['WQbd', 'TB2'])
                mm(PS[1][:, 0:T], WKbd[:, l, j, :], TB[2][:, 0:T], True, True, ['PS1'], ['WKbd', 'TB2'])
                cpy('act', QB[j][:, 0:T], PS[0][:, 0:T], [kQB[j]], ['PS0'])
                act(KS[j][:, 0:T], PS[1][:, 0:T], AF.Copy, [kKS[j]], ['PS1'], scale=0.125)
                if SMDT != F32:
                    cpy('dve', VBt[j][:, 0:T], PJ[:, 3 + j, 3:3 + T], ['QKV%d' % (6 + j)], ['PJ%d' % (3 + j)])

            def vsrc(j, cs_):
                if SMDT != F32:
                    return VBt[j][:, cs_.start:cs_.stop], 'QKV%d' % (6 + j)
                return PJ[:, 3 + j, 3 + cs_.start:3 + cs_.stop], 'PJ%d' % (3 + j)
            allT = slice(0, T)
            srcs = [(QB[j][:, 0:T], kQB[j]) for j in range(3)] + [(KS[j][:, 0:T], kKS[j]) for j in range(3)] + \
                   [vsrc(j, allT) for j in range(3)]
            for kc, (s_, sk) in enumerate(srcs):
                mm(PS[2][0:6, 0:T], WIF[:, l, kc, 0:6], s_, kc == 0, kc == 8, ['PS2'], ['WIF', sk])
            for kc, (s_, sk) in enumerate(srcs):
                mm(PS[3][0:6, 0:T], WIF[:, l, kc, 6:12], s_, kc == 0, kc == 8, ['PS3'], ['WIF', sk])
            G6 = [TF[i][0:6, :] for i in range(4, 12)]
            IP, L1, CS, AH, MX, MT, WL, NMX = G6
            kIP, kL1, kCS, kAH, kMX, kMT, kWL, kNMX = ['TF%d' % i for i in range(4, 12)]
            act(IP[:, 0:T], PS[2][0:6, 0:T], AF.Identity, [kIP], ['PS2', 'BI'], bias=BI[:, l:l + 1])
            act(L1[:, 0:T], PS[3][0:6, 0:T], AF.Exp, [kL1], ['PS3', 'NBF'], scale=-1.0, bias=NBF[:, l:l + 1])
            act(L1[:, 0:T], L1[:, 0:T], AF.Ln, [kL1], [kL1], bias=1.0)
            scan(CS[:, 0:T], ones_c[0:6, 0:1].to_broadcast([6, T]), L1[:, 0:T], 0.0, ALU.mult, ALU.add, [kCS], ['ones_f', kL1])
            tt('dve', AH[:, 0:T], IP[:, 0:T], CS[:, 0:T], ALU.add, [kAH], [kIP, kCS])
            scan(MX[:, 0:T], ones_c[0:6, 0:1].to_broadcast([6, T]), AH[:, 0:T], MS[:, l:l + 1], ALU.mult, ALU.max, [kMX], ['ones_f', kAH, 'MS'])
            tt('dve', MT[:, 0:T], MX[:, 0:T], CS[:, 0:T], ALU.subtract, [kMT], [kMX, kCS])
            ts('dve', NMX[:, 0:T], MX[:, 0:T], -1.0, None, ALU.mult, None, [kNMX], [kMX])
            for n in range(NCH):
                cs = slice(n * L, (n + 1) * L)
                prev = MS[:, l:l + 1] if n == 0 else MX[:, n * L - 1:n * L]
                ts('dve', WL[:, cs], MX[:, cs], prev, None, ALU.subtract, None, [kWL], [kMX, 'MS'])
            cpy('dve', MS[:, l:l + 1], MT[:, T - 1:T], ['MS'], [kMT])
            YC = [TF[1], TF[2], TF[3]]
            Cm = CA[:, l, :, :]
            if SMDT != F32:
                cpy('dve', CAs[:], Cm, ['CAs'], ['CA'])
                Cs, csk = CAs[:], 'CAs'
            else:
                Cs, csk = Cm, 'CA'
            for n in range(NCH):
                cs = slice(n * L, (n + 1) * L)
                for j in range(3):
                    vs_, vk_ = vsrc(j, cs)
                    tp(PTA[0:L, j * 128:(j + 1) * 128], vs_, ident_s, [kPTA], [vk_, 'CT', 'ident_st'])
                cpy('act', VA[0:L, :, 0:64], PTA[0:L, 0:384].rearrange("p (h v) -> p h v", h=6), ['VA'], [kPTA])
                for j in range(3):
                    tp(PTB[0:L, j * 128:(j + 1) * 128], KS[j][:, cs], ident_s, [kPTB], [kKS[j], 'CT', 'ident_st'])
                cpy('dve', KTm[0:L, :, :], PTB[0:L, 0:384].rearrange("p (h v) -> p h v", h=6), ['KTm'], [kPTB])
                tp(PS[6][0:L, 0:6], WL[:, cs], ident[0:6, 0:6], ['PS6'], [kWL, 'CT'])
                tp(PS[6][0:L, 6:12], MT[:, cs], ident[0:6, 0:6], ['PS6'], [kMT, 'CT'])
                act(WE[0:L, :, :], PS[6][0:L, 0:12].rearrange("p (a h) -> p a h", a=2), AF.Exp, ['WE'], ['PS6'], scale=-1.0)
                mm(PS[2][0:L, 0:384].rearrange("p (h l) -> p h l", h=6)[:, :, 0:L], ident[0:L, 0:L], negm6[0:L, :, 0:L],
                   True, False, ['PS2'], ['CT'])
                for h in range(6):
                    o_ = PS[2][0:L, h * 64:h * 64 + L]
                    mm(o_, sel6[:, h, 0:L], NMX[:, cs], False, False, ['PS2'], ['CT', kNMX])
                    mm(o_, AH[:, cs], sel6[:, h, 0:L], False, h == 5, ['PS2'], ['CT', kAH])
                act(DD[0:L, :, 0:L], PS[2][0:L, 0:384].rearrange("p (h l) -> p h l", h=6)[:, :, 0:L], AF.Exp, ['DD'], ['PS2'])
                for h in range(6):
                    j, hh = h // 2, h % 2
                    rw = slice(64 * hh, 64 * hh + 64)
                    mm(PS[5][0:L, h * 64:h * 64 + L], KS[j][rw, cs], QB[j][rw, cs], True, True, ['PS5'], [kKS[j], kQB[j]])
                tt('dve', STt[0:L, :, 0:L], PS[5][0:L, 0:384].rearrange("p (h l) -> p h l", h=6)[:, :, 0:L],
                   DD[0:L, :, 0:L], ALU.mult, ['STt'], ['PS5', 'DD'])
                for h in range(6):
                    j, hh = h // 2, h % 2
                    rw = slice(64 * hh, 64 * hh + 64)
                    mm(PS[1][0:L, h * 65:h * 65 + 65], STt[0:L, h, 0:L], VA[0:L, h, :], True, True, ['PS1'], ['STt', 'VA'])
                    mm(PS[6][0:L, 64 + h * 65:64 + h * 65 + 65], QB[j][rw, cs], Cs[rw, j, :], True, True, ['PS6'], [kQB[j], csk])
                tt('dve', T1[0:L], PS[6][0:L, 64:64 + 390].rearrange("p (h v) -> p h v", h=6),
                   WE[0:L, 0, :].rearrange("p (h o) -> p h o", o=1).to_broadcast([L, 6, 65]), ALU.mult, ['T1'], ['PS6', 'WE'])
                tt('dve', NUM[0:L], PS[1][0:L, 0:390].rearrange("p (h v) -> p h v", h=6), T1[0:L], ALU.add, ['NUM'], ['PS1', 'T1'])
                act(DEN[0:L, :], NUM[0:L, :, 64], AF.Abs, ['DEN'], ['NUM'])
                tt('dve', DEN[0:L, :], DEN[0:L, :], WE[0:L, 1, :], ALU.max, ['DEN'], ['DEN', 'WE'])
                recip(DEN[0:L, :], DEN[0:L, :], ['DEN'], ['DEN'])
                tt('dve', HTt[0:L], NUM[0:L, :, 0:64], DEN[0:L, :].rearrange("p (h o) -> p h o", o=1).to_broadcast([L, 6, 64]),
                   ALU.mult, ['HTt'], ['NUM', 'DEN'])
                red(MEAN[0:L, :], HTt[0:L], ['MEAN'], ['HTt'])
                stt(HC[0:L], MEAN[0:L, :].rearrange("p (h o) -> p h o", o=1).to_broadcast([L, 6, 64]), -1.0 / 64, HTt[0:L],
                    ALU.mult, ALU.add, ['HC'], ['MEAN', 'HTt'])
                act(SQ[0:L], HC[0:L], AF.Square, ['SQ'], ['HC'])
                red(VAR[0:L, :], SQ[0:L], ['VAR'], ['SQ'])
                act(VAR[0:L, :], VAR[0:L, :], AF.Sqrt, ['VAR'], ['VAR'], scale=1.0 / 64, bias=GN_EPS_C)
                recip(VAR[0:L, :], VAR[0:L, :], ['VAR'], ['VAR'])
                tt('dve', HC[0:L], HC[0:L], VAR[0:L, :].rearrange("p (h o) -> p h o", o=1).to_broadcast([L, 6, 64]),
                   ALU.mult, ['HC'], ['HC', 'VAR'])
                for j in range(3):
                    tp(PS[0][:, j * 64:j * 64 + L], HC[0:L, 2 * j:2 * j + 2, :].rearrange("p h v -> p (h v)"), ident[0:L, 0:L],
                       ['PS0'], ['HC', 'CT'])
                for j in range(3):
                    act(YC[j][:, cs], PS[0][:, j * 64:j * 64 + L], AF.Identity, ['TF%d' % (1 + j)], ['PS0', 'PV'], scale=pv(l, 81 + j))
                tt('dve', VW[0:L], VA[0:L], DD[0:L, :, L - 1:L].to_broadcast([L, 6, 65]), ALU.mult, ['VW'], ['VA', 'DD'])
                for h in range(6):
                    j, hh = h // 2, h % 2
                    rw = slice(64 * hh, 64 * hh + 64)
                    mm(PS[3][rw, j * 65:j * 65 + 65], KTm[0:L, h, :], VW[0:L, h, :], True, True, ['PS3'], ['KTm', 'VW'])
                ts('dve', WLE[:, :], pairsel, WL[:, n * L + L - 1:n * L + L], None, ALU.mult, None, ['WLE'], ['CT', kWL])
                mm(PS[6][:, 480:483], selb, WLE[:, :], True, True, ['PS6'], ['CT', 'WLE'])
                act(W0[:, :], PS[6][:, 480:483], AF.Exp, ['W0'], ['PS6'], scale=-1.0)
                tt('dve', CT2[:], Cm, W0[:, :].rearrange("p (h o) -> p h o", o=1).to_broadcast([128, 3, 65]), ALU.mult, ['CT2'], ['CA', 'W0'])
                tt('dve', Cm, PS[3][:, 0:195].rearrange("p (h v) -> p h v", h=3), CT2[:], ALU.add, ['CA'], ['PS3', 'CT2'])
                if SMDT != F32:
                    cpy('act', CAs[:], Cm, ['CAs'], ['CA'])
            for j in range(3):
                act(TF[0][:, 0:T], PJ[:, 6 + j, 3:3 + T], AF.Sigmoid, ['TF0'], ['PJ%d' % (6 + j)])
                tt('dve', YM[:, 5 + j, 0:T], TF[0][:, 0:T], YC[j][:, 0:T], ALU.mult, ['YM%d' % (5 + j)], ['TF0', 'TF%d' % (1 + j)])

            for half in range(2):
                wt, wkey = wload([(0, 8, 512, WSC['w_out'].ap()[l][:, half * 512:(half + 1) * 512].rearrange("(k p) n -> p k n", p=128), 'DW_w_out%d' % l)])
                for mi in range(4):
                    m = half * 4 + mi
                    bank = mi % 4
                    for k in range(8):
                        mm(PS[bank][:, 0:T], wt[:, k * 512 + mi * 128:k * 512 + mi * 128 + 128], YM[:, k, 0:T], k == 0, k == 7,
                           ['PS%d' % bank], [wkey, 'YM%d' % k])
                    tt('dve', X[:, m, 0:T], PS[bank][:, 0:T], X[:, m, 0:T], ALU.add, ['X%d' % m], ['PS%d' % bank, 'X%d' % m])
            def o2(k):
                stt(HN[:, k, 0:T], X[:, k, 0:T], pv(l, 8 + k), RS[:, 0:T], ALU.mult, ALU.mult, ['HN%d' % k], ['X%d' % k, 'PV', 'TF13'])
            rmsnorm_to(T, 8, l, o2)
            for i in range(11):
                wf = WSC['w_ffn_in'].ap()[l]
                wt, wkey = wload([(0, 8, 256, wf[:, 256 * i:256 * i + 256].rearrange("(k p) n -> p k n", p=128), 'DW_w_ffn_in%d' % l),
                                  (2048, 8, 256, wf[:, DFF + 256 * i:DFF + 256 * i + 256].rearrange("(k p) n -> p k n", p=128), 'DW_w_ffn_in%d' % l)])
                for q in range(2):
                    f = 2 * i + q
                    bg, bu = (0, 1) if q == 0 else (2, 3)
                    for k in range(8):
                        mm(PS[bg][:, 0:T], wt[:, k * 256 + q * 128:k * 256 + q * 128 + 128], HN[:, k, 0:T], k == 0, k == 7,
                           ['PS%d' % bg], [wkey, 'HN%d' % k])
                    for k in range(8):
                        mm(PS[bu][:, 0:T], wt[:, 2048 + k * 256 + q * 128:2048 + k * 256 + q * 128 + 128], HN[:, k, 0:T], k == 0, k == 7,
                           ['PS%d' % bu], [wkey, 'HN%d' % k])
                    tk_ = TF[q]
                    act(tk_[:, 0:T], PS[bg][:, 0:T], AF.Silu, ['TF%d' % q], ['PS%d' % bg])
                    tt('dve', ACTT[:, f, 0:T], tk_[:, 0:T], PS[bu][:, 0:T], ALU.mult, actk(f), ['TF%d' % q, 'PS%d' % bu])
            wo = WSC['w_ffn_out'].ap()[l]
            for half in range(2):
                for (k0, kc_) in ((0, 8), (8, 8), (16, 6)):
                    wt, wkey = wload([(0, kc_, 512, wo[k0 * 128:(k0 + kc_) * 128, half * 512:(half + 1) * 512].rearrange("(k p) n -> p k n", p=128),
                                       'DW_w_ffn_out%d' % l)])
                    for mi in range(4):
                        for kk_ in range(kc_):
                            k = k0 + kk_
                            mm(PS[mi][:, 0:T], wt[:, kk_ * 512 + mi * 128:kk_ * 512 + mi * 128 + 128], ACTT[:, k, 0:T], k == 0, k == 21,
                               ['PS%d' % mi], [wkey] + actk(k))
                for mi in range(4):
                    m = half * 4 + mi
                    tt('dve', X[:, m, 0:T], PS[mi][:, 0:T], X[:, m, 0:T], ALU.add, ['X%d' % m], ['PS%d' % mi, 'X%d' % m])
            if last_layer:
                def o3(k):
                    stt(TF[k][:, 0:T], X[:, k, 0:T], NF[:, k:k + 1], RS[:, 0:T], ALU.mult, ALU.mult, ['TF%d' % k], ['X%d' % k, 'NF', 'TF13'])
                rmsnorm_to(T, 0, l, o3)
                nblk = (T + 127) // 128
                for tb in range(nblk):
                    n = min(128, T - tb * 128)
                    for half in range(2):
                        xt, xk = STG[half]
                        bank = 4 + half
                        for mmi in range(4):
                            m = half * 4 + mmi
                            tp(PS[bank][0:n, mmi * 128:(mmi + 1) * 128], TF[m][:, tb * 128:tb * 128 + n], ident,
                               ['PS%d' % bank], ['TF%d' % m, 'CT'])
                        cpy('act' if half else 'dve', xt[0:n, 0:512], PS[bank][0:n, 0:512], [xk], ['PS%d' % bank])
                        dma('sp', yout[tok0 + tb * 128:tok0 + tb * 128 + n, half * 512:(half + 1) * 512], xt[0:n, 0:512], r=[xk], is_output=True)

        for job in jobs:
            if job == 'p':
                T, L, nseg, xin, yout = TS, 64, TP // TS, I['xp'], O['yp']
                for t_, k_ in ((HISTA, 'HISTA'), (HISTB, 'HISTB'), (HISTC, 'HISTC'), (HL, 'HL'), (HS, 'HS'), (CA, 'CA'), (MS, 'MS')):
                    memset('dve', t_[:], 0.0, [k_])
            else:
                T, L, nseg, xin, yout = TSMP, 32, 1, I['xs'], O['ys']
                for l in range(DEPTH):
                    for j_ in range(3):
                        dma('sp', HISTA[:, l, :, j_], I['s_conv_a'][l, j_].rearrange("(c p) -> p c", p=128), w=['HISTA'], slow=True)
                        dma('sp', HISTC[:, l, :, j_], I['s_conv_c'][l, j_].rearrange("(c p) -> p c", p=128), w=['HISTC'], slow=True)
                    dma('sp', HISTB[:, l, :, 0], I['s_shift'][l].rearrange("(c p) -> p c", p=128), w=['HISTB'], slow=True)
                    dma('sp', HL[:, l, :], I['s_lru'][l].rearrange("(c p) -> p c", p=128), w=['HL'], slow=True)
                    dma('sp', CA[:, l, :, 0:64], I['s_mem_c'][l].rearrange("(jp hh) n v -> (hh n) jp v", hh=2), w=['CA'])
                    dma('sp', CA[:, l, :, 64], I['s_mem_n'][l].rearrange("(jp hh) n -> (hh n) jp", hh=2), w=['CA'], slow=True)
                    dma('sp', WS[:], I['s_wkv'][l].rearrange("h v k -> v h k"), w=['WS'])
                    for j in range(3):
                        tp(PS[4][:, j * 64:j * 64 + 64], WS[:, 2 * j:2 * j + 2, :].rearrange("p h k -> p (h k)"), ident[0:64, 0:64],
                           ['PS4'], ['WS', 'CT'])
                    cpy('dve', HS[:, l, :, :], PS[4][:, 0:192].rearrange("p (j v) -> p j v", j=3), ['HS'], ['PS4'])
                dma('sp', MS[:], I['s_mem_m'].rearrange("l h -> h l"), w=['MS'], slow=True)
            for seg in range(nseg):
                for l in range(DEPTH):
                    task(job, l, T, L, seg * T, seg == 0, seg == nseg - 1, xin, yout, l == DEPTH - 1)
            pre = job + '_'
            for l in range(DEPTH):
                for j_ in range(3):
                    dma('sp', O[pre + 'conv_a'][l, j_].rearrange("(c p) -> p c", p=128), HISTA[:, l, :, j_], r=['HISTA'], slow=True, is_output=True)
                    dma('sp', O[pre + 'conv_c'][l, j_].rearrange("(c p) -> p c", p=128), HISTC[:, l, :, j_], r=['HISTC'], slow=True, is_output=True)
                dma('sp', O[pre + 'shift'][l].rearrange("(c p) -> p c", p=128), HISTB[:, l, :, 0], r=['HISTB'], slow=True, is_output=True)
                dma('sp', O[pre + 'lru'][l].rearrange("(c p) -> p c", p=128), HL[:, l, :], r=['HL'], slow=True, is_output=True)
                dma('sp', O[pre + 'mem_c'][l].rearrange("(jp hh) n v -> (hh n) jp v", hh=2), CA[:, l, :, 0:64], r=['CA'], is_output=True)
                dma('sp', O[pre + 'mem_n'][l].rearrange("(jp hh) n -> (hh n) jp", hh=2), CA[:, l, :, 64], r=['CA'], slow=True, is_output=True)
                for j in range(3):
                    tp(PS[4][0:64, j * 128:(j + 1) * 128], HS[:, l, j, :], ident, ['PS4'], ['HS', 'CT'])
                cpy('dve', WS[:], PS[4][0:64, 0:384].rearrange("p (h k) -> p h k", h=6), ['WS'], ['PS4'])
                dma('sp', O[pre + 'wkv'][l].rearrange("h v k -> v h k"), WS[:], r=['WS'], is_output=True)
            dma('sp', O[pre + 'mem_m'].rearrange("l h -> h l"), MS[:], r=['MS'], slow=True, is_output=True)
        P.finish()
        P.emit(nc, st)
    return nc, cnp


_CACHE = {}
TS_DEFAULT = 512
SMALL_DT = BF16


def kernel(**inputs):
    inputs = {k: np.asarray(v) for k, v in inputs.items()}
    xp = inputs['x_prompt']
    xs = inputs['x_sample']
    DEPTH = inputs['norm1'].shape[0]
    B, TP, _ = xp.shape
    TS = min(TS_DEFAULT, TP)
    key = (DEPTH, TP, TS)
    if key not in _CACHE:
        _CACHE[key] = build(DEPTH, TP, TS, SMDT=SMALL_DT)
    nc, cnp = _CACHE[key]
    f32 = lambda a: np.ascontiguousarray(a, dtype=np.float32)
    in_maps = []
    for c in range(NCORES):
        m = {'xp': f32(xp[c % B]), 'xs': f32(xs[c]), 'consts': cnp}
        m['s_conv_a'] = f32(inputs['state_conv_a'][:, c])
        m['s_lru'] = f32(inputs['state_lru'][:, c])
        m['s_shift'] = f32(inputs['state_shift_b'][:, c, 0])
        m['s_wkv'] = f32(inputs['state_wkv'][:, c])
        m['s_conv_c'] = f32(inputs['state_conv_c'][:, c])
        m['s_mem_c'] = f32(inputs['state_mem_c'][:, c])
        m['s_mem_n'] = f32(inputs['state_mem_n'][:, c])
        m['s_mem_m'] = f32(inputs['state_mem_m'][:, c])
        for k in WNAMES:
            m[k] = f32(inputs[k])
        in_maps.append(m)
    res = run_bass_kernel_spmd(nc, in_maps, core_ids=list(range(NCORES)))
    R = res.results
    y_prompt = np.stack([R[b]['yp'] for b in range(B)], 0)
    y_sample = np.stack([R[c]['ys'] for c in range(NCORES)], 0)
    outs = [y_prompt, y_sample]
    names = ['conv_a', 'lru', 'shift', 'wkv', 'conv_c', 'mem_c', 'mem_n', 'mem_m']
    for jb, n in (('p', B), ('s', NCORES)):
        for nm in names:
            a = np.stack([R[c]['o' + jb + '_' + nm] for c in range(n)], 1)
            if nm == 'shift':
                a = a[:, :, None, :]
            outs.append(np.ascontiguousarray(a.astype(np.float32)))
    return tuple(outs)
```

```python
import math
from contextlib import ExitStack
import numpy as np
import concourse.bass as bass
import concourse.mybir as mybir
from concourse.bass_utils import run_bass_kernel_spmd
from concourse.alu_op_type import AluOpType as ALU

AF = mybir.ActivationFunctionType
F32 = mybir.dt.float32
BF16 = mybir.dt.bfloat16
AX = mybir.AxisListType

EPOCH = 16000
NDMA_SLOTS = 10
SAME_ENGINE_SYNC = True
NOSYNC_ENGINES = ('pe',)

D = 1024
DIN = 3072
DA = 256
DB = 384
DBIN = 1408
DC = 384
DFF = 2816
NCORES = 8
TSMP = 32
C_DEC = math.exp(-0.5)
RMS_EPS = 1e-6
GN_EPS_B = 64e-5
GN_EPS_C = 1e-6


class Prog:
    ENG = ['pe', 'act', 'dve', 'pool', 'sp']

    def __init__(self):
        self.stream = {e: [] for e in self.ENG}
        self.n = {e: 0 for e in self.ENG}
        self.lastw = {}
        self.readers = {}
        self.seen = {e: {} for e in self.ENG}
        self.dma_slot_next = {e: 0 for e in self.ENG}
        self.dma_slot_val = {}
        self.out_tokens = []
        self.pe_rg = {}

    def _deps(self, eng, reads, writes, extra=(), force=()):
        toks = list(extra) + list(force)
        forced_src = set(t[0] for t in force)
        for k in reads:
            t = self.lastw.get(k)
            if t:
                toks.append(t)
        for k in writes:
            t = self.lastw.get(k)
            if t:
                toks.append(t)
            toks.extend(self.readers.get(k, {}).values())
        need = {}
        for (src, val) in toks:
            if need.get(src, 0) < val:
                need[src] = val
        out = []
        for src, val in need.items():
            if src == ('e', eng) and (not SAME_ENGINE_SYNC or eng in NOSYNC_ENGINES) and src not in forced_src:
                continue
            if self.seen[eng].get(src, 0) >= val:
                continue
            self.seen[eng][src] = val
            out.append((src, val))
        return out

    def op(self, eng, fn, w=(), r=(), rg=None):
        w = list(w) + [k for k in r if k.startswith('PS') and k[2:].isdigit()]
        force = []
        if eng == 'pe':
            for k in w:
                if k.startswith('PS'):
                    prev = self.pe_rg.get(k)
                    if prev is not None and prev[0] != rg and self.lastw.get(k) == prev[1]:
                        force.append(prev[1])
        waits = self._deps(eng, r, w, force=force)
        self.n[eng] += 1
        tok = (('e', eng), self.n[eng])
        for k in w:
            self.lastw[k] = tok
            self.readers[k] = {}
        for k in r:
            self.readers.setdefault(k, {})[('e', eng)] = tok
        self.stream[eng].append((waits, fn, tok))
        if eng == 'pe':
            for k in w:
                if k.startswith('PS'):
                    self.pe_rg[k] = (rg, tok)
        return tok

    def dma(self, q, fn, w=(), r=(), is_output=False):
        slot = (q, self.dma_slot_next[q] % NDMA_SLOTS)
        self.dma_slot_next[q] += 1
        src = ('d', slot)
        prev = self.dma_slot_val.get(slot, 0)
        extra = [(src, prev)] if prev else []
        waits = self._deps(q, r, w, extra)
        val = prev + 16
        self.dma_slot_val[slot] = val
        tok = (src, val)
        for k in w:
            self.lastw[k] = tok
            self.readers[k] = {}
        for k in r:
            self.readers.setdefault(k, {})[src] = tok
        self.stream[q].append((waits, fn, tok))
        if is_output:
            self.out_tokens.append(tok)
        return tok

    def finish(self):
        need = {}
        for (src, val) in self.out_tokens:
            need[src] = max(need.get(src, 0), val)
        self.stream['sp'].append((list(need.items()), None, None))

    def emit(self, nc, stack):
        sems = {}

        def getsem(src, val):
            if src[0] == 'e':
                ep = (val - 1) // EPOCH
                key = (src, ep)
                v = val - ep * EPOCH
            else:
                key = (src, 0)
                v = val
            if key not in sems:
                sems[key] = stack.enter_context(nc.semaphore("s%d" % len(sems)))
            return sems[key], v

        for e in self.ENG:
            for (waits, fn, tok) in self.stream[e]:
                for (src, val) in waits:
                    getsem(src, val)
                if tok is not None:
                    getsem(*tok)
        block = stack.enter_context(nc.Block())
        names = {'pe': 'tensor', 'act': 'scalar', 'dve': 'vector', 'pool': 'gpsimd', 'sp': 'sync'}
        for e in self.ENG:
            items = self.stream[e]
            if not items:
                continue

            def body(engh, items=items):
                for (waits, fn, tok) in items:
                    for (src, val) in waits:
                        s, v = getsem(src, val)
                        engh.wait_ge(s, v)
                    if fn is None:
                        continue
                    ins = fn(engh)
                    s, v = getsem(*tok)
                    ins.then_inc(s, 16 if tok[0][0] == 'd' else 1)
            getattr(block, names[e])(body)
        self.nsems = len(sems)


def _consts(TS):
    cw = {}
    cols = []

    def add(name, arr):
        a = np.zeros((128, arr.shape[1]), np.float32)
        a[:arr.shape[0]] = arr
        cw[name] = (sum(c.shape[1] for c in cols), arr.shape[1])
        cols.append(a)
    add('ident', np.eye(128, dtype=np.float32))
    bd = np.zeros((128, 128), np.float32)
    bd[:64, :64] = 1
    bd[64:, 64:] = 1
    add('onesbd', bd)
    add('ident2', np.concatenate([np.eye(64, dtype=np.float32)] * 2, 0))
    r = np.arange(128)[:, None] % 64
    c = np.arange(64)[None, :]
    su = (c > r).astype(np.float32)
    sl = (c < r).astype(np.float32)
    ui = (c >= r).astype(np.float32)
    add('mask5', np.concatenate([-su, -sl, ui, su, ui], 1))
    negm = np.where(np.arange(64)[:, None] > c, -30000.0, 0.0).astype(np.float32)
    add('negm6', np.tile(negm, (1, 6)))
    cm = np.ones((128, TS), np.float32)
    cm[:, ::64] = 0
    add('cmask', cm)
    sel6 = np.zeros((6, 6, 64), np.float32)
    for h in range(6):
        sel6[h, h, :] = 1
    add('sel6', sel6.reshape(6, 384))
    selb = np.zeros((6, 128), np.float32)
    for k in range(6):
        selb[k, (k % 2) * 64:(k % 2) * 64 + 64] = 1
    add('selb', selb)
    ps_ = np.zeros((6, 3), np.float32)
    for k in range(6):
        ps_[k, k // 2] = 1
    add('pairsel', ps_)
    return np.concatenate(cols, 1), cw


WNAMES = ['norm1', 'w_in', 'conv_a_w', 'conv_a_b', 'lru_wr', 'lru_br', 'lru_wi', 'lru_bi', 'lru_lambda',
          'norm_a', 'rwkv_mu', 'rwkv_w0', 'rwkv_w2', 'rwkv_a0', 'rwkv_a2', 'rwkv_g2', 'rwkv_kk', 'rwkv_ka',
          'rwkv_rk', 'rwkv_lnw', 'rwkv_lnb', 'conv_c_w', 'conv_c_b', 'mlstm_wq', 'mlstm_wk', 'mlstm_wif',
          'mlstm_bif', 'mlstm_gn', 'w_out', 'norm2', 'w_ffn_in', 'w_ffn_out', 'norm_f']


def build(DEPTH, TP, TS, SMDT=F32, jobs=('p', 's')):
    assert TP % TS == 0 and TS % 128 == 0
    nc = bass.Bass("TRN2", target_bir_lowering=False)
    cnp, cw = _consts(TS)
    CW = cnp.shape[1]

    def din(name, shape):
        return nc.dram_tensor(name, list(shape), F32, kind="ExternalInput").ap()

    def dout(name, shape):
        return nc.dram_tensor(name, list(shape), F32, kind="ExternalOutput").ap()

    I = {}
    I['xp'] = din('xp', [TP, D])
    I['xs'] = din('xs', [TSMP, D])
    I['consts'] = din('consts', [128, CW])
    st_shapes = {'conv_a': [DEPTH, 3, DA], 'lru': [DEPTH, DA], 'shift': [DEPTH, DBIN], 'wkv': [DEPTH, 6, 64, 64],
                 'conv_c': [DEPTH, 3, DC], 'mem_c': [DEPTH, 6, 64, 64], 'mem_n': [DEPTH, 6, 64], 'mem_m': [DEPTH, 6]}
    for k, s in st_shapes.items():
        I['s_' + k] = din('s_' + k, s)
    wshapes = {'norm1': [DEPTH, D], 'w_in': [DEPTH, D, DIN], 'conv_a_w': [DEPTH, 4, DA], 'conv_a_b': [DEPTH, DA],
               'lru_wr': [DEPTH, 4, 64, 64], 'lru_br': [DEPTH, DA], 'lru_wi': [DEPTH, 4, 64, 64], 'lru_bi': [DEPTH, DA],
               'lru_lambda': [DEPTH, DA], 'norm_a': [DEPTH, DA], 'rwkv_mu': [DEPTH, DBIN], 'rwkv_w0': [DEPTH, DB],
               'rwkv_w2': [DEPTH, 64, DB], 'rwkv_a0': [DEPTH, DB], 'rwkv_a2': [DEPTH, 64, DB], 'rwkv_g2': [DEPTH, 128, DB],
               'rwkv_kk': [DEPTH, DB], 'rwkv_ka': [DEPTH, DB], 'rwkv_rk': [DEPTH, 6, 64], 'rwkv_lnw': [DEPTH, DB],
               'rwkv_lnb': [DEPTH, DB], 'conv_c_w': [DEPTH, 4, DC], 'conv_c_b': [DEPTH, DC], 'mlstm_wq': [DEPTH, 6, 64, 64],
               'mlstm_wk': [DEPTH, 6, 64, 64], 'mlstm_wif': [DEPTH, 3 * DC, 12], 'mlstm_bif': [DEPTH, 12],
               'mlstm_gn': [DEPTH, DC], 'w_out': [DEPTH, D, D], 'norm2': [DEPTH, D], 'w_ffn_in': [DEPTH, D, 2 * DFF],
               'w_ffn_out': [DEPTH, DFF, D], 'norm_f': [D]}
    for k in WNAMES:
        I[k] = din(k, wshapes[k])
    O = {}
    O['yp'] = dout('yp', [TP, D])
    O['ys'] = dout('ys', [TSMP, D])
    for jb in ('p', 's'):
        for k, s in st_shapes.items():
            O[jb + '_' + k] = dout('o' + jb + '_' + k, s)

    WSC = {'w_in': nc.dram_tensor('w_in_b', [DEPTH, D, DIN], BF16), 'w_out': nc.dram_tensor('w_out_b', [DEPTH, D, D], BF16),
           'w_ffn_in': nc.dram_tensor('w_ffn_in_b', [DEPTH, D, 2 * DFF], BF16), 'w_ffn_out': nc.dram_tensor('w_ffn_out_b', [DEPTH, DFF, D], BF16)}
    P = Prog()
    with ExitStack() as st:
        def sb(name, shape, dt=F32):
            return st.enter_context(nc.sbuf_tensor(name, list(shape), dt))

        def psum(name, shape, dt=F32):
            return st.enter_context(nc.psum_tensor(name, list(shape), dt))

        def tt(eng, out, in0, in1, op, w, r):
            P.op(eng, lambda e: e.tensor_tensor(out=out, in0=in0, in1=in1, op=op), w, r)

        def ts(eng, out, in0, s1, s2, op0, op1, w, r):
            if op1 is None:
                P.op(eng, lambda e: e.tensor_scalar(out=out, in0=in0, scalar1=s1, scalar2=None, op0=op0), w, r)
            else:
                P.op(eng, lambda e: e.tensor_scalar(out=out, in0=in0, scalar1=s1, scalar2=s2, op0=op0, op1=op1), w, r)

        def stt(out, in0, scalar, in1, op0, op1, w, r):
            P.op('dve', lambda e: e.scalar_tensor_tensor(out=out, in0=in0, scalar=scalar, in1=in1, op0=op0, op1=op1), w, r)

        def act(out, in_, func, w, r, scale=1.0, bias=0.0):
            P.op('act', lambda e: e.activation(out=out, in_=in_, func=func, scale=scale, bias=bias), w, r)

        def cpy(eng, out, in_, w, r):
            if eng == 'act':
                P.op('act', lambda e: e.activation(out=out, in_=in_, func=AF.Copy), w, r)
            else:
                P.op(eng, lambda e: e.tensor_copy(out=out, in_=in_), w, r)

        def _rg(ap):
            n = ap.partition_size()
            return (ap.base_partition(), 32 if n <= 32 else (64 if n <= 64 else 128))

        def mm(out, lhsT, rhs, start, stop, w, r):
            P.op('pe', lambda e: e.matmul(out, lhsT=lhsT, rhs=rhs, start=start, stop=stop), w, r, rg=_rg(lhsT))

        def tp(out, in_, ident, w, r):
            P.op('pe', lambda e: e.transpose(out, in_, ident), w, r, rg=_rg(in_))

        def recip(out, in_, w, r):
            P.op('dve', lambda e: e.reciprocal(out=out, in_=in_), w, r)

        def scan(out, d0, d1, init, op0, op1, w, r):
            P.op('dve', lambda e: e.tensor_tensor_scan(out=out, data0=d0, data1=d1, initial=init, op0=op0, op1=op1), w, r)

        def red(out, in_, w, r):
            P.op('dve', lambda e: e.tensor_reduce(out=out, in_=in_, axis=AX.X, op=ALU.add), w, r)

        def memset(eng, ap, val, w):
            P.op(eng, lambda e: e.memset(ap, val), w, ())

        def dma(q, out, in_, w=(), r=(), slow=False, is_output=False):
            if slow:
                P.dma(q, lambda e: e.dma_start(out=out, in_=in_, allow_slow_non_contiguous=True), w, r, is_output)
            else:
                P.dma(q, lambda e: e.dma_start(out=out, in_=in_), w, r, is_output)

        CT = sb('CT', [128, CW])
        dma('sp', CT[:], I['consts'], w=['CT'])

        def cst(name, rows=128):
            o, n = cw[name]
            return CT[0:rows, o:o + n]
        ident = cst('ident')
        ident2 = cst('ident2')
        onesbd_f = cst('onesbd')
        cmask = cst('cmask')
        onesb = sb('onesb', [128, 128], BF16)
        memset('dve', onesb[:], 1.0, ['onesb'])
        onesbd_b = sb('onesbd_b', [128, 128], BF16)
        cpy('dve', onesbd_b[:], onesbd_f, ['onesbd_b'], ['CT'])
        ones_c = sb('ones_c', [128, 1])
        memset('dve', ones_c[:], 1.0, ['ones_f'])
        if SMDT == F32:
            mask5 = cst('mask5').rearrange("p (b l) -> p b l", b=5)
            ident_s = ident
            mk5key = 'CT'
        else:
            mask5t = sb('mask5t', [128, 5, 64], SMDT)
            cpy('dve', mask5t[:], cst('mask5').rearrange("p (b l) -> p b l", b=5), ['mask5t'], ['CT'])
            mask5 = mask5t[:]
            ident_st = sb('ident_st', [128, 128], SMDT)
            cpy('dve', ident_st[:], ident, ['ident_st'], ['CT'])
            ident_s = ident_st[:]
            mk5key = 'mask5t'
        negm6 = cst('negm6', 64).rearrange("p (h l) -> p h l", h=6)
        sel6 = cst('sel6', 6).rearrange("p (h l) -> p h l", h=6)
        selb = cst('selb', 6)
        pairsel = cst('pairsel', 6)

        NV = 88
        PV = sb('PV', [128, DEPTH, NV])
        NF = sb('NF', [128, 8])
        BI = sb('BI', [6, DEPTH])
        NBF = sb('NBF', [6, DEPTH])

        def pvload(name, col, n):
            for l in range(DEPTH):
                dma('sp', PV[:, l, col:col + n], I[name][l].rearrange("(c p) -> p c", p=128), w=['PV'], slow=True)
        pvload('norm1', 0, 8)
        pvload('norm2', 8, 16 - 8)
        for l in range(DEPTH):
            for j in range(4):
                dma('sp', PV[:, l, 16 + 2 * j:18 + 2 * j], I['conv_a_w'][l, j].rearrange("(c p) -> p c", p=128), w=['PV'], slow=True)
                dma('sp', PV[:, l, 66 + 3 * j:69 + 3 * j], I['conv_c_w'][l, j].rearrange("(c p) -> p c", p=128), w=['PV'], slow=True)
        pvload('conv_a_b', 24, 2)
        pvload('lru_br', 26, 2)
        pvload('lru_bi', 28, 2)
        pvload('lru_lambda', 30, 2)
        pvload('norm_a', 32, 2)
        pvload('rwkv_mu', 34, 11)
        pvload('rwkv_w0', 45, 3)
        pvload('rwkv_a0', 48, 3)
        pvload('rwkv_kk', 51, 3)
        pvload('rwkv_ka', 54, 3)
        for l in range(DEPTH):
            dma('sp', PV[:, l, 57:60], I['rwkv_rk'][l].rearrange("(c hh) n -> (hh n) c", hh=2), w=['PV'], slow=True)
        pvload('rwkv_lnw', 60, 3)
        pvload('rwkv_lnb', 63, 3)
        pvload('conv_c_b', 78, 3)
        pvload('mlstm_gn', 81, 3)
        dma('sp', NF[:], I['norm_f'].rearrange("(c p) -> p c", p=128), w=['NF'], slow=True)
        dma('sp', BI[:], I['mlstm_bif'][:, 0:6].rearrange("l h -> h l"), w=['BI'], slow=True)
        dma('sp', NBF[:], I['mlstm_bif'][:, 6:12].rearrange("l h -> h l"), w=['NBF'], slow=True)
        ts('dve', NBF[:], NBF[:], -1.0, None, ALU.mult, None, ['NBF'], ['NBF'])
        SPT = sb('SPT', [128, DEPTH, 2])
        act(SPT[:], PV[:, :, 30:32], AF.Exp, ['SPT'], ['PV'], scale=-1.0)
        act(SPT[:], SPT[:], AF.Ln, ['SPT'], ['SPT'], bias=1.0)
        ts('dve', PV[:, :, 84:86], SPT[:], -8.0, None, ALU.mult, None, ['PV'], ['SPT'])
        ts('dve', PV[:, :, 86:88], SPT[:], -16.0, None, ALU.mult, None, ['PV'], ['SPT'])

        WRbd = sb('WRbd', [128, DEPTH, 2, 128])
        WIbd = sb('WIbd', [128, DEPTH, 2, 128])
        memset('dve', WRbd[:], 0.0, ['WRbd'])
        memset('dve', WIbd[:], 0.0, ['WIbd'])
        WQbd = sb('WQbd', [128, DEPTH, 3, 128], BF16)
        WKbd = sb('WKbd', [128, DEPTH, 3, 128], BF16)
        memset('dve', WQbd[:], 0.0, ['WQbd'])
        memset('dve', WKbd[:], 0.0, ['WKbd'])
        W2 = sb('W2', [128, DEPTH, DB], BF16)
        A2 = sb('A2', [128, DEPTH, DB], BF16)
        G2 = sb('G2', [128, DEPTH, DB], BF16)
        WIF = sb('WIF', [128, DEPTH, 9, 12], SMDT)
        for l in range(DEPTH):
            for n in range(4):
                hb, c = n % 2, n // 2
                dma('sp', WRbd[64 * hb:64 * hb + 64, l, c, 64 * hb:64 * hb + 64], I['lru_wr'][l, n], w=['WRbd'])
                dma('sp', WIbd[64 * hb:64 * hb + 64, l, c, 64 * hb:64 * hb + 64], I['lru_wi'][l, n], w=['WIbd'])
            for h in range(6):
                hb, c = h % 2, h // 2
                dma('pool', WQbd[64 * hb:64 * hb + 64, l, c, 64 * hb:64 * hb + 64], I['mlstm_wq'][l, h], w=['WQbd'])
                dma('pool', WKbd[64 * hb:64 * hb + 64, l, c, 64 * hb:64 * hb + 64], I['mlstm_wk'][l, h], w=['WKbd'])
            dma('pool', W2[0:64, l, :], I['rwkv_w2'][l], w=['W2'])
            dma('pool', A2[64:128, l, :], I['rwkv_a2'][l], w=['A2'])
            dma('pool', G2[:, l, :], I['rwkv_g2'][l], w=['G2'])
            dma('pool' if SMDT != F32 else 'sp', WIF[:, l, :, :], I['mlstm_wif'][l].rearrange("(kc p) n -> p kc n", p=128), w=['WIF'])
        ts('dve', WIF[:, :, 3:6, :], WIF[:, :, 3:6, :], 8.0, None, ALU.mult, None, ['WIF'], ['WIF'])

        HISTA = sb('HISTA', [128, DEPTH, 2, 3])
        HISTB = sb('HISTB', [128, DEPTH, 11, 1])
        HISTC = sb('HISTC', [128, DEPTH, 3, 3])
        HL = sb('HL', [128, DEPTH, 2])
        HS = sb('HS', [128, DEPTH, 3, 64])
        CA = sb('CA', [128, DEPTH, 3, 65])
        MS = sb('MS', [6, DEPTH])
        WS = sb('WS', [64, 6, 64])

        X = sb('X', [128, 8, TS])
        HN = sb('HN', [128, 8, TS], BF16)
        NSLOT = 11
        PJ = sb('PJ', [128, NSLOT, 3 + TS])
        YM = sb('YM', [128, 8, TS], BF16)
        assert NSLOT * (3 + TS) * 4 >= 22 * TS * 2
        ACTT = PJ[:].rearrange("p s c -> p (s c)").bitcast(BF16)[:, 0:22 * TS].rearrange("p (f t) -> p f t", f=22)
        def actk(f):
            b0, b1 = f * TS * 2, (f + 1) * TS * 2 - 1
            sl_ = (3 + TS) * 4
            return ['PJ%d' % s_ for s_ in range(b0 // sl_, b1 // sl_ + 1)]
        NWB = 2
        WB = [sb('WB%d' % i, [128, 4096], BF16) for i in range(NWB)]
        NTF = 14
        TF = [sb('TF%d' % i, [128, TS]) for i in range(NTF)]
        NTB = 5
        TB = [sb('TB%d' % i, [128, TS], BF16) for i in range(NTB)]
        KR = sb('KR', [128, TS // 64, 2, 64], SMDT)
        KT_ = sb('KT_', [128, TS], SMDT)
        BT_ = sb('BT_', [128, TS], SMDT)
        VS_ = sb('VS_', [128, TS], SMDT)
        W5 = sb('W5', [64, 2, 5, 64], SMDT)
        QMa = sb('QMa', [64, 2, 2, 64], SMDT)
        QMb = sb('QMb', [64, 2, 2, 64], SMDT)
        TTa = sb('TTa', [64, 2, 2, 64], SMDT)
        TTb = sb('TTb', [64, 2, 2, 64], SMDT)
        TM = sb('TM', [64, 3, 128], SMDT)
        XN = sb('XN', [64, 2, 64], SMDT)
        UU = sb('UU', [64, 2, 64], SMDT)
        HG = sb('HG', [128, 64])
        HSs = sb('HSs', [128, 3, 64], SMDT) if SMDT != F32 else None
        CAs = sb('CAs', [128, 3, 65], SMDT) if SMDT != F32 else None
        QKVb = [sb('QKV%d' % i, [128, TS], SMDT) for i in range(9)] if SMDT != F32 else None
        VA = sb('VA', [64, 6, 65], SMDT)
        KTm = sb('KTm', [64, 6, 64], SMDT)
        WE = sb('WE', [64, 2, 6])
        DD = sb('DD', [64, 6, 64])
        STt = sb('STt', [64, 6, 64], SMDT)
        T1 = sb('T1', [64, 6, 65])
        NUM = sb('NUM', [64, 6, 65])
        DEN = sb('DEN', [64, 6])
        HTt = sb('HTt', [64, 6, 64])
        HC = sb('HC', [64, 6, 64])
        SQ = sb('SQ', [64, 6, 64])
        MEAN = sb('MEAN', [64, 6])
        VAR = sb('VAR', [64, 6])
        VW = sb('VW', [64, 6, 65], SMDT)
        WLE = sb('WLE', [6, 3])
        W0 = sb('W0', [128, 3])
        CT2 = sb('CT2', [128, 3, 65])
        RS = TF[13]
        GC = [sb('GC%d' % i, [6, TS]) for i in range(4)]
        YCp = [sb('YCp%d' % i, [128, TS]) for i in range(3)]
        SZ = [sb('SZ%d' % i, [128, TS], BF16) for i in range(3)]
        OVERLAP = (SMDT != F32)
        if TS >= 512:
            STG = [(TF[11], 'TF11'), (TF[12], 'TF12')]
        else:
            STG = [(sb('STG0', [128, 512]), 'STG0'), (sb('STG1', [128, 512]), 'STG1')]
        if SMDT == F32:
            PS = [psum('PS%d' % i, [128, 512]) for i in range(8)]
            PTA, kPTA, PTB, kPTB = PS[7], 'PS7', PS[4], 'PS4'
        else:
            PS = [psum('PS%d' % i, [128, 512]) for i in range(7)]
            PSB = psum('PSB', [128, 1024], SMDT)
            PTA, kPTA, PTB, kPTB = PSB[:, 0:512], 'PS7', PSB[:, 512:1024], 'PS7'

        memset('dve', VA[:], 1.0, ['VA'])

        def wconvert(l):
            for nm in ('w_in', 'w_out', 'w_ffn_in', 'w_ffn_out'):
                dma('pool', WSC[nm].ap()[l].rearrange("(k p) n -> p k n", p=128), I[nm][l].rearrange("(k p) n -> p k n", p=128),
                    w=['DW_%s%d' % (nm, l)])
        wconvert(0)
        wstate = {'i': 0}

        def wload(parts):
            i = wstate['i'] % NWB
            wstate['i'] += 1
            key = 'WB%d' % i
            for (c0, KC, ncol, src, rk) in parts:
                dst = WB[i][:, c0:c0 + KC * ncol].rearrange("p (k n) -> p k n", k=KC)
                dma('pool', dst, src, w=[key], r=[rk])
            return WB[i], key

        def win_src(l, c0, ncol):
            return WSC['w_in'].ap()[l][:, c0:c0 + ncol].rearrange("(k p) n -> p k n", p=128), 'DW_w_in%d' % l

        def rmsnorm_to(T, gcol_base, l, out_fn):
            for k in range(8):
                act(TB[0][:, 0:T] if k % 2 == 0 else TB[1][:, 0:T], X[:, k, 0:T], AF.Square,
                    ['TB%d' % (k % 2)], ['X%d' % k])
                mm(PS[0][:, 0:T], onesb[:], TB[k % 2][:, 0:T], k == 0, k == 7, ['PS0'], ['onesb', 'TB%d' % (k % 2)])
            act(RS[:, 0:T], PS[0][:, 0:T], AF.Sqrt, ['TF13'], ['PS0'], scale=1.0 / D, bias=RMS_EPS)
            recip(RS[:, 0:T], RS[:, 0:T], ['TF13'], ['TF13'])
            for k in range(8):
                out_fn(k)

        def inproj(l, T, wt, wkey, ncols_tile, chunk_list):
            for ci, (cit, slot) in enumerate(chunk_list):
                bank = ci % 4
                for k in range(8):
                    mm(PS[bank][:, 0:T], wt[:, k * ncols_tile + cit * 128: k * ncols_tile + cit * 128 + 128],
                       HN[:, k, 0:T], k == 0, k == 7, ['PS%d' % bank], [wkey] + ['HN%d' % k])
                cpy('act' if ci % 2 == 0 else 'dve', PJ[:, slot, 3:3 + T], PS[bank][:, 0:T], ['PJ%d' % slot], ['PS%d' % bank])

        def pv(l, c):
            return PV[:, l, c:c + 1]

        def task(job, l, T, L, tok0, first_seg, last_seg, xin, yout, last_layer):
            NCH = T // L
            if job == jobs[0] and first_seg and l + 1 < DEPTH:
                wconvert(l + 1)
            if l == 0:
                nblk = (T + 127) // 128
                for tb in range(nblk):
                    n = min(128, T - tb * 128)
                    for half in range(2):
                        xt, xk = STG[half]
                        bank = 4 + half
                        dma('sp', xt[0:n, 0:512], xin[tok0 + tb * 128: tok0 + tb * 128 + n, half * 512:(half + 1) * 512], w=[xk])
                        for mmi in range(4):
                            tp(PS[bank][:, mmi * 128: mmi * 128 + n], xt[0:n, mmi * 128:(mmi + 1) * 128], ident[0:n, 0:n],
                               ['PS%d' % bank], [xk, 'CT'])
                        for mmi in range(4):
                            m = half * 4 + mmi
                            cpy('act' if mmi % 2 else 'dve', X[:, m, tb * 128: tb * 128 + n],
                                PS[bank][:, mmi * 128: mmi * 128 + n], ['X%d' % m], ['PS%d' % bank])
            def o1(k):
                stt(HN[:, k, 0:T], X[:, k, 0:T], pv(l, k), RS[:, 0:T], ALU.mult, ALU.mult, ['HN%d' % k], ['X%d' % k, 'PV', 'TF13'])
            rmsnorm_to(T, 0, l, o1)

            wt, wkey = wload([(0, 8, 512) + win_src(l, 0, 512)])
            cpy('dve', PJ[:, 0:2, 0:3], HISTA[:, l, :, :], ['PJ0', 'PJ1'], ['HISTA'])
            inproj(l, T, wt, wkey, 512, [(0, 0), (1, 1), (2, 2), (3, 3)])
            cpy('dve', HISTA[:, l, :, :], PJ[:, 0:2, T:T + 3], ['HISTA'], ['PJ0', 'PJ1'])
            for c in range(2):
                xa, gr, gi, aa, a2t, mu_, uu, hh = TF[0], TF[1], TF[2], TF[3], TF[4], TF[5], TF[6], TF[7 + c]
                act(xa[:, 0:T], PJ[:, c, 3:3 + T], AF.Identity, ['TF0'], ['PJ%d' % c, 'PV'],
                    scale=pv(l, 16 + 2 * 3 + c), bias=pv(l, 24 + c))
                for j in range(3):
                    stt(xa[:, 0:T], PJ[:, c, j:j + T], pv(l, 16 + 2 * j + c), xa[:, 0:T], ALU.mult, ALU.add,
                        ['TF0'], ['TF0', 'PJ%d' % c, 'PV'])
                mm(PS[0][:, 0:T], WRbd[:, l, c, :], xa[:, 0:T], True, True, ['PS0'], ['WRbd', 'TF0'])
                mm(PS[1][:, 0:T], WIbd[:, l, c, :], xa[:, 0:T], True, True, ['PS1'], ['WIbd', 'TF0'])
                act(gr[:, 0:T], PS[0][:, 0:T], AF.Sigmoid, ['TF1'], ['PS0', 'PV'], bias=pv(l, 26 + c))
                act(gi[:, 0:T], PS[1][:, 0:T], AF.Sigmoid, ['TF2'], ['PS1', 'PV'], bias=pv(l, 28 + c))
                act(aa[:, 0:T], gr[:, 0:T], AF.Exp, ['TF3'], ['TF1', 'PV'], scale=pv(l, 84 + c))
                act(a2t[:, 0:T], gr[:, 0:T], AF.Exp, ['TF4'], ['TF1', 'PV'], scale=pv(l, 86 + c))
                act(mu_[:, 0:T], a2t[:, 0:T], AF.Sqrt, ['TF5'], ['TF4'], scale=-1.0, bias=1.0)
                tt('dve', uu[:, 0:T], gi[:, 0:T], xa[:, 0:T], ALU.mult, ['TF6'], ['TF2', 'TF0'])
                tt('dve', uu[:, 0:T], uu[:, 0:T], mu_[:, 0:T], ALU.mult, ['TF6'], ['TF6', 'TF5'])
                scan(hh[:, 0:T], aa[:, 0:T], uu[:, 0:T], HL[:, l, c:c + 1], ALU.mult, ALU.add,
                     ['TF%d' % (7 + c)], ['TF3', 'TF6', 'HL'])
                cpy('dve', HL[:, l, c:c + 1], hh[:, T - 1:T], ['HL'], ['TF%d' % (7 + c)])
                act(TB[c][:, 0:T], hh[:, 0:T], AF.Square, ['TB%d' % c], ['TF%d' % (7 + c)])
            for c in range(2):
                mm(PS[2][:, 0:T], onesb[:], TB[c][:, 0:T], c == 0, c == 1, ['PS2'], ['onesb', 'TB%d' % c])
            act(TF[9][:, 0:T], PS[2][:, 0:T], AF.Sqrt, ['TF9'], ['PS2'], scale=1.0 / DA, bias=RMS_EPS)
            recip(TF[9][:, 0:T], TF[9][:, 0:T], ['TF9'], ['TF9'])
            for c in range(2):
                g = PJ[:, 2 + c, 3:3 + T]
                gk = 'PJ%d' % (2 + c)
                t1, t2 = TF[0], TF[1]
                act(t1[:, 0:T], g, AF.Square, ['TF0'], [gk])
                ts('dve', t1[:, 0:T], t1[:, 0:T], 0.044715, 1.0, ALU.mult, ALU.add, ['TF0'], ['TF0'])
                tt('dve', t1[:, 0:T], t1[:, 0:T], g, ALU.mult, ['TF0'], ['TF0', gk])
                act(t2[:, 0:T], t1[:, 0:T], AF.Sigmoid, ['TF1'], ['TF0'], scale=2.0 * math.sqrt(2.0 / math.pi))
                tt('dve', t2[:, 0:T], t2[:, 0:T], g, ALU.mult, ['TF1'], ['TF1', gk])
                stt(t1[:, 0:T], TF[7 + c][:, 0:T], pv(l, 32 + c), TF[9][:, 0:T], ALU.mult, ALU.mult,
                    ['TF0'], ['TF%d' % (7 + c), 'PV', 'TF9'])
                tt('dve', YM[:, c, 0:T], t1[:, 0:T], t2[:, 0:T], ALU.mult, ['YM%d' % c], ['TF0', 'TF1'])

            def groupB():
                cpy('dve', PJ[:, 0:11, 2:3], HISTB[:, l, :, :], ['PJ%d' % s for s in range(0, 11)], ['HISTB'])
                for (c0, nch_, s0) in ((512, 4, 0), (1024, 4, 4), (1536, 3, 8)):
                    wt, wkey = wload([(0, 8, nch_ * 128) + win_src(l, c0, nch_ * 128)])
                    inproj(l, T, wt, wkey, nch_ * 128, [(i, s0 + i) for i in range(nch_)])
                cpy('dve', HISTB[:, l, :, :], PJ[:, 0:11, T + 2:T + 3], ['HISTB'], ['PJ%d' % s for s in range(0, 11)])

                def mix(out, slot, mcol, w):
                    tt('dve', out, PJ[:, slot, 2:2 + T], PJ[:, slot, 3:3 + T], ALU.subtract, w, ['PJ%d' % slot])
                    stt(out, out, pv(l, 34 + mcol), PJ[:, slot, 3:3 + T], ALU.mult, ALU.add, w, w + ['PJ%d' % slot, 'PV'])
                LW, SG = TB[2], TB[3]
                mix(TF[0][:, 0:T], 9, 9, ['TF0'])
                act(LW[0:64, 0:T], TF[0][0:64, 0:T], AF.Tanh, ['TB2'], ['TF0'])
                act(LW[64:128, 0:T], TF[0][64:128, 0:T], AF.Copy, ['TB2'], ['TF0'])
                mix(TF[0][:, 0:T], 10, 10, ['TF0'])
                act(SG[:, 0:T], TF[0][:, 0:T], AF.Sigmoid, ['TB3'], ['TF0'])
                for j in range(3):
                    R, K, V, SGW, A, G, KK, RN, K2, BV, BON, CL = [TF[i] for i in range(12)]
                    kR, kK, kV, kSGW, kA, kG, kKK, kRN, kK2, kBV, kBON, kCL = ['TF%d' % i for i in range(12)]
                    Y = TF[12]
                    mix(R[:, 0:T], j, j, [kR])
                    mix(K[:, 0:T], 3 + j, 3 + j, [kK])
                    mix(V[:, 0:T], 6 + j, 6 + j, [kV])
                    jc = slice(j * 128, (j + 1) * 128)
                    mm(PS[4][:, 0:T], W2[0:64, l, jc], LW[0:64, 0:T], True, True, ['PS4'], ['W2', 'TB2'])
                    act(SGW[:, 0:T], PS[4][:, 0:T], AF.Sigmoid, [kSGW], ['PS4', 'PV'], bias=pv(l, 45 + j))
                    mm(PS[5][:, 0:T], A2[64:128, l, jc], LW[64:128, 0:T], True, True, ['PS5'], ['A2', 'TB2'])
                    act(A[:, 0:T], PS[5][:, 0:T], AF.Sigmoid, [kA], ['PS5', 'PV'], bias=pv(l, 48 + j))
                    mm(PS[6][:, 0:T], G2[:, l, jc], SG[:, 0:T], True, True, ['PS6'], ['G2', 'TB3'])
                    cpy('act', G[:, 0:T], PS[6][:, 0:T], [kG], ['PS6'])
                    yield
                    ts('dve', KK[:, 0:T], K[:, 0:T], pv(l, 51 + j), None, ALU.mult, None, [kKK], [kK, 'PV'])
                    act(TB[4][:, 0:T], KK[:, 0:T], AF.Square, ['TB4'], [kKK])
                    mm(PS[6][:, 0:T], onesbd_b[:], TB[4][:, 0:T], True, True, ['PS6'], ['onesbd_b', 'TB4'])
                    act(RN[:, 0:T], PS[6][:, 0:T], AF.Sqrt, [kRN], ['PS6'])
                    ts('dve', RN[:, 0:T], RN[:, 0:T], 1e-12, None, ALU.max, None, [kRN], [kRN])
                    recip(RN[:, 0:T], RN[:, 0:T], [kRN], [kRN])
                    tt('dve', KK[:, 0:T], KK[:, 0:T], RN[:, 0:T], ALU.mult, [kKK], [kKK, kRN])
                    yield
                    ts('dve', K2[:, 0:T], A[:, 0:T], -1.0, pv(l, 54 + j), ALU.add, ALU.mult, [kK2], [kA, 'PV'])
                    stt(K2[:, 0:T], K2[:, 0:T], 1.0, K[:, 0:T], ALU.add, ALU.mult, [kK2], [kK2, kK])
                    tt('dve', BV[:, 0:T], KK[:, 0:T], A[:, 0:T], ALU.mult, [kBV], [kKK, kA])
                    yield
                    tt('dve', RN[:, 0:T], R[:, 0:T], K2[:, 0:T], ALU.mult, [kRN], [kR, kK2])
                    ts('dve', TB[4][:, 0:T], RN[:, 0:T], pv(l, 57 + j), None, ALU.mult, None, ['TB4'], [kRN, 'PV'])
                    mm(PS[6][:, 0:T], onesbd_b[:], TB[4][:, 0:T], True, True, ['PS6'], ['onesbd_b', 'TB4'])
                    tt('dve', BON[:, 0:T], PS[6][:, 0:T], V[:, 0:T], ALU.mult, [kBON], ['PS6', kV])
                    yield
                    scan(CL[:, 0:T], cmask[:, 0:T], SGW[:, 0:T], 0.0, ALU.mult, ALU.add, [kCL], ['CT', kSGW])
                    EG, EGI, EGX = TF[13], RN, K
                    kEG, kEGI = 'TF13', kRN
                    act(EG[:, 0:T], CL[:, 0:T], AF.Exp, [kEG], [kCL], scale=-C_DEC)
                    act(EGI[:, 0:T], CL[:, 0:T], AF.Exp, [kEGI], [kCL], scale=C_DEC)
                    tt('dve', SGW[:, 0:T], CL[:, 0:T], SGW[:, 0:T], ALU.subtract, [kSGW], [kCL, kSGW])
                    act(SGW[:, 0:T], SGW[:, 0:T], AF.Exp, [kSGW], [kSGW], scale=-C_DEC)
                    v3 = lambda ap: ap.rearrange("p (n l) -> p n l", l=L)
                    tt('dve', KR[:, 0:NCH, 1, 0:L], v3(R[:, 0:T]), v3(EG[:, 0:T]), ALU.mult, ['KR'], [kR, kEG])
                    tt('dve', KR[:, 0:NCH, 0, 0:L], v3(KK[:, 0:T]), v3(SGW[:, 0:T]), ALU.mult, ['KR'], [kKK, kSGW])
                    tt('dve', KT_[:, 0:T], K2[:, 0:T], EGI[:, 0:T], ALU.mult, ['KT_'], [kK2, kEGI])
                    tt('dve', BT_[:, 0:T], BV[:, 0:T], EGI[:, 0:T], ALU.mult, ['BT_'], [kBV, kEGI])
                    cpy('act', VS_[:, 0:T], V[:, 0:T], ['VS_'], [kV])
                    Hm = HS[:, l, j, :]
                    if SMDT != F32:
                        cpy('dve', HSs[:, j, :], Hm, ['HSs'], ['HS'])
                        Hs = HSs[:, j, :]
                        hsk = 'HSs'
                    else:
                        Hs = Hm
                        hsk = 'HS'
                    for n in range(NCH):
                        cs = slice(n * L, (n + 1) * L)
                        yield
                        for hh in range(2):
                            rw = slice(64 * hh, 64 * hh + 64)
                            pg = PS[4 + hh]
                            pk = 'PS%d' % (4 + hh)
                            mm(pg[0:L, 0:L], BT_[rw, cs], KR[rw, n, 0, 0:L], True, True, [pk], ['BT_', 'KR'])
                            mm(pg[0:L, 64:64 + L], KR[rw, n, 0, 0:L], BT_[rw, cs], True, True, [pk], ['BT_', 'KR'])
                            mm(pg[0:L, 128:128 + L], BT_[rw, cs], KR[rw, n, 1, 0:L], True, True, [pk], ['BT_', 'KR'])
                            mm(pg[0:L, 192:192 + L], KT_[rw, cs], KR[rw, n, 0, 0:L], True, True, [pk], ['KT_', 'KR'])
                            mm(pg[0:L, 256:256 + L], KT_[rw, cs], KR[rw, n, 1, 0:L], True, True, [pk], ['KT_', 'KR'])
                            tt('dve', W5[:, hh, :, :], pg[0:64, 0:320].rearrange("p (b l) -> p b l", b=5),
                               mask5[0:64, :, :], ALU.mult, ['W5'], [pk, mk5key])
                        yield
                        for bi, (src, sk) in enumerate(((KT_, 'KT_'), (BT_, 'BT_'), (VS_, 'VS_'))):
                            tp(PTA[0:L, bi * 128:(bi + 1) * 128], src[:, cs], ident_s, [kPTA], [sk, 'CT', 'ident_st'])
                        cpy('act', TM[:, :, :], PTA[0:64, 0:384].rearrange("p (b l) -> p b l", b=3), ['TM'], [kPTA])
                        yield
                        for hh in range(2):
                            tt('dve', TTa[:, hh, :, :], W5[:, hh, 0:2, :],
                               ident[0:64, 0:64].rearrange("p (o l) -> p o l", o=1).to_broadcast([64, 2, 64]),
                               ALU.add, ['TTa'], ['W5', 'CT'])
                        qm_cur, qk, qoff = W5, 'W5', 0
                        tt_cur, tk = TTa, 'TTa'
                        nlev = int(math.log2(L)) - 1
                        for lev in range(nlev):
                            lastl = (lev == nlev - 1)
                            qm_nx, qnk = (QMa, 'QMa') if lev % 2 == 0 else (QMb, 'QMb')
                            tt_nx, tnk = (TTb, 'TTb') if lev % 2 == 0 else (TTa, 'TTa')
                            for hh in range(2):
                                mm(PS[6][0:L, hh * 128:hh * 128 + L], qm_cur[0:L, hh, 1, 0:L], qm_cur[0:L, hh, 0, 0:L], True, True, ['PS6'], [qk])
                                if not lastl:
                                    mm(PS[6][0:L, hh * 128 + 64:hh * 128 + 64 + L], qm_cur[0:L, hh, 0, 0:L], qm_cur[0:L, hh, 1, 0:L], True, True, ['PS6'], [qk])
                            cpy('act', qm_nx[:].rearrange("p a b l -> p (a b l)"), PS[6][0:64, 0:256], [qnk], ['PS6'])
                            for hh in range(2):
                                mm(PS[5][0:L, hh * 128:hh * 128 + L], tt_cur[0:L, hh, 1, 0:L], qm_nx[0:L, hh, 0, 0:L], True, True, ['PS5'], [tk, qnk])
                                if not lastl:
                                    mm(PS[5][0:L, hh * 128 + 64:hh * 128 + 64 + L], qm_nx[0:L, hh, 0, 0:L], tt_cur[0:L, hh, 1, 0:L], True, True, ['PS5'], [tk, qnk])
                            tt('dve', tt_nx[:].rearrange("p a b l -> p (a b l)"), PS[5][0:64, 0:256],
                               tt_cur[:].rearrange("p a b l -> p (a b l)"), ALU.add, [tnk], ['PS5', tk])
                            qm_cur, qk = qm_nx, qnk
                            tt_cur, tk = tt_nx, tnk
                            yield
                        yield
                        for hh in range(2):
                            rw = slice(64 * hh, 64 * hh + 64)
                            fo = slice(64 * hh, 64 * hh + 64)
                            mm(PS[5][0:L, 384 + 64 * hh:448 + 64 * hh], KR[rw, n, 0, 0:L], Hs[rw, :], True, False, ['PS5'], ['KR', hsk])
                            mm(PS[5][0:L, 384 + 64 * hh:448 + 64 * hh], W5[0:L, hh, 3, 0:L], TM[0:L, 2, fo], False, True, ['PS5'], ['W5', 'TM'])
                        act(XN[:].rearrange("p a v -> p (a v)"), PS[5][0:64, 384:512], AF.Copy, ['XN'], ['PS5'], scale=-1.0)
                        yield
                        for hh in range(2):
                            mm(PS[5][0:L, 384 + 64 * hh:448 + 64 * hh], tt_cur[0:L, hh, 0, 0:L], XN[0:L, hh, :], True, True, ['PS5'], [tk, 'XN'])
                        cpy('dve', UU[:].rearrange("p a v -> p (a v)"), PS[5][0:64, 384:512], ['UU'], ['PS5'])
                        yield
                        for hh in range(2):
                            rw = slice(64 * hh, 64 * hh + 64)
                            fo = slice(64 * hh, 64 * hh + 64)
                            mm(PS[4][rw, 384:384 + L], Hs[rw, :], KR[rw, n, 1, 0:L], True, False, ['PS4'], [hsk, 'KR'])
                            mm(PS[4][rw, 384:384 + L], UU[0:L, hh, :], W5[0:L, hh, 2, 0:L], False, False, ['PS4'], ['UU', 'W5'])
                            mm(PS[4][rw, 384:384 + L], TM[0:L, 2, fo], W5[0:L, hh, 4, 0:L], False, True, ['PS4'], ['TM', 'W5'])
                        cpy('act', Y[:, cs], PS[4][:, 384:384 + L], ['TF12'], ['PS4'])
                        yield
                        for hh in range(2):
                            rw = slice(64 * hh, 64 * hh + 64)
                            fo = slice(64 * hh, 64 * hh + 64)
                            mm(PS[4][rw, 448:512], TM[0:L, 1, fo], UU[0:L, hh, :], True, False, ['PS4'], ['TM', 'UU'])
                            mm(PS[4][rw, 448:512], TM[0:L, 0, fo], TM[0:L, 2, fo], False, True, ['PS4'], ['TM'])
                        gl = EG[:, n * L + L - 1: n * L + L]
                        act(HG[:, :], Hm, AF.Identity, ['HG'], ['HS', kEG], scale=gl)
                        stt(Hm, PS[4][:, 448:512], gl, HG[:, :], ALU.mult, ALU.add, ['HS'], ['PS4', kEG, 'HG'])
                        if SMDT != F32:
                            cpy('act', HSs[:, j, :], Hm, ['HSs'], ['HS'])

                    yield
                    mm(PS[4][:, 0:T], onesbd_f, Y[:, 0:T], True, True, ['PS4'], ['CT', 'TF12'])
                    stt(Y[:, 0:T], PS[4][:, 0:T], -1.0 / 64, Y[:, 0:T], ALU.mult, ALU.add, ['TF12'], ['PS4', 'TF12'])
                    act(CL[:, 0:T], Y[:, 0:T], AF.Square, [kCL], ['TF12'])
                    mm(PS[5][:, 0:T], onesbd_f, CL[:, 0:T], True, True, ['PS5'], ['CT', kCL])
                    act(CL[:, 0:T], PS[5][:, 0:T], AF.Sqrt, [kCL], ['PS5'], scale=1.0 / 64, bias=GN_EPS_B)
                    recip(CL[:, 0:T], CL[:, 0:T], [kCL], [kCL])
                    tt('dve', Y[:, 0:T], Y[:, 0:T], CL[:, 0:T], ALU.mult, ['TF12'], ['TF12', kCL])
                    ts('dve', Y[:, 0:T], Y[:, 0:T], pv(l, 60 + j), pv(l, 63 + j), ALU.mult, ALU.add, ['TF12'], ['TF12', 'PV'])
                    tt('dve', Y[:, 0:T], Y[:, 0:T], BON[:, 0:T], ALU.add, ['TF12'], ['TF12', kBON])
                    tt('dve', YM[:, 2 + j, 0:T], Y[:, 0:T], G[:, 0:T], ALU.mult, ['YM%d' % (2 + j)], ['TF12', kG])

                yield

            def prepC():
                cpy('dve', PJ[:, 0:3, 0:3], HISTC[:, l, :, :], ['PJ0', 'PJ1', 'PJ2'], ['HISTC'])
                for (c0, nch_, s0) in ((1920, 4, 0), (2432, 4, 4), (2944, 1, 8)):
                    wt, wkey = wload([(0, 8, nch_ * 128) + win_src(l, c0, nch_ * 128)])
                    inproj(l, T, wt, wkey, nch_ * 128, [(i, s0 + i) for i in range(nch_)])
                cpy('dve', HISTC[:, l, :, :], PJ[:, 0:3, T:T + 3], ['HISTC'], ['PJ0', 'PJ1', 'PJ2'])
                if SMDT == F32:
                    KRf = KR[:].rearrange("p n a l -> p (n a l)")
                    QB = [TF[12], TF[13], KT_]
                    KS = [BT_, VS_, KRf]
                    kQB = ['TF12', 'TF13', 'KT_']
                    kKS = ['BT_', 'VS_', 'KR']
                else:
                    QB, KS, VBt = QKVb[0:3], QKVb[3:6], QKVb[6:9]
                    kQB = ['QKV%d' % i for i in range(3)]
                    kKS = ['QKV%d' % i for i in range(3, 6)]
                for j in range(3):
                    xc = TF[0]
                    act(xc[:, 0:T], PJ[:, j, 3:3 + T], AF.Identity, ['TF0'], ['PJ%d' % j, 'PV'],
                        scale=pv(l, 66 + 3 * 3 + j), bias=pv(l, 78 + j))
                    for t_ in range(3):
                        stt(xc[:, 0:T], PJ[:, j, t_:t_ + T], pv(l, 66 + 3 * t_ + j), xc[:, 0:T], ALU.mult, ALU.add,
                            ['TF0'], ['TF0', 'PJ%d' % j, 'PV'])
                    act(TB[2][:, 0:T], xc[:, 0:T], AF.Silu, ['TB2'], ['TF0'])
                    mm(PS[0][:, 0:T], WQbd[:, l, j, :], TB[2][:, 0:T], True, True, ['PS0'], ['WQbd', 'TB2'])
                    mm(PS[1][:, 0:T], WKbd[:, l, j, :], TB[2][:, 0:T], True, True, ['PS1'], ['WKbd', 'TB2'])
                    cpy('act', QB[j][:, 0:T], PS[0][:, 0:T], [kQB[j]], ['PS0'])
                    act(KS[j][:, 0:T], PS[1][:, 0:T], AF.Copy, [kKS[j]], ['PS1'], scale=0.125)
                    if SMDT != F32:
                        cpy('dve', VBt[j][:, 0:T], PJ[:, 3 + j, 3:3 + T], ['QKV%d' % (6 + j)], ['PJ%d' % (3 + j)])

                def vsrc(j, cs_):
                    if SMDT != F32:
                        return VBt[j][:, cs_.start:cs_.stop], 'QKV%d' % (6 + j)
                    return PJ[:, 3 + j, 3 + cs_.start:3 + cs_.stop], 'PJ%d' % (3 + j)
                allT = slice(0, T)
                srcs = [(QB[j][:, 0:T], kQB[j]) for j in range(3)] + [(KS[j][:, 0:T], kKS[j]) for j in range(3)] + \
                       [vsrc(j, allT) for j in range(3)]
                for kc, (s_, sk) in enumerate(srcs):
                    mm(PS[2][0:6, 0:T], WIF[:, l, kc, 0:6], s_, kc == 0, kc == 8, ['PS2'], ['WIF', sk])
                for kc, (s_, sk) in enumerate(srcs):
                    mm(PS[3][0:6, 0:T], WIF[:, l, kc, 6:12], s_, kc == 0, kc == 8, ['PS3'], ['WIF', sk])
                IP, L1, CS, MX = [TF[i][0:6, :] for i in range(4, 8)]
                kIP, kL1, kCS, kMX = ['TF%d' % i for i in range(4, 8)]
                AH, NMX, WL, MT = [GC[i] for i in range(4)]
                kAH, kNMX, kWL, kMT = ['GC%d' % i for i in range(4)]
                act(IP[:, 0:T], PS[2][0:6, 0:T], AF.Identity, [kIP], ['PS2', 'BI'], bias=BI[:, l:l + 1])
                act(L1[:, 0:T], PS[3][0:6, 0:T], AF.Exp, [kL1], ['PS3', 'NBF'], scale=-1.0, bias=NBF[:, l:l + 1])
                act(L1[:, 0:T], L1[:, 0:T], AF.Ln, [kL1], [kL1], bias=1.0)
                scan(CS[:, 0:T], ones_c[0:6, 0:1].to_broadcast([6, T]), L1[:, 0:T], 0.0, ALU.mult, ALU.add, [kCS], ['ones_f', kL1])
                tt('dve', AH[:, 0:T], IP[:, 0:T], CS[:, 0:T], ALU.add, [kAH], [kIP, kCS])
                scan(MX[:, 0:T], ones_c[0:6, 0:1].to_broadcast([6, T]), AH[:, 0:T], MS[:, l:l + 1], ALU.mult, ALU.max, [kMX], ['ones_f', kAH, 'MS'])
                tt('dve', MT[:, 0:T], MX[:, 0:T], CS[:, 0:T], ALU.subtract, [kMT], [kMX, kCS])
                ts('dve', NMX[:, 0:T], MX[:, 0:T], -1.0, None, ALU.mult, None, [kNMX], [kMX])
                for n in range(NCH):
                    cs = slice(n * L, (n + 1) * L)
                    prev = MS[:, l:l + 1] if n == 0 else MX[:, n * L - 1:n * L]
                    ts('dve', WL[:, cs], MX[:, cs], prev, None, ALU.subtract, None, [kWL], [kMX, 'MS'])
                cpy('dve', MS[:, l:l + 1], MT[:, T - 1:T], ['MS'], [kMT])
                YC = YCp
                Cm = CA[:, l, :, :]
                if SMDT != F32:
                    cpy('dve', CAs[:], Cm, ['CAs'], ['CA'])
                    Cs, csk = CAs[:], 'CAs'
                else:
                    Cs, csk = Cm, 'CA'
                for j in range(3):
                    act(SZ[j][:, 0:T], PJ[:, 6 + j, 3:3 + T], AF.Sigmoid, ['SZ%d' % j], ['PJ%d' % (6 + j)])
                return dict(QB=QB, KS=KS, kQB=kQB, kKS=kKS, vsrc=vsrc, AH=AH, NMX=NMX, WL=WL, MT=MT, kAH=kAH, kNMX=kNMX, kWL=kWL, kMT=kMT, YC=YC, Cm=Cm, Cs=Cs, csk=csk)

            def loopC(cc):
                QB, KS, kQB, kKS, vsrc = cc['QB'], cc['KS'], cc['kQB'], cc['kKS'], cc['vsrc']
                AH, NMX, WL, MT, kAH, kNMX, kWL, kMT = cc['AH'], cc['NMX'], cc['WL'], cc['MT'], cc['kAH'], cc['kNMX'], cc['kWL'], cc['kMT']
                YC, Cm, Cs, csk = cc['YC'], cc['Cm'], cc['Cs'], cc['csk']
                for n in range(NCH):
                    cs = slice(n * L, (n + 1) * L)
                    for j in range(3):
                        vs_, vk_ = vsrc(j, cs)
                        tp(PTB[0:L, j * 128:(j + 1) * 128], vs_, ident_s, [kPTB], [vk_, 'CT', 'ident_st'])
                    cpy('act', VA[0:L, :, 0:64], PTB[0:L, 0:384].rearrange("p (h v) -> p h v", h=6), ['VA'], [kPTB])
                    for j in range(3):
                        tp(PTB[0:L, j * 128:(j + 1) * 128], KS[j][:, cs], ident_s, [kPTB], [kKS[j], 'CT', 'ident_st'])
                    cpy('dve', KTm[0:L, :, :], PTB[0:L, 0:384].rearrange("p (h v) -> p h v", h=6), ['KTm'], [kPTB])
                    tp(PS[1][0:L, 400:406], WL[:, cs], ident[0:6, 0:6], ['PS1'], [kWL, 'CT'])
                    tp(PS[1][0:L, 406:412], MT[:, cs], ident[0:6, 0:6], ['PS1'], [kMT, 'CT'])
                    act(WE[0:L, :, :], PS[1][0:L, 400:412].rearrange("p (a h) -> p a h", a=2), AF.Exp, ['WE'], ['PS1'], scale=-1.0)
                    yield
                    mm(PS[0][0:L, 0:384].rearrange("p (h l) -> p h l", h=6)[:, :, 0:L], ident[0:L, 0:L], negm6[0:L, :, 0:L],
                       True, False, ['PS0'], ['CT'])
                    for h in range(6):
                        o_ = PS[0][0:L, h * 64:h * 64 + L]
                        mm(o_, sel6[:, h, 0:L], NMX[:, cs], False, False, ['PS0'], ['CT', kNMX])
                        mm(o_, AH[:, cs], sel6[:, h, 0:L], False, h == 5, ['PS0'], ['CT', kAH])
                    act(DD[0:L, :, 0:L], PS[0][0:L, 0:384].rearrange("p (h l) -> p h l", h=6)[:, :, 0:L], AF.Exp, ['DD'], ['PS0'])
                    yield
                    for h in range(6):
                        j, hh = h // 2, h % 2
                        rw = slice(64 * hh, 64 * hh + 64)
                        mm(PS[1][0:L, h * 64:h * 64 + L], KS[j][rw, cs], QB[j][rw, cs], True, True, ['PS1'], [kKS[j], kQB[j]])
                    tt('dve', STt[0:L, :, 0:L], PS[1][0:L, 0:384].rearrange("p (h l) -> p h l", h=6)[:, :, 0:L],
                       DD[0:L, :, 0:L], ALU.mult, ['STt'], ['PS1', 'DD'])
                    yield
                    for h in range(6):
                        j, hh = h // 2, h % 2
                        rw = slice(64 * hh, 64 * hh + 64)
                        mm(PS[2][0:L, h * 65:h * 65 + 65], STt[0:L, h, 0:L], VA[0:L, h, :], True, True, ['PS2'], ['STt', 'VA'])
                        mm(PS[3][0:L, h * 65:h * 65 + 65], QB[j][rw, cs], Cs[rw, j, :], True, True, ['PS3'], [kQB[j], csk])
                    tt('dve', T1[0:L], PS[3][0:L, 0:390].rearrange("p (h v) -> p h v", h=6),
                       WE[0:L, 0, :].rearrange("p (h o) -> p h o", o=1).to_broadcast([L, 6, 65]), ALU.mult, ['T1'], ['PS3', 'WE'])
                    tt('dve', NUM[0:L], PS[2][0:L, 0:390].rearrange("p (h v) -> p h v", h=6), T1[0:L], ALU.add, ['NUM'], ['PS2', 'T1'])
                    act(DEN[0:L, :], NUM[0:L, :, 64], AF.Abs, ['DEN'], ['NUM'])
                    tt('dve', DEN[0:L, :], DEN[0:L, :], WE[0:L, 1, :], ALU.max, ['DEN'], ['DEN', 'WE'])
                    recip(DEN[0:L, :], DEN[0:L, :], ['DEN'], ['DEN'])
                    yield
                    tt('dve', HTt[0:L], NUM[0:L, :, 0:64], DEN[0:L, :].rearrange("p (h o) -> p h o", o=1).to_broadcast([L, 6, 64]),
                       ALU.mult, ['HTt'], ['NUM', 'DEN'])
                    red(MEAN[0:L, :], HTt[0:L], ['MEAN'], ['HTt'])
                    stt(HC[0:L], MEAN[0:L, :].rearrange("p (h o) -> p h o", o=1).to_broadcast([L, 6, 64]), -1.0 / 64, HTt[0:L],
                        ALU.mult, ALU.add, ['HC'], ['MEAN', 'HTt'])
                    act(SQ[0:L], HC[0:L], AF.Square, ['SQ'], ['HC'])
                    red(VAR[0:L, :], SQ[0:L], ['VAR'], ['SQ'])
                    yield
                    act(VAR[0:L, :], VAR[0:L, :], AF.Sqrt, ['VAR'], ['VAR'], scale=1.0 / 64, bias=GN_EPS_C)
                    recip(VAR[0:L, :], VAR[0:L, :], ['VAR'], ['VAR'])
                    tt('dve', HC[0:L], HC[0:L], VAR[0:L, :].rearrange("p (h o) -> p h o", o=1).to_broadcast([L, 6, 64]),
                       ALU.mult, ['HC'], ['HC', 'VAR'])
                    yield
                    for j in range(3):
                        tp(PS[1][:, j * 64:j * 64 + L], HC[0:L, 2 * j:2 * j + 2, :].rearrange("p h v -> p (h v)"), ident[0:L, 0:L],
                           ['PS1'], ['HC', 'CT'])
                    for j in range(3):
                        act(YC[j][:, cs], PS[1][:, j * 64:j * 64 + L], AF.Identity, ['YCp%d' % j], ['PS1', 'PV'], scale=pv(l, 81 + j))
                    yield
                    tt('dve', VW[0:L], VA[0:L], DD[0:L, :, L - 1:L].to_broadcast([L, 6, 65]), ALU.mult, ['VW'], ['VA', 'DD'])
                    for h in range(6):
                        j, hh = h // 2, h % 2
                        rw = slice(64 * hh, 64 * hh + 64)
                        mm(PS[0][rw, j * 65:j * 65 + 65], KTm[0:L, h, :], VW[0:L, h, :], True, True, ['PS0'], ['KTm', 'VW'])
                    ts('dve', WLE[:, :], pairsel, WL[:, n * L + L - 1:n * L + L], None, ALU.mult, None, ['WLE'], ['CT', kWL])
                    mm(PS[0][:, 400:403], selb, WLE[:, :], True, True, ['PS0'], ['CT', 'WLE'])
                    act(W0[:, :], PS[0][:, 400:403], AF.Exp, ['W0'], ['PS0'], scale=-1.0)
                    tt('dve', CT2[:], Cm, W0[:, :].rearrange("p (h o) -> p h o", o=1).to_broadcast([128, 3, 65]), ALU.mult, ['CT2'], ['CA', 'W0'])
                    tt('dve', Cm, PS[0][:, 0:195].rearrange("p (h v) -> p h v", h=3), CT2[:], ALU.add, ['CA'], ['PS0', 'CT2'])
                    if SMDT != F32:
                        cpy('act', CAs[:], Cm, ['CAs'], ['CA'])

                    yield

            def postC():
                for j in range(3):
                    tt('dve', YM[:, 5 + j, 0:T], SZ[j][:, 0:T], YCp[j][:, 0:T], ALU.mult, ['YM%d' % (5 + j)], ['SZ%d' % j, 'YCp%d' % j])

            if OVERLAP:
                cc = prepC()
                gB, gC = groupB(), loopC(cc)
                aliveB = aliveC = True
                while aliveB or aliveC:
                    for _ in range(5):
                        if aliveB:
                            try:
                                next(gB)
                            except StopIteration:
                                aliveB = False
                    if aliveC:
                        try:
                            next(gC)
                        except StopIteration:
                            aliveC = False
            else:
                for _ in groupB():
                    pass
                cc = prepC()
                for _ in loopC(cc):
                    pass
            postC()

            for half in range(2):
                wt, wkey = wload([(0, 8, 512, WSC['w_out'].ap()[l][:, half * 512:(half + 1) * 512].rearrange("(k p) n -> p k n", p=128), 'DW_w_out%d' % l)])
                for mi in range(4):
                    m = half * 4 + mi
                    bank = mi % 4
                    for k in range(8):
                        mm(PS[bank][:, 0:T], wt[:, k * 512 + mi * 128:k * 512 + mi * 128 + 128], YM[:, k, 0:T], k == 0, k == 7,
                           ['PS%d' % bank], [wkey, 'YM%d' % k])
                    tt('dve', X[:, m, 0:T], PS[bank][:, 0:T], X[:, m, 0:T], ALU.add, ['X%d' % m], ['PS%d' % bank, 'X%d' % m])
            def o2(k):
                stt(HN[:, k, 0:T], X[:, k, 0:T], pv(l, 8 + k), RS[:, 0:T], ALU.mult, ALU.mult, ['HN%d' % k], ['X%d' % k, 'PV', 'TF13'])
            rmsnorm_to(T, 8, l, o2)
            for i in range(11):
                wf = WSC['w_ffn_in'].ap()[l]
                wt, wkey = wload([(0, 8, 256, wf[:, 256 * i:256 * i + 256].rearrange("(k p) n -> p k n", p=128), 'DW_w_ffn_in%d' % l),
                                  (2048, 8, 256, wf[:, DFF + 256 * i:DFF + 256 * i + 256].rearrange("(k p) n -> p k n", p=128), 'DW_w_ffn_in%d' % l)])
                for q in range(2):
                    f = 2 * i + q
                    bg, bu = (0, 1) if q == 0 else (2, 3)
                    for k in range(8):
                        mm(PS[bg][:, 0:T], wt[:, k * 256 + q * 128:k * 256 + q * 128 + 128], HN[:, k, 0:T], k == 0, k == 7,
                           ['PS%d' % bg], [wkey, 'HN%d' % k])
                    for k in range(8):
                        mm(PS[bu][:, 0:T], wt[:, 2048 + k * 256 + q * 128:2048 + k * 256 + q * 128 + 128], HN[:, k, 0:T], k == 0, k == 7,
                           ['PS%d' % bu], [wkey, 'HN%d' % k])
                    tk_ = TF[q]
                    act(tk_[:, 0:T], PS[bg][:, 0:T], AF.Silu, ['TF%d' % q], ['PS%d' % bg])
                    tt('dve', ACTT[:, f, 0:T], tk_[:, 0:T], PS[bu][:, 0:T], ALU.mult, actk(f), ['TF%d' % q, 'PS%d' % bu])
            wo = WSC['w_ffn_out'].ap()[l]
            for half in range(2):
                for (k0, kc_) in ((0, 8), (8, 8), (16, 6)):
                    wt, wkey = wload([(0, kc_, 512, wo[k0 * 128:(k0 + kc_) * 128, half * 512:(half + 1) * 512].rearrange("(k p) n -> p k n", p=128),
                                       'DW_w_ffn_out%d' % l)])
                    for mi in range(4):
                        for kk_ in range(kc_):
                            k = k0 + kk_
                            mm(PS[mi][:, 0:T], wt[:, kk_ * 512 + mi * 128:kk_ * 512 + mi * 128 + 128], ACTT[:, k, 0:T], k == 0, k == 21,
                               ['PS%d' % mi], [wkey] + actk(k))
                for mi in range(4):
                    m = half * 4 + mi
                    tt('dve', X[:, m, 0:T], PS[mi][:, 0:T], X[:, m, 0:T], ALU.add, ['X%d' % m], ['PS%d' % mi, 'X%d' % m])
            if last_layer:
                def o3(k):
                    stt(TF[k][:, 0:T], X[:, k, 0:T], NF[:, k:k + 1], RS[:, 0:T], ALU.mult, ALU.mult, ['TF%d' % k], ['X%d' % k, 'NF', 'TF13'])
                rmsnorm_to(T, 0, l, o3)
                nblk = (T + 127) // 128
                for tb in range(nblk):
                    n = min(128, T - tb * 128)
                    for half in range(2):
                        xt, xk = STG[half]
                        bank = 4 + half
                        for mmi in range(4):
                            m = half * 4 + mmi
                            tp(PS[bank][0:n, mmi * 128:(mmi + 1) * 128], TF[m][:, tb * 128:tb * 128 + n], ident,
                               ['PS%d' % bank], ['TF%d' % m, 'CT'])
                        cpy('act' if half else 'dve', xt[0:n, 0:512], PS[bank][0:n, 0:512], [xk], ['PS%d' % bank])
                        dma('sp', yout[tok0 + tb * 128:tok0 + tb * 128 + n, half * 512:(half + 1) * 512], xt[0:n, 0:512], r=[xk], is_output=True)

        for job in jobs:
            if job == 'p':
                T, L, nseg, xin, yout = TS, 64, TP // TS, I['xp'], O['yp']
                for t_, k_ in ((HISTA, 'HISTA'), (HISTB, 'HISTB'), (HISTC, 'HISTC'), (HL, 'HL'), (HS, 'HS'), (CA, 'CA'), (MS, 'MS')):
                    memset('dve', t_[:], 0.0, [k_])
            else:
                T, L, nseg, xin, yout = TSMP, 32, 1, I['xs'], O['ys']
                for l in range(DEPTH):
                    for j_ in range(3):
                        dma('sp', HISTA[:, l, :, j_], I['s_conv_a'][l, j_].rearrange("(c p) -> p c", p=128), w=['HISTA'], slow=True)
                        dma('sp', HISTC[:, l, :, j_], I['s_conv_c'][l, j_].rearrange("(c p) -> p c", p=128), w=['HISTC'], slow=True)
                    dma('sp', HISTB[:, l, :, 0], I['s_shift'][l].rearrange("(c p) -> p c", p=128), w=['HISTB'], slow=True)
                    dma('sp', HL[:, l, :], I['s_lru'][l].rearrange("(c p) -> p c", p=128), w=['HL'], slow=True)
                    dma('sp', CA[:, l, :, 0:64], I['s_mem_c'][l].rearrange("(jp hh) n v -> (hh n) jp v", hh=2), w=['CA'])
                    dma('sp', CA[:, l, :, 64], I['s_mem_n'][l].rearrange("(jp hh) n -> (hh n) jp", hh=2), w=['CA'], slow=True)
                    dma('sp', WS[:], I['s_wkv'][l].rearrange("h v k -> v h k"), w=['WS'])
                    for j in range(3):
                        tp(PS[4][:, j * 64:j * 64 + 64], WS[:, 2 * j:2 * j + 2, :].rearrange("p h k -> p (h k)"), ident[0:64, 0:64],
                           ['PS4'], ['WS', 'CT'])
                    cpy('dve', HS[:, l, :, :], PS[4][:, 0:192].rearrange("p (j v) -> p j v", j=3), ['HS'], ['PS4'])
                dma('sp', MS[:], I['s_mem_m'].rearrange("l h -> h l"), w=['MS'], slow=True)
            for seg in range(nseg):
                for l in range(DEPTH):
                    task(job, l, T, L, seg * T, seg == 0, seg == nseg - 1, xin, yout, l == DEPTH - 1)
            pre = job + '_'
            for l in range(DEPTH):
                for j_ in range(3):
                    dma('sp', O[pre + 'conv_a'][l, j_].rearrange("(c p) -> p c", p=128), HISTA[:, l, :, j_], r=['HISTA'], slow=True, is_output=True)
                    dma('sp', O[pre + 'conv_c'][l, j_].rearrange("(c p) -> p c", p=128), HISTC[:, l, :, j_], r=['HISTC'], slow=True, is_output=True)
                dma('sp', O[pre + 'shift'][l].rearrange("(c p) -> p c", p=128), HISTB[:, l, :, 0], r=['HISTB'], slow=True, is_output=True)
                dma('sp', O[pre + 'lru'][l].rearrange("(c p) -> p c", p=128), HL[:, l, :], r=['HL'], slow=True, is_output=True)
                dma('sp', O[pre + 'mem_c'][l].rearrange("(jp hh) n v -> (hh n) jp v", hh=2), CA[:, l, :, 0:64], r=['CA'], is_output=True)
                dma('sp', O[pre + 'mem_n'][l].rearrange("(jp hh) n -> (hh n) jp", hh=2), CA[:, l, :, 64], r=['CA'], slow=True, is_output=True)
                for j in range(3):
                    tp(PS[4][0:64, j * 128:(j + 1) * 128], HS[:, l, j, :], ident, ['PS4'], ['HS', 'CT'])
                cpy('dve', WS[:], PS[4][0:64, 0:384].rearrange("p (h k) -> p h k", h=6), ['WS'], ['PS4'])
                dma('sp', O[pre + 'wkv'][l].rearrange("h v k -> v h k"), WS[:], r=['WS'], is_output=True)
            dma('sp', O[pre + 'mem_m'].rearrange("l h -> h l"), MS[:], r=['MS'], slow=True, is_output=True)
        P.finish()
        P.emit(nc, st)
    return nc, cnp


_CACHE = {}
TS_DEFAULT = 512
SMALL_DT = BF16


def kernel(**inputs):
    inputs = {k: np.asarray(v) for k, v in inputs.items()}
    xp = inputs['x_prompt']
    xs = inputs['x_sample']
    DEPTH = inputs['norm1'].shape[0]
    B, TP, _ = xp.shape
    TS = min(TS_DEFAULT, TP)
    key = (DEPTH, TP, TS)
    if key not in _CACHE:
        _CACHE[key] = build(DEPTH, TP, TS, SMDT=SMALL_DT)
    nc, cnp = _CACHE[key]
    f32 = lambda a: np.ascontiguousarray(a, dtype=np.float32)
    in_maps = []
    for c in range(NCORES):
        m = {'xp': f32(xp[c % B]), 'xs': f32(xs[c]), 'consts': cnp}
        m['s_conv_a'] = f32(inputs['state_conv_a'][:, c])
        m['s_lru'] = f32(inputs['state_lru'][:, c])
        m['s_shift'] = f32(inputs['state_shift_b'][:, c, 0])
        m['s_wkv'] = f32(inputs['state_wkv'][:, c])
        m['s_conv_c'] = f32(inputs['state_conv_c'][:, c])
        m['s_mem_c'] = f32(inputs['state_mem_c'][:, c])
        m['s_mem_n'] = f32(inputs['state_mem_n'][:, c])
        m['s_mem_m'] = f32(inputs['state_mem_m'][:, c])
        for k in WNAMES:
            m[k] = f32(inputs[k])
        in_maps.append(m)
    res = run_bass_kernel_spmd(nc, in_maps, core_ids=list(range(NCORES)))
    R = res.results
    y_prompt = np.stack([R[b]['yp'] for b in range(B)], 0)
    y_sample = np.stack([R[c]['ys'] for c in range(NCORES)], 0)
    outs = [y_prompt, y_sample]
    names = ['conv_a', 'lru', 'shift', 'wkv', 'conv_c', 'mem_c', 'mem_n', 'mem_m']
    for jb, n in (('p', B), ('s', NCORES)):
        for nm in names:
            a = np.stack([R[c]['o' + jb + '_' + nm] for c in range(n)], 1)
            if nm == 'shift':
                a = a[:, :, None, :]
            outs.append(np.ascontiguousarray(a.astype(np.float32)))
    return tuple(outs)
```

```python
import math
from contextlib import ExitStack
import numpy as np
import concourse.bass as bass
import concourse.mybir as mybir
from concourse.bass_utils import run_bass_kernel_spmd
from concourse.alu_op_type import AluOpType as ALU

AF = mybir.ActivationFunctionType
F32 = mybir.dt.float32
BF16 = mybir.dt.bfloat16
AX = mybir.AxisListType

EPOCH = 16000
NDMA_SLOTS = 10
SAME_ENGINE_SYNC = True
NOSYNC_ENGINES = ('pe',)

D = 1024
DIN = 3072
DA = 256
DB = 384
DBIN = 1408
DC = 384
DFF = 2816
NCORES = 8
TSMP = 32
C_DEC = math.exp(-0.5)
RMS_EPS = 1e-6
GN_EPS_B = 64e-5
GN_EPS_C = 1e-6


class Prog:
    ENG = ['pe', 'act', 'dve', 'pool', 'sp']

    def __init__(self):
        self.stream = {e: [] for e in self.ENG}
        self.n = {e: 0 for e in self.ENG}
        self.lastw = {}
        self.readers = {}
        self.seen = {e: {} for e in self.ENG}
        self.dma_slot_next = {e: 0 for e in self.ENG}
        self.dma_slot_val = {}
        self.out_tokens = []
        self.pe_rg = {}

    def _deps(self, eng, reads, writes, extra=(), force=()):
        toks = list(extra) + list(force)
        forced_src = set(t[0] for t in force)
        for k in reads:
            t = self.lastw.get(k)
            if t:
                toks.append(t)
        for k in writes:
            t = self.lastw.get(k)
            if t:
                toks.append(t)
            toks.extend(self.readers.get(k, {}).values())
        need = {}
        for (src, val) in toks:
            if need.get(src, 0) < val:
                need[src] = val
        out = []
        for src, val in need.items():
            if src == ('e', eng) and (not SAME_ENGINE_SYNC or eng in NOSYNC_ENGINES) and src not in forced_src:
                continue
            if self.seen[eng].get(src, 0) >= val:
                continue
            self.seen[eng][src] = val
            out.append((src, val))
        return out

    def op(self, eng, fn, w=(), r=(), rg=None):
        w = list(w) + [k for k in r if k.startswith('PS') and k[2:].isdigit()]
        force = []
        if eng == 'pe':
            for k in w:
                if k.startswith('PS'):
                    prev = self.pe_rg.get(k)
                    if prev is not None and prev[0] != rg and self.lastw.get(k) == prev[1]:
                        force.append(prev[1])
        waits = self._deps(eng, r, w, force=force)
        self.n[eng] += 1
        tok = (('e', eng), self.n[eng])
        for k in w:
            self.lastw[k] = tok
            self.readers[k] = {}
        for k in r:
            self.readers.setdefault(k, {})[('e', eng)] = tok
        self.stream[eng].append((waits, fn, tok))
        if eng == 'pe':
            for k in w:
                if k.startswith('PS'):
                    self.pe_rg[k] = (rg, tok)
        return tok

    def dma(self, q, fn, w=(), r=(), is_output=False):
        slot = (q, self.dma_slot_next[q] % NDMA_SLOTS)
        self.dma_slot_next[q] += 1
        src = ('d', slot)
        prev = self.dma_slot_val.get(slot, 0)
        extra = [(src, prev)] if prev else []
        waits = self._deps(q, r, w, extra)
        val = prev + 16
        self.dma_slot_val[slot] = val
        tok = (src, val)
        for k in w:
            self.lastw[k] = tok
            self.readers[k] = {}
        for k in r:
            self.readers.setdefault(k, {})[src] = tok
        self.stream[q].append((waits, fn, tok))
        if is_output:
            self.out_tokens.append(tok)
        return tok

    def finish(self):
        need = {}
        for (src, val) in self.out_tokens:
            need[src] = max(need.get(src, 0), val)
        self.stream['sp'].append((list(need.items()), None, None))

    def emit(self, nc, stack):
        sems = {}

        def getsem(src, val):
            if src[0] == 'e':
                ep = (val - 1) // EPOCH
                key = (src, ep)
                v = val - ep * EPOCH
            else:
                key = (src, 0)
                v = val
            if key not in sems:
                sems[key] = stack.enter_context(nc.semaphore("s%d" % len(sems)))
            return sems[key], v

        for e in self.ENG:
            for (waits, fn, tok) in self.stream[e]:
                for (src, val) in waits:
                    getsem(src, val)
                if tok is not None:
                    getsem(*tok)
        block = stack.enter_context(nc.Block())
        names = {'pe': 'tensor', 'act': 'scalar', 'dve': 'vector', 'pool': 'gpsimd', 'sp': 'sync'}
        for e in self.ENG:
            items = self.stream[e]
            if not items:
                continue

            def body(engh, items=items):
                for (waits, fn, tok) in items:
                    for (src, val) in waits:
                        s, v = getsem(src, val)
                        engh.wait_ge(s, v)
                    if fn is None:
                        continue
                    ins = fn(engh)
                    s, v = getsem(*tok)
                    ins.then_inc(s, 16 if tok[0][0] == 'd' else 1)
            getattr(block, names[e])(body)
        self.nsems = len(sems)


def _consts(TS):
    cw = {}
    cols = []

    def add(name, arr):
        a = np.zeros((128, arr.shape[1]), np.float32)
        a[:arr.shape[0]] = arr
        cw[name] = (sum(c.shape[1] for c in cols), arr.shape[1])
        cols.append(a)
    add('ident', np.eye(128, dtype=np.float32))
    bd = np.zeros((128, 128), np.float32)
    bd[:64, :64] = 1
    bd[64:, 64:] = 1
    add('onesbd', bd)
    add('ident2', np.concatenate([np.eye(64, dtype=np.float32)] * 2, 0))
    r = np.arange(128)[:, None] % 64
    c = np.arange(64)[None, :]
    su = (c > r).astype(np.float32)
    sl = (c < r).astype(np.float32)
    ui = (c >= r).astype(np.float32)
    add('mask5', np.concatenate([-su, -sl, ui, su, ui], 1))
    negm = np.where(np.arange(64)[:, None] > c, -30000.0, 0.0).astype(np.float32)
    add('negm6', np.tile(negm, (1, 6)))
    cm = np.ones((128, TS), np.float32)
    cm[:, ::64] = 0
    add('cmask', cm)
    sel6 = np.zeros((6, 6, 64), np.float32)
    for h in range(6):
        sel6[h, h, :] = 1
    add('sel6', sel6.reshape(6, 384))
    selb = np.zeros((6, 128), np.float32)
    for k in range(6):
        selb[k, (k % 2) * 64:(k % 2) * 64 + 64] = 1
    add('selb', selb)
    ps_ = np.zeros((6, 3), np.float32)
    for k in range(6):
        ps_[k, k // 2] = 1
    add('pairsel', ps_)
    return np.concatenate(cols, 1), cw


WNAMES = ['norm1', 'w_in', 'conv_a_w', 'conv_a_b', 'lru_wr', 'lru_br', 'lru_wi', 'lru_bi', 'lru_lambda',
          'norm_a', 'rwkv_mu', 'rwkv_w0', 'rwkv_w2', 'rwkv_a0', 'rwkv_a2', 'rwkv_g2', 'rwkv_kk', 'rwkv_ka',
          'rwkv_rk', 'rwkv_lnw', 'rwkv_lnb', 'conv_c_w', 'conv_c_b', 'mlstm_wq', 'mlstm_wk', 'mlstm_wif',
          'mlstm_bif', 'mlstm_gn', 'w_out', 'norm2', 'w_ffn_in', 'w_ffn_out', 'norm_f']


def build(DEPTH, TP, TS, SMDT=F32, jobs=('p', 's')):
    assert TP % TS == 0 and TS % 128 == 0
    nc = bass.Bass("TRN2", target_bir_lowering=False)
    cnp, cw = _consts(TS)
    CW = cnp.shape[1]

    def din(name, shape):
        return nc.dram_tensor(name, list(shape), F32, kind="ExternalInput").ap()

    def dout(name, shape):
        return nc.dram_tensor(name, list(shape), F32, kind="ExternalOutput").ap()

    I = {}
    I['xp'] = din('xp', [TP, D])
    I['xs'] = din('xs', [TSMP, D])
    I['consts'] = din('consts', [128, CW])
    st_shapes = {'conv_a': [DEPTH, 3, DA], 'lru': [DEPTH, DA], 'shift': [DEPTH, DBIN], 'wkv': [DEPTH, 6, 64, 64],
                 'conv_c': [DEPTH, 3, DC], 'mem_c': [DEPTH, 6, 64, 64], 'mem_n': [DEPTH, 6, 64], 'mem_m': [DEPTH, 6]}
    for k, s in st_shapes.items():
        I['s_' + k] = din('s_' + k, s)
    wshapes = {'norm1': [DEPTH, D], 'w_in': [DEPTH, D, DIN], 'conv_a_w': [DEPTH, 4, DA], 'conv_a_b': [DEPTH, DA],
               'lru_wr': [DEPTH, 4, 64, 64], 'lru_br': [DEPTH, DA], 'lru_wi': [DEPTH, 4, 64, 64], 'lru_bi': [DEPTH, DA],
               'lru_lambda': [DEPTH, DA], 'norm_a': [DEPTH, DA], 'rwkv_mu': [DEPTH, DBIN], 'rwkv_w0': [DEPTH, DB],
               'rwkv_w2': [DEPTH, 64, DB], 'rwkv_a0': [DEPTH, DB], 'rwkv_a2': [DEPTH, 64, DB], 'rwkv_g2': [DEPTH, 128, DB],
               'rwkv_kk': [DEPTH, DB], 'rwkv_ka': [DEPTH, DB], 'rwkv_rk': [DEPTH, 6, 64], 'rwkv_lnw': [DEPTH, DB],
               'rwkv_lnb': [DEPTH, DB], 'conv_c_w': [DEPTH, 4, DC], 'conv_c_b': [DEPTH, DC], 'mlstm_wq': [DEPTH, 6, 64, 64],
               'mlstm_wk': [DEPTH, 6, 64, 64], 'mlstm_wif': [DEPTH, 3 * DC, 12], 'mlstm_bif': [DEPTH, 12],
               'mlstm_gn': [DEPTH, DC], 'w_out': [DEPTH, D, D], 'norm2': [DEPTH, D], 'w_ffn_in': [DEPTH, D, 2 * DFF],
               'w_ffn_out': [DEPTH, DFF, D], 'norm_f': [D]}
    for k in WNAMES:
        I[k] = din(k, wshapes[k])
    O = {}
    O['yp'] = dout('yp', [TP, D])
    O['ys'] = dout('ys', [TSMP, D])
    for jb in ('p', 's'):
        for k, s in st_shapes.items():
            O[jb + '_' + k] = dout('o' + jb + '_' + k, s)

    WSC = {'w_in': nc.dram_tensor('w_in_b', [DEPTH, D, DIN], BF16), 'w_out': nc.dram_tensor('w_out_b', [DEPTH, D, D], BF16),
           'w_ffn_in': nc.dram_tensor('w_ffn_in_b', [DEPTH, D, 2 * DFF], BF16), 'w_ffn_out': nc.dram_tensor('w_ffn_out_b', [DEPTH, DFF, D], BF16)}
    P = Prog()
    with ExitStack() as st:
        def sb(name, shape, dt=F32):
            return st.enter_context(nc.sbuf_tensor(name, list(shape), dt))

        def psum(name, shape, dt=F32):
            return st.enter_context(nc.psum_tensor(name, list(shape), dt))

        def tt(eng, out, in0, in1, op, w, r):
            P.op(eng, lambda e: e.tensor_tensor(out=out, in0=in0, in1=in1, op=op), w, r)

        def ts(eng, out, in0, s1, s2, op0, op1, w, r):
            if op1 is None:
                P.op(eng, lambda e: e.tensor_scalar(out=out, in0=in0, scalar1=s1, scalar2=None, op0=op0), w, r)
            else:
                P.op(eng, lambda e: e.tensor_scalar(out=out, in0=in0, scalar1=s1, scalar2=s2, op0=op0, op1=op1), w, r)

        def stt(out, in0, scalar, in1, op0, op1, w, r):
            P.op('dve', lambda e: e.scalar_tensor_tensor(out=out, in0=in0, scalar=scalar, in1=in1, op0=op0, op1=op1), w, r)

        def act(out, in_, func, w, r, scale=1.0, bias=0.0):
            P.op('act', lambda e: e.activation(out=out, in_=in_, func=func, scale=scale, bias=bias), w, r)

        def cpy(eng, out, in_, w, r):
            if eng == 'act':
                P.op('act', lambda e: e.activation(out=out, in_=in_, func=AF.Copy), w, r)
            else:
                P.op(eng, lambda e: e.tensor_copy(out=out, in_=in_), w, r)

        def _rg(ap):
            n = ap.partition_size()
            return (ap.base_partition(), 32 if n <= 32 else (64 if n <= 64 else 128))

        def mm(out, lhsT, rhs, start, stop, w, r):
            P.op('pe', lambda e: e.matmul(out, lhsT=lhsT, rhs=rhs, start=start, stop=stop), w, r, rg=_rg(lhsT))

        def tp(out, in_, ident, w, r):
            P.op('pe', lambda e: e.transpose(out, in_, ident), w, r, rg=_rg(in_))

        def recip(out, in_, w, r):
            P.op('dve', lambda e: e.reciprocal(out=out, in_=in_), w, r)

        def scan(out, d0, d1, init, op0, op1, w, r):
            P.op('dve', lambda e: e.tensor_tensor_scan(out=out, data0=d0, data1=d1, initial=init, op0=op0, op1=op1), w, r)

        def red(out, in_, w, r):
            P.op('dve', lambda e: e.tensor_reduce(out=out, in_=in_, axis=AX.X, op=ALU.add), w, r)

        def memset(eng, ap, val, w):
            P.op(eng, lambda e: e.memset(ap, val), w, ())

        def dma(q, out, in_, w=(), r=(), slow=False, is_output=False):
            if slow:
                P.dma(q, lambda e: e.dma_start(out=out, in_=in_, allow_slow_non_contiguous=True), w, r, is_output)
            else:
                P.dma(q, lambda e: e.dma_start(out=out, in_=in_), w, r, is_output)

        CT = sb('CT', [128, CW])
        dma('sp', CT[:], I['consts'], w=['CT'])

        def cst(name, rows=128):
            o, n = cw[name]
            return CT[0:rows, o:o + n]
        ident = cst('ident')
        ident2 = cst('ident2')
        onesbd_f = cst('onesbd')
        cmask = cst('cmask')
        onesb = sb('onesb', [128, 128], BF16)
        memset('dve', onesb[:], 1.0, ['onesb'])
        onesbd_b = sb('onesbd_b', [128, 128], BF16)
        cpy('dve', onesbd_b[:], onesbd_f, ['onesbd_b'], ['CT'])
        ones_c = sb('ones_c', [128, 1])
        memset('dve', ones_c[:], 1.0, ['ones_f'])
        if SMDT == F32:
            mask5 = cst('mask5').rearrange("p (b l) -> p b l", b=5)
            ident_s = ident
            mk5key = 'CT'
        else:
            mask5t = sb('mask5t', [128, 5, 64], SMDT)
            cpy('dve', mask5t[:], cst('mask5').rearrange("p (b l) -> p b l", b=5), ['mask5t'], ['CT'])
            mask5 = mask5t[:]
            ident_st = sb('ident_st', [128, 128], SMDT)
            cpy('dve', ident_st[:], ident, ['ident_st'], ['CT'])
            ident_s = ident_st[:]
            mk5key = 'mask5t'
        negm6 = cst('negm6', 64).rearrange("p (h l) -> p h l", h=6)
        sel6 = cst('sel6', 6).rearrange("p (h l) -> p h l", h=6)
        selb = cst('selb', 6)
        pairsel = cst('pairsel', 6)

        NV = 88
        PV = sb('PV', [128, DEPTH, NV])
        NF = sb('NF', [128, 8])
        BI = sb('BI', [6, DEPTH])
        NBF = sb('NBF', [6, DEPTH])

        def pvload(name, col, n):
            for l in range(DEPTH):
                dma('sp', PV[:, l, col:col + n], I[name][l].rearrange("(c p) -> p c", p=128), w=['PV'], slow=True)
        pvload('norm1', 0, 8)
        pvload('norm2', 8, 16 - 8)
        for l in range(DEPTH):
            for j in range(4):
                dma('sp', PV[:, l, 16 + 2 * j:18 + 2 * j], I['conv_a_w'][l, j].rearrange("(c p) -> p c", p=128), w=['PV'], slow=True)
                dma('sp', PV[:, l, 66 + 3 * j:69 + 3 * j], I['conv_c_w'][l, j].rearrange("(c p) -> p c", p=128), w=['PV'], slow=True)
        pvload('conv_a_b', 24, 2)
        pvload('lru_br', 26, 2)
        pvload('lru_bi', 28, 2)
        pvload('lru_lambda', 30, 2)
        pvload('norm_a', 32, 2)
        pvload('rwkv_mu', 34, 11)
        pvload('rwkv_w0', 45, 3)
        pvload('rwkv_a0', 48, 3)
        pvload('rwkv_kk', 51, 3)
        pvload('rwkv_ka', 54, 3)
        for l in range(DEPTH):
            dma('sp', PV[:, l, 57:60], I['rwkv_rk'][l].rearrange("(c hh) n -> (hh n) c", hh=2), w=['PV'], slow=True)
        pvload('rwkv_lnw', 60, 3)
        pvload('rwkv_lnb', 63, 3)
        pvload('conv_c_b', 78, 3)
        pvload('mlstm_gn', 81, 3)
        dma('sp', NF[:], I['norm_f'].rearrange("(c p) -> p c", p=128), w=['NF'], slow=True)
        dma('sp', BI[:], I['mlstm_bif'][:, 0:6].rearrange("l h -> h l"), w=['BI'], slow=True)
        dma('sp', NBF[:], I['mlstm_bif'][:, 6:12].rearrange("l h -> h l"), w=['NBF'], slow=True)
        ts('dve', NBF[:], NBF[:], -1.0, None, ALU.mult, None, ['NBF'], ['NBF'])
        SPT = sb('SPT', [128, DEPTH, 2])
        act(SPT[:], PV[:, :, 30:32], AF.Exp, ['SPT'], ['PV'], scale=-1.0)
        act(SPT[:], SPT[:], AF.Ln, ['SPT'], ['SPT'], bias=1.0)
        ts('dve', PV[:, :, 84:86], SPT[:], -8.0, None, ALU.mult, None, ['PV'], ['SPT'])
        ts('dve', PV[:, :, 86:88], SPT[:], -16.0, None, ALU.mult, None, ['PV'], ['SPT'])

        WRbd = sb('WRbd', [128, DEPTH, 2, 128])
        WIbd = sb('WIbd', [128, DEPTH, 2, 128])
        memset('dve', WRbd[:], 0.0, ['WRbd'])
        memset('dve', WIbd[:], 0.0, ['WIbd'])
        WQbd = sb('WQbd', [128, DEPTH, 3, 128], BF16)
        WKbd = sb('WKbd', [128, DEPTH, 3, 128], BF16)
        memset('dve', WQbd[:], 0.0, ['WQbd'])
        memset('dve', WKbd[:], 0.0, ['WKbd'])
        W2 = sb('W2', [128, DEPTH, DB], BF16)
        A2 = sb('A2', [128, DEPTH, DB], BF16)
        G2 = sb('G2', [128, DEPTH, DB], BF16)
        WIF = sb('WIF', [128, DEPTH, 9, 12], SMDT)
        for l in range(DEPTH):
            for n in range(4):
                hb, c = n % 2, n // 2
                dma('sp', WRbd[64 * hb:64 * hb + 64, l, c, 64 * hb:64 * hb + 64], I['lru_wr'][l, n], w=['WRbd'])
                dma('sp', WIbd[64 * hb:64 * hb + 64, l, c, 64 * hb:64 * hb + 64], I['lru_wi'][l, n], w=['WIbd'])
            for h in range(6):
                hb, c = h % 2, h // 2
                dma('pool', WQbd[64 * hb:64 * hb + 64, l, c, 64 * hb:64 * hb + 64], I['mlstm_wq'][l, h], w=['WQbd'])
                dma('pool', WKbd[64 * hb:64 * hb + 64, l, c, 64 * hb:64 * hb + 64], I['mlstm_wk'][l, h], w=['WKbd'])
            dma('pool', W2[0:64, l, :], I['rwkv_w2'][l], w=['W2'])
            dma('pool', A2[64:128, l, :], I['rwkv_a2'][l], w=['A2'])
            dma('pool', G2[:, l, :], I['rwkv_g2'][l], w=['G2'])
            dma('pool' if SMDT != F32 else 'sp', WIF[:, l, :, :], I['mlstm_wif'][l].rearrange("(kc p) n -> p kc n", p=128), w=['WIF'])
        ts('dve', WIF[:, :, 3:6, :], WIF[:, :, 3:6, :], 8.0, None, ALU.mult, None, ['WIF'], ['WIF'])

        HISTA = sb('HISTA', [128, DEPTH, 2, 3])
        HISTB = sb('HISTB', [128, DEPTH, 11, 1])
        HISTC = sb('HISTC', [128, DEPTH, 3, 3])
        HL = sb('HL', [128, DEPTH, 2])
        HS = sb('HS', [128, DEPTH, 3, 64])
        CA = sb('CA', [128, DEPTH, 3, 65])
        MS = sb('MS', [6, DEPTH])
        WS = sb('WS', [64, 6, 64])

        X = sb('X', [128, 8, TS])
        HN = sb('HN', [128, 8, TS], BF16)
        NSLOT = 11
        PJ = sb('PJ', [128, NSLOT, 3 + TS])
        YM = sb('YM', [128, 8, TS], BF16)
        assert NSLOT * (3 + TS) * 4 >= 22 * TS * 2
        ACTT = PJ[:].rearrange("p s c -> p (s c)").bitcast(BF16)[:, 0:22 * TS].rearrange("p (f t) -> p f t", f=22)
        def actk(f):
            b0, b1 = f * TS * 2, (f + 1) * TS * 2 - 1
            sl_ = (3 + TS) * 4
            return ['PJ%d' % s_ for s_ in range(b0 // sl_, b1 // sl_ + 1)]
        NWB = 2
        WB = [sb('WB%d' % i, [128, 4096], BF16) for i in range(NWB)]
        NTF = 14
        TF = [sb('TF%d' % i, [128, TS]) for i in range(NTF)]
        NTB = 5
        TB = [sb('TB%d' % i, [128, TS], BF16) for i in range(NTB)]
        KR = sb('KR', [128, TS // 64, 2, 64], SMDT)
        KT_ = sb('KT_', [128, TS], SMDT)
        BT_ = sb('BT_', [128, TS], SMDT)
        VS_ = sb('VS_', [128, TS], SMDT)
        W5 = sb('W5', [64, 2, 5, 64], SMDT)
        QMa = sb('QMa', [64, 2, 2, 64], SMDT)
        QMb = sb('QMb', [64, 2, 2, 64], SMDT)
        TTa = sb('TTa', [64, 2, 2, 64], SMDT)
        TTb = sb('TTb', [64, 2, 2, 64], SMDT)
        TM = sb('TM', [64, 3, 128], SMDT)
        XN = sb('XN', [64, 2, 64], SMDT)
        UU = sb('UU', [64, 2, 64], SMDT)
        HG = sb('HG', [128, 64])
        HSs = sb('HSs', [128, 3, 64], SMDT) if SMDT != F32 else None
        CAs = sb('CAs', [128, 3, 65], SMDT) if SMDT != F32 else None
        QKVb = [sb('QKV%d' % i, [128, TS], SMDT) for i in range(9)] if SMDT != F32 else None
        VA = sb('VA', [64, 6, 65], SMDT)
        KTm = sb('KTm', [64, 6, 64], SMDT)
        WE = sb('WE', [64, 2, 6])
        DD = sb('DD', [64, 6, 64])
        STt = sb('STt', [64, 6, 64], SMDT)
        T1 = sb('T1', [64, 6, 65])
        NUM = sb('NUM', [64, 6, 65])
        DEN = sb('DEN', [64, 6])
        HTt = sb('HTt', [64, 6, 64])
        HC = sb('HC', [64, 6, 64])
        SQ = sb('SQ', [64, 6, 64])
        MEAN = sb('MEAN', [64, 6])
        VAR = sb('VAR', [64, 6])
        VW = sb('VW', [64, 6, 65], SMDT)
        WLE = sb('WLE', [6, 3])
        W0 = sb('W0', [128, 3])
        CT2 = sb('CT2', [128, 3, 65])
        RS = TF[13]
        GC = [sb('GC%d' % i, [6, TS]) for i in range(4)]
        YCp = [sb('YCp%d' % i, [128, TS]) for i in range(3)]
        SZ = [sb('SZ%d' % i, [128, TS], BF16) for i in range(3)]
        OVERLAP = (SMDT != F32)
        if TS >= 512:
            STG = [(TF[11], 'TF11'), (TF[12], 'TF12')]
        else:
            STG = [(sb('STG0', [128, 512]), 'STG0'), (sb('STG1', [128, 512]), 'STG1')]
        if SMDT == F32:
            PS = [psum('PS%d' % i, [128, 512]) for i in range(8)]
            PTA, kPTA, PTB, kPTB = PS[7], 'PS7', PS[4], 'PS4'
        else:
            PS = [psum('PS%d' % i, [128, 512]) for i in range(7)]
            PSB = psum('PSB', [128, 1024], SMDT)
            PTA, kPTA, PTB, kPTB = PSB[:, 0:512], 'PS7', PSB[:, 512:1024], 'PS7'

        memset('dve', VA[:], 1.0, ['VA'])

        def wconvert(l):
            for nm in ('w_in', 'w_out', 'w_ffn_in', 'w_ffn_out'):
                dma('pool', WSC[nm].ap()[l].rearrange("(k p) n -> p k n", p=128), I[nm][l].rearrange("(k p) n -> p k n", p=128),
                    w=['DW_%s%d' % (nm, l)])
        wconvert(0)
        wstate = {'i': 0}

        def wload(parts):
            i = wstate['i'] % NWB
            wstate['i'] += 1
            key = 'WB%d' % i
            for (c0, KC, ncol, src, rk) in parts:
                dst = WB[i][:, c0:c0 + KC * ncol].rearrange("p (k n) -> p k n", k=KC)
                dma('pool', dst, src, w=[key], r=[rk])
            return WB[i], key

        def win_src(l, c0, ncol):
            return WSC['w_in'].ap()[l][:, c0:c0 + ncol].rearrange("(k p) n -> p k n", p=128), 'DW_w_in%d' % l

        def rmsnorm_to(T, gcol_base, l, out_fn):
            for k in range(8):
                act(TB[0][:, 0:T] if k % 2 == 0 else TB[1][:, 0:T], X[:, k, 0:T], AF.Square,
                    ['TB%d' % (k % 2)], ['X%d' % k])
                mm(PS[0][:, 0:T], onesb[:], TB[k % 2][:, 0:T], k == 0, k == 7, ['PS0'], ['onesb', 'TB%d' % (k % 2)])
            act(RS[:, 0:T], PS[0][:, 0:T], AF.Sqrt, ['TF13'], ['PS0'], scale=1.0 / D, bias=RMS_EPS)
            recip(RS[:, 0:T], RS[:, 0:T], ['TF13'], ['TF13'])
            for k in range(8):
                out_fn(k)

        def inproj(l, T, wt, wkey, ncols_tile, chunk_list):
            for ci, (cit, slot) in enumerate(chunk_list):
                bank = ci % 4
                for k in range(8):
                    mm(PS[bank][:, 0:T], wt[:, k * ncols_tile + cit * 128: k * ncols_tile + cit * 128 + 128],
                       HN[:, k, 0:T], k == 0, k == 7, ['PS%d' % bank], [wkey] + ['HN%d' % k])
                cpy('act' if ci % 2 == 0 else 'dve', PJ[:, slot, 3:3 + T], PS[bank][:, 0:T], ['PJ%d' % slot], ['PS%d' % bank])

        def pv(l, c):
            return PV[:, l, c:c + 1]

        def task(job, l, T, L, tok0, first_seg, last_seg, xin, yout, last_layer):
            NCH = T // L
            if job == jobs[0] and first_seg and l + 1 < DEPTH:
                wconvert(l + 1)
            if l == 0:
                nblk = (T + 127) // 128
                for tb in range(nblk):
                    n = min(128, T - tb * 128)
                    for half in range(2):
                        xt, xk = STG[half]
                        bank = 4 + half
                        dma('sp', xt[0:n, 0:512], xin[tok0 + tb * 128: tok0 + tb * 128 + n, half * 512:(half + 1) * 512], w=[xk])
                        for mmi in range(4):
                            tp(PS[bank][:, mmi * 128: mmi * 128 + n], xt[0:n, mmi * 128:(mmi + 1) * 128], ident[0:n, 0:n],
                               ['PS%d' % bank], [xk, 'CT'])
                        for mmi in range(4):
                            m = half * 4 + mmi
                            cpy('act' if mmi % 2 else 'dve', X[:, m, tb * 128: tb * 128 + n],
                                PS[bank][:, mmi * 128: mmi * 128 + n], ['X%d' % m], ['PS%d' % bank])
            def o1(k):
                stt(HN[:, k, 0:T], X[:, k, 0:T], pv(l, k), RS[:, 0:T], ALU.mult, ALU.mult, ['HN%d' % k], ['X%d' % k, 'PV', 'TF13'])
            rmsnorm_to(T, 0, l, o1)

            wt, wkey = wload([(0, 8, 512) + win_src(l, 0, 512)])
            cpy('dve', PJ[:, 0:2, 0:3], HISTA[:, l, :, :], ['PJ0', 'PJ1'], ['HISTA'])
            inproj(l, T, wt, wkey, 512, [(0, 0), (1, 1), (2, 2), (3, 3)])
            cpy('dve', HISTA[:, l, :, :], PJ[:, 0:2, T:T + 3], ['HISTA'], ['PJ0', 'PJ1'])
            for c in range(2):
                xa, gr, gi, aa, a2t, mu_, uu, hh = TF[0], TF[1], TF[2], TF[3], TF[4], TF[5], TF[6], TF[7 + c]
                act(xa[:, 0:T], PJ[:, c, 3:3 + T], AF.Identity, ['TF0'], ['PJ%d' % c, 'PV'],
                    scale=pv(l, 16 + 2 * 3 + c), bias=pv(l, 24 + c))
                for j in range(3):
                    stt(xa[:, 0:T], PJ[:, c, j:j + T], pv(l, 16 + 2 * j + c), xa[:, 0:T], ALU.mult, ALU.add,
                        ['TF0'], ['TF0', 'PJ%d' % c, 'PV'])
                mm(PS[0][:, 0:T], WRbd[:, l, c, :], xa[:, 0:T], True, True, ['PS0'], ['WRbd', 'TF0'])
                mm(PS[1][:, 0:T], WIbd[:, l, c, :], xa[:, 0:T], True, True, ['PS1'], ['WIbd', 'TF0'])
                act(gr[:, 0:T], PS[0][:, 0:T], AF.Sigmoid, ['TF1'], ['PS0', 'PV'], bias=pv(l, 26 + c))
                act(gi[:, 0:T], PS[1][:, 0:T], AF.Sigmoid, ['TF2'], ['PS1', 'PV'], bias=pv(l, 28 + c))
                act(aa[:, 0:T], gr[:, 0:T], AF.Exp, ['TF3'], ['TF1', 'PV'], scale=pv(l, 84 + c))
                act(a2t[:, 0:T], gr[:, 0:T], AF.Exp, ['TF4'], ['TF1', 'PV'], scale=pv(l, 86 + c))
                act(mu_[:, 0:T], a2t[:, 0:T], AF.Sqrt, ['TF5'], ['TF4'], scale=-1.0, bias=1.0)
                tt('dve', uu[:, 0:T], gi[:, 0:T], xa[:, 0:T], ALU.mult, ['TF6'], ['TF2', 'TF0'])
                tt('dve', uu[:, 0:T], uu[:, 0:T], mu_[:, 0:T], ALU.mult, ['TF6'], ['TF6', 'TF5'])
                scan(hh[:, 0:T], aa[:, 0:T], uu[:, 0:T], HL[:, l, c:c + 1], ALU.mult, ALU.add,
                     ['TF%d' % (7 + c)], ['TF3', 'TF6', 'HL'])
                cpy('dve', HL[:, l, c:c + 1], hh[:, T - 1:T], ['HL'], ['TF%d' % (7 + c)])
                act(TB[c][:, 0:T], hh[:, 0:T], AF.Square, ['TB%d' % c], ['TF%d' % (7 + c)])
            for c in range(2):
                mm(PS[2][:, 0:T], onesb[:], TB[c][:, 0:T], c == 0, c == 1, ['PS2'], ['onesb', 'TB%d' % c])
            act(TF[9][:, 0:T], PS[2][:, 0:T], AF.Sqrt, ['TF9'], ['PS2'], scale=1.0 / DA, bias=RMS_EPS)
            recip(TF[9][:, 0:T], TF[9][:, 0:T], ['TF9'], ['TF9'])
            for c in range(2):
                g = PJ[:, 2 + c, 3:3 + T]
                gk = 'PJ%d' % (2 + c)
                t1, t2 = TF[0], TF[1]
                act(t1[:, 0:T], g, AF.Square, ['TF0'], [gk])
                ts('dve', t1[:, 0:T], t1[:, 0:T], 0.044715, 1.0, ALU.mult, ALU.add, ['TF0'], ['TF0'])
                tt('dve', t1[:, 0:T], t1[:, 0:T], g, ALU.mult, ['TF0'], ['TF0', gk])
                act(t2[:, 0:T], t1[:, 0:T], AF.Sigmoid, ['TF1'], ['TF0'], scale=2.0 * math.sqrt(2.0 / math.pi))
                tt('dve', t2[:, 0:T], t2[:, 0:T], g, ALU.mult, ['TF1'], ['TF1', gk])
                stt(t1[:, 0:T], TF[7 + c][:, 0:T], pv(l, 32 + c), TF[9][:, 0:T], ALU.mult, ALU.mult,
                    ['TF0'], ['TF%d' % (7 + c), 'PV', 'TF9'])
                tt('dve', YM[:, c, 0:T], t1[:, 0:T], t2[:, 0:T], ALU.mult, ['YM%d' % c], ['TF0', 'TF1'])

            def groupB():
                cpy('dve', PJ[:, 0:11, 2:3], HISTB[:, l, :, :], ['PJ%d' % s for s in range(0, 11)], ['HISTB'])
                for (c0, nch_, s0) in ((512, 4, 0), (1024, 4, 4), (1536, 3, 8)):
                    wt, wkey = wload([(0, 8, nch_ * 128) + win_src(l, c0, nch_ * 128)])
                    inproj(l, T, wt, wkey, nch_ * 128, [(i, s0 + i) for i in range(nch_)])
                cpy('dve', HISTB[:, l, :, :], PJ[:, 0:11, T + 2:T + 3], ['HISTB'], ['PJ%d' % s for s in range(0, 11)])

                def mix(out, slot, mcol, w):
                    tt('dve', out, PJ[:, slot, 2:2 + T], PJ[:, slot, 3:3 + T], ALU.subtract, w, ['PJ%d' % slot])
                    stt(out, out, pv(l, 34 + mcol), PJ[:, slot, 3:3 + T], ALU.mult, ALU.add, w, w + ['PJ%d' % slot, 'PV'])
                LW, SG = TB[2], TB[3]
                mix(TF[0][:, 0:T], 9, 9, ['TF0'])
                act(LW[0:64, 0:T], TF[0][0:64, 0:T], AF.Tanh, ['TB2'], ['TF0'])
                act(LW[64:128, 0:T], TF[0][64:128, 0:T], AF.Copy, ['TB2'], ['TF0'])
                mix(TF[0][:, 0:T], 10, 10, ['TF0'])
                act(SG[:, 0:T], TF[0][:, 0:T], AF.Sigmoid, ['TB3'], ['TF0'])
                for j in range(3):
                    R, K, V, SGW, A, G, KK, RN, K2, BV, BON, CL = [TF[i] for i in range(12)]
                    kR, kK, kV, kSGW, kA, kG, kKK, kRN, kK2, kBV, kBON, kCL = ['TF%d' % i for i in range(12)]
                    Y = TF[12]
                    mix(R[:, 0:T], j, j, [kR])
                    mix(K[:, 0:T], 3 + j, 3 + j, [kK])
                    mix(V[:, 0:T], 6 + j, 6 + j, [kV])
                    jc = slice(j * 128, (j + 1) * 128)
                    mm(PS[4][:, 0:T], W2[0:64, l, jc], LW[0:64, 0:T], True, True, ['PS4'], ['W2', 'TB2'])
                    act(SGW[:, 0:T], PS[4][:, 0:T], AF.Sigmoid, [kSGW], ['PS4', 'PV'], bias=pv(l, 45 + j))
                    mm(PS[5][:, 0:T], A2[64:128, l, jc], LW[64:128, 0:T], True, True, ['PS5'], ['A2', 'TB2'])
                    act(A[:, 0:T], PS[5][:, 0:T], AF.Sigmoid, [kA], ['PS5', 'PV'], bias=pv(l, 48 + j))
                    mm(PS[6][:, 0:T], G2[:, l, jc], SG[:, 0:T], True, True, ['PS6'], ['G2', 'TB3'])
                    cpy('act', G[:, 0:T], PS[6][:, 0:T], [kG], ['PS6'])
                    yield
                    ts('dve', KK[:, 0:T], K[:, 0:T], pv(l, 51 + j), None, ALU.mult, None, [kKK], [kK, 'PV'])
                    act(TB[4][:, 0:T], KK[:, 0:T], AF.Square, ['TB4'], [kKK])
                    mm(PS[6][:, 0:T], onesbd_b[:], TB[4][:, 0:T], True, True, ['PS6'], ['onesbd_b', 'TB4'])
                    act(RN[:, 0:T], PS[6][:, 0:T], AF.Sqrt, [kRN], ['PS6'])
                    ts('dve', RN[:, 0:T], RN[:, 0:T], 1e-12, None, ALU.max, None, [kRN], [kRN])
                    recip(RN[:, 0:T], RN[:, 0:T], [kRN], [kRN])
                    tt('dve', KK[:, 0:T], KK[:, 0:T], RN[:, 0:T], ALU.mult, [kKK], [kKK, kRN])
                    yield
                    ts('dve', K2[:, 0:T], A[:, 0:T], -1.0, pv(l, 54 + j), ALU.add, ALU.mult, [kK2], [kA, 'PV'])
                    stt(K2[:, 0:T], K2[:, 0:T], 1.0, K[:, 0:T], ALU.add, ALU.mult, [kK2], [kK2, kK])
                    tt('dve', BV[:, 0:T], KK[:, 0:T], A[:, 0:T], ALU.mult, [kBV], [kKK, kA])
                    yield
                    tt('dve', RN[:, 0:T], R[:, 0:T], K2[:, 0:T], ALU.mult, [kRN], [kR, kK2])
                    ts('dve', TB[4][:, 0:T], RN[:, 0:T], pv(l, 57 + j), None, ALU.mult, None, ['TB4'], [kRN, 'PV'])
                    mm(PS[6][:, 0:T], onesbd_b[:], TB[4][:, 0:T], True, True, ['PS6'], ['onesbd_b', 'TB4'])
                    tt('dve', BON[:, 0:T], PS[6][:, 0:T], V[:, 0:T], ALU.mult, [kBON], ['PS6', kV])
                    yield
                    scan(CL[:, 0:T], cmask[:, 0:T], SGW[:, 0:T], 0.0, ALU.mult, ALU.add, [kCL], ['CT', kSGW])
                    EG, EGI, EGX = TF[13], RN, K
                    kEG, kEGI = 'TF13', kRN
                    act(EG[:, 0:T], CL[:, 0:T], AF.Exp, [kEG], [kCL], scale=-C_DEC)
                    act(EGI[:, 0:T], CL[:, 0:T], AF.Exp, [kEGI], [kCL], scale=C_DEC)
                    tt('dve', SGW[:, 0:T], CL[:, 0:T], SGW[:, 0:T], ALU.subtract, [kSGW], [kCL, kSGW])
                    act(SGW[:, 0:T], SGW[:, 0:T], AF.Exp, [kSGW], [kSGW], scale=-C_DEC)
                    v3 = lambda ap: ap.rearrange("p (n l) -> p n l", l=L)
                    tt('dve', KR[:, 0:NCH, 1, 0:L], v3(R[:, 0:T]), v3(EG[:, 0:T]), ALU.mult, ['KR'], [kR, kEG])
                    tt('dve', KR[:, 0:NCH, 0, 0:L], v3(KK[:, 0:T]), v3(SGW[:, 0:T]), ALU.mult, ['KR'], [kKK, kSGW])
                    tt('dve', KT_[:, 0:T], K2[:, 0:T], EGI[:, 0:T], ALU.mult, ['KT_'], [kK2, kEGI])
                    tt('dve', BT_[:, 0:T], BV[:, 0:T], EGI[:, 0:T], ALU.mult, ['BT_'], [kBV, kEGI])
                    cpy('act', VS_[:, 0:T], V[:, 0:T], ['VS_'], [kV])
                    Hm = HS[:, l, j, :]
                    if SMDT != F32:
                        cpy('dve', HSs[:, j, :], Hm, ['HSs'], ['HS'])
                        Hs = HSs[:, j, :]
                        hsk = 'HSs'
                    else:
                        Hs = Hm
                        hsk = 'HS'
                    for n in range(NCH):
                        cs = slice(n * L, (n + 1) * L)
                        yield
                        yield
                        for hh in range(2):
                            rw = slice(64 * hh, 64 * hh + 64)
                            pg = PS[4 + hh]
                            pk = 'PS%d' % (4 + hh)
                            mm(pg[0:L, 0:L], BT_[rw, cs], KR[rw, n, 0, 0:L], True, True, [pk], ['BT_', 'KR'])
                            mm(pg[0:L, 64:64 + L], KR[rw, n, 0, 0:L], BT_[rw, cs], True, True, [pk], ['BT_', 'KR'])
                            mm(pg[0:L, 128:128 + L], BT_[rw, cs], KR[rw, n, 1, 0:L], True, True, [pk], ['BT_', 'KR'])
                            mm(pg[0:L, 192:192 + L], KT_[rw, cs], KR[rw, n, 0, 0:L], True, True, [pk], ['KT_', 'KR'])
                            mm(pg[0:L, 256:256 + L], KT_[rw, cs], KR[rw, n, 1, 0:L], True, True, [pk], ['KT_', 'KR'])
                            tt('dve', W5[:, hh, :, :], pg[0:64, 0:320].rearrange("p (b l) -> p b l", b=5),
                               mask5[0:64, :, :], ALU.mult, ['W5'], [pk, mk5key])
                        yield
                        yield
                        for bi, (src, sk) in enumerate(((KT_, 'KT_'), (BT_, 'BT_'), (VS_, 'VS_'))):
                            tp(PTA[0:L, bi * 128:(bi + 1) * 128], src[:, cs], ident_s, [kPTA], [sk, 'CT', 'ident_st'])
                        yield
                        cpy('act', TM[:, :, :], PTA[0:64, 0:384].rearrange("p (b l) -> p b l", b=3), ['TM'], [kPTA])
                        yield
                        yield
                        for hh in range(2):
                            tt('dve', TTa[:, hh, :, :], W5[:, hh, 0:2, :],
                               ident[0:64, 0:64].rearrange("p (o l) -> p o l", o=1).to_broadcast([64, 2, 64]),
                               ALU.add, ['TTa'], ['W5', 'CT'])
                        yield
                        qm_cur, qk, qoff = W5, 'W5', 0
                        yield
                        tt_cur, tk = TTa, 'TTa'
                        yield
                        nlev = int(math.log2(L)) - 1
                        yield
                        for lev in range(nlev):
                            lastl = (lev == nlev - 1)
                            qm_nx, qnk = (QMa, 'QMa') if lev % 2 == 0 else (QMb, 'QMb')
                            tt_nx, tnk = (TTb, 'TTb') if lev % 2 == 0 else (TTa, 'TTa')
                            for hh in range(2):
                                mm(PS[6][0:L, hh * 128:hh * 128 + L], qm_cur[0:L, hh, 1, 0:L], qm_cur[0:L, hh, 0, 0:L], True, True, ['PS6'], [qk])
                                if not lastl:
                                    mm(PS[6][0:L, hh * 128 + 64:hh * 128 + 64 + L], qm_cur[0:L, hh, 0, 0:L], qm_cur[0:L, hh, 1, 0:L], True, True, ['PS6'], [qk])
                            cpy('act', qm_nx[:].rearrange("p a b l -> p (a b l)"), PS[6][0:64, 0:256], [qnk], ['PS6'])
                            for hh in range(2):
                                mm(PS[5][0:L, hh * 128:hh * 128 + L], tt_cur[0:L, hh, 1, 0:L], qm_nx[0:L, hh, 0, 0:L], True, True, ['PS5'], [tk, qnk])
                                if not lastl:
                                    mm(PS[5][0:L, hh * 128 + 64:hh * 128 + 64 + L], qm_nx[0:L, hh, 0, 0:L], tt_cur[0:L, hh, 1, 0:L], True, True, ['PS5'], [tk, qnk])
                            tt('dve', tt_nx[:].rearrange("p a b l -> p (a b l)"), PS[5][0:64, 0:256],
                               tt_cur[:].rearrange("p a b l -> p (a b l)"), ALU.add, [tnk], ['PS5', tk])
                            qm_cur, qk = qm_nx, qnk
                            tt_cur, tk = tt_nx, tnk
                            yield
                        yield
                        yield
                        for hh in range(2):
                            rw = slice(64 * hh, 64 * hh + 64)
                            fo = slice(64 * hh, 64 * hh + 64)
                            mm(PS[5][0:L, 384 + 64 * hh:448 + 64 * hh], KR[rw, n, 0, 0:L], Hs[rw, :], True, False, ['PS5'], ['KR', hsk])
                            mm(PS[5][0:L, 384 + 64 * hh:448 + 64 * hh], W5[0:L, hh, 3, 0:L], TM[0:L, 2, fo], False, True, ['PS5'], ['W5', 'TM'])
                        yield
                        act(XN[:].rearrange("p a v -> p (a v)"), PS[5][0:64, 384:512], AF.Copy, ['XN'], ['PS5'], scale=-1.0)
                        yield
                        yield
                        for hh in range(2):
                            mm(PS[5][0:L, 384 + 64 * hh:448 + 64 * hh], tt_cur[0:L, hh, 0, 0:L], XN[0:L, hh, :], True, True, ['PS5'], [tk, 'XN'])
                        yield
                        cpy('dve', UU[:].rearrange("p a v -> p (a v)"), PS[5][0:64, 384:512], ['UU'], ['PS5'])
                        yield
                        yield
                        for hh in range(2):
                            rw = slice(64 * hh, 64 * hh + 64)
                            fo = slice(64 * hh, 64 * hh + 64)
                            mm(PS[4][rw, 384:384 + L], Hs[rw, :], KR[rw, n, 1, 0:L], True, False, ['PS4'], [hsk, 'KR'])
                            mm(PS[4][rw, 384:384 + L], UU[0:L, hh, :], W5[0:L, hh, 2, 0:L], False, False, ['PS4'], ['UU', 'W5'])
                            mm(PS[4][rw, 384:384 + L], TM[0:L, 2, fo], W5[0:L, hh, 4, 0:L], False, True, ['PS4'], ['TM', 'W5'])
                        yield
                        cpy('act', Y[:, cs], PS[4][:, 384:384 + L], ['TF12'], ['PS4'])
                        yield
                        yield
                        for hh in range(2):
                            rw = slice(64 * hh, 64 * hh + 64)
                            fo = slice(64 * hh, 64 * hh + 64)
                            mm(PS[4][rw, 448:512], TM[0:L, 1, fo], UU[0:L, hh, :], True, False, ['PS4'], ['TM', 'UU'])
                            mm(PS[4][rw, 448:512], TM[0:L, 0, fo], TM[0:L, 2, fo], False, True, ['PS4'], ['TM'])
                        yield
                        gl = EG[:, n * L + L - 1: n * L + L]
                        yield
                        act(HG[:, :], Hm, AF.Identity, ['HG'], ['HS', kEG], scale=gl)
                        yield
                        stt(Hm, PS[4][:, 448:512], gl, HG[:, :], ALU.mult, ALU.add, ['HS'], ['PS4', kEG, 'HG'])
                        if SMDT != F32:
                            cpy('act', HSs[:, j, :], Hm, ['HSs'], ['HS'])

                    yield
                    mm(PS[4][:, 0:T], onesbd_f, Y[:, 0:T], True, True, ['PS4'], ['CT', 'TF12'])
                    stt(Y[:, 0:T], PS[4][:, 0:T], -1.0 / 64, Y[:, 0:T], ALU.mult, ALU.add, ['TF12'], ['PS4', 'TF12'])
                    act(CL[:, 0:T], Y[:, 0:T], AF.Square, [kCL], ['TF12'])
                    mm(PS[5][:, 0:T], onesbd_f, CL[:, 0:T], True, True, ['PS5'], ['CT', kCL])
                    act(CL[:, 0:T], PS[5][:, 0:T], AF.Sqrt, [kCL], ['PS5'], scale=1.0 / 64, bias=GN_EPS_B)
                    recip(CL[:, 0:T], CL[:, 0:T], [kCL], [kCL])
                    tt('dve', Y[:, 0:T], Y[:, 0:T], CL[:, 0:T], ALU.mult, ['TF12'], ['TF12', kCL])
                    ts('dve', Y[:, 0:T], Y[:, 0:T], pv(l, 60 + j), pv(l, 63 + j), ALU.mult, ALU.add, ['TF12'], ['TF12', 'PV'])
                    tt('dve', Y[:, 0:T], Y[:, 0:T], BON[:, 0:T], ALU.add, ['TF12'], ['TF12', kBON])
                    tt('dve', YM[:, 2 + j, 0:T], Y[:, 0:T], G[:, 0:T], ALU.mult, ['YM%d' % (2 + j)], ['TF12', kG])

                yield

            def prepC():
                cpy('dve', PJ[:, 0:3, 0:3], HISTC[:, l, :, :], ['PJ0', 'PJ1', 'PJ2'], ['HISTC'])
                for (c0, nch_, s0) in ((1920, 4, 0), (2432, 4, 4), (2944, 1, 8)):
                    wt, wkey = wload([(0, 8, nch_ * 128) + win_src(l, c0, nch_ * 128)])
                    inproj(l, T, wt, wkey, nch_ * 128, [(i, s0 + i) for i in range(nch_)])
                cpy('dve', HISTC[:, l, :, :], PJ[:, 0:3, T:T + 3], ['HISTC'], ['PJ0', 'PJ1', 'PJ2'])
                if SMDT == F32:
                    KRf = KR[:].rearrange("p n a l -> p (n a l)")
                    QB = [TF[12], TF[13], KT_]
                    KS = [BT_, VS_, KRf]
                    kQB = ['TF12', 'TF13', 'KT_']
                    kKS = ['BT_', 'VS_', 'KR']
                else:
                    QB, KS, VBt = QKVb[0:3], QKVb[3:6], QKVb[6:9]
                    kQB = ['QKV%d' % i for i in range(3)]
                    kKS = ['QKV%d' % i for i in range(3, 6)]
                for j in range(3):
                    xc = TF[0]
                    act(xc[:, 0:T], PJ[:, j, 3:3 + T], AF.Identity, ['TF0'], ['PJ%d' % j, 'PV'],
                        scale=pv(l, 66 + 3 * 3 + j), bias=pv(l, 78 + j))
                    for t_ in range(3):
                        stt(xc[:, 0:T], PJ[:, j, t_:t_ + T], pv(l, 66 + 3 * t_ + j), xc[:, 0:T], ALU.mult, ALU.add,
                            ['TF0'], ['TF0', 'PJ%d' % j, 'PV'])
                    act(TB[2][:, 0:T], xc[:, 0:T], AF.Silu, ['TB2'], ['TF0'])
                    mm(PS[0][:, 0:T], WQbd[:, l, j, :], TB[2][:, 0:T], True, True, ['PS0'], ['WQbd', 'TB2'])
                    mm(PS[1][:, 0:T], WKbd[:, l, j, :], TB[2][:, 0:T], True, True, ['PS1'], ['WKbd', 'TB2'])
                    cpy('act', QB[j][:, 0:T], PS[0][:, 0:T], [kQB[j]], ['PS0'])
                    act(KS[j][:, 0:T], PS[1][:, 0:T], AF.Copy, [kKS[j]], ['PS1'], scale=0.125)
                    if SMDT != F32:
                        cpy('dve', VBt[j][:, 0:T], PJ[:, 3 + j, 3:3 + T], ['QKV%d' % (6 + j)], ['PJ%d' % (3 + j)])

                def vsrc(j, cs_):
                    if SMDT != F32:
                        return VBt[j][:, cs_.start:cs_.stop], 'QKV%d' % (6 + j)
                    return PJ[:, 3 + j, 3 + cs_.start:3 + cs_.stop], 'PJ%d' % (3 + j)
                allT = slice(0, T)
                srcs = [(QB[j][:, 0:T], kQB[j]) for j in range(3)] + [(KS[j][:, 0:T], kKS[j]) for j in range(3)] + \
                       [vsrc(j, allT) for j in range(3)]
                for kc, (s_, sk) in enumerate(srcs):
                    mm(PS[2][0:6, 0:T], WIF[:, l, kc, 0:6], s_, kc == 0, kc == 8, ['PS2'], ['WIF', sk])
                for kc, (s_, sk) in enumerate(srcs):
                    mm(PS[3][0:6, 0:T], WIF[:, l, kc, 6:12], s_, kc == 0, kc == 8, ['PS3'], ['WIF', sk])
                IP, L1, CS, MX = [TF[i][0:6, :] for i in range(4, 8)]
                kIP, kL1, kCS, kMX = ['TF%d' % i for i in range(4, 8)]
                AH, NMX, WL, MT = [GC[i] for i in range(4)]
                kAH, kNMX, kWL, kMT = ['GC%d' % i for i in range(4)]
                act(IP[:, 0:T], PS[2][0:6, 0:T], AF.Identity, [kIP], ['PS2', 'BI'], bias=BI[:, l:l + 1])
                act(L1[:, 0:T], PS[3][0:6, 0:T], AF.Exp, [kL1], ['PS3', 'NBF'], scale=-1.0, bias=NBF[:, l:l + 1])
                act(L1[:, 0:T], L1[:, 0:T], AF.Ln, [kL1], [kL1], bias=1.0)
                scan(CS[:, 0:T], ones_c[0:6, 0:1].to_broadcast([6, T]), L1[:, 0:T], 0.0, ALU.mult, ALU.add, [kCS], ['ones_f', kL1])
                tt('dve', AH[:, 0:T], IP[:, 0:T], CS[:, 0:T], ALU.add, [kAH], [kIP, kCS])
                scan(MX[:, 0:T], ones_c[0:6, 0:1].to_broadcast([6, T]), AH[:, 0:T], MS[:, l:l + 1], ALU.mult, ALU.max, [kMX], ['ones_f', kAH, 'MS'])
                tt('dve', MT[:, 0:T], MX[:, 0:T], CS[:, 0:T], ALU.subtract, [kMT], [kMX, kCS])
                ts('dve', NMX[:, 0:T], MX[:, 0:T], -1.0, None, ALU.mult, None, [kNMX], [kMX])
                for n in range(NCH):
                    cs = slice(n * L, (n + 1) * L)
                    prev = MS[:, l:l + 1] if n == 0 else MX[:, n * L - 1:n * L]
                    ts('dve', WL[:, cs], MX[:, cs], prev, None, ALU.subtract, None, [kWL], [kMX, 'MS'])
                cpy('dve', MS[:, l:l + 1], MT[:, T - 1:T], ['MS'], [kMT])
                YC = YCp
                Cm = CA[:, l, :, :]
                if SMDT != F32:
                    cpy('dve', CAs[:], Cm, ['CAs'], ['CA'])
                    Cs, csk = CAs[:], 'CAs'
                else:
                    Cs, csk = Cm, 'CA'
                for j in range(3):
                    act(SZ[j][:, 0:T], PJ[:, 6 + j, 3:3 + T], AF.Sigmoid, ['SZ%d' % j], ['PJ%d' % (6 + j)])
                return dict(QB=QB, KS=KS, kQB=kQB, kKS=kKS, vsrc=vsrc, AH=AH, NMX=NMX, WL=WL, MT=MT, kAH=kAH, kNMX=kNMX, kWL=kWL, kMT=kMT, YC=YC, Cm=Cm, Cs=Cs, csk=csk)

            def loopC(cc):
                QB, KS, kQB, kKS, vsrc = cc['QB'], cc['KS'], cc['kQB'], cc['kKS'], cc['vsrc']
                AH, NMX, WL, MT, kAH, kNMX, kWL, kMT = cc['AH'], cc['NMX'], cc['WL'], cc['MT'], cc['kAH'], cc['kNMX'], cc['kWL'], cc['kMT']
                YC, Cm, Cs, csk = cc['YC'], cc['Cm'], cc['Cs'], cc['csk']
                for n in range(NCH):
                    cs = slice(n * L, (n + 1) * L)
                    yield
                    for j in range(3):
                        vs_, vk_ = vsrc(j, cs)
                        tp(PTB[0:L, j * 128:(j + 1) * 128], vs_, ident_s, [kPTB], [vk_, 'CT', 'ident_st'])
                    yield
                    cpy('act', VA[0:L, :, 0:64], PTB[0:L, 0:384].rearrange("p (h v) -> p h v", h=6), ['VA'], [kPTB])
                    yield
                    for j in range(3):
                        tp(PTB[0:L, j * 128:(j + 1) * 128], KS[j][:, cs], ident_s, [kPTB], [kKS[j], 'CT', 'ident_st'])
                    yield
                    cpy('dve', KTm[0:L, :, :], PTB[0:L, 0:384].rearrange("p (h v) -> p h v", h=6), ['KTm'], [kPTB])
                    yield
                    tp(PS[1][0:L, 400:406], WL[:, cs], ident[0:6, 0:6], ['PS1'], [kWL, 'CT'])
                    yield
                    tp(PS[1][0:L, 406:412], MT[:, cs], ident[0:6, 0:6], ['PS1'], [kMT, 'CT'])
                    yield
                    act(WE[0:L, :, :], PS[1][0:L, 400:412].rearrange("p (a h) -> p a h", a=2), AF.Exp, ['WE'], ['PS1'], scale=-1.0)
                    yield
                    yield
                    mm(PS[0][0:L, 0:384].rearrange("p (h l) -> p h l", h=6)[:, :, 0:L], ident[0:L, 0:L], negm6[0:L, :, 0:L],
                       True, False, ['PS0'], ['CT'])
                    yield
                    for h in range(6):
                        o_ = PS[0][0:L, h * 64:h * 64 + L]
                        mm(o_, sel6[:, h, 0:L], NMX[:, cs], False, False, ['PS0'], ['CT', kNMX])
                        mm(o_, AH[:, cs], sel6[:, h, 0:L], False, h == 5, ['PS0'], ['CT', kAH])
                    yield
                    act(DD[0:L, :, 0:L], PS[0][0:L, 0:384].rearrange("p (h l) -> p h l", h=6)[:, :, 0:L], AF.Exp, ['DD'], ['PS0'])
                    yield
                    yield
                    for h in range(6):
                        j, hh = h // 2, h % 2
                        rw = slice(64 * hh, 64 * hh + 64)
                        mm(PS[1][0:L, h * 64:h * 64 + L], KS[j][rw, cs], QB[j][rw, cs], True, True, ['PS1'], [kKS[j], kQB[j]])
                    yield
                    tt('dve', STt[0:L, :, 0:L], PS[1][0:L, 0:384].rearrange("p (h l) -> p h l", h=6)[:, :, 0:L],
                       DD[0:L, :, 0:L], ALU.mult, ['STt'], ['PS1', 'DD'])
                    yield
                    yield
                    for h in range(6):
                        j, hh = h // 2, h % 2
                        rw = slice(64 * hh, 64 * hh + 64)
                        mm(PS[2][0:L, h * 65:h * 65 + 65], STt[0:L, h, 0:L], VA[0:L, h, :], True, True, ['PS2'], ['STt', 'VA'])
                        mm(PS[3][0:L, h * 65:h * 65 + 65], QB[j][rw, cs], Cs[rw, j, :], True, True, ['PS3'], [kQB[j], csk])
                    yield
                    tt('dve', T1[0:L], PS[3][0:L, 0:390].rearrange("p (h v) -> p h v", h=6),
                       WE[0:L, 0, :].rearrange("p (h o) -> p h o", o=1).to_broadcast([L, 6, 65]), ALU.mult, ['T1'], ['PS3', 'WE'])
                    yield
                    tt('dve', NUM[0:L], PS[2][0:L, 0:390].rearrange("p (h v) -> p h v", h=6), T1[0:L], ALU.add, ['NUM'], ['PS2', 'T1'])
                    yield
                    ts('dve', DEN[0:L, :], NUM[0:L, :, 64], -1.0, None, ALU.mult, None, ['DEN'], ['NUM'])
                    yield
                    tt('dve', DEN[0:L, :], DEN[0:L, :], NUM[0:L, :, 64], ALU.max, ['DEN'], ['DEN', 'NUM'])
                    yield
                    tt('dve', DEN[0:L, :], DEN[0:L, :], WE[0:L, 1, :], ALU.max, ['DEN'], ['DEN', 'WE'])
                    yield
                    recip(DEN[0:L, :], DEN[0:L, :], ['DEN'], ['DEN'])
                    yield
                    yield
                    tt('dve', HTt[0:L], NUM[0:L, :, 0:64], DEN[0:L, :].rearrange("p (h o) -> p h o", o=1).to_broadcast([L, 6, 64]),
                       ALU.mult, ['HTt'], ['NUM', 'DEN'])
                    yield
                    red(MEAN[0:L, :], HTt[0:L], ['MEAN'], ['HTt'])
                    yield
                    stt(HC[0:L], MEAN[0:L, :].rearrange("p (h o) -> p h o", o=1).to_broadcast([L, 6, 64]), -1.0 / 64, HTt[0:L],
                        ALU.mult, ALU.add, ['HC'], ['MEAN', 'HTt'])
                    yield
                    tt('dve', SQ[0:L], HC[0:L], HC[0:L], ALU.mult, ['SQ'], ['HC'])
                    yield
                    red(VAR[0:L, :], SQ[0:L], ['VAR'], ['SQ'])
                    yield
                    yield
                    act(VAR[0:L, :], VAR[0:L, :], AF.Ln, ['VAR'], ['VAR'], scale=1.0 / 64, bias=GN_EPS_C)
                    yield
                    act(VAR[0:L, :], VAR[0:L, :], AF.Exp, ['VAR'], ['VAR'], scale=-0.5)
                    yield
                    tt('dve', HC[0:L], HC[0:L], VAR[0:L, :].rearrange("p (h o) -> p h o", o=1).to_broadcast([L, 6, 64]),
                       ALU.mult, ['HC'], ['HC', 'VAR'])
                    yield
                    yield
                    for j in range(3):
                        tp(PS[1][:, j * 64:j * 64 + L], HC[0:L, 2 * j:2 * j + 2, :].rearrange("p h v -> p (h v)"), ident[0:L, 0:L],
                           ['PS1'], ['HC', 'CT'])
                    yield
                    for j in range(3):
                        act(YC[j][:, cs], PS[1][:, j * 64:j * 64 + L], AF.Identity, ['YCp%d' % j], ['PS1', 'PV'], scale=pv(l, 81 + j))
                    yield
                    yield
                    tt('dve', VW[0:L], VA[0:L], DD[0:L, :, L - 1:L].to_broadcast([L, 6, 65]), ALU.mult, ['VW'], ['VA', 'DD'])
                    yield
                    for h in range(6):
                        j, hh = h // 2, h % 2
                        rw = slice(64 * hh, 64 * hh + 64)
                        mm(PS[0][rw, j * 65:j * 65 + 65], KTm[0:L, h, :], VW[0:L, h, :], True, True, ['PS0'], ['KTm', 'VW'])
                    yield
                    ts('dve', WLE[:, :], pairsel, WL[:, n * L + L - 1:n * L + L], None, ALU.mult, None, ['WLE'], ['CT', kWL])
                    yield
                    mm(PS[0][:, 400:403], selb, WLE[:, :], True, True, ['PS0'], ['CT', 'WLE'])
                    yield
                    act(W0[:, :], PS[0][:, 400:403], AF.Exp, ['W0'], ['PS0'], scale=-1.0)
                    yield
                    tt('dve', CT2[:], Cm, W0[:, :].rearrange("p (h o) -> p h o", o=1).to_broadcast([128, 3, 65]), ALU.mult, ['CT2'], ['CA', 'W0'])
                    yield
                    tt('dve', Cm, PS[0][:, 0:195].rearrange("p (h v) -> p h v", h=3), CT2[:], ALU.add, ['CA'], ['PS0', 'CT2'])
                    if SMDT != F32:
                        cpy('act', CAs[:], Cm, ['CAs'], ['CA'])

                    yield

            def postC():
                for j in range(3):
                    tt('dve', YM[:, 5 + j, 0:T], SZ[j][:, 0:T], YCp[j][:, 0:T], ALU.mult, ['YM%d' % (5 + j)], ['SZ%d' % j, 'YCp%d' % j])

            if OVERLAP:
                cc = prepC()
                gB, gC = groupB(), loopC(cc)
                aliveB = aliveC = True
                while aliveB or aliveC:
                    for _ in range(1):
                        if aliveB:
                            try:
                                next(gB)
                            except StopIteration:
                                aliveB = False
                    if aliveC:
                        try:
                            next(gC)
                        except StopIteration:
                            aliveC = False
            else:
                for _ in groupB():
                    pass
                cc = prepC()
                for _ in loopC(cc):
                    pass
            postC()

            for half in range(2):
                wt, wkey = wload([(0, 8, 512, WSC['w_out'].ap()[l][:, half * 512:(half + 1) * 512].rearrange("(k p) n -> p k n", p=128), 'DW_w_out%d' % l)])
                for mi in range(4):
                    m = half * 4 + mi
                    bank = mi % 4
                    for k in range(8):
                        mm(PS[bank][:, 0:T], wt[:, k * 512 + mi * 128:k * 512 + mi * 128 + 128], YM[:, k, 0:T], k == 0, k == 7,
                           ['PS%d' % bank], [wkey, 'YM%d' % k])
                    tt('dve', X[:, m, 0:T], PS[bank][:, 0:T], X[:, m, 0:T], ALU.add, ['X%d' % m], ['PS%d' % bank, 'X%d' % m])
            def o2(k):
                stt(HN[:, k, 0:T], X[:, k, 0:T], pv(l, 8 + k), RS[:, 0:T], ALU.mult, ALU.mult, ['HN%d' % k], ['X%d' % k, 'PV', 'TF13'])
            rmsnorm_to(T, 8, l, o2)
            for i in range(11):
                wf = WSC['w_ffn_in'].ap()[l]
                wt, wkey = wload([(0, 8, 256, wf[:, 256 * i:256 * i + 256].rearrange("(k p) n -> p k n", p=128), 'DW_w_ffn_in%d' % l),
                                  (2048, 8, 256, wf[:, DFF + 256 * i:DFF + 256 * i + 256].rearrange("(k p) n -> p k n", p=128), 'DW_w_ffn_in%d' % l)])
                for q in range(2):
                    f = 2 * i + q
                    bg, bu = (0, 1) if q == 0 else (2, 3)
                    for k in range(8):
                        mm(PS[bg][:, 0:T], wt[:, k * 256 + q * 128:k * 256 + q * 128 + 128], HN[:, k, 0:T], k == 0, k == 7,
                           ['PS%d' % bg], [wkey, 'HN%d' % k])
                    for k in range(8):
                        mm(PS[bu][:, 0:T], wt[:, 2048 + k * 256 + q * 128:2048 + k * 256 + q * 128 + 128], HN[:, k, 0:T], k == 0, k == 7,
                           ['PS%d' % bu], [wkey, 'HN%d' % k])
                    tk_ = TF[q]
                    act(tk_[:, 0:T], PS[bg][:, 0:T], AF.Silu, ['TF%d' % q], ['PS%d' % bg])
                    tt('dve', ACTT[:, f, 0:T], tk_[:, 0:T], PS[bu][:, 0:T], ALU.mult, actk(f), ['TF%d' % q, 'PS%d' % bu])
            wo = WSC['w_ffn_out'].ap()[l]
            for half in range(2):
                for (k0, kc_) in ((0, 8), (8, 8), (16, 6)):
                    wt, wkey = wload([(0, kc_, 512, wo[k0 * 128:(k0 + kc_) * 128, half * 512:(half + 1) * 512].rearrange("(k p) n -> p k n", p=128),
                                       'DW_w_ffn_out%d' % l)])
                    for mi in range(4):
                        for kk_ in range(kc_):
                            k = k0 + kk_
                            mm(PS[mi][:, 0:T], wt[:, kk_ * 512 + mi * 128:kk_ * 512 + mi * 128 + 128], ACTT[:, k, 0:T], k == 0, k == 21,
                               ['PS%d' % mi], [wkey] + actk(k))
                for mi in range(4):
                    m = half * 4 + mi
                    tt('dve', X[:, m, 0:T], PS[mi][:, 0:T], X[:, m, 0:T], ALU.add, ['X%d' % m], ['PS%d' % mi, 'X%d' % m])
            if last_layer:
                def o3(k):
                    stt(TF[k][:, 0:T], X[:, k, 0:T], NF[:, k:k + 1], RS[:, 0:T], ALU.mult, ALU.mult, ['TF%d' % k], ['X%d' % k, 'NF', 'TF13'])
                rmsnorm_to(T, 0, l, o3)
                nblk = (T + 127) // 128
                for tb in range(nblk):
                    n = min(128, T - tb * 128)
                    for half in range(2):
                        xt, xk = STG[half]
                        bank = 4 + half
                        for mmi in range(4):
                            m = half * 4 + mmi
                            tp(PS[bank][0:n, mmi * 128:(mmi + 1) * 128], TF[m][:, tb * 128:tb * 128 + n], ident,
                               ['PS%d' % bank], ['TF%d' % m, 'CT'])
                        cpy('act' if half else 'dve', xt[0:n, 0:512], PS[bank][0:n, 0:512], [xk], ['PS%d' % bank])
                        dma('sp', yout[tok0 + tb * 128:tok0 + tb * 128 + n, half * 512:(half + 1) * 512], xt[0:n, 0:512], r=[xk], is_output=True)

        for job in jobs:
            if job == 'p':
                T, L, nseg, xin, yout = TS, 64, TP // TS, I['xp'], O['yp']
                for t_, k_ in ((HISTA, 'HISTA'), (HISTB, 'HISTB'), (HISTC, 'HISTC'), (HL, 'HL'), (HS, 'HS'), (CA, 'CA'), (MS, 'MS')):
                    memset('dve', t_[:], 0.0, [k_])
            else:
                T, L, nseg, xin, yout = TSMP, 32, 1, I['xs'], O['ys']
                for l in range(DEPTH):
                    for j_ in range(3):
                        dma('sp', HISTA[:, l, :, j_], I['s_conv_a'][l, j_].rearrange("(c p) -> p c", p=128), w=['HISTA'], slow=True)
                        dma('sp', HISTC[:, l, :, j_], I['s_conv_c'][l, j_].rearrange("(c p) -> p c", p=128), w=['HISTC'], slow=True)
                    dma('sp', HISTB[:, l, :, 0], I['s_shift'][l].rearrange("(c p) -> p c", p=128), w=['HISTB'], slow=True)
                    dma('sp', HL[:, l, :], I['s_lru'][l].rearrange("(c p) -> p c", p=128), w=['HL'], slow=True)
                    dma('sp', CA[:, l, :, 0:64], I['s_mem_c'][l].rearrange("(jp hh) n v -> (hh n) jp v", hh=2), w=['CA'])
                    dma('sp', CA[:, l, :, 64], I['s_mem_n'][l].rearrange("(jp hh) n -> (hh n) jp", hh=2), w=['CA'], slow=True)
                    dma('sp', WS[:], I['s_wkv'][l].rearrange("h v k -> v h k"), w=['WS'])
                    for j in range(3):
                        tp(PS[4][:, j * 64:j * 64 + 64], WS[:, 2 * j:2 * j + 2, :].rearrange("p h k -> p (h k)"), ident[0:64, 0:64],
                           ['PS4'], ['WS', 'CT'])
                    cpy('dve', HS[:, l, :, :], PS[4][:, 0:192].rearrange("p (j v) -> p j v", j=3), ['HS'], ['PS4'])
                dma('sp', MS[:], I['s_mem_m'].rearrange("l h -> h l"), w=['MS'], slow=True)
            for seg in range(nseg):
                for l in range(DEPTH):
                    task(job, l, T, L, seg * T, seg == 0, seg == nseg - 1, xin, yout, l == DEPTH - 1)
            pre = job + '_'
            for l in range(DEPTH):
                for j_ in range(3):
                    dma('sp', O[pre + 'conv_a'][l, j_].rearrange("(c p) -> p c", p=128), HISTA[:, l, :, j_], r=['HISTA'], slow=True, is_output=True)
                    dma('sp', O[pre + 'conv_c'][l, j_].rearrange("(c p) -> p c", p=128), HISTC[:, l, :, j_], r=['HISTC'], slow=True, is_output=True)
                dma('sp', O[pre + 'shift'][l].rearrange("(c p) -> p c", p=128), HISTB[:, l, :, 0], r=['HISTB'], slow=True, is_output=True)
                dma('sp', O[pre + 'lru'][l].rearrange("(c p) -> p c", p=128), HL[:, l, :], r=['HL'], slow=True, is_output=True)
                dma('sp', O[pre + 'mem_c'][l].rearrange("(jp hh) n v -> (hh n) jp v", hh=2), CA[:, l, :, 0:64], r=['CA'], is_output=True)
                dma('sp', O[pre + 'mem_n'][l].rearrange("(jp hh) n -> (hh n) jp", hh=2), CA[:, l, :, 64], r=['CA'], slow=True, is_output=True)
                for j in range(3):
                    tp(PS[4][0:64, j * 128:(j + 1) * 128], HS[:, l, j, :], ident, ['PS4'], ['HS', 'CT'])
                cpy('dve', WS[:], PS[4][0:64, 0:384].rearrange("p (h k) -> p h k", h=6), ['WS'], ['PS4'])
                dma('sp', O[pre + 'wkv'][l].rearrange("h v k -> v h k"), WS[:], r=['WS'], is_output=True)
            dma('sp', O[pre + 'mem_m'].rearrange("l h -> h l"), MS[:], r=['MS'], slow=True, is_output=True)
        P.finish()
        P.emit(nc, st)
    return nc, cnp


_CACHE = {}
TS_DEFAULT = 512
SMALL_DT = BF16


def kernel(**inputs):
    inputs = {k: np.asarray(v) for k, v in inputs.items()}
    xp = inputs['x_prompt']
    xs = inputs['x_sample']
    DEPTH = inputs['norm1'].shape[0]
    B, TP, _ = xp.shape
    TS = min(TS_DEFAULT, TP)
    key = (DEPTH, TP, TS)
    if key not in _CACHE:
        _CACHE[key] = build(DEPTH, TP, TS, SMDT=SMALL_DT)
    nc, cnp = _CACHE[key]
    f32 = lambda a: np.ascontiguousarray(a, dtype=np.float32)
    in_maps = []
    for c in range(NCORES):
        m = {'xp': f32(xp[c % B]), 'xs': f32(xs[c]), 'consts': cnp}
        m['s_conv_a'] = f32(inputs['state_conv_a'][:, c])
        m['s_lru'] = f32(inputs['state_lru'][:, c])
        m['s_shift'] = f32(inputs['state_shift_b'][:, c, 0])
        m['s_wkv'] = f32(inputs['state_wkv'][:, c])
        m['s_conv_c'] = f32(inputs['state_conv_c'][:, c])
        m['s_mem_c'] = f32(inputs['state_mem_c'][:, c])
        m['s_mem_n'] = f32(inputs['state_mem_n'][:, c])
        m['s_mem_m'] = f32(inputs['state_mem_m'][:, c])
        for k in WNAMES:
            m[k] = f32(inputs[k])
        in_maps.append(m)
    res = run_bass_kernel_spmd(nc, in_maps, core_ids=list(range(NCORES)))
    R = res.results
    y_prompt = np.stack([R[b]['yp'] for b in range(B)], 0)
    y_sample = np.stack([R[c]['ys'] for c in range(NCORES)], 0)
    outs = [y_prompt, y_sample]
    names = ['conv_a', 'lru', 'shift', 'wkv', 'conv_c', 'mem_c', 'mem_n', 'mem_m']
    for jb, n in (('p', B), ('s', NCORES)):
        for nm in names:
            a = np.stack([R[c]['o' + jb + '_' + nm] for c in range(n)], 1)
            if nm == 'shift':
                a = a[:, :, None, :]
            outs.append(np.ascontiguousarray(a.astype(np.float32)))
    return tuple(outs)
```

```python
import math
from contextlib import ExitStack
import numpy as np
import concourse.bass as bass
import concourse.mybir as mybir
from concourse.bass_utils import run_bass_kernel_spmd
from concourse.alu_op_type import AluOpType as ALU

AF = mybir.ActivationFunctionType
F32 = mybir.dt.float32
BF16 = mybir.dt.bfloat16
AX = mybir.AxisListType

EPOCH = 16000
NDMA_SLOTS = 10
SAME_ENGINE_SYNC = True
NOSYNC_ENGINES = ('pe',)

D = 1024
DIN = 3072
DA = 256
DB = 384
DBIN = 1408
DC = 384
DFF = 2816
NCORES = 8
TSMP = 32
C_DEC = math.exp(-0.5)
RMS_EPS = 1e-6
GN_EPS_B = 64e-5
GN_EPS_C = 1e-6


class Prog:
    ENG = ['pe', 'act', 'dve', 'pool', 'sp']

    def __init__(self):
        self.stream = {e: [] for e in self.ENG}
        self.n = {e: 0 for e in self.ENG}
        self.lastw = {}
        self.readers = {}
        self.seen = {e: {} for e in self.ENG}
        self.dma_slot_next = {e: 0 for e in self.ENG}
        self.dma_slot_val = {}
        self.out_tokens = []
        self.pe_rg = {}

    def _deps(self, eng, reads, writes, extra=(), force=()):
        toks = list(extra) + list(force)
        forced_src = set(t[0] for t in force)
        for k in reads:
            t = self.lastw.get(k)
            if t:
                toks.append(t)
        for k in writes:
            t = self.lastw.get(k)
            if t:
                toks.append(t)
            toks.extend(self.readers.get(k, {}).values())
        need = {}
        for (src, val) in toks:
            if need.get(src, 0) < val:
                need[src] = val
        out = []
        for src, val in need.items():
            if src == ('e', eng) and (not SAME_ENGINE_SYNC or eng in NOSYNC_ENGINES) and src not in forced_src:
                continue
            if self.seen[eng].get(src, 0) >= val:
                continue
            self.seen[eng][src] = val
            out.append((src, val))
        return out

    def op(self, eng, fn, w=(), r=(), rg=None):
        w = list(w) + [k for k in r if k.startswith('PS') and k[2:].isdigit()]
        force = []
        if eng == 'pe':
            for k in w:
                if k.startswith('PS'):
                    prev = self.pe_rg.get(k)
                    if prev is not None and prev[0] != rg and self.lastw.get(k) == prev[1]:
                        force.append(prev[1])
        waits = self._deps(eng, r, w, force=force)
        self.n[eng] += 1
        tok = (('e', eng), self.n[eng])
        for k in w:
            self.lastw[k] = tok
            self.readers[k] = {}
        for k in r:
            self.readers.setdefault(k, {})[('e', eng)] = tok
        self.stream[eng].append((waits, fn, tok))
        if eng == 'pe':
            for k in w:
                if k.startswith('PS'):
                    self.pe_rg[k] = (rg, tok)
        return tok

    def dma(self, q, fn, w=(), r=(), is_output=False):
        slot = (q, self.dma_slot_next[q] % NDMA_SLOTS)
        self.dma_slot_next[q] += 1
        src = ('d', slot)
        prev = self.dma_slot_val.get(slot, 0)
        extra = [(src, prev)] if prev else []
        waits = self._deps(q, r, w, extra)
        val = prev + 16
        self.dma_slot_val[slot] = val
        tok = (src, val)
        for k in w:
            self.lastw[k] = tok
            self.readers[k] = {}
        for k in r:
            self.readers.setdefault(k, {})[src] = tok
        self.stream[q].append((waits, fn, tok))
        if is_output:
            self.out_tokens.append(tok)
        return tok

    def finish(self):
        need = {}
        for (src, val) in self.out_tokens:
            need[src] = max(need.get(src, 0), val)
        self.stream['sp'].append((list(need.items()), None, None))

    def emit(self, nc, stack):
        sems = {}

        def getsem(src, val):
            if src[0] == 'e':
                ep = (val - 1) // EPOCH
                key = (src, ep)
                v = val - ep * EPOCH
            else:
                key = (src, 0)
                v = val
            if key not in sems:
                sems[key] = stack.enter_context(nc.semaphore("s%d" % len(sems)))
            return sems[key], v

        for e in self.ENG:
            for (waits, fn, tok) in self.stream[e]:
                for (src, val) in waits:
                    getsem(src, val)
                if tok is not None:
                    getsem(*tok)
        block = stack.enter_context(nc.Block())
        names = {'pe': 'tensor', 'act': 'scalar', 'dve': 'vector', 'pool': 'gpsimd', 'sp': 'sync'}
        for e in self.ENG:
            items = self.stream[e]
            if not items:
                continue

            def body(engh, items=items):
                for (waits, fn, tok) in items:
                    for (src, val) in waits:
                        s, v = getsem(src, val)
                        engh.wait_ge(s, v)
                    if fn is None:
                        continue
                    ins = fn(engh)
                    s, v = getsem(*tok)
                    ins.then_inc(s, 16 if tok[0][0] == 'd' else 1)
            getattr(block, names[e])(body)
        self.nsems = len(sems)


def _consts(TS):
    cw = {}
    cols = []

    def add(name, arr):
        a = np.zeros((128, arr.shape[1]), np.float32)
        a[:arr.shape[0]] = arr
        cw[name] = (sum(c.shape[1] for c in cols), arr.shape[1])
        cols.append(a)
    add('ident', np.eye(128, dtype=np.float32))
    bd = np.zeros((128, 128), np.float32)
    bd[:64, :64] = 1
    bd[64:, 64:] = 1
    add('onesbd', bd)
    add('ident2', np.concatenate([np.eye(64, dtype=np.float32)] * 2, 0))
    r = np.arange(128)[:, None] % 64
    c = np.arange(64)[None, :]
    su = (c > r).astype(np.float32)
    sl = (c < r).astype(np.float32)
    ui = (c >= r).astype(np.float32)
    add('mask5', np.concatenate([-su, -sl, ui, su, ui], 1))
    negm = np.where(np.arange(64)[:, None] > c, -30000.0, 0.0).astype(np.float32)
    add('negm6', np.tile(negm, (1, 6)))
    cm = np.ones((128, TS), np.float32)
    cm[:, ::64] = 0
    add('cmask', cm)
    sel6 = np.zeros((6, 6, 64), np.float32)
    for h in range(6):
        sel6[h, h, :] = 1
    add('sel6', sel6.reshape(6, 384))
    selb = np.zeros((6, 128), np.float32)
    for k in range(6):
        selb[k, (k % 2) * 64:(k % 2) * 64 + 64] = 1
    add('selb', selb)
    ps_ = np.zeros((6, 3), np.float32)
    for k in range(6):
        ps_[k, k // 2] = 1
    add('pairsel', ps_)
    return np.concatenate(cols, 1), cw


WNAMES = ['norm1', 'w_in', 'conv_a_w', 'conv_a_b', 'lru_wr', 'lru_br', 'lru_wi', 'lru_bi', 'lru_lambda',
          'norm_a', 'rwkv_mu', 'rwkv_w0', 'rwkv_w2', 'rwkv_a0', 'rwkv_a2', 'rwkv_g2', 'rwkv_kk', 'rwkv_ka',
          'rwkv_rk', 'rwkv_lnw', 'rwkv_lnb', 'conv_c_w', 'conv_c_b', 'mlstm_wq', 'mlstm_wk', 'mlstm_wif',
          'mlstm_bif', 'mlstm_gn', 'w_out', 'norm2', 'w_ffn_in', 'w_ffn_out', 'norm_f']


def build(DEPTH, TP, TS, SMDT=F32, jobs=('p', 's')):
    assert TP % TS == 0 and TS % 128 == 0
    nc = bass.Bass("TRN2", target_bir_lowering=False)
    cnp, cw = _consts(TS)
    CW = cnp.shape[1]

    def din(name, shape):
        return nc.dram_tensor(name, list(shape), F32, kind="ExternalInput").ap()

    def dout(name, shape):
        return nc.dram_tensor(name, list(shape), F32, kind="ExternalOutput").ap()

    I = {}
    I['xp'] = din('xp', [TP, D])
    I['xs'] = din('xs', [TSMP, D])
    I['consts'] = din('consts', [128, CW])
    st_shapes = {'conv_a': [DEPTH, 3, DA], 'lru': [DEPTH, DA], 'shift': [DEPTH, DBIN], 'wkv': [DEPTH, 6, 64, 64],
                 'conv_c': [DEPTH, 3, DC], 'mem_c': [DEPTH, 6, 64, 64], 'mem_n': [DEPTH, 6, 64], 'mem_m': [DEPTH, 6]}
    for k, s in st_shapes.items():
        I['s_' + k] = din('s_' + k, s)
    wshapes = {'norm1': [DEPTH, D], 'w_in': [DEPTH, D, DIN], 'conv_a_w': [DEPTH, 4, DA], 'conv_a_b': [DEPTH, DA],
               'lru_wr': [DEPTH, 4, 64, 64], 'lru_br': [DEPTH, DA], 'lru_wi': [DEPTH, 4, 64, 64], 'lru_bi': [DEPTH, DA],
               'lru_lambda': [DEPTH, DA], 'norm_a': [DEPTH, DA], 'rwkv_mu': [DEPTH, DBIN], 'rwkv_w0': [DEPTH, DB],
               'rwkv_w2': [DEPTH, 64, DB], 'rwkv_a0': [DEPTH, DB], 'rwkv_a2': [DEPTH, 64, DB], 'rwkv_g2': [DEPTH, 128, DB],
               'rwkv_kk': [DEPTH, DB], 'rwkv_ka': [DEPTH, DB], 'rwkv_rk': [DEPTH, 6, 64], 'rwkv_lnw': [DEPTH, DB],
               'rwkv_lnb': [DEPTH, DB], 'conv_c_w': [DEPTH, 4, DC], 'conv_c_b': [DEPTH, DC], 'mlstm_wq': [DEPTH, 6, 64, 64],
               'mlstm_wk': [DEPTH, 6, 64, 64], 'mlstm_wif': [DEPTH, 3 * DC, 12], 'mlstm_bif': [DEPTH, 12],
               'mlstm_gn': [DEPTH, DC], 'w_out': [DEPTH, D, D], 'norm2': [DEPTH, D], 'w_ffn_in': [DEPTH, D, 2 * DFF],
               'w_ffn_out': [DEPTH, DFF, D], 'norm_f': [D]}
    for k in WNAMES:
        I[k] = din(k, wshapes[k])
    O = {}
    O['yp'] = dout('yp', [TP, D])
    O['ys'] = dout('ys', [TSMP, D])
    for jb in ('p', 's'):
        for k, s in st_shapes.items():
            O[jb + '_' + k] = dout('o' + jb + '_' + k, s)

    WSC = {'w_in': nc.dram_tensor('w_in_b', [DEPTH, D, DIN], BF16), 'w_out': nc.dram_tensor('w_out_b', [DEPTH, D, D], BF16),
           'w_ffn_in': nc.dram_tensor('w_ffn_in_b', [DEPTH, D, 2 * DFF], BF16), 'w_ffn_out': nc.dram_tensor('w_ffn_out_b', [DEPTH, DFF, D], BF16)}
    P = Prog()
    with ExitStack() as st:
        def sb(name, shape, dt=F32):
            return st.enter_context(nc.sbuf_tensor(name, list(shape), dt))

        def psum(name, shape, dt=F32):
            return st.enter_context(nc.psum_tensor(name, list(shape), dt))

        def tt(eng, out, in0, in1, op, w, r):
            P.op(eng, lambda e: e.tensor_tensor(out=out, in0=in0, in1=in1, op=op), w, r)

        def ts(eng, out, in0, s1, s2, op0, op1, w, r):
            if op1 is None:
                P.op(eng, lambda e: e.tensor_scalar(out=out, in0=in0, scalar1=s1, scalar2=None, op0=op0), w, r)
            else:
                P.op(eng, lambda e: e.tensor_scalar(out=out, in0=in0, scalar1=s1, scalar2=s2, op0=op0, op1=op1), w, r)

        def stt(out, in0, scalar, in1, op0, op1, w, r):
            P.op('dve', lambda e: e.scalar_tensor_tensor(out=out, in0=in0, scalar=scalar, in1=in1, op0=op0, op1=op1), w, r)

        def act(out, in_, func, w, r, scale=1.0, bias=0.0):
            P.op('act', lambda e: e.activation(out=out, in_=in_, func=func, scale=scale, bias=bias), w, r)

        def cpy(eng, out, in_, w, r):
            if eng == 'act':
                P.op('act', lambda e: e.activation(out=out, in_=in_, func=AF.Copy), w, r)
            else:
                P.op(eng, lambda e: e.tensor_copy(out=out, in_=in_), w, r)

        def _rg(ap):
            n = ap.partition_size()
            return (ap.base_partition(), 32 if n <= 32 else (64 if n <= 64 else 128))

        def mm(out, lhsT, rhs, start, stop, w, r):
            P.op('pe', lambda e: e.matmul(out, lhsT=lhsT, rhs=rhs, start=start, stop=stop), w, r, rg=_rg(lhsT))

        def tp(out, in_, ident, w, r):
            P.op('pe', lambda e: e.transpose(out, in_, ident), w, r, rg=_rg(in_))

        def recip(out, in_, w, r):
            P.op('dve', lambda e: e.reciprocal(out=out, in_=in_), w, r)

        def scan(out, d0, d1, init, op0, op1, w, r):
            P.op('dve', lambda e: e.tensor_tensor_scan(out=out, data0=d0, data1=d1, initial=init, op0=op0, op1=op1), w, r)

        def red(out, in_, w, r):
            P.op('dve', lambda e: e.tensor_reduce(out=out, in_=in_, axis=AX.X, op=ALU.add), w, r)

        def memset(eng, ap, val, w):
            P.op(eng, lambda e: e.memset(ap, val), w, ())

        def dma(q, out, in_, w=(), r=(), slow=False, is_output=False):
            if slow:
                P.dma(q, lambda e: e.dma_start(out=out, in_=in_, allow_slow_non_contiguous=True), w, r, is_output)
            else:
                P.dma(q, lambda e: e.dma_start(out=out, in_=in_), w, r, is_output)

        CT = sb('CT', [128, CW])
        dma('sp', CT[:], I['consts'], w=['CT'])

        def cst(name, rows=128):
            o, n = cw[name]
            return CT[0:rows, o:o + n]
        ident = cst('ident')
        ident2 = cst('ident2')
        onesbd_f = cst('onesbd')
        cmask = cst('cmask')
        onesb = sb('onesb', [128, 128], BF16)
        memset('dve', onesb[:], 1.0, ['onesb'])
        onesbd_b = sb('onesbd_b', [128, 128], BF16)
        cpy('dve', onesbd_b[:], onesbd_f, ['onesbd_b'], ['CT'])
        ones_c = sb('ones_c', [128, 1])
        memset('dve', ones_c[:], 1.0, ['ones_f'])
        if SMDT == F32:
            mask5 = cst('mask5').rearrange("p (b l) -> p b l", b=5)
            ident_s = ident
            mk5key = 'CT'
        else:
            mask5t = sb('mask5t', [128, 5, 64], SMDT)
            cpy('dve', mask5t[:], cst('mask5').rearrange("p (b l) -> p b l", b=5), ['mask5t'], ['CT'])
            mask5 = mask5t[:]
            ident_st = sb('ident_st', [128, 128], SMDT)
            cpy('dve', ident_st[:], ident, ['ident_st'], ['CT'])
            ident_s = ident_st[:]
            mk5key = 'mask5t'
        negm6 = cst('negm6', 64).rearrange("p (h l) -> p h l", h=6)
        sel6 = cst('sel6', 6).rearrange("p (h l) -> p h l", h=6)
        selb = cst('selb', 6)
        pairsel = cst('pairsel', 6)

        NV = 88
        PV = sb('PV', [128, DEPTH, NV])
        NF = sb('NF', [128, 8])
        BI = sb('BI', [6, DEPTH])
        NBF = sb('NBF', [6, DEPTH])

        def pvload(name, col, n):
            for l in range(DEPTH):
                dma('sp', PV[:, l, col:col + n], I[name][l].rearrange("(c p) -> p c", p=128), w=['PV'], slow=True)
        pvload('norm1', 0, 8)
        pvload('norm2', 8, 16 - 8)
        for l in range(DEPTH):
            for j in range(4):
                dma('sp', PV[:, l, 16 + 2 * j:18 + 2 * j], I['conv_a_w'][l, j].rearrange("(c p) -> p c", p=128), w=['PV'], slow=True)
                dma('sp', PV[:, l, 66 + 3 * j:69 + 3 * j], I['conv_c_w'][l, j].rearrange("(c p) -> p c", p=128), w=['PV'], slow=True)
        pvload('conv_a_b', 24, 2)
        pvload('lru_br', 26, 2)
        pvload('lru_bi', 28, 2)
        pvload('lru_lambda', 30, 2)
        pvload('norm_a', 32, 2)
        pvload('rwkv_mu', 34, 11)
        pvload('rwkv_w0', 45, 3)
        pvload('rwkv_a0', 48, 3)
        pvload('rwkv_kk', 51, 3)
        pvload('rwkv_ka', 54, 3)
        for l in range(DEPTH):
            dma('sp', PV[:, l, 57:60], I['rwkv_rk'][l].rearrange("(c hh) n -> (hh n) c", hh=2), w=['PV'], slow=True)
        pvload('rwkv_lnw', 60, 3)
        pvload('rwkv_lnb', 63, 3)
        pvload('conv_c_b', 78, 3)
        pvload('mlstm_gn', 81, 3)
        dma('sp', NF[:], I['norm_f'].rearrange("(c p) -> p c", p=128), w=['NF'], slow=True)
        dma('sp', BI[:], I['mlstm_bif'][:, 0:6].rearrange("l h -> h l"), w=['BI'], slow=True)
        dma('sp', NBF[:], I['mlstm_bif'][:, 6:12].rearrange("l h -> h l"), w=['NBF'], slow=True)
        ts('dve', NBF[:], NBF[:], -1.0, None, ALU.mult, None, ['NBF'], ['NBF'])
        SPT = sb('SPT', [128, DEPTH, 2])
        act(SPT[:], PV[:, :, 30:32], AF.Exp, ['SPT'], ['PV'], scale=-1.0)
        act(SPT[:], SPT[:], AF.Ln, ['SPT'], ['SPT'], bias=1.0)
        ts('dve', PV[:, :, 84:86], SPT[:], -8.0, None, ALU.mult, None, ['PV'], ['SPT'])
        ts('dve', PV[:, :, 86:88], SPT[:], -16.0, None, ALU.mult, None, ['PV'], ['SPT'])

        WRbd = sb('WRbd', [128, DEPTH, 2, 128])
        WIbd = sb('WIbd', [128, DEPTH, 2, 128])
        memset('dve', WRbd[:], 0.0, ['WRbd'])
        memset('dve', WIbd[:], 0.0, ['WIbd'])
        WQbd = sb('WQbd', [128, DEPTH, 3, 128], BF16)
        WKbd = sb('WKbd', [128, DEPTH, 3, 128], BF16)
        memset('dve', WQbd[:], 0.0, ['WQbd'])
        memset('dve', WKbd[:], 0.0, ['WKbd'])
        W2 = sb('W2', [128, DEPTH, DB], BF16)
        A2 = sb('A2', [128, DEPTH, DB], BF16)
        G2 = sb('G2', [128, DEPTH, DB], BF16)
        WIF = sb('WIF', [128, DEPTH, 9, 12], SMDT)
        for l in range(DEPTH):
            for n in range(4):
                hb, c = n % 2, n // 2
                dma('sp', WRbd[64 * hb:64 * hb + 64, l, c, 64 * hb:64 * hb + 64], I['lru_wr'][l, n], w=['WRbd'])
                dma('sp', WIbd[64 * hb:64 * hb + 64, l, c, 64 * hb:64 * hb + 64], I['lru_wi'][l, n], w=['WIbd'])
            for h in range(6):
                hb, c = h % 2, h // 2
                dma('pool', WQbd[64 * hb:64 * hb + 64, l, c, 64 * hb:64 * hb + 64], I['mlstm_wq'][l, h], w=['WQbd'])
                dma('pool', WKbd[64 * hb:64 * hb + 64, l, c, 64 * hb:64 * hb + 64], I['mlstm_wk'][l, h], w=['WKbd'])
            dma('pool', W2[0:64, l, :], I['rwkv_w2'][l], w=['W2'])
            dma('pool', A2[64:128, l, :], I['rwkv_a2'][l], w=['A2'])
            dma('pool', G2[:, l, :], I['rwkv_g2'][l], w=['G2'])
            dma('pool' if SMDT != F32 else 'sp', WIF[:, l, :, :], I['mlstm_wif'][l].rearrange("(kc p) n -> p kc n", p=128), w=['WIF'])
        ts('dve', WIF[:, :, 3:6, :], WIF[:, :, 3:6, :], 8.0, None, ALU.mult, None, ['WIF'], ['WIF'])

        HISTA = sb('HISTA', [128, DEPTH, 2, 3])
        HISTB = sb('HISTB', [128, DEPTH, 11, 1])
        HISTC = sb('HISTC', [128, DEPTH, 3, 3])
        HL = sb('HL', [128, DEPTH, 2])
        HS = sb('HS', [128, DEPTH, 3, 64])
        CA = sb('CA', [128, DEPTH, 3, 65])
        MS = sb('MS', [6, DEPTH])
        WS = sb('WS', [64, 6, 64])

        X = sb('X', [128, 8, TS])
        HN = sb('HN', [128, 8, TS], BF16)
        NSLOT = 11
        PJ = sb('PJ', [128, NSLOT, 3 + TS])
        YM = sb('YM', [128, 8, TS], BF16)
        assert NSLOT * (3 + TS) * 4 >= 22 * TS * 2
        ACTT = PJ[:].rearrange("p s c -> p (s c)").bitcast(BF16)[:, 0:22 * TS].rearrange("p (f t) -> p f t", f=22)
        def actk(f):
            b0, b1 = f * TS * 2, (f + 1) * TS * 2 - 1
            sl_ = (3 + TS) * 4
            return ['PJ%d' % s_ for s_ in range(b0 // sl_, b1 // sl_ + 1)]
        NWB = 2
        WB = [sb('WB%d' % i, [128, 4096], BF16) for i in range(NWB)]
        NTF = 14
        TF = [sb('TF%d' % i, [128, TS]) for i in range(NTF)]
        NTB = 5
        TB = [sb('TB%d' % i, [128, TS], BF16) for i in range(NTB)]
        KR = sb('KR', [128, TS // 64, 2, 64], SMDT)
        KT_ = sb('KT_', [128, TS], SMDT)
        BT_ = sb('BT_', [128, TS], SMDT)
        VS_ = sb('VS_', [128, TS], SMDT)
        W5 = sb('W5', [64, 2, 5, 64], SMDT)
        QMa = sb('QMa', [64, 2, 2, 64], SMDT)
        QMb = sb('QMb', [64, 2, 2, 64], SMDT)
        TTa = sb('TTa', [64, 2, 2, 64], SMDT)
        TTb = sb('TTb', [64, 2, 2, 64], SMDT)
        TM = sb('TM', [64, 3, 128], SMDT)
        XN = sb('XN', [64, 2, 64], SMDT)
        UU = sb('UU', [64, 2, 64], SMDT)
        HG = sb('HG', [128, 64])
        HSs = sb('HSs', [128, 3, 64], SMDT) if SMDT != F32 else None
        CAs = sb('CAs', [128, 3, 65], SMDT) if SMDT != F32 else None
        QKVb = [sb('QKV%d' % i, [128, TS], SMDT) for i in range(9)] if SMDT != F32 else None
        VA = sb('VA', [64, 6, 65], SMDT)
        KTm = sb('KTm', [64, 6, 64], SMDT)
        WE = sb('WE', [64, 2, 6])
        DD = sb('DD', [64, 6, 64])
        STt = sb('STt', [64, 6, 64], SMDT)
        T1 = sb('T1', [64, 6, 65])
        NUM = sb('NUM', [64, 6, 65])
        DEN = sb('DEN', [64, 6])
        HTt = sb('HTt', [64, 6, 64])
        HC = sb('HC', [64, 6, 64])
        SQ = sb('SQ', [64, 6, 64])
        MEAN = sb('MEAN', [64, 6])
        VAR = sb('VAR', [64, 6])
        VW = sb('VW', [64, 6, 65], SMDT)
        WLE = sb('WLE', [6, 3])
        W0 = sb('W0', [128, 3])
        CT2 = sb('CT2', [128, 3, 65])
        RS = TF[13]
        GC = [sb('GC%d' % i, [6, TS]) for i in range(4)]
        YCt = sb('YCt', [128, 3, 64])
        SZ = [sb('SZ%d' % i, [128, TS], BF16) for i in range(3)]
        OVERLAP = (SMDT != F32)
        if TS >= 512:
            STG = [(TF[11], 'TF11'), (TF[12], 'TF12')]
        else:
            STG = [(sb('STG0', [128, 512]), 'STG0'), (sb('STG1', [128, 512]), 'STG1')]
        if SMDT == F32:
            PS = [psum('PS%d' % i, [128, 512]) for i in range(8)]
            PTA, kPTA, PTB, kPTB = PS[7], 'PS7', PS[4], 'PS4'
        else:
            PS = [psum('PS%d' % i, [128, 512]) for i in range(7)]
            PSB = psum('PSB', [128, 1024], SMDT)
            PTA, kPTA, PTB, kPTB = PSB[:, 0:512], 'PS7', PSB[:, 512:1024], 'PS7'

        memset('dve', VA[:], 1.0, ['VA'])
        class Slot:
            pass

        def mkslot(si):
            S = Slot()
            sfx = '' if si == 0 else 'b'
            S.k = lambda nm: nm + sfx
            if si == 0:
                S.W5, S.QMa, S.QMb, S.TTa, S.TTb, S.TM, S.XN, S.UU, S.HG = W5, QMa, QMb, TTa, TTb, TM, XN, UU, HG
                S.KR, S.KT_, S.BT_, S.VS_ = KR, KT_, BT_, VS_
                S.Y, S.BON, S.G = TF[12], TF[10], TF[5]
                S.k = lambda nm: {'Y': 'TF12', 'BON': 'TF10', 'G': 'TF5'}.get(nm, nm)
                S.pg, S.kpg, S.pq, S.kpq, S.pt, S.kpt = [PS[4], PS[5]], ['PS4', 'PS5'], PS[6], 'PS6', PTA, kPTA
            else:
                S.W5 = sb('W5b', [64, 2, 5, 64], SMDT)
                S.QMa, S.QMb, S.TTa, S.TTb = [sb(n_ + 'b', [64, 2, 2, 64], SMDT) for n_ in ('QMa', 'QMb', 'TTa', 'TTb')]
                S.TM = sb('TMb', [64, 3, 128], SMDT)
                S.XN, S.UU = sb('XNb', [64, 2, 64], SMDT), sb('UUb', [64, 2, 64], SMDT)
                S.HG = sb('HGb', [128, 64])
                S.KR = sb('KRb', [128, TS // 64, 2, 64], SMDT)
                S.KT_, S.BT_, S.VS_ = [sb(n_ + 'b', [128, TS], SMDT) for n_ in ('KT_', 'BT_', 'VS_')]
                S.Y = sb('Yb', [128, TS])
                S.BON, S.G = sb('BONb', [128, TS], BF16), sb('Gb', [128, TS], BF16)
                S.pg, S.kpg, S.pq, S.kpq, S.pt, S.kpt = [PS[0], PS[1]], ['PS0', 'PS1'], PS[2], 'PS2', PTB, kPTB
            S.GL = sb('GL%d' % si, [128, max(TS // 64, 1)])
            return S
        SL = [mkslot(0)] + ([mkslot(1)] if OVERLAP else [])

        def wconvert(l):
            for nm in ('w_in', 'w_out', 'w_ffn_in', 'w_ffn_out'):
                dma('pool', WSC[nm].ap()[l].rearrange("(k p) n -> p k n", p=128), I[nm][l].rearrange("(k p) n -> p k n", p=128),
                    w=['DW_%s%d' % (nm, l)])
        wconvert(0)
        wstate = {'i': 0}

        def wload(parts):
            i = wstate['i'] % NWB
            wstate['i'] += 1
            key = 'WB%d' % i
            for (c0, KC, ncol, src, rk) in parts:
                dst = WB[i][:, c0:c0 + KC * ncol].rearrange("p (k n) -> p k n", k=KC)
                dma('pool', dst, src, w=[key], r=[rk])
            return WB[i], key

        def win_src(l, c0, ncol):
            return WSC['w_in'].ap()[l][:, c0:c0 + ncol].rearrange("(k p) n -> p k n", p=128), 'DW_w_in%d' % l

        def rmsnorm_to(T, gcol_base, l, out_fn):
            for k in range(8):
                act(TB[0][:, 0:T] if k % 2 == 0 else TB[1][:, 0:T], X[:, k, 0:T], AF.Square,
                    ['TB%d' % (k % 2)], ['X%d' % k])
                mm(PS[0][:, 0:T], onesb[:], TB[k % 2][:, 0:T], k == 0, k == 7, ['PS0'], ['onesb', 'TB%d' % (k % 2)])
            act(RS[:, 0:T], PS[0][:, 0:T], AF.Sqrt, ['TF13'], ['PS0'], scale=1.0 / D, bias=RMS_EPS)
            recip(RS[:, 0:T], RS[:, 0:T], ['TF13'], ['TF13'])
            for k in range(8):
                out_fn(k)

        def inproj(l, T, wt, wkey, ncols_tile, chunk_list):
            for ci, (cit, slot) in enumerate(chunk_list):
                bank = ci % 4
                for k in range(8):
                    mm(PS[bank][:, 0:T], wt[:, k * ncols_tile + cit * 128: k * ncols_tile + cit * 128 + 128],
                       HN[:, k, 0:T], k == 0, k == 7, ['PS%d' % bank], [wkey] + ['HN%d' % k])
                cpy('act' if ci % 2 == 0 else 'dve', PJ[:, slot, 3:3 + T], PS[bank][:, 0:T], ['PJ%d' % slot], ['PS%d' % bank])

        def pv(l, c):
            return PV[:, l, c:c + 1]

        def task(job, l, T, L, tok0, first_seg, last_seg, xin, yout, last_layer):
            NCH = T // L
            if job == jobs[0] and first_seg and l + 1 < DEPTH:
                wconvert(l + 1)
            if l == 0:
                nblk = (T + 127) // 128
                for tb in range(nblk):
                    n = min(128, T - tb * 128)
                    for half in range(2):
                        xt, xk = STG[half]
                        bank = 4 + half
                        dma('sp', xt[0:n, 0:512], xin[tok0 + tb * 128: tok0 + tb * 128 + n, half * 512:(half + 1) * 512], w=[xk])
                        for mmi in range(4):
                            tp(PS[bank][:, mmi * 128: mmi * 128 + n], xt[0:n, mmi * 128:(mmi + 1) * 128], ident[0:n, 0:n],
                               ['PS%d' % bank], [xk, 'CT'])
                        for mmi in range(4):
                            m = half * 4 + mmi
                            cpy('act' if mmi % 2 else 'dve', X[:, m, tb * 128: tb * 128 + n],
                                PS[bank][:, mmi * 128: mmi * 128 + n], ['X%d' % m], ['PS%d' % bank])
            def o1(k):
                stt(HN[:, k, 0:T], X[:, k, 0:T], pv(l, k), RS[:, 0:T], ALU.mult, ALU.mult, ['HN%d' % k], ['X%d' % k, 'PV', 'TF13'])
            rmsnorm_to(T, 0, l, o1)

            wt, wkey = wload([(0, 8, 512) + win_src(l, 0, 512)])
            cpy('dve', PJ[:, 0:2, 0:3], HISTA[:, l, :, :], ['PJ0', 'PJ1'], ['HISTA'])
            inproj(l, T, wt, wkey, 512, [(0, 0), (1, 1), (2, 2), (3, 3)])
            cpy('dve', HISTA[:, l, :, :], PJ[:, 0:2, T:T + 3], ['HISTA'], ['PJ0', 'PJ1'])
            for c in range(2):
                xa, gr, gi, aa, a2t, mu_, uu, hh = TF[0], TF[1], TF[2], TF[3], TF[4], TF[5], TF[6], TF[7 + c]
                act(xa[:, 0:T], PJ[:, c, 3:3 + T], AF.Identity, ['TF0'], ['PJ%d' % c, 'PV'],
                    scale=pv(l, 16 + 2 * 3 + c), bias=pv(l, 24 + c))
                for j in range(3):
                    stt(xa[:, 0:T], PJ[:, c, j:j + T], pv(l, 16 + 2 * j + c), xa[:, 0:T], ALU.mult, ALU.add,
                        ['TF0'], ['TF0', 'PJ%d' % c, 'PV'])
                mm(PS[0][:, 0:T], WRbd[:, l, c, :], xa[:, 0:T], True, True, ['PS0'], ['WRbd', 'TF0'])
                mm(PS[1][:, 0:T], WIbd[:, l, c, :], xa[:, 0:T], True, True, ['PS1'], ['WIbd', 'TF0'])
                act(gr[:, 0:T], PS[0][:, 0:T], AF.Sigmoid, ['TF1'], ['PS0', 'PV'], bias=pv(l, 26 + c))
                act(gi[:, 0:T], PS[1][:, 0:T], AF.Sigmoid, ['TF2'], ['PS1', 'PV'], bias=pv(l, 28 + c))
                act(aa[:, 0:T], gr[:, 0:T], AF.Exp, ['TF3'], ['TF1', 'PV'], scale=pv(l, 84 + c))
                act(a2t[:, 0:T], gr[:, 0:T], AF.Exp, ['TF4'], ['TF1', 'PV'], scale=pv(l, 86 + c))
                act(mu_[:, 0:T], a2t[:, 0:T], AF.Sqrt, ['TF5'], ['TF4'], scale=-1.0, bias=1.0)
                tt('dve', uu[:, 0:T], gi[:, 0:T], xa[:, 0:T], ALU.mult, ['TF6'], ['TF2', 'TF0'])
                tt('dve', uu[:, 0:T], uu[:, 0:T], mu_[:, 0:T], ALU.mult, ['TF6'], ['TF6', 'TF5'])
                scan(hh[:, 0:T], aa[:, 0:T], uu[:, 0:T], HL[:, l, c:c + 1], ALU.mult, ALU.add,
                     ['TF%d' % (7 + c)], ['TF3', 'TF6', 'HL'])
                cpy('dve', HL[:, l, c:c + 1], hh[:, T - 1:T], ['HL'], ['TF%d' % (7 + c)])
                act(TB[c][:, 0:T], hh[:, 0:T], AF.Square, ['TB%d' % c], ['TF%d' % (7 + c)])
            for c in range(2):
                mm(PS[2][:, 0:T], onesb[:], TB[c][:, 0:T], c == 0, c == 1, ['PS2'], ['onesb', 'TB%d' % c])
            act(TF[9][:, 0:T], PS[2][:, 0:T], AF.Sqrt, ['TF9'], ['PS2'], scale=1.0 / DA, bias=RMS_EPS)
            recip(TF[9][:, 0:T], TF[9][:, 0:T], ['TF9'], ['TF9'])
            for c in range(2):
                g = PJ[:, 2 + c, 3:3 + T]
                gk = 'PJ%d' % (2 + c)
                t1, t2 = TF[0], TF[1]
                act(t1[:, 0:T], g, AF.Square, ['TF0'], [gk])
                ts('dve', t1[:, 0:T], t1[:, 0:T], 0.044715, 1.0, ALU.mult, ALU.add, ['TF0'], ['TF0'])
                tt('dve', t1[:, 0:T], t1[:, 0:T], g, ALU.mult, ['TF0'], ['TF0', gk])
                act(t2[:, 0:T], t1[:, 0:T], AF.Sigmoid, ['TF1'], ['TF0'], scale=2.0 * math.sqrt(2.0 / math.pi))
                tt('dve', t2[:, 0:T], t2[:, 0:T], g, ALU.mult, ['TF1'], ['TF1', gk])
                stt(t1[:, 0:T], TF[7 + c][:, 0:T], pv(l, 32 + c), TF[9][:, 0:T], ALU.mult, ALU.mult,
                    ['TF0'], ['TF%d' % (7 + c), 'PV', 'TF9'])
                tt('dve', YM[:, c, 0:T], t1[:, 0:T], t2[:, 0:T], ALU.mult, ['YM%d' % c], ['TF0', 'TF1'])

            def mix(out, slot, mcol, w):
                tt('dve', out, PJ[:, slot, 2:2 + T], PJ[:, slot, 3:3 + T], ALU.subtract, w, ['PJ%d' % slot])
                stt(out, out, pv(l, 34 + mcol), PJ[:, slot, 3:3 + T], ALU.mult, ALU.add, w, w + ['PJ%d' % slot, 'PV'])
            LW, SG = TB[2], TB[3]
            def preB():
                cpy('dve', PJ[:, 0:11, 2:3], HISTB[:, l, :, :], ['PJ%d' % s for s in range(0, 11)], ['HISTB'])
                for (c0, nch_, s0) in ((512, 4, 0), (1024, 4, 4), (1536, 3, 8)):
                    wt, wkey = wload([(0, 8, nch_ * 128) + win_src(l, c0, nch_ * 128)])
                    inproj(l, T, wt, wkey, nch_ * 128, [(i, s0 + i) for i in range(nch_)])
                cpy('dve', HISTB[:, l, :, :], PJ[:, 0:11, T + 2:T + 3], ['HISTB'], ['PJ%d' % s for s in range(0, 11)])

                mix(TF[0][:, 0:T], 9, 9, ['TF0'])
                act(LW[0:64, 0:T], TF[0][0:64, 0:T], AF.Tanh, ['TB2'], ['TF0'])
                act(LW[64:128, 0:T], TF[0][64:128, 0:T], AF.Copy, ['TB2'], ['TF0'])
                mix(TF[0][:, 0:T], 10, 10, ['TF0'])
                act(SG[:, 0:T], TF[0][:, 0:T], AF.Sigmoid, ['TB3'], ['TF0'])

            def prepB(j, S):
                R, K, V, SGW, A, G_, KK, RN, K2, BV, BON_, CL = [TF[i] for i in range(12)]
                G, BON, Y = S.G, S.BON, S.Y
                kR, kK, kV, kSGW, kA, kG_, kKK, kRN, kK2, kBV, kBON_, kCL = ['TF%d' % i for i in range(12)]
                kG, kBON, kY = S.k('G'), S.k('BON'), S.k('Y')
                mix(R[:, 0:T], j, j, [kR])
                mix(K[:, 0:T], 3 + j, 3 + j, [kK])
                mix(V[:, 0:T], 6 + j, 6 + j, [kV])
                jc = slice(j * 128, (j + 1) * 128)
                mm(PS[4][:, 0:T], W2[0:64, l, jc], LW[0:64, 0:T], True, True, ['PS4'], ['W2', 'TB2'])
                act(SGW[:, 0:T], PS[4][:, 0:T], AF.Sigmoid, [kSGW], ['PS4', 'PV'], bias=pv(l, 45 + j))
                mm(PS[5][:, 0:T], A2[64:128, l, jc], LW[64:128, 0:T], True, True, ['PS5'], ['A2', 'TB2'])
                act(A[:, 0:T], PS[5][:, 0:T], AF.Sigmoid, [kA], ['PS5', 'PV'], bias=pv(l, 48 + j))
                mm(PS[6][:, 0:T], G2[:, l, jc], SG[:, 0:T], True, True, ['PS6'], ['G2', 'TB3'])
                cpy('act', G[:, 0:T], PS[6][:, 0:T], [kG], ['PS6'])
                ts('dve', KK[:, 0:T], K[:, 0:T], pv(l, 51 + j), None, ALU.mult, None, [kKK], [kK, 'PV'])
                act(TB[4][:, 0:T], KK[:, 0:T], AF.Square, ['TB4'], [kKK])
                mm(PS[6][:, 0:T], onesbd_b[:], TB[4][:, 0:T], True, True, ['PS6'], ['onesbd_b', 'TB4'])
                act(RN[:, 0:T], PS[6][:, 0:T], AF.Sqrt, [kRN], ['PS6'])
                ts('dve', RN[:, 0:T], RN[:, 0:T], 1e-12, None, ALU.max, None, [kRN], [kRN])
                recip(RN[:, 0:T], RN[:, 0:T], [kRN], [kRN])
                tt('dve', KK[:, 0:T], KK[:, 0:T], RN[:, 0:T], ALU.mult, [kKK], [kKK, kRN])
                ts('dve', K2[:, 0:T], A[:, 0:T], -1.0, pv(l, 54 + j), ALU.add, ALU.mult, [kK2], [kA, 'PV'])
                stt(K2[:, 0:T], K2[:, 0:T], 1.0, K[:, 0:T], ALU.add, ALU.mult, [kK2], [kK2, kK])
                tt('dve', BV[:, 0:T], KK[:, 0:T], A[:, 0:T], ALU.mult, [kBV], [kKK, kA])
                tt('dve', RN[:, 0:T], R[:, 0:T], K2[:, 0:T], ALU.mult, [kRN], [kR, kK2])
                ts('dve', TB[4][:, 0:T], RN[:, 0:T], pv(l, 57 + j), None, ALU.mult, None, ['TB4'], [kRN, 'PV'])
                mm(PS[6][:, 0:T], onesbd_b[:], TB[4][:, 0:T], True, True, ['PS6'], ['onesbd_b', 'TB4'])
                tt('dve', BON[:, 0:T], PS[6][:, 0:T], V[:, 0:T], ALU.mult, [kBON], ['PS6', kV])
                scan(CL[:, 0:T], cmask[:, 0:T], SGW[:, 0:T], 0.0, ALU.mult, ALU.add, [kCL], ['CT', kSGW])
                EG, EGI, EGX = TF[13], RN, K
                kEG, kEGI = 'TF13', kRN
                act(EG[:, 0:T], CL[:, 0:T], AF.Exp, [kEG], [kCL], scale=-C_DEC)
                act(EGI[:, 0:T], CL[:, 0:T], AF.Exp, [kEGI], [kCL], scale=C_DEC)
                tt('dve', SGW[:, 0:T], CL[:, 0:T], SGW[:, 0:T], ALU.subtract, [kSGW], [kCL, kSGW])
                act(SGW[:, 0:T], SGW[:, 0:T], AF.Exp, [kSGW], [kSGW], scale=-C_DEC)
                v3 = lambda ap: ap.rearrange("p (n l) -> p n l", l=L)
                tt('dve', S.KR[:, 0:NCH, 1, 0:L], v3(R[:, 0:T]), v3(EG[:, 0:T]), ALU.mult, [S.k('KR')], [kR, kEG])
                tt('dve', S.KR[:, 0:NCH, 0, 0:L], v3(KK[:, 0:T]), v3(SGW[:, 0:T]), ALU.mult, [S.k('KR')], [kKK, kSGW])
                tt('dve', S.KT_[:, 0:T], K2[:, 0:T], EGI[:, 0:T], ALU.mult, [S.k('KT_')], [kK2, kEGI])
                tt('dve', S.BT_[:, 0:T], BV[:, 0:T], EGI[:, 0:T], ALU.mult, [S.k('BT_')], [kBV, kEGI])
                cpy('act', S.VS_[:, 0:T], V[:, 0:T], [S.k('VS_')], [kV])
                cpy('dve', S.GL[:, 0:NCH], EG[:, 0:T].rearrange("p (n l) -> p n l", l=L)[:, :, L - 1], [S.k('GL')], [kEG])
                Hm = HS[:, l, j, :]
                if SMDT != F32:
                    cpy('dve', HSs[:, j, :], Hm, ['HSs'], ['HS'])
                    Hs = HSs[:, j, :]
                    hsk = 'HSs'
                else:
                    Hs = Hm
                    hsk = 'HS'
                return dict(Hm=Hm, Hs=Hs, hsk=hsk)


            def loopB(j, S, pp):
                Hm, Hs, hsk = pp['Hm'], pp['Hs'], pp['hsk']
                for n in range(NCH):
                    cs = slice(n * L, (n + 1) * L)
                    yield
                    yield
                    for hh in range(2):
                        rw = slice(64 * hh, 64 * hh + 64)
                        pg = S.pg[hh]
                        pk = S.kpg[hh]
                        mm(pg[0:L, 0:L], S.BT_[rw, cs], S.KR[rw, n, 0, 0:L], True, True, [pk], [S.k('BT_'), S.k('KR')])
                        mm(pg[0:L, 64:64 + L], S.KR[rw, n, 0, 0:L], S.BT_[rw, cs], True, True, [pk], [S.k('BT_'), S.k('KR')])
                        mm(pg[0:L, 128:128 + L], S.BT_[rw, cs], S.KR[rw, n, 1, 0:L], True, True, [pk], [S.k('BT_'), S.k('KR')])
                        mm(pg[0:L, 192:192 + L], S.KT_[rw, cs], S.KR[rw, n, 0, 0:L], True, True, [pk], [S.k('KT_'), S.k('KR')])
                        mm(pg[0:L, 256:256 + L], S.KT_[rw, cs], S.KR[rw, n, 1, 0:L], True, True, [pk], [S.k('KT_'), S.k('KR')])
                        tt('dve', S.W5[:, hh, :, :], pg[0:64, 0:320].rearrange("p (b l) -> p b l", b=5),
                           mask5[0:64, :, :], ALU.mult, [S.k('W5')], [pk, mk5key])
                    yield
                    yield
                    for bi, (src, sk) in enumerate(((S.KT_, S.k('KT_')), (S.BT_, S.k('BT_')), (S.VS_, S.k('VS_')))):
                        tp(S.pt[0:L, bi * 128:(bi + 1) * 128], src[:, cs], ident_s, [S.kpt], [sk, 'CT', 'ident_st'])
                    yield
                    cpy('act', S.TM[:, :, :], S.pt[0:64, 0:384].rearrange("p (b l) -> p b l", b=3), [S.k('TM')], [S.kpt])
                    yield
                    yield
                    for hh in range(2):
                        tt('dve', S.TTa[:, hh, :, :], S.W5[:, hh, 0:2, :],
                           ident[0:64, 0:64].rearrange("p (o l) -> p o l", o=1).to_broadcast([64, 2, 64]),
                           ALU.add, [S.k('TTa')], [S.k('W5'), 'CT'])
                    yield
                    qm_cur, qk, qoff = S.W5, S.k('W5'), 0
                    yield
                    tt_cur, tk = S.TTa, S.k('TTa')
                    yield
                    nlev = int(math.log2(L)) - 1
                    yield
                    for lev in range(nlev):
                        lastl = (lev == nlev - 1)
                        qm_nx, qnk = (S.QMa, S.k('QMa')) if lev % 2 == 0 else (S.QMb, S.k('QMb'))
                        tt_nx, tnk = (S.TTb, S.k('TTb')) if lev % 2 == 0 else (S.TTa, S.k('TTa'))
                        for hh in range(2):
                            mm(S.pq[0:L, hh * 128:hh * 128 + L], qm_cur[0:L, hh, 1, 0:L], qm_cur[0:L, hh, 0, 0:L], True, True, [S.kpq], [qk])
                            if not lastl:
                                mm(S.pq[0:L, hh * 128 + 64:hh * 128 + 64 + L], qm_cur[0:L, hh, 0, 0:L], qm_cur[0:L, hh, 1, 0:L], True, True, [S.kpq], [qk])
                        cpy('act', qm_nx[:].rearrange("p a b l -> p (a b l)"), S.pq[0:64, 0:256], [qnk], [S.kpq])
                        for hh in range(2):
                            mm(S.pg[1][0:L, hh * 128:hh * 128 + L], tt_cur[0:L, hh, 1, 0:L], qm_nx[0:L, hh, 0, 0:L], True, True, [S.kpg[1]], [tk, qnk])
                            if not lastl:
                                mm(S.pg[1][0:L, hh * 128 + 64:hh * 128 + 64 + L], qm_nx[0:L, hh, 0, 0:L], tt_cur[0:L, hh, 1, 0:L], True, True, [S.kpg[1]], [tk, qnk])
                        tt('dve', tt_nx[:].rearrange("p a b l -> p (a b l)"), S.pg[1][0:64, 0:256],
                           tt_cur[:].rearrange("p a b l -> p (a b l)"), ALU.add, [tnk], [S.kpg[1], tk])
                        qm_cur, qk = qm_nx, qnk
                        tt_cur, tk = tt_nx, tnk
                        yield
                    yield
                    yield
                    for hh in range(2):
                        rw = slice(64 * hh, 64 * hh + 64)
                        fo = slice(64 * hh, 64 * hh + 64)
                        mm(S.pg[1][0:L, 384 + 64 * hh:448 + 64 * hh], S.KR[rw, n, 0, 0:L], Hs[rw, :], True, False, [S.kpg[1]], [S.k('KR'), hsk])
                        mm(S.pg[1][0:L, 384 + 64 * hh:448 + 64 * hh], S.W5[0:L, hh, 3, 0:L], S.TM[0:L, 2, fo], False, True, [S.kpg[1]], [S.k('W5'), S.k('TM')])
                    yield
                    act(S.XN[:].rearrange("p a v -> p (a v)"), S.pg[1][0:64, 384:512], AF.Copy, [S.k('XN')], [S.kpg[1]], scale=-1.0)
                    yield
                    yield
                    for hh in range(2):
                        mm(S.pg[1][0:L, 384 + 64 * hh:448 + 64 * hh], tt_cur[0:L, hh, 0, 0:L], S.XN[0:L, hh, :], True, True, [S.kpg[1]], [tk, S.k('XN')])
                    yield
                    cpy('dve', S.UU[:].rearrange("p a v -> p (a v)"), S.pg[1][0:64, 384:512], [S.k('UU')], [S.kpg[1]])
                    yield
                    yield
                    for hh in range(2):
                        rw = slice(64 * hh, 64 * hh + 64)
                        fo = slice(64 * hh, 64 * hh + 64)
                        mm(S.pg[0][rw, 384:384 + L], Hs[rw, :], S.KR[rw, n, 1, 0:L], True, False, [S.kpg[0]], [hsk, S.k('KR')])
                        mm(S.pg[0][rw, 384:384 + L], S.UU[0:L, hh, :], S.W5[0:L, hh, 2, 0:L], False, False, [S.kpg[0]], [S.k('UU'), S.k('W5')])
                        mm(S.pg[0][rw, 384:384 + L], S.TM[0:L, 2, fo], S.W5[0:L, hh, 4, 0:L], False, True, [S.kpg[0]], [S.k('TM'), S.k('W5')])
                    yield
                    cpy('act', S.Y[:, cs], S.pg[0][:, 384:384 + L], [S.k('Y')], [S.kpg[0]])
                    yield
                    yield
                    for hh in range(2):
                        rw = slice(64 * hh, 64 * hh + 64)
                        fo = slice(64 * hh, 64 * hh + 64)
                        mm(S.pg[0][rw, 448:512], S.TM[0:L, 1, fo], S.UU[0:L, hh, :], True, False, [S.kpg[0]], [S.k('TM'), S.k('UU')])
                        mm(S.pg[0][rw, 448:512], S.TM[0:L, 0, fo], S.TM[0:L, 2, fo], False, True, [S.kpg[0]], [S.k('TM')])
                    yield
                    gl = S.GL[:, n:n + 1]
                    yield
                    act(S.HG[:, :], Hm, AF.Identity, [S.k('HG')], ['HS', S.k('GL')], scale=gl)
                    yield
                    stt(Hm, S.pg[0][:, 448:512], gl, S.HG[:, :], ALU.mult, ALU.add, ['HS'], [S.kpg[0], S.k('GL'), S.k('HG')])
                    if SMDT != F32:
                        cpy('act', HSs[:, j, :], Hm, ['HSs'], ['HS'])

                yield
                yield


            def postB(j, S):
                Y, BON, G, CL = S.Y, S.BON, S.G, TF[11]
                kBON, kG, kCL = S.k('BON'), S.k('G'), 'TF11'
                mm(PS[4][:, 0:T], onesbd_f, Y[:, 0:T], True, True, ['PS4'], ['CT', S.k('Y')])
                stt(Y[:, 0:T], PS[4][:, 0:T], -1.0 / 64, Y[:, 0:T], ALU.mult, ALU.add, [S.k('Y')], ['PS4', S.k('Y')])
                act(CL[:, 0:T], Y[:, 0:T], AF.Square, [kCL], [S.k('Y')])
                mm(PS[5][:, 0:T], onesbd_f, CL[:, 0:T], True, True, ['PS5'], ['CT', kCL])
                act(CL[:, 0:T], PS[5][:, 0:T], AF.Sqrt, [kCL], ['PS5'], scale=1.0 / 64, bias=GN_EPS_B)
                recip(CL[:, 0:T], CL[:, 0:T], [kCL], [kCL])
                tt('dve', Y[:, 0:T], Y[:, 0:T], CL[:, 0:T], ALU.mult, [S.k('Y')], [S.k('Y'), kCL])
                ts('dve', Y[:, 0:T], Y[:, 0:T], pv(l, 60 + j), pv(l, 63 + j), ALU.mult, ALU.add, [S.k('Y')], [S.k('Y'), 'PV'])
                tt('dve', Y[:, 0:T], Y[:, 0:T], BON[:, 0:T], ALU.add, [S.k('Y')], [S.k('Y'), kBON])
                tt('dve', YM[:, 2 + j, 0:T], Y[:, 0:T], G[:, 0:T], ALU.mult, ['YM%d' % (2 + j)], [S.k('Y'), kG])

            def prepC():
                cpy('dve', PJ[:, 0:3, 0:3], HISTC[:, l, :, :], ['PJ0', 'PJ1', 'PJ2'], ['HISTC'])
                for (c0, nch_, s0) in ((1920, 4, 0), (2432, 4, 4), (2944, 1, 8)):
                    wt, wkey = wload([(0, 8, nch_ * 128) + win_src(l, c0, nch_ * 128)])
                    inproj(l, T, wt, wkey, nch_ * 128, [(i, s0 + i) for i in range(nch_)])
                cpy('dve', HISTC[:, l, :, :], PJ[:, 0:3, T:T + 3], ['HISTC'], ['PJ0', 'PJ1', 'PJ2'])
                if SMDT == F32:
                    KRf = KR[:].rearrange("p n a l -> p (n a l)")
                    QB = [TF[12], TF[13], KT_]
                    KS = [BT_, VS_, KRf]
                    kQB = ['TF12', 'TF13', 'KT_']
                    kKS = ['BT_', 'VS_', 'KR']
                else:
                    QB, KS, VBt = QKVb[0:3], QKVb[3:6], QKVb[6:9]
                    kQB = ['QKV%d' % i for i in range(3)]
                    kKS = ['QKV%d' % i for i in range(3, 6)]
                for j in range(3):
                    xc = TF[0]
                    act(xc[:, 0:T], PJ[:, j, 3:3 + T], AF.Identity, ['TF0'], ['PJ%d' % j, 'PV'],
                        scale=pv(l, 66 + 3 * 3 + j), bias=pv(l, 78 + j))
                    for t_ in range(3):
                        stt(xc[:, 0:T], PJ[:, j, t_:t_ + T], pv(l, 66 + 3 * t_ + j), xc[:, 0:T], ALU.mult, ALU.add,
                            ['TF0'], ['TF0', 'PJ%d' % j, 'PV'])
                    act(TB[2][:, 0:T], xc[:, 0:T], AF.Silu, ['TB2'], ['TF0'])
                    mm(PS[0][:, 0:T], WQbd[:, l, j, :], TB[2][:, 0:T], True, True, ['PS0'], ['WQbd', 'TB2'])
                    mm(PS[1][:, 0:T], WKbd[:, l, j, :], TB[2][:, 0:T], True, True, ['PS1'], ['WKbd', 'TB2'])
                    cpy('act', QB[j][:, 0:T], PS[0][:, 0:T], [kQB[j]], ['PS0'])
                    act(KS[j][:, 0:T], PS[1][:, 0:T], AF.Copy, [kKS[j]], ['PS1'], scale=0.125)
                    if SMDT != F32:
                        cpy('dve', VBt[j][:, 0:T], PJ[:, 3 + j, 3:3 + T], ['QKV%d' % (6 + j)], ['PJ%d' % (3 + j)])

                def vsrc(j, cs_):
                    if SMDT != F32:
                        return VBt[j][:, cs_.start:cs_.stop], 'QKV%d' % (6 + j)
                    return PJ[:, 3 + j, 3 + cs_.start:3 + cs_.stop], 'PJ%d' % (3 + j)
                allT = slice(0, T)
                srcs = [(QB[j][:, 0:T], kQB[j]) for j in range(3)] + [(KS[j][:, 0:T], kKS[j]) for j in range(3)] + \
                       [vsrc(j, allT) for j in range(3)]
                for kc, (s_, sk) in enumerate(srcs):
                    mm(PS[2][0:6, 0:T], WIF[:, l, kc, 0:6], s_, kc == 0, kc == 8, ['PS2'], ['WIF', sk])
                for kc, (s_, sk) in enumerate(srcs):
                    mm(PS[3][0:6, 0:T], WIF[:, l, kc, 6:12], s_, kc == 0, kc == 8, ['PS3'], ['WIF', sk])
                IP, L1, CS, MX = [TF[i][0:6, :] for i in range(4, 8)]
                kIP, kL1, kCS, kMX = ['TF%d' % i for i in range(4, 8)]
                AH, NMX, WL, MT = [GC[i] for i in range(4)]
                kAH, kNMX, kWL, kMT = ['GC%d' % i for i in range(4)]
                act(IP[:, 0:T], PS[2][0:6, 0:T], AF.Identity, [kIP], ['PS2', 'BI'], bias=BI[:, l:l + 1])
                act(L1[:, 0:T], PS[3][0:6, 0:T], AF.Exp, [kL1], ['PS3', 'NBF'], scale=-1.0, bias=NBF[:, l:l + 1])
                act(L1[:, 0:T], L1[:, 0:T], AF.Ln, [kL1], [kL1], bias=1.0)
                scan(CS[:, 0:T], ones_c[0:6, 0:1].to_broadcast([6, T]), L1[:, 0:T], 0.0, ALU.mult, ALU.add, [kCS], ['ones_f', kL1])
                tt('dve', AH[:, 0:T], IP[:, 0:T], CS[:, 0:T], ALU.add, [kAH], [kIP, kCS])
                scan(MX[:, 0:T], ones_c[0:6, 0:1].to_broadcast([6, T]), AH[:, 0:T], MS[:, l:l + 1], ALU.mult, ALU.max, [kMX], ['ones_f', kAH, 'MS'])
                tt('dve', MT[:, 0:T], MX[:, 0:T], CS[:, 0:T], ALU.subtract, [kMT], [kMX, kCS])
                ts('dve', NMX[:, 0:T], MX[:, 0:T], -1.0, None, ALU.mult, None, [kNMX], [kMX])
                for n in range(NCH):
                    cs = slice(n * L, (n + 1) * L)
                    prev = MS[:, l:l + 1] if n == 0 else MX[:, n * L - 1:n * L]
                    ts('dve', WL[:, cs], MX[:, cs], prev, None, ALU.subtract, None, [kWL], [kMX, 'MS'])
                cpy('dve', MS[:, l:l + 1], MT[:, T - 1:T], ['MS'], [kMT])
                YC = None
                Cm = CA[:, l, :, :]
                if SMDT != F32:
                    cpy('dve', CAs[:], Cm, ['CAs'], ['CA'])
                    Cs, csk = CAs[:], 'CAs'
                else:
                    Cs, csk = Cm, 'CA'
                for j in range(3):
                    act(SZ[j][:, 0:T], PJ[:, 6 + j, 3:3 + T], AF.Sigmoid, ['SZ%d' % j], ['PJ%d' % (6 + j)])
                return dict(QB=QB, KS=KS, kQB=kQB, kKS=kKS, vsrc=vsrc, AH=AH, NMX=NMX, WL=WL, MT=MT, kAH=kAH, kNMX=kNMX, kWL=kWL, kMT=kMT, YC=YC, Cm=Cm, Cs=Cs, csk=csk)

            def loopC(cc):
                QB, KS, kQB, kKS, vsrc = cc['QB'], cc['KS'], cc['kQB'], cc['kKS'], cc['vsrc']
                AH, NMX, WL, MT, kAH, kNMX, kWL, kMT = cc['AH'], cc['NMX'], cc['WL'], cc['MT'], cc['kAH'], cc['kNMX'], cc['kWL'], cc['kMT']
                YC, Cm, Cs, csk = cc['YC'], cc['Cm'], cc['Cs'], cc['csk']
                for n in range(NCH):
                    cs = slice(n * L, (n + 1) * L)
                    yield
                    for j in range(3):
                        vs_, vk_ = vsrc(j, cs)
                        tp(PTB[0:L, j * 128:(j + 1) * 128], vs_, ident_s, [kPTB], [vk_, 'CT', 'ident_st'])
                    yield
                    cpy('act', VA[0:L, :, 0:64], PTB[0:L, 0:384].rearrange("p (h v) -> p h v", h=6), ['VA'], [kPTB])
                    yield
                    for j in range(3):
                        tp(PTB[0:L, j * 128:(j + 1) * 128], KS[j][:, cs], ident_s, [kPTB], [kKS[j], 'CT', 'ident_st'])
                    yield
                    cpy('dve', KTm[0:L, :, :], PTB[0:L, 0:384].rearrange("p (h v) -> p h v", h=6), ['KTm'], [kPTB])
                    yield
                    tp(PS[1][0:L, 400:406], WL[:, cs], ident[0:6, 0:6], ['PS1'], [kWL, 'CT'])
                    yield
                    tp(PS[1][0:L, 406:412], MT[:, cs], ident[0:6, 0:6], ['PS1'], [kMT, 'CT'])
                    yield
                    act(WE[0:L, :, :], PS[1][0:L, 400:412].rearrange("p (a h) -> p a h", a=2), AF.Exp, ['WE'], ['PS1'], scale=-1.0)
                    yield
                    yield
                    mm(PS[0][0:L, 0:384].rearrange("p (h l) -> p h l", h=6)[:, :, 0:L], ident[0:L, 0:L], negm6[0:L, :, 0:L],
                       True, False, ['PS0'], ['CT'])
                    yield
                    for h in range(6):
                        o_ = PS[0][0:L, h * 64:h * 64 + L]
                        mm(o_, sel6[:, h, 0:L], NMX[:, cs], False, False, ['PS0'], ['CT', kNMX])
                        mm(o_, AH[:, cs], sel6[:, h, 0:L], False, h == 5, ['PS0'], ['CT', kAH])
                    yield
                    act(DD[0:L, :, 0:L], PS[0][0:L, 0:384].rearrange("p (h l) -> p h l", h=6)[:, :, 0:L], AF.Exp, ['DD'], ['PS0'])
                    yield
                    yield
                    for h in range(6):
                        j, hh = h // 2, h % 2
                        rw = slice(64 * hh, 64 * hh + 64)
                        mm(PS[1][0:L, h * 64:h * 64 + L], KS[j][rw, cs], QB[j][rw, cs], True, True, ['PS1'], [kKS[j], kQB[j]])
                    yield
                    tt('dve', STt[0:L, :, 0:L], PS[1][0:L, 0:384].rearrange("p (h l) -> p h l", h=6)[:, :, 0:L],
                       DD[0:L, :, 0:L], ALU.mult, ['STt'], ['PS1', 'DD'])
                    yield
                    yield
                    for h in range(6):
                        j, hh = h // 2, h % 2
                        rw = slice(64 * hh, 64 * hh + 64)
                        mm(PS[2][0:L, h * 65:h * 65 + 65], STt[0:L, h, 0:L], VA[0:L, h, :], True, True, ['PS2'], ['STt', 'VA'])
                        mm(PS[3][0:L, h * 65:h * 65 + 65], QB[j][rw, cs], Cs[rw, j, :], True, True, ['PS3'], [kQB[j], csk])
                    yield
                    tt('dve', T1[0:L], PS[3][0:L, 0:390].rearrange("p (h v) -> p h v", h=6),
                       WE[0:L, 0, :].rearrange("p (h o) -> p h o", o=1).to_broadcast([L, 6, 65]), ALU.mult, ['T1'], ['PS3', 'WE'])
                    yield
                    tt('dve', NUM[0:L], PS[2][0:L, 0:390].rearrange("p (h v) -> p h v", h=6), T1[0:L], ALU.add, ['NUM'], ['PS2', 'T1'])
                    yield
                    ts('dve', DEN[0:L, :], NUM[0:L, :, 64], -1.0, None, ALU.mult, None, ['DEN'], ['NUM'])
                    yield
                    tt('dve', DEN[0:L, :], DEN[0:L, :], NUM[0:L, :, 64], ALU.max, ['DEN'], ['DEN', 'NUM'])
                    yield
                    tt('dve', DEN[0:L, :], DEN[0:L, :], WE[0:L, 1, :], ALU.max, ['DEN'], ['DEN', 'WE'])
                    yield
                    recip(DEN[0:L, :], DEN[0:L, :], ['DEN'], ['DEN'])
                    yield
                    yield
                    tt('dve', HTt[0:L], NUM[0:L, :, 0:64], DEN[0:L, :].rearrange("p (h o) -> p h o", o=1).to_broadcast([L, 6, 64]),
                       ALU.mult, ['HTt'], ['NUM', 'DEN'])
                    yield
                    red(MEAN[0:L, :], HTt[0:L], ['MEAN'], ['HTt'])
                    yield
                    stt(HC[0:L], MEAN[0:L, :].rearrange("p (h o) -> p h o", o=1).to_broadcast([L, 6, 64]), -1.0 / 64, HTt[0:L],
                        ALU.mult, ALU.add, ['HC'], ['MEAN', 'HTt'])
                    yield
                    tt('dve', SQ[0:L], HC[0:L], HC[0:L], ALU.mult, ['SQ'], ['HC'])
                    yield
                    red(VAR[0:L, :], SQ[0:L], ['VAR'], ['SQ'])
                    yield
                    yield
                    act(VAR[0:L, :], VAR[0:L, :], AF.Ln, ['VAR'], ['VAR'], scale=1.0 / 64, bias=GN_EPS_C)
                    yield
                    act(VAR[0:L, :], VAR[0:L, :], AF.Exp, ['VAR'], ['VAR'], scale=-0.5)
                    yield
                    tt('dve', HC[0:L], HC[0:L], VAR[0:L, :].rearrange("p (h o) -> p h o", o=1).to_broadcast([L, 6, 64]),
                       ALU.mult, ['HC'], ['HC', 'VAR'])
                    yield
                    yield
                    for j in range(3):
                        tp(PS[1][:, j * 64:j * 64 + L], HC[0:L, 2 * j:2 * j + 2, :].rearrange("p h v -> p (h v)"), ident[0:L, 0:L],
                           ['PS1'], ['HC', 'CT'])
                    yield
                    for j in range(3):
                        act(YCt[:, j, 0:L], PS[1][:, j * 64:j * 64 + L], AF.Identity, ['YCt'], ['PS1', 'PV'], scale=pv(l, 81 + j))
                        tt('dve', YM[:, 5 + j, cs], YCt[:, j, 0:L], SZ[j][:, cs], ALU.mult, ['YM%d' % (5 + j)], ['YCt', 'SZ%d' % j])
                    yield
                    yield
                    tt('dve', VW[0:L], VA[0:L], DD[0:L, :, L - 1:L].to_broadcast([L, 6, 65]), ALU.mult, ['VW'], ['VA', 'DD'])
                    yield
                    for h in range(6):
                        j, hh = h // 2, h % 2
                        rw = slice(64 * hh, 64 * hh + 64)
                        mm(PS[0][rw, j * 65:j * 65 + 65], KTm[0:L, h, :], VW[0:L, h, :], True, True, ['PS0'], ['KTm', 'VW'])
                    yield
                    ts('dve', WLE[:, :], pairsel, WL[:, n * L + L - 1:n * L + L], None, ALU.mult, None, ['WLE'], ['CT', kWL])
                    yield
                    mm(PS[0][:, 400:403], selb, WLE[:, :], True, True, ['PS0'], ['CT', 'WLE'])
                    yield
                    act(W0[:, :], PS[0][:, 400:403], AF.Exp, ['W0'], ['PS0'], scale=-1.0)
                    yield
                    tt('dve', CT2[:], Cm, W0[:, :].rearrange("p (h o) -> p h o", o=1).to_broadcast([128, 3, 65]), ALU.mult, ['CT2'], ['CA', 'W0'])
                    yield
                    tt('dve', Cm, PS[0][:, 0:195].rearrange("p (h v) -> p h v", h=3), CT2[:], ALU.add, ['CA'], ['PS0', 'CT2'])
                    if SMDT != F32:
                        cpy('act', CAs[:], Cm, ['CAs'], ['CA'])

                    yield

            def postC():
                for j in range(3):
                    pass

            def lockstep(gens):
                alive = [True] * len(gens)
                while any(alive):
                    for gi, g in enumerate(gens):
                        if alive[gi]:
                            try:
                                next(g)
                            except StopIteration:
                                alive[gi] = False

            if OVERLAP:
                cc = prepC()
                preB()
                pp0 = prepB(0, SL[0])
                pp1 = prepB(1, SL[1])
                lockstep([loopB(0, SL[0], pp0), loopB(1, SL[1], pp1)])
                postB(0, SL[0])
                postB(1, SL[1])
                pp2 = prepB(2, SL[0])
                lockstep([loopB(2, SL[0], pp2), loopC(cc)])
                postB(2, SL[0])
            else:
                preB()
                for j in range(3):
                    ppj = prepB(j, SL[0])
                    for _ in loopB(j, SL[0], ppj):
                        pass
                    postB(j, SL[0])
                cc = prepC()
                for _ in loopC(cc):
                    pass
            postC()

            for half in range(2):
                wt, wkey = wload([(0, 8, 512, WSC['w_out'].ap()[l][:, half * 512:(half + 1) * 512].rearrange("(k p) n -> p k n", p=128), 'DW_w_out%d' % l)])
                for mi in range(4):
                    m = half * 4 + mi
                    bank = mi % 4
                    for k in range(8):
                        mm(PS[bank][:, 0:T], wt[:, k * 512 + mi * 128:k * 512 + mi * 128 + 128], YM[:, k, 0:T], k == 0, k == 7,
                           ['PS%d' % bank], [wkey, 'YM%d' % k])
                    tt('dve', X[:, m, 0:T], PS[bank][:, 0:T], X[:, m, 0:T], ALU.add, ['X%d' % m], ['PS%d' % bank, 'X%d' % m])
            def o2(k):
                stt(HN[:, k, 0:T], X[:, k, 0:T], pv(l, 8 + k), RS[:, 0:T], ALU.mult, ALU.mult, ['HN%d' % k], ['X%d' % k, 'PV', 'TF13'])
            rmsnorm_to(T, 8, l, o2)
            for i in range(11):
                wf = WSC['w_ffn_in'].ap()[l]
                wt, wkey = wload([(0, 8, 256, wf[:, 256 * i:256 * i + 256].rearrange("(k p) n -> p k n", p=128), 'DW_w_ffn_in%d' % l),
                                  (2048, 8, 256, wf[:, DFF + 256 * i:DFF + 256 * i + 256].rearrange("(k p) n -> p k n", p=128), 'DW_w_ffn_in%d' % l)])
                for q in range(2):
                    f = 2 * i + q
                    bg, bu = (0, 1) if q == 0 else (2, 3)
                    for k in range(8):
                        mm(PS[bg][:, 0:T], wt[:, k * 256 + q * 128:k * 256 + q * 128 + 128], HN[:, k, 0:T], k == 0, k == 7,
                           ['PS%d' % bg], [wkey, 'HN%d' % k])
                    for k in range(8):
                        mm(PS[bu][:, 0:T], wt[:, 2048 + k * 256 + q * 128:2048 + k * 256 + q * 128 + 128], HN[:, k, 0:T], k == 0, k == 7,
                           ['PS%d' % bu], [wkey, 'HN%d' % k])
                    tk_ = TF[q]
                    act(tk_[:, 0:T], PS[bg][:, 0:T], AF.Silu, ['TF%d' % q], ['PS%d' % bg])
                    tt('dve', ACTT[:, f, 0:T], tk_[:, 0:T], PS[bu][:, 0:T], ALU.mult, actk(f), ['TF%d' % q, 'PS%d' % bu])
            wo = WSC['w_ffn_out'].ap()[l]
            for half in range(2):
                for (k0, kc_) in ((0, 8), (8, 8), (16, 6)):
                    wt, wkey = wload([(0, kc_, 512, wo[k0 * 128:(k0 + kc_) * 128, half * 512:(half + 1) * 512].rearrange("(k p) n -> p k n", p=128),
                                       'DW_w_ffn_out%d' % l)])
                    for mi in range(4):
                        for kk_ in range(kc_):
                            k = k0 + kk_
                            mm(PS[mi][:, 0:T], wt[:, kk_ * 512 + mi * 128:kk_ * 512 + mi * 128 + 128], ACTT[:, k, 0:T], k == 0, k == 21,
                               ['PS%d' % mi], [wkey] + actk(k))
                for mi in range(4):
                    m = half * 4 + mi
                    tt('dve', X[:, m, 0:T], PS[mi][:, 0:T], X[:, m, 0:T], ALU.add, ['X%d' % m], ['PS%d' % mi, 'X%d' % m])
            if last_layer:
                def o3(k):
                    stt(TF[k][:, 0:T], X[:, k, 0:T], NF[:, k:k + 1], RS[:, 0:T], ALU.mult, ALU.mult, ['TF%d' % k], ['X%d' % k, 'NF', 'TF13'])
                rmsnorm_to(T, 0, l, o3)
                nblk = (T + 127) // 128
                for tb in range(nblk):
                    n = min(128, T - tb * 128)
                    for half in range(2):
                        xt, xk = STG[half]
                        bank = 4 + half
                        for mmi in range(4):
                            m = half * 4 + mmi
                            tp(PS[bank][0:n, mmi * 128:(mmi + 1) * 128], TF[m][:, tb * 128:tb * 128 + n], ident,
                               ['PS%d' % bank], ['TF%d' % m, 'CT'])
                        cpy('act' if half else 'dve', xt[0:n, 0:512], PS[bank][0:n, 0:512], [xk], ['PS%d' % bank])
                        dma('sp', yout[tok0 + tb * 128:tok0 + tb * 128 + n, half * 512:(half + 1) * 512], xt[0:n, 0:512], r=[xk], is_output=True)

        for job in jobs:
            if job == 'p':
                T, L, nseg, xin, yout = TS, 64, TP // TS, I['xp'], O['yp']
                for t_, k_ in ((HISTA, 'HISTA'), (HISTB, 'HISTB'), (HISTC, 'HISTC'), (HL, 'HL'), (HS, 'HS'), (CA, 'CA'), (MS, 'MS')):
                    memset('dve', t_[:], 0.0, [k_])
            else:
                T, L, nseg, xin, yout = TSMP, 32, 1, I['xs'], O['ys']
                for l in range(DEPTH):
                    for j_ in range(3):
                        dma('sp', HISTA[:, l, :, j_], I['s_conv_a'][l, j_].rearrange("(c p) -> p c", p=128), w=['HISTA'], slow=True)
                        dma('sp', HISTC[:, l, :, j_], I['s_conv_c'][l, j_].rearrange("(c p) -> p c", p=128), w=['HISTC'], slow=True)
                    dma('sp', HISTB[:, l, :, 0], I['s_shift'][l].rearrange("(c p) -> p c", p=128), w=['HISTB'], slow=True)
                    dma('sp', HL[:, l, :], I['s_lru'][l].rearrange("(c p) -> p c", p=128), w=['HL'], slow=True)
                    dma('sp', CA[:, l, :, 0:64], I['s_mem_c'][l].rearrange("(jp hh) n v -> (hh n) jp v", hh=2), w=['CA'])
                    dma('sp', CA[:, l, :, 64], I['s_mem_n'][l].rearrange("(jp hh) n -> (hh n) jp", hh=2), w=['CA'], slow=True)
                    dma('sp', WS[:], I['s_wkv'][l].rearrange("h v k -> v h k"), w=['WS'])
                    for j in range(3):
                        tp(PS[4][:, j * 64:j * 64 + 64], WS[:, 2 * j:2 * j + 2, :].rearrange("p h k -> p (h k)"), ident[0:64, 0:64],
                           ['PS4'], ['WS', 'CT'])
                    cpy('dve', HS[:, l, :, :], PS[4][:, 0:192].rearrange("p (j v) -> p j v", j=3), ['HS'], ['PS4'])
                dma('sp', MS[:], I['s_mem_m'].rearrange("l h -> h l"), w=['MS'], slow=True)
            for seg in range(nseg):
                for l in range(DEPTH):
                    task(job, l, T, L, seg * T, seg == 0, seg == nseg - 1, xin, yout, l == DEPTH - 1)
            pre = job + '_'
            for l in range(DEPTH):
                for j_ in range(3):
                    dma('sp', O[pre + 'conv_a'][l, j_].rearrange("(c p) -> p c", p=128), HISTA[:, l, :, j_], r=['HISTA'], slow=True, is_output=True)
                    dma('sp', O[pre + 'conv_c'][l, j_].rearrange("(c p) -> p c", p=128), HISTC[:, l, :, j_], r=['HISTC'], slow=True, is_output=True)
                dma('sp', O[pre + 'shift'][l].rearrange("(c p) -> p c", p=128), HISTB[:, l, :, 0], r=['HISTB'], slow=True, is_output=True)
                dma('sp', O[pre + 'lru'][l].rearrange("(c p) -> p c", p=128), HL[:, l, :], r=['HL'], slow=True, is_output=True)
                dma('sp', O[pre + 'mem_c'][l].rearrange("(jp hh) n v -> (hh n) jp v", hh=2), CA[:, l, :, 0:64], r=['CA'], is_output=True)
                dma('sp', O[pre + 'mem_n'][l].rearrange("(jp hh) n -> (hh n) jp", hh=2), CA[:, l, :, 64], r=['CA'], slow=True, is_output=True)
                for j in range(3):
                    tp(PS[4][0:64, j * 128:(j + 1) * 128], HS[:, l, j, :], ident, ['PS4'], ['HS', 'CT'])
                cpy('dve', WS[:], PS[4][0:64, 0:384].rearrange("p (h k) -> p h k", h=6), ['WS'], ['PS4'])
                dma('sp', O[pre + 'wkv'][l].rearrange("h v k -> v h k"), WS[:], r=['WS'], is_output=True)
            dma('sp', O[pre + 'mem_m'].rearrange("l h -> h l"), MS[:], r=['MS'], slow=True, is_output=True)
        P.finish()
        P.emit(nc, st)
    return nc, cnp


_CACHE = {}
TS_DEFAULT = 512
SMALL_DT = BF16


def kernel(**inputs):
    inputs = {k: np.asarray(v) for k, v in inputs.items()}
    xp = inputs['x_prompt']
    xs = inputs['x_sample']
    DEPTH = inputs['norm1'].shape[0]
    B, TP, _ = xp.shape
    TS = min(TS_DEFAULT, TP)
    key = (DEPTH, TP, TS)
    if key not in _CACHE:
        _CACHE[key] = build(DEPTH, TP, TS, SMDT=SMALL_DT)
    nc, cnp = _CACHE[key]
    f32 = lambda a: np.ascontiguousarray(a, dtype=np.float32)
    in_maps = []
    for c in range(NCORES):
        m = {'xp': f32(xp[c % B]), 'xs': f32(xs[c]), 'consts': cnp}
        m['s_conv_a'] = f32(inputs['state_conv_a'][:, c])
        m['s_lru'] = f32(inputs['state_lru'][:, c])
        m['s_shift'] = f32(inputs['state_shift_b'][:, c, 0])
        m['s_wkv'] = f32(inputs['state_wkv'][:, c])
        m['s_conv_c'] = f32(inputs['state_conv_c'][:, c])
        m['s_mem_c'] = f32(inputs['state_mem_c'][:, c])
        m['s_mem_n'] = f32(inputs['state_mem_n'][:, c])
        m['s_mem_m'] = f32(inputs['state_mem_m'][:, c])
        for k in WNAMES:
            m[k] = f32(inputs[k])
        in_maps.append(m)
    res = run_bass_kernel_spmd(nc, in_maps, core_ids=list(range(NCORES)))
    R = res.results
    y_prompt = np.stack([R[b]['yp'] for b in range(B)], 0)
    y_sample = np.stack([R[c]['ys'] for c in range(NCORES)], 0)
    outs = [y_prompt, y_sample]
    names = ['conv_a', 'lru', 'shift', 'wkv', 'conv_c', 'mem_c', 'mem_n', 'mem_m']
    for jb, n in (('p', B), ('s', NCORES)):
        for nm in names:
            a = np.stack([R[c]['o' + jb + '_' + nm] for c in range(n)], 1)
            if nm == 'shift':
                a = a[:, :, None, :]
            outs.append(np.ascontiguousarray(a.astype(np.float32)))
    return tuple(outs)
```

```python
import math
from contextlib import ExitStack
import numpy as np
import concourse.bass as bass
import concourse.mybir as mybir
from concourse.bass_utils import run_bass_kernel_spmd
from concourse.alu_op_type import AluOpType as ALU

AF = mybir.ActivationFunctionType
F32 = mybir.dt.float32
BF16 = mybir.dt.bfloat16
AX = mybir.AxisListType

EPOCH = 16000
NDMA_SLOTS = 10
SAME_ENGINE_SYNC = True
NOSYNC_ENGINES = ('pe',)

D = 1024
DIN = 3072
DA = 256
DB = 384
DBIN = 1408
DC = 384
DFF = 2816
NCORES = 8
TSMP = 32
C_DEC = math.exp(-0.5)
RMS_EPS = 1e-6
GN_EPS_B = 64e-5
GN_EPS_C = 1e-6


class Prog:
    ENG = ['pe', 'act', 'dve', 'pool', 'sp']

    def __init__(self):
        self.stream = {e: [] for e in self.ENG}
        self.n = {e: 0 for e in self.ENG}
        self.lastw = {}
        self.readers = {}
        self.seen = {e: {} for e in self.ENG}
        self.dma_slot_next = {e: 0 for e in self.ENG}
        self.dma_slot_val = {}
        self.out_tokens = []
        self.pe_rg = {}

    def _deps(self, eng, reads, writes, extra=(), force=()):
        toks = list(extra) + list(force)
        forced_src = set(t[0] for t in force)
        for k in reads:
            t = self.lastw.get(k)
            if t:
                toks.append(t)
        for k in writes:
            t = self.lastw.get(k)
            if t:
                toks.append(t)
            toks.extend(self.readers.get(k, {}).values())
        need = {}
        for (src, val) in toks:
            if need.get(src, 0) < val:
                need[src] = val
        out = []
        for src, val in need.items():
            if src == ('e', eng) and (not SAME_ENGINE_SYNC or eng in NOSYNC_ENGINES) and src not in forced_src:
                continue
            if self.seen[eng].get(src, 0) >= val:
                continue
            self.seen[eng][src] = val
            out.append((src, val))
        return out

    def op(self, eng, fn, w=(), r=(), rg=None):
        w = list(w) + [k for k in r if k.startswith('PS') and k[2:].isdigit()]
        force = []
        if eng == 'pe':
            for k in w:
                if k.startswith('PS'):
                    prev = self.pe_rg.get(k)
                    if prev is not None and prev[0] != rg and self.lastw.get(k) == prev[1]:
                        force.append(prev[1])
        waits = self._deps(eng, r, w, force=force)
        self.n[eng] += 1
        tok = (('e', eng), self.n[eng])
        for k in w:
            self.lastw[k] = tok
            self.readers[k] = {}
        for k in r:
            self.readers.setdefault(k, {})[('e', eng)] = tok
        self.stream[eng].append((waits, fn, tok))
        if eng == 'pe':
            for k in w:
                if k.startswith('PS'):
                    self.pe_rg[k] = (rg, tok)
        return tok

    def dma(self, q, fn, w=(), r=(), is_output=False):
        slot = (q, self.dma_slot_next[q] % NDMA_SLOTS)
        self.dma_slot_next[q] += 1
        src = ('d', slot)
        prev = self.dma_slot_val.get(slot, 0)
        extra = [(src, prev)] if prev else []
        waits = self._deps(q, r, w, extra)
        val = prev + 16
        self.dma_slot_val[slot] = val
        tok = (src, val)
        for k in w:
            self.lastw[k] = tok
            self.readers[k] = {}
        for k in r:
            self.readers.setdefault(k, {})[src] = tok
        self.stream[q].append((waits, fn, tok))
        if is_output:
            self.out_tokens.append(tok)
        return tok

    def finish(self):
        need = {}
        for (src, val) in self.out_tokens:
            need[src] = max(need.get(src, 0), val)
        self.stream['sp'].append((list(need.items()), None, None))

    def emit(self, nc, stack):
        sems = {}

        def getsem(src, val):
            if src[0] == 'e':
                ep = (val - 1) // EPOCH
                key = (src, ep)
                v = val - ep * EPOCH
            else:
                key = (src, 0)
                v = val
            if key not in sems:
                sems[key] = stack.enter_context(nc.semaphore("s%d" % len(sems)))
            return sems[key], v

        for e in self.ENG:
            for (waits, fn, tok) in self.stream[e]:
                for (src, val) in waits:
                    getsem(src, val)
                if tok is not None:
                    getsem(*tok)
        block = stack.enter_context(nc.Block())
        names = {'pe': 'tensor', 'act': 'scalar', 'dve': 'vector', 'pool': 'gpsimd', 'sp': 'sync'}
        for e in self.ENG:
            items = self.stream[e]
            if not items:
                continue

            def body(engh, items=items):
                for (waits, fn, tok) in items:
                    for (src, val) in waits:
                        s, v = getsem(src, val)
                        engh.wait_ge(s, v)
                    if fn is None:
                        continue
                    ins = fn(engh)
                    s, v = getsem(*tok)
                    ins.then_inc(s, 16 if tok[0][0] == 'd' else 1)
            getattr(block, names[e])(body)
        self.nsems = len(sems)


def _consts(TS):
    cw = {}
    cols = []

    def add(name, arr):
        a = np.zeros((128, arr.shape[1]), np.float32)
        a[:arr.shape[0]] = arr
        cw[name] = (sum(c.shape[1] for c in cols), arr.shape[1])
        cols.append(a)
    add('ident', np.eye(128, dtype=np.float32))
    bd = np.zeros((128, 128), np.float32)
    bd[:64, :64] = 1
    bd[64:, 64:] = 1
    add('onesbd', bd)
    add('ident2', np.concatenate([np.eye(64, dtype=np.float32)] * 2, 0))
    r = np.arange(128)[:, None] % 64
    c = np.arange(64)[None, :]
    su = (c > r).astype(np.float32)
    sl = (c < r).astype(np.float32)
    ui = (c >= r).astype(np.float32)
    add('mask5', np.concatenate([-su, -sl, ui, su, ui], 1))
    negm = np.where(np.arange(64)[:, None] > c, -30000.0, 0.0).astype(np.float32)
    add('negm6', np.tile(negm, (1, 6)))
    cm = np.ones((128, TS), np.float32)
    cm[:, ::64] = 0
    add('cmask', cm)
    sel6 = np.zeros((6, 6, 64), np.float32)
    for h in range(6):
        sel6[h, h, :] = 1
    add('sel6', sel6.reshape(6, 384))
    selb = np.zeros((6, 128), np.float32)
    for k in range(6):
        selb[k, (k % 2) * 64:(k % 2) * 64 + 64] = 1
    add('selb', selb)
    ps_ = np.zeros((6, 3), np.float32)
    for k in range(6):
        ps_[k, k // 2] = 1
    add('pairsel', ps_)
    return np.concatenate(cols, 1), cw


WNAMES = ['norm1', 'w_in', 'conv_a_w', 'conv_a_b', 'lru_wr', 'lru_br', 'lru_wi', 'lru_bi', 'lru_lambda',
          'norm_a', 'rwkv_mu', 'rwkv_w0', 'rwkv_w2', 'rwkv_a0', 'rwkv_a2', 'rwkv_g2', 'rwkv_kk', 'rwkv_ka',
          'rwkv_rk', 'rwkv_lnw', 'rwkv_lnb', 'conv_c_w', 'conv_c_b', 'mlstm_wq', 'mlstm_wk', 'mlstm_wif',
          'mlstm_bif', 'mlstm_gn', 'w_out', 'norm2', 'w_ffn_in', 'w_ffn_out', 'norm_f']


def build(DEPTH, TP, TS, SMDT=F32, jobs=('p', 's')):
    assert TP % TS == 0 and TS % 128 == 0
    nc = bass.Bass("TRN2", target_bir_lowering=False)
    cnp, cw = _consts(TS)
    CW = cnp.shape[1]

    def din(name, shape):
        return nc.dram_tensor(name, list(shape), F32, kind="ExternalInput").ap()

    def dout(name, shape):
        return nc.dram_tensor(name, list(shape), F32, kind="ExternalOutput").ap()

    I = {}
    I['xp'] = din('xp', [TP, D])
    I['xs'] = din('xs', [TSMP, D])
    I['consts'] = din('consts', [128, CW])
    st_shapes = {'conv_a': [DEPTH, 3, DA], 'lru': [DEPTH, DA], 'shift': [DEPTH, DBIN], 'wkv': [DEPTH, 6, 64, 64],
                 'conv_c': [DEPTH, 3, DC], 'mem_c': [DEPTH, 6, 64, 64], 'mem_n': [DEPTH, 6, 64], 'mem_m': [DEPTH, 6]}
    for k, s in st_shapes.items():
        I['s_' + k] = din('s_' + k, s)
    wshapes = {'norm1': [DEPTH, D], 'w_in': [DEPTH, D, DIN], 'conv_a_w': [DEPTH, 4, DA], 'conv_a_b': [DEPTH, DA],
               'lru_wr': [DEPTH, 4, 64, 64], 'lru_br': [DEPTH, DA], 'lru_wi': [DEPTH, 4, 64, 64], 'lru_bi': [DEPTH, DA],
               'lru_lambda': [DEPTH, DA], 'norm_a': [DEPTH, DA], 'rwkv_mu': [DEPTH, DBIN], 'rwkv_w0': [DEPTH, DB],
               'rwkv_w2': [DEPTH, 64, DB], 'rwkv_a0': [DEPTH, DB], 'rwkv_a2': [DEPTH, 64, DB], 'rwkv_g2': [DEPTH, 128, DB],
               'rwkv_kk': [DEPTH, DB], 'rwkv_ka': [DEPTH, DB], 'rwkv_rk': [DEPTH, 6, 64], 'rwkv_lnw': [DEPTH, DB],
               'rwkv_lnb': [DEPTH, DB], 'conv_c_w': [DEPTH, 4, DC], 'conv_c_b': [DEPTH, DC], 'mlstm_wq': [DEPTH, 6, 64, 64],
               'mlstm_wk': [DEPTH, 6, 64, 64], 'mlstm_wif': [DEPTH, 3 * DC, 12], 'mlstm_bif': [DEPTH, 12],
               'mlstm_gn': [DEPTH, DC], 'w_out': [DEPTH, D, D], 'norm2': [DEPTH, D], 'w_ffn_in': [DEPTH, D, 2 * DFF],
               'w_ffn_out': [DEPTH, DFF, D], 'norm_f': [D]}
    for k in WNAMES:
        I[k] = din(k, wshapes[k])
    O = {}
    O['yp'] = dout('yp', [TP, D])
    O['ys'] = dout('ys', [TSMP, D])
    for jb in ('p', 's'):
        for k, s in st_shapes.items():
            O[jb + '_' + k] = dout('o' + jb + '_' + k, s)

    WSC = {'w_in': nc.dram_tensor('w_in_b', [DEPTH, D, DIN], BF16), 'w_out': nc.dram_tensor('w_out_b', [DEPTH, D, D], BF16),
           'w_ffn_in': nc.dram_tensor('w_ffn_in_b', [DEPTH, D, 2 * DFF], BF16), 'w_ffn_out': nc.dram_tensor('w_ffn_out_b', [DEPTH, DFF, D], BF16)}
    P = Prog()
    with ExitStack() as st:
        def sb(name, shape, dt=F32):
            return st.enter_context(nc.sbuf_tensor(name, list(shape), dt))

        def psum(name, shape, dt=F32):
            return st.enter_context(nc.psum_tensor(name, list(shape), dt))

        def tt(eng, out, in0, in1, op, w, r):
            P.op(eng, lambda e: e.tensor_tensor(out=out, in0=in0, in1=in1, op=op), w, r)

        def ts(eng, out, in0, s1, s2, op0, op1, w, r):
            if op1 is None:
                P.op(eng, lambda e: e.tensor_scalar(out=out, in0=in0, scalar1=s1, scalar2=None, op0=op0), w, r)
            else:
                P.op(eng, lambda e: e.tensor_scalar(out=out, in0=in0, scalar1=s1, scalar2=s2, op0=op0, op1=op1), w, r)

        def stt(out, in0, scalar, in1, op0, op1, w, r):
            P.op('dve', lambda e: e.scalar_tensor_tensor(out=out, in0=in0, scalar=scalar, in1=in1, op0=op0, op1=op1), w, r)

        def act(out, in_, func, w, r, scale=1.0, bias=0.0):
            P.op('act', lambda e: e.activation(out=out, in_=in_, func=func, scale=scale, bias=bias), w, r)

        def cpy(eng, out, in_, w, r):
            if eng == 'act':
                P.op('act', lambda e: e.activation(out=out, in_=in_, func=AF.Copy), w, r)
            else:
                P.op(eng, lambda e: e.tensor_copy(out=out, in_=in_), w, r)

        def _rg(ap):
            n = ap.partition_size()
            return (ap.base_partition(), 32 if n <= 32 else (64 if n <= 64 else 128))

        def mm(out, lhsT, rhs, start, stop, w, r):
            P.op('pe', lambda e: e.matmul(out, lhsT=lhsT, rhs=rhs, start=start, stop=stop), w, r, rg=_rg(lhsT))

        def tp(out, in_, ident, w, r):
            P.op('pe', lambda e: e.transpose(out, in_, ident), w, r, rg=_rg(in_))

        def recip(out, in_, w, r):
            P.op('dve', lambda e: e.reciprocal(out=out, in_=in_), w, r)

        def scan(out, d0, d1, init, op0, op1, w, r):
            P.op('dve', lambda e: e.tensor_tensor_scan(out=out, data0=d0, data1=d1, initial=init, op0=op0, op1=op1), w, r)

        def red(out, in_, w, r):
            P.op('dve', lambda e: e.tensor_reduce(out=out, in_=in_, axis=AX.X, op=ALU.add), w, r)

        def memset(eng, ap, val, w):
            P.op(eng, lambda e: e.memset(ap, val), w, ())

        def dma(q, out, in_, w=(), r=(), slow=False, is_output=False):
            if slow:
                P.dma(q, lambda e: e.dma_start(out=out, in_=in_, allow_slow_non_contiguous=True), w, r, is_output)
            else:
                P.dma(q, lambda e: e.dma_start(out=out, in_=in_), w, r, is_output)

        CT = sb('CT', [128, CW])
        dma('sp', CT[:], I['consts'], w=['CT'])

        def cst(name, rows=128):
            o, n = cw[name]
            return CT[0:rows, o:o + n]
        ident = cst('ident')
        ident2 = cst('ident2')
        onesbd_f = cst('onesbd')
        cmask = cst('cmask')
        onesb = sb('onesb', [128, 128], BF16)
        memset('dve', onesb[:], 1.0, ['onesb'])
        onesbd_b = sb('onesbd_b', [128, 128], BF16)
        cpy('dve', onesbd_b[:], onesbd_f, ['onesbd_b'], ['CT'])
        ones_c = sb('ones_c', [128, 1])
        memset('dve', ones_c[:], 1.0, ['ones_f'])
        if SMDT == F32:
            mask5 = cst('mask5').rearrange("p (b l) -> p b l", b=5)
            ident_s = ident
            mk5key = 'CT'
        else:
            mask5t = sb('mask5t', [128, 5, 64], SMDT)
            cpy('dve', mask5t[:], cst('mask5').rearrange("p (b l) -> p b l", b=5), ['mask5t'], ['CT'])
            mask5 = mask5t[:]
            ident_st = sb('ident_st', [128, 128], SMDT)
            cpy('dve', ident_st[:], ident, ['ident_st'], ['CT'])
            ident_s = ident_st[:]
            mk5key = 'mask5t'
        negm6 = cst('negm6', 64).rearrange("p (h l) -> p h l", h=6)
        sel6 = cst('sel6', 6).rearrange("p (h l) -> p h l", h=6)
        selb = cst('selb', 6)
        pairsel = cst('pairsel', 6)

        NV = 88
        PV = sb('PV', [128, DEPTH, NV])
        NF = sb('NF', [128, 8])
        BI = sb('BI', [6, DEPTH])
        NBF = sb('NBF', [6, DEPTH])

        def pvload(name, col, n):
            for l in range(DEPTH):
                dma('sp', PV[:, l, col:col + n], I[name][l].rearrange("(c p) -> p c", p=128), w=['PV'], slow=True)
        pvload('norm1', 0, 8)
        pvload('norm2', 8, 16 - 8)
        for l in range(DEPTH):
            for j in range(4):
                dma('sp', PV[:, l, 16 + 2 * j:18 + 2 * j], I['conv_a_w'][l, j].rearrange("(c p) -> p c", p=128), w=['PV'], slow=True)
                dma('sp', PV[:, l, 66 + 3 * j:69 + 3 * j], I['conv_c_w'][l, j].rearrange("(c p) -> p c", p=128), w=['PV'], slow=True)
        pvload('conv_a_b', 24, 2)
        pvload('lru_br', 26, 2)
        pvload('lru_bi', 28, 2)
        pvload('lru_lambda', 30, 2)
        pvload('norm_a', 32, 2)
        pvload('rwkv_mu', 34, 11)
        pvload('rwkv_w0', 45, 3)
        pvload('rwkv_a0', 48, 3)
        pvload('rwkv_kk', 51, 3)
        pvload('rwkv_ka', 54, 3)
        for l in range(DEPTH):
            dma('sp', PV[:, l, 57:60], I['rwkv_rk'][l].rearrange("(c hh) n -> (hh n) c", hh=2), w=['PV'], slow=True)
        pvload('rwkv_lnw', 60, 3)
        pvload('rwkv_lnb', 63, 3)
        pvload('conv_c_b', 78, 3)
        pvload('mlstm_gn', 81, 3)
        dma('sp', NF[:], I['norm_f'].rearrange("(c p) -> p c", p=128), w=['NF'], slow=True)
        dma('sp', BI[:], I['mlstm_bif'][:, 0:6].rearrange("l h -> h l"), w=['BI'], slow=True)
        dma('sp', NBF[:], I['mlstm_bif'][:, 6:12].rearrange("l h -> h l"), w=['NBF'], slow=True)
        ts('dve', NBF[:], NBF[:], -1.0, None, ALU.mult, None, ['NBF'], ['NBF'])
        SPT = sb('SPT', [128, DEPTH, 2])
        act(SPT[:], PV[:, :, 30:32], AF.Exp, ['SPT'], ['PV'], scale=-1.0)
        act(SPT[:], SPT[:], AF.Ln, ['SPT'], ['SPT'], bias=1.0)
        ts('dve', PV[:, :, 84:86], SPT[:], -8.0, None, ALU.mult, None, ['PV'], ['SPT'])
        ts('dve', PV[:, :, 86:88], SPT[:], -16.0, None, ALU.mult, None, ['PV'], ['SPT'])

        WRbd = sb('WRbd', [128, DEPTH, 2, 128])
        WIbd = sb('WIbd', [128, DEPTH, 2, 128])
        memset('dve', WRbd[:], 0.0, ['WRbd'])
        memset('dve', WIbd[:], 0.0, ['WIbd'])
        WQbd = sb('WQbd', [128, DEPTH, 3, 128], BF16)
        WKbd = sb('WKbd', [128, DEPTH, 3, 128], BF16)
        memset('dve', WQbd[:], 0.0, ['WQbd'])
        memset('dve', WKbd[:], 0.0, ['WKbd'])
        W2 = sb('W2', [128, DEPTH, DB], BF16)
        A2 = sb('A2', [128, DEPTH, DB], BF16)
        G2 = sb('G2', [128, DEPTH, DB], BF16)
        WIF = sb('WIF', [128, DEPTH, 9, 12], SMDT)
        for l in range(DEPTH):
            for n in range(4):
                hb, c = n % 2, n // 2
                dma('sp', WRbd[64 * hb:64 * hb + 64, l, c, 64 * hb:64 * hb + 64], I['lru_wr'][l, n], w=['WRbd'])
                dma('sp', WIbd[64 * hb:64 * hb + 64, l, c, 64 * hb:64 * hb + 64], I['lru_wi'][l, n], w=['WIbd'])
            for h in range(6):
                hb, c = h % 2, h // 2
                dma('pool', WQbd[64 * hb:64 * hb + 64, l, c, 64 * hb:64 * hb + 64], I['mlstm_wq'][l, h], w=['WQbd'])
                dma('pool', WKbd[64 * hb:64 * hb + 64, l, c, 64 * hb:64 * hb + 64], I['mlstm_wk'][l, h], w=['WKbd'])
            dma('pool', W2[0:64, l, :], I['rwkv_w2'][l], w=['W2'])
            dma('pool', A2[64:128, l, :], I['rwkv_a2'][l], w=['A2'])
            dma('pool', G2[:, l, :], I['rwkv_g2'][l], w=['G2'])
            dma('pool' if SMDT != F32 else 'sp', WIF[:, l, :, :], I['mlstm_wif'][l].rearrange("(kc p) n -> p kc n", p=128), w=['WIF'])
        ts('dve', WIF[:, :, 3:6, :], WIF[:, :, 3:6, :], 8.0, None, ALU.mult, None, ['WIF'], ['WIF'])

        HISTA = sb('HISTA', [128, DEPTH, 2, 3])
        HISTB = sb('HISTB', [128, DEPTH, 11, 1])
        HISTC = sb('HISTC', [128, DEPTH, 3, 3])
        HL = sb('HL', [128, DEPTH, 2])
        HS = sb('HS', [128, DEPTH, 3, 64])
        CA = sb('CA', [128, DEPTH, 3, 65])
        MS = sb('MS', [6, DEPTH])
        WS = sb('WS', [64, 6, 64])

        X = sb('X', [128, 8, TS])
        HN = sb('HN', [128, 8, TS], BF16)
        NSLOT = 11
        PJ = sb('PJ', [128, NSLOT, 3 + TS])
        YM = sb('YM', [128, 8, TS], BF16)
        assert NSLOT * (3 + TS) * 4 >= 22 * TS * 2
        ACTT = PJ[:].rearrange("p s c -> p (s c)").bitcast(BF16)[:, 0:22 * TS].rearrange("p (f t) -> p f t", f=22)
        def actk(f):
            b0, b1 = f * TS * 2, (f + 1) * TS * 2 - 1
            sl_ = (3 + TS) * 4
            return ['PJ%d' % s_ for s_ in range(b0 // sl_, b1 // sl_ + 1)]
        NWB = 2
        WB = [sb('WB%d' % i, [128, 4096], BF16) for i in range(NWB)]
        NTF = 14
        TF = [sb('TF%d' % i, [128, TS]) for i in range(NTF)]
        NTB = 5
        TB = [sb('TB%d' % i, [128, TS], BF16) for i in range(NTB)]
        KR = sb('KR', [128, TS // 64, 2, 64], SMDT)
        KT_ = sb('KT_', [128, TS], SMDT)
        BT_ = sb('BT_', [128, TS], SMDT)
        VS_ = sb('VS_', [128, TS], SMDT)
        W5 = sb('W5', [64, 2, 5, 64], SMDT)
        QMa = sb('QMa', [64, 2, 2, 64], SMDT)
        QMb = sb('QMb', [64, 2, 2, 64], SMDT)
        TTa = sb('TTa', [64, 2, 2, 64], SMDT)
        TTb = sb('TTb', [64, 2, 2, 64], SMDT)
        TM = sb('TM', [64, 3, 128], SMDT)
        XN = sb('XN', [64, 2, 64], SMDT)
        UU = sb('UU', [64, 2, 64], SMDT)
        HG = sb('HG', [128, 64])
        HSs = sb('HSs', [128, 3, 64], SMDT) if SMDT != F32 else None
        CAs = sb('CAs', [128, 3, 65], SMDT) if SMDT != F32 else None
        QKVb = [sb('QKV%d' % i, [128, TS], SMDT) for i in range(9)] if SMDT != F32 else None
        VA = sb('VA', [64, 6, 65], SMDT)
        KTm = sb('KTm', [64, 6, 64], SMDT)
        WE = sb('WE', [64, 2, 6])
        DD = sb('DD', [64, 6, 64])
        STt = sb('STt', [64, 6, 64], SMDT)
        T1 = sb('T1', [64, 6, 65])
        NUM = sb('NUM', [64, 6, 65])
        DEN = sb('DEN', [64, 6])
        HTt = sb('HTt', [64, 6, 64])
        HC = sb('HC', [64, 6, 64])
        SQ = sb('SQ', [64, 6, 64])
        MEAN = sb('MEAN', [64, 6])
        VAR = sb('VAR', [64, 6])
        VW = sb('VW', [64, 6, 65], SMDT)
        WLE = sb('WLE', [6, 3])
        W0 = sb('W0', [128, 3])
        CT2 = sb('CT2', [128, 3, 65])
        RS = TF[13]
        GC = [sb('GC%d' % i, [6, TS]) for i in range(4)]
        YCt = sb('YCt', [128, 3, 64])
        SZ = [sb('SZ%d' % i, [128, TS], BF16) for i in range(3)]
        OVERLAP = (SMDT != F32)
        if TS >= 512:
            STG = [(TF[11], 'TF11'), (TF[12], 'TF12')]
        else:
            STG = [(sb('STG0', [128, 512]), 'STG0'), (sb('STG1', [128, 512]), 'STG1')]
        if SMDT == F32:
            PS = [psum('PS%d' % i, [128, 512]) for i in range(8)]
            PTA, kPTA, PTB, kPTB = PS[7], 'PS7', PS[4], 'PS4'
        else:
            PS = [psum('PS%d' % i, [128, 512]) for i in range(7)]
            PSB = psum('PSB', [128, 1024], SMDT)
            PTA, kPTA, PTB, kPTB = PSB[:, 0:512], 'PS7', PSB[:, 512:1024], 'PS7'

        memset('dve', VA[:], 1.0, ['VA'])
        class Slot:
            pass

        def mkslot(si):
            S = Slot()
            sfx = '' if si == 0 else 'b'
            S.k = lambda nm: nm + sfx
            if si == 0:
                S.W5, S.QMa, S.QMb, S.TTa, S.TTb, S.TM, S.XN, S.UU, S.HG = W5, QMa, QMb, TTa, TTb, TM, XN, UU, HG
                S.KR, S.KT_, S.BT_, S.VS_ = KR, KT_, BT_, VS_
                S.Y, S.BON, S.G = TF[12], TF[10], TF[5]
                S.k = lambda nm: {'Y': 'TF12', 'BON': 'TF10', 'G': 'TF5'}.get(nm, nm)
                S.pg, S.kpg, S.pq, S.kpq, S.pt, S.kpt = [PS[4], PS[5]], ['PS4', 'PS5'], PS[6], 'PS6', PTA, kPTA
            else:
                S.W5 = sb('W5b', [64, 2, 5, 64], SMDT)
                S.QMa, S.QMb, S.TTa, S.TTb = [sb(n_ + 'b', [64, 2, 2, 64], SMDT) for n_ in ('QMa', 'QMb', 'TTa', 'TTb')]
                S.TM = sb('TMb', [64, 3, 128], SMDT)
                S.XN, S.UU = sb('XNb', [64, 2, 64], SMDT), sb('UUb', [64, 2, 64], SMDT)
                S.HG = sb('HGb', [128, 64])
                S.KR = sb('KRb', [128, TS // 64, 2, 64], SMDT)
                S.KT_, S.BT_, S.VS_ = [sb(n_ + 'b', [128, TS], SMDT) for n_ in ('KT_', 'BT_', 'VS_')]
                S.Y = sb('Yb', [128, TS])
                S.BON, S.G = sb('BONb', [128, TS], BF16), sb('Gb', [128, TS], BF16)
                S.pg, S.kpg, S.pq, S.kpq, S.pt, S.kpt = [PS[0], PS[1]], ['PS0', 'PS1'], PS[2], 'PS2', PTB, kPTB
            S.GL = sb('GL%d' % si, [128, max(TS // 64, 1)])
            return S
        SL = [mkslot(0)] + ([mkslot(1)] if OVERLAP else [])

        def wconvert(l):
            for nm in ('w_in', 'w_out', 'w_ffn_in', 'w_ffn_out'):
                dma('pool', WSC[nm].ap()[l].rearrange("(k p) n -> p k n", p=128), I[nm][l].rearrange("(k p) n -> p k n", p=128),
                    w=['DW_%s%d' % (nm, l)])
        wconvert(0)
        wstate = {'i': 0}

        def wload(parts):
            i = wstate['i'] % NWB
            wstate['i'] += 1
            key = 'WB%d' % i
            for (c0, KC, ncol, src, rk) in parts:
                dst = WB[i][:, c0:c0 + KC * ncol].rearrange("p (k n) -> p k n", k=KC)
                dma('pool', dst, src, w=[key], r=[rk])
            return WB[i], key

        def win_src(l, c0, ncol):
            return WSC['w_in'].ap()[l][:, c0:c0 + ncol].rearrange("(k p) n -> p k n", p=128), 'DW_w_in%d' % l

        def rmsnorm_to(T, gcol_base, l, out_fn):
            for k in range(8):
                act(TB[0][:, 0:T] if k % 2 == 0 else TB[1][:, 0:T], X[:, k, 0:T], AF.Square,
                    ['TB%d' % (k % 2)], ['X%d' % k])
                mm(PS[0][:, 0:T], onesb[:], TB[k % 2][:, 0:T], k == 0, k == 7, ['PS0'], ['onesb', 'TB%d' % (k % 2)])
            act(RS[:, 0:T], PS[0][:, 0:T], AF.Sqrt, ['TF13'], ['PS0'], scale=1.0 / D, bias=RMS_EPS)
            recip(RS[:, 0:T], RS[:, 0:T], ['TF13'], ['TF13'])
            for k in range(8):
                out_fn(k)

        def inproj(l, T, wt, wkey, ncols_tile, chunk_list):
            for ci, (cit, slot) in enumerate(chunk_list):
                bank = ci % 4
                for k in range(8):
                    mm(PS[bank][:, 0:T], wt[:, k * ncols_tile + cit * 128: k * ncols_tile + cit * 128 + 128],
                       HN[:, k, 0:T], k == 0, k == 7, ['PS%d' % bank], [wkey] + ['HN%d' % k])
                cpy('act' if ci % 2 == 0 else 'dve', PJ[:, slot, 3:3 + T], PS[bank][:, 0:T], ['PJ%d' % slot], ['PS%d' % bank])

        def pv(l, c):
            return PV[:, l, c:c + 1]

        def task(job, l, T, L, tok0, first_seg, last_seg, xin, yout, last_layer):
            NCH = T // L
            if job == jobs[0] and first_seg and l + 1 < DEPTH:
                wconvert(l + 1)
            if l == 0:
                nblk = (T + 127) // 128
                for tb in range(nblk):
                    n = min(128, T - tb * 128)
                    for half in range(2):
                        xt, xk = STG[half]
                        bank = 4 + half
                        dma('sp', xt[0:n, 0:512], xin[tok0 + tb * 128: tok0 + tb * 128 + n, half * 512:(half + 1) * 512], w=[xk])
                        for mmi in range(4):
                            tp(PS[bank][:, mmi * 128: mmi * 128 + n], xt[0:n, mmi * 128:(mmi + 1) * 128], ident[0:n, 0:n],
                               ['PS%d' % bank], [xk, 'CT'])
                        for mmi in range(4):
                            m = half * 4 + mmi
                            cpy('act' if mmi % 2 else 'dve', X[:, m, tb * 128: tb * 128 + n],
                                PS[bank][:, mmi * 128: mmi * 128 + n], ['X%d' % m], ['PS%d' % bank])
            def o1(k):
                stt(HN[:, k, 0:T], X[:, k, 0:T], pv(l, k), RS[:, 0:T], ALU.mult, ALU.mult, ['HN%d' % k], ['X%d' % k, 'PV', 'TF13'])
            rmsnorm_to(T, 0, l, o1)

            wt, wkey = wload([(0, 8, 512) + win_src(l, 0, 512)])
            cpy('dve', PJ[:, 0:2, 0:3], HISTA[:, l, :, :], ['PJ0', 'PJ1'], ['HISTA'])
            inproj(l, T, wt, wkey, 512, [(0, 0), (1, 1), (2, 2), (3, 3)])
            cpy('dve', HISTA[:, l, :, :], PJ[:, 0:2, T:T + 3], ['HISTA'], ['PJ0', 'PJ1'])
            for c in range(2):
                xa, gr, gi, aa, a2t, mu_, uu, hh = TF[0], TF[1], TF[2], TF[3], TF[4], TF[5], TF[6], TF[7 + c]
                act(xa[:, 0:T], PJ[:, c, 3:3 + T], AF.Identity, ['TF0'], ['PJ%d' % c, 'PV'],
                    scale=pv(l, 16 + 2 * 3 + c), bias=pv(l, 24 + c))
                for j in range(3):
                    stt(xa[:, 0:T], PJ[:, c, j:j + T], pv(l, 16 + 2 * j + c), xa[:, 0:T], ALU.mult, ALU.add,
                        ['TF0'], ['TF0', 'PJ%d' % c, 'PV'])
                mm(PS[0][:, 0:T], WRbd[:, l, c, :], xa[:, 0:T], True, True, ['PS0'], ['WRbd', 'TF0'])
                mm(PS[1][:, 0:T], WIbd[:, l, c, :], xa[:, 0:T], True, True, ['PS1'], ['WIbd', 'TF0'])
                act(gr[:, 0:T], PS[0][:, 0:T], AF.Sigmoid, ['TF1'], ['PS0', 'PV'], bias=pv(l, 26 + c))
                act(gi[:, 0:T], PS[1][:, 0:T], AF.Sigmoid, ['TF2'], ['PS1', 'PV'], bias=pv(l, 28 + c))
                act(aa[:, 0:T], gr[:, 0:T], AF.Exp, ['TF3'], ['TF1', 'PV'], scale=pv(l, 84 + c))
                act(a2t[:, 0:T], gr[:, 0:T], AF.Exp, ['TF4'], ['TF1', 'PV'], scale=pv(l, 86 + c))
                act(mu_[:, 0:T], a2t[:, 0:T], AF.Sqrt, ['TF5'], ['TF4'], scale=-1.0, bias=1.0)
                tt('dve', uu[:, 0:T], gi[:, 0:T], xa[:, 0:T], ALU.mult, ['TF6'], ['TF2', 'TF0'])
                tt('dve', uu[:, 0:T], uu[:, 0:T], mu_[:, 0:T], ALU.mult, ['TF6'], ['TF6', 'TF5'])
                scan(hh[:, 0:T], aa[:, 0:T], uu[:, 0:T], HL[:, l, c:c + 1], ALU.mult, ALU.add,
                     ['TF%d' % (7 + c)], ['TF3', 'TF6', 'HL'])
                cpy('dve', HL[:, l, c:c + 1], hh[:, T - 1:T], ['HL'], ['TF%d' % (7 + c)])
                act(TB[c][:, 0:T], hh[:, 0:T], AF.Square, ['TB%d' % c], ['TF%d' % (7 + c)])
            for c in range(2):
                mm(PS[2][:, 0:T], onesb[:], TB[c][:, 0:T], c == 0, c == 1, ['PS2'], ['onesb', 'TB%d' % c])
            act(TF[9][:, 0:T], PS[2][:, 0:T], AF.Sqrt, ['TF9'], ['PS2'], scale=1.0 / DA, bias=RMS_EPS)
            recip(TF[9][:, 0:T], TF[9][:, 0:T], ['TF9'], ['TF9'])
            for c in range(2):
                g = PJ[:, 2 + c, 3:3 + T]
                gk = 'PJ%d' % (2 + c)
                t1, t2 = TF[0], TF[1]
                act(t1[:, 0:T], g, AF.Square, ['TF0'], [gk])
                ts('dve', t1[:, 0:T], t1[:, 0:T], 0.044715, 1.0, ALU.mult, ALU.add, ['TF0'], ['TF0'])
                tt('dve', t1[:, 0:T], t1[:, 0:T], g, ALU.mult, ['TF0'], ['TF0', gk])
                act(t2[:, 0:T], t1[:, 0:T], AF.Sigmoid, ['TF1'], ['TF0'], scale=2.0 * math.sqrt(2.0 / math.pi))
                tt('dve', t2[:, 0:T], t2[:, 0:T], g, ALU.mult, ['TF1'], ['TF1', gk])
                stt(t1[:, 0:T], TF[7 + c][:, 0:T], pv(l, 32 + c), TF[9][:, 0:T], ALU.mult, ALU.mult,
                    ['TF0'], ['TF%d' % (7 + c), 'PV', 'TF9'])
                tt('dve', YM[:, c, 0:T], t1[:, 0:T], t2[:, 0:T], ALU.mult, ['YM%d' % c], ['TF0', 'TF1'])

            def mix(out, slot, mcol, w):
                tt('dve', out, PJ[:, slot, 2:2 + T], PJ[:, slot, 3:3 + T], ALU.subtract, w, ['PJ%d' % slot])
                stt(out, out, pv(l, 34 + mcol), PJ[:, slot, 3:3 + T], ALU.mult, ALU.add, w, w + ['PJ%d' % slot, 'PV'])
            LW, SG = TB[2], TB[3]
            def preB():
                cpy('dve', PJ[:, 0:11, 2:3], HISTB[:, l, :, :], ['PJ%d' % s for s in range(0, 11)], ['HISTB'])
                for (c0, nch_, s0) in ((512, 4, 0), (1024, 4, 4), (1536, 3, 8)):
                    wt, wkey = wload([(0, 8, nch_ * 128) + win_src(l, c0, nch_ * 128)])
                    inproj(l, T, wt, wkey, nch_ * 128, [(i, s0 + i) for i in range(nch_)])
                cpy('dve', HISTB[:, l, :, :], PJ[:, 0:11, T + 2:T + 3], ['HISTB'], ['PJ%d' % s for s in range(0, 11)])

                mix(TF[0][:, 0:T], 9, 9, ['TF0'])
                act(LW[0:64, 0:T], TF[0][0:64, 0:T], AF.Tanh, ['TB2'], ['TF0'])
                act(LW[64:128, 0:T], TF[0][64:128, 0:T], AF.Copy, ['TB2'], ['TF0'])
                mix(TF[0][:, 0:T], 10, 10, ['TF0'])
                act(SG[:, 0:T], TF[0][:, 0:T], AF.Sigmoid, ['TB3'], ['TF0'])

            def prepB(j, S):
                R, K, V, SGW, A, G_, KK, RN, K2, BV, BON_, CL = [TF[i] for i in range(12)]
                G, BON, Y = S.G, S.BON, S.Y
                kR, kK, kV, kSGW, kA, kG_, kKK, kRN, kK2, kBV, kBON_, kCL = ['TF%d' % i for i in range(12)]
                kG, kBON, kY = S.k('G'), S.k('BON'), S.k('Y')
                mix(R[:, 0:T], j, j, [kR])
                mix(K[:, 0:T], 3 + j, 3 + j, [kK])
                mix(V[:, 0:T], 6 + j, 6 + j, [kV])
                jc = slice(j * 128, (j + 1) * 128)
                mm(PS[4][:, 0:T], W2[0:64, l, jc], LW[0:64, 0:T], True, True, ['PS4'], ['W2', 'TB2'])
                act(SGW[:, 0:T], PS[4][:, 0:T], AF.Sigmoid, [kSGW], ['PS4', 'PV'], bias=pv(l, 45 + j))
                mm(PS[5][:, 0:T], A2[64:128, l, jc], LW[64:128, 0:T], True, True, ['PS5'], ['A2', 'TB2'])
                act(A[:, 0:T], PS[5][:, 0:T], AF.Sigmoid, [kA], ['PS5', 'PV'], bias=pv(l, 48 + j))
                mm(PS[6][:, 0:T], G2[:, l, jc], SG[:, 0:T], True, True, ['PS6'], ['G2', 'TB3'])
                cpy('act', G[:, 0:T], PS[6][:, 0:T], [kG], ['PS6'])
                ts('dve', KK[:, 0:T], K[:, 0:T], pv(l, 51 + j), None, ALU.mult, None, [kKK], [kK, 'PV'])
                act(TB[4][:, 0:T], KK[:, 0:T], AF.Square, ['TB4'], [kKK])
                mm(PS[6][:, 0:T], onesbd_b[:], TB[4][:, 0:T], True, True, ['PS6'], ['onesbd_b', 'TB4'])
                act(RN[:, 0:T], PS[6][:, 0:T], AF.Sqrt, [kRN], ['PS6'])
                ts('dve', RN[:, 0:T], RN[:, 0:T], 1e-12, None, ALU.max, None, [kRN], [kRN])
                recip(RN[:, 0:T], RN[:, 0:T], [kRN], [kRN])
                tt('dve', KK[:, 0:T], KK[:, 0:T], RN[:, 0:T], ALU.mult, [kKK], [kKK, kRN])
                ts('dve', K2[:, 0:T], A[:, 0:T], -1.0, pv(l, 54 + j), ALU.add, ALU.mult, [kK2], [kA, 'PV'])
                stt(K2[:, 0:T], K2[:, 0:T], 1.0, K[:, 0:T], ALU.add, ALU.mult, [kK2], [kK2, kK])
                tt('dve', BV[:, 0:T], KK[:, 0:T], A[:, 0:T], ALU.mult, [kBV], [kKK, kA])
                tt('dve', RN[:, 0:T], R[:, 0:T], K2[:, 0:T], ALU.mult, [kRN], [kR, kK2])
                ts('dve', TB[4][:, 0:T], RN[:, 0:T], pv(l, 57 + j), None, ALU.mult, None, ['TB4'], [kRN, 'PV'])
                mm(PS[6][:, 0:T], onesbd_b[:], TB[4][:, 0:T], True, True, ['PS6'], ['onesbd_b', 'TB4'])
                tt('dve', BON[:, 0:T], PS[6][:, 0:T], V[:, 0:T], ALU.mult, [kBON], ['PS6', kV])
                scan(CL[:, 0:T], cmask[:, 0:T], SGW[:, 0:T], 0.0, ALU.mult, ALU.add, [kCL], ['CT', kSGW])
                EG, EGI, EGX = TF[13], RN, K
                kEG, kEGI = 'TF13', kRN
                act(EG[:, 0:T], CL[:, 0:T], AF.Exp, [kEG], [kCL], scale=-C_DEC)
                act(EGI[:, 0:T], CL[:, 0:T], AF.Exp, [kEGI], [kCL], scale=C_DEC)
                tt('dve', SGW[:, 0:T], CL[:, 0:T], SGW[:, 0:T], ALU.subtract, [kSGW], [kCL, kSGW])
                act(SGW[:, 0:T], SGW[:, 0:T], AF.Exp, [kSGW], [kSGW], scale=-C_DEC)
                v3 = lambda ap: ap.rearrange("p (n l) -> p n l", l=L)
                tt('dve', S.KR[:, 0:NCH, 1, 0:L], v3(R[:, 0:T]), v3(EG[:, 0:T]), ALU.mult, [S.k('KR')], [kR, kEG])
                tt('dve', S.KR[:, 0:NCH, 0, 0:L], v3(KK[:, 0:T]), v3(SGW[:, 0:T]), ALU.mult, [S.k('KR')], [kKK, kSGW])
                tt('dve', S.KT_[:, 0:T], K2[:, 0:T], EGI[:, 0:T], ALU.mult, [S.k('KT_')], [kK2, kEGI])
                tt('dve', S.BT_[:, 0:T], BV[:, 0:T], EGI[:, 0:T], ALU.mult, [S.k('BT_')], [kBV, kEGI])
                cpy('act', S.VS_[:, 0:T], V[:, 0:T], [S.k('VS_')], [kV])
                cpy('dve', S.GL[:, 0:NCH], EG[:, 0:T].rearrange("p (n l) -> p n l", l=L)[:, :, L - 1], [S.k('GL')], [kEG])
                Hm = HS[:, l, j, :]
                if SMDT != F32:
                    cpy('dve', HSs[:, j, :], Hm, ['HSs'], ['HS'])
                    Hs = HSs[:, j, :]
                    hsk = 'HSs'
                else:
                    Hs = Hm
                    hsk = 'HS'
                return dict(Hm=Hm, Hs=Hs, hsk=hsk)


            def loopB(j, S, pp):
                Hm, Hs, hsk = pp['Hm'], pp['Hs'], pp['hsk']
                for n in range(NCH):
                    cs = slice(n * L, (n + 1) * L)
                    yield
                    yield
                    for hh in range(2):
                        rw = slice(64 * hh, 64 * hh + 64)
                        pg = S.pg[hh]
                        pk = S.kpg[hh]
                        mm(pg[0:L, 0:L], S.BT_[rw, cs], S.KR[rw, n, 0, 0:L], True, True, [pk], [S.k('BT_'), S.k('KR')])
                        mm(pg[0:L, 64:64 + L], S.KR[rw, n, 0, 0:L], S.BT_[rw, cs], True, True, [pk], [S.k('BT_'), S.k('KR')])
                        mm(pg[0:L, 128:128 + L], S.BT_[rw, cs], S.KR[rw, n, 1, 0:L], True, True, [pk], [S.k('BT_'), S.k('KR')])
                        mm(pg[0:L, 192:192 + L], S.KT_[rw, cs], S.KR[rw, n, 0, 0:L], True, True, [pk], [S.k('KT_'), S.k('KR')])
                        mm(pg[0:L, 256:256 + L], S.KT_[rw, cs], S.KR[rw, n, 1, 0:L], True, True, [pk], [S.k('KT_'), S.k('KR')])
                        tt('dve', S.W5[:, hh, :, :], pg[0:64, 0:320].rearrange("p (b l) -> p b l", b=5),
                           mask5[0:64, :, :], ALU.mult, [S.k('W5')], [pk, mk5key])
                    yield
                    yield
                    for bi, (src, sk) in enumerate(((S.KT_, S.k('KT_')), (S.BT_, S.k('BT_')), (S.VS_, S.k('VS_')))):
                        tp(S.pt[0:L, bi * 128:(bi + 1) * 128], src[:, cs], ident_s, [S.kpt], [sk, 'CT', 'ident_st'])
                    yield
                    cpy('act', S.TM[:, :, :], S.pt[0:64, 0:384].rearrange("p (b l) -> p b l", b=3), [S.k('TM')], [S.kpt])
                    yield
                    yield
                    for hh in range(2):
                        tt('dve', S.TTa[:, hh, :, :], S.W5[:, hh, 0:2, :],
                           ident[0:64, 0:64].rearrange("p (o l) -> p o l", o=1).to_broadcast([64, 2, 64]),
                           ALU.add, [S.k('TTa')], [S.k('W5'), 'CT'])
                    yield
                    qm_cur, qk, qoff = S.W5, S.k('W5'), 0
                    yield
                    tt_cur, tk = S.TTa, S.k('TTa')
                    yield
                    nlev = int(math.log2(L)) - 1
                    yield
                    for lev in range(nlev):
                        lastl = (lev == nlev - 1)
                        qm_nx, qnk = (S.QMa, S.k('QMa')) if lev % 2 == 0 else (S.QMb, S.k('QMb'))
                        tt_nx, tnk = (S.TTb, S.k('TTb')) if lev % 2 == 0 else (S.TTa, S.k('TTa'))
                        for hh in range(2):
                            mm(S.pq[0:L, hh * 128:hh * 128 + L], qm_cur[0:L, hh, 1, 0:L], qm_cur[0:L, hh, 0, 0:L], True, True, [S.kpq], [qk])
                            if not lastl:
                                mm(S.pq[0:L, hh * 128 + 64:hh * 128 + 64 + L], qm_cur[0:L, hh, 0, 0:L], qm_cur[0:L, hh, 1, 0:L], True, True, [S.kpq], [qk])
                        yield
                        cpy('act', qm_nx[:].rearrange("p a b l -> p (a b l)"), S.pq[0:64, 0:256], [qnk], [S.kpq])
                        yield
                        for hh in range(2):
                            mm(S.pg[1][0:L, hh * 128:hh * 128 + L], tt_cur[0:L, hh, 1, 0:L], qm_nx[0:L, hh, 0, 0:L], True, True, [S.kpg[1]], [tk, qnk])
                            if not lastl:
                                mm(S.pg[1][0:L, hh * 128 + 64:hh * 128 + 64 + L], qm_nx[0:L, hh, 0, 0:L], tt_cur[0:L, hh, 1, 0:L], True, True, [S.kpg[1]], [tk, qnk])
                        yield
                        tt('dve', tt_nx[:].rearrange("p a b l -> p (a b l)"), S.pg[1][0:64, 0:256],
                           tt_cur[:].rearrange("p a b l -> p (a b l)"), ALU.add, [tnk], [S.kpg[1], tk])
                        qm_cur, qk = qm_nx, qnk
                        tt_cur, tk = tt_nx, tnk
                        yield
                    yield
                    yield
                    for hh in range(2):
                        rw = slice(64 * hh, 64 * hh + 64)
                        fo = slice(64 * hh, 64 * hh + 64)
                        mm(S.pg[1][0:L, 384 + 64 * hh:448 + 64 * hh], S.KR[rw, n, 0, 0:L], Hs[rw, :], True, False, [S.kpg[1]], [S.k('KR'), hsk])
                        mm(S.pg[1][0:L, 384 + 64 * hh:448 + 64 * hh], S.W5[0:L, hh, 3, 0:L], S.TM[0:L, 2, fo], False, True, [S.kpg[1]], [S.k('W5'), S.k('TM')])
                    yield
                    act(S.XN[:].rearrange("p a v -> p (a v)"), S.pg[1][0:64, 384:512], AF.Copy, [S.k('XN')], [S.kpg[1]], scale=-1.0)
                    yield
                    yield
                    for hh in range(2):
                        mm(S.pg[1][0:L, 384 + 64 * hh:448 + 64 * hh], tt_cur[0:L, hh, 0, 0:L], S.XN[0:L, hh, :], True, True, [S.kpg[1]], [tk, S.k('XN')])
                    yield
                    cpy('dve', S.UU[:].rearrange("p a v -> p (a v)"), S.pg[1][0:64, 384:512], [S.k('UU')], [S.kpg[1]])
                    yield
                    yield
                    for hh in range(2):
                        rw = slice(64 * hh, 64 * hh + 64)
                        fo = slice(64 * hh, 64 * hh + 64)
                        mm(S.pg[0][rw, 384:384 + L], Hs[rw, :], S.KR[rw, n, 1, 0:L], True, False, [S.kpg[0]], [hsk, S.k('KR')])
                        mm(S.pg[0][rw, 384:384 + L], S.UU[0:L, hh, :], S.W5[0:L, hh, 2, 0:L], False, False, [S.kpg[0]], [S.k('UU'), S.k('W5')])
                        mm(S.pg[0][rw, 384:384 + L], S.TM[0:L, 2, fo], S.W5[0:L, hh, 4, 0:L], False, True, [S.kpg[0]], [S.k('TM'), S.k('W5')])
                    yield
                    cpy('act', S.Y[:, cs], S.pg[0][:, 384:384 + L], [S.k('Y')], [S.kpg[0]])
                    yield
                    yield
                    for hh in range(2):
                        rw = slice(64 * hh, 64 * hh + 64)
                        fo = slice(64 * hh, 64 * hh + 64)
                        mm(S.pg[0][rw, 448:512], S.TM[0:L, 1, fo], S.UU[0:L, hh, :], True, False, [S.kpg[0]], [S.k('TM'), S.k('UU')])
                        mm(S.pg[0][rw, 448:512], S.TM[0:L, 0, fo], S.TM[0:L, 2, fo], False, True, [S.kpg[0]], [S.k('TM')])
                    yield
                    gl = S.GL[:, n:n + 1]
                    yield
                    act(S.HG[:, :], Hm, AF.Identity, [S.k('HG')], ['HS', S.k('GL')], scale=gl)
                    yield
                    stt(Hm, S.pg[0][:, 448:512], gl, S.HG[:, :], ALU.mult, ALU.add, ['HS'], [S.kpg[0], S.k('GL'), S.k('HG')])
                    if SMDT != F32:
                        cpy('act', HSs[:, j, :], Hm, ['HSs'], ['HS'])

                yield
                yield


            def postB(j, S):
                Y, BON, G, CL = S.Y, S.BON, S.G, TF[11]
                kBON, kG, kCL = S.k('BON'), S.k('G'), 'TF11'
                mm(PS[4][:, 0:T], onesbd_f, Y[:, 0:T], True, True, ['PS4'], ['CT', S.k('Y')])
                stt(Y[:, 0:T], PS[4][:, 0:T], -1.0 / 64, Y[:, 0:T], ALU.mult, ALU.add, [S.k('Y')], ['PS4', S.k('Y')])
                act(CL[:, 0:T], Y[:, 0:T], AF.Square, [kCL], [S.k('Y')])
                mm(PS[5][:, 0:T], onesbd_f, CL[:, 0:T], True, True, ['PS5'], ['CT', kCL])
                act(CL[:, 0:T], PS[5][:, 0:T], AF.Sqrt, [kCL], ['PS5'], scale=1.0 / 64, bias=GN_EPS_B)
                recip(CL[:, 0:T], CL[:, 0:T], [kCL], [kCL])
                tt('dve', Y[:, 0:T], Y[:, 0:T], CL[:, 0:T], ALU.mult, [S.k('Y')], [S.k('Y'), kCL])
                ts('dve', Y[:, 0:T], Y[:, 0:T], pv(l, 60 + j), pv(l, 63 + j), ALU.mult, ALU.add, [S.k('Y')], [S.k('Y'), 'PV'])
                tt('dve', Y[:, 0:T], Y[:, 0:T], BON[:, 0:T], ALU.add, [S.k('Y')], [S.k('Y'), kBON])
                tt('dve', YM[:, 2 + j, 0:T], Y[:, 0:T], G[:, 0:T], ALU.mult, ['YM%d' % (2 + j)], [S.k('Y'), kG])

            def prepC():
                cpy('dve', PJ[:, 0:3, 0:3], HISTC[:, l, :, :], ['PJ0', 'PJ1', 'PJ2'], ['HISTC'])
                for (c0, nch_, s0) in ((1920, 4, 0), (2432, 4, 4), (2944, 1, 8)):
                    wt, wkey = wload([(0, 8, nch_ * 128) + win_src(l, c0, nch_ * 128)])
                    inproj(l, T, wt, wkey, nch_ * 128, [(i, s0 + i) for i in range(nch_)])
                cpy('dve', HISTC[:, l, :, :], PJ[:, 0:3, T:T + 3], ['HISTC'], ['PJ0', 'PJ1', 'PJ2'])
                if SMDT == F32:
                    KRf = KR[:].rearrange("p n a l -> p (n a l)")
                    QB = [TF[12], TF[13], KT_]
                    KS = [BT_, VS_, KRf]
                    kQB = ['TF12', 'TF13', 'KT_']
                    kKS = ['BT_', 'VS_', 'KR']
                else:
                    QB, KS, VBt = QKVb[0:3], QKVb[3:6], QKVb[6:9]
                    kQB = ['QKV%d' % i for i in range(3)]
                    kKS = ['QKV%d' % i for i in range(3, 6)]
                for j in range(3):
                    xc = TF[0]
                    act(xc[:, 0:T], PJ[:, j, 3:3 + T], AF.Identity, ['TF0'], ['PJ%d' % j, 'PV'],
                        scale=pv(l, 66 + 3 * 3 + j), bias=pv(l, 78 + j))
                    for t_ in range(3):
                        stt(xc[:, 0:T], PJ[:, j, t_:t_ + T], pv(l, 66 + 3 * t_ + j), xc[:, 0:T], ALU.mult, ALU.add,
                            ['TF0'], ['TF0', 'PJ%d' % j, 'PV'])
                    act(TB[2][:, 0:T], xc[:, 0:T], AF.Silu, ['TB2'], ['TF0'])
                    mm(PS[0][:, 0:T], WQbd[:, l, j, :], TB[2][:, 0:T], True, True, ['PS0'], ['WQbd', 'TB2'])
                    mm(PS[1][:, 0:T], WKbd[:, l, j, :], TB[2][:, 0:T], True, True, ['PS1'], ['WKbd', 'TB2'])
                    cpy('act', QB[j][:, 0:T], PS[0][:, 0:T], [kQB[j]], ['PS0'])
                    act(KS[j][:, 0:T], PS[1][:, 0:T], AF.Copy, [kKS[j]], ['PS1'], scale=0.125)
                    if SMDT != F32:
                        cpy('dve', VBt[j][:, 0:T], PJ[:, 3 + j, 3:3 + T], ['QKV%d' % (6 + j)], ['PJ%d' % (3 + j)])

                def vsrc(j, cs_):
                    if SMDT != F32:
                        return VBt[j][:, cs_.start:cs_.stop], 'QKV%d' % (6 + j)
                    return PJ[:, 3 + j, 3 + cs_.start:3 + cs_.stop], 'PJ%d' % (3 + j)
                allT = slice(0, T)
                srcs = [(QB[j][:, 0:T], kQB[j]) for j in range(3)] + [(KS[j][:, 0:T], kKS[j]) for j in range(3)] + \
                       [vsrc(j, allT) for j in range(3)]
                for kc, (s_, sk) in enumerate(srcs):
                    mm(PS[2][0:6, 0:T], WIF[:, l, kc, 0:6], s_, kc == 0, kc == 8, ['PS2'], ['WIF', sk])
                for kc, (s_, sk) in enumerate(srcs):
                    mm(PS[3][0:6, 0:T], WIF[:, l, kc, 6:12], s_, kc == 0, kc == 8, ['PS3'], ['WIF', sk])
                IP, L1, CS, MX = [TF[i][0:6, :] for i in range(4, 8)]
                kIP, kL1, kCS, kMX = ['TF%d' % i for i in range(4, 8)]
                AH, NMX, WL, MT = [GC[i] for i in range(4)]
                kAH, kNMX, kWL, kMT = ['GC%d' % i for i in range(4)]
                act(IP[:, 0:T], PS[2][0:6, 0:T], AF.Identity, [kIP], ['PS2', 'BI'], bias=BI[:, l:l + 1])
                act(L1[:, 0:T], PS[3][0:6, 0:T], AF.Exp, [kL1], ['PS3', 'NBF'], scale=-1.0, bias=NBF[:, l:l + 1])
                act(L1[:, 0:T], L1[:, 0:T], AF.Ln, [kL1], [kL1], bias=1.0)
                scan(CS[:, 0:T], ones_c[0:6, 0:1].to_broadcast([6, T]), L1[:, 0:T], 0.0, ALU.mult, ALU.add, [kCS], ['ones_f', kL1])
                tt('dve', AH[:, 0:T], IP[:, 0:T], CS[:, 0:T], ALU.add, [kAH], [kIP, kCS])
                scan(MX[:, 0:T], ones_c[0:6, 0:1].to_broadcast([6, T]), AH[:, 0:T], MS[:, l:l + 1], ALU.mult, ALU.max, [kMX], ['ones_f', kAH, 'MS'])
                tt('dve', MT[:, 0:T], MX[:, 0:T], CS[:, 0:T], ALU.subtract, [kMT], [kMX, kCS])
                ts('dve', NMX[:, 0:T], MX[:, 0:T], -1.0, None, ALU.mult, None, [kNMX], [kMX])
                for n in range(NCH):
                    cs = slice(n * L, (n + 1) * L)
                    prev = MS[:, l:l + 1] if n == 0 else MX[:, n * L - 1:n * L]
                    ts('dve', WL[:, cs], MX[:, cs], prev, None, ALU.subtract, None, [kWL], [kMX, 'MS'])
                cpy('dve', MS[:, l:l + 1], MT[:, T - 1:T], ['MS'], [kMT])
                YC = None
                Cm = CA[:, l, :, :]
                if SMDT != F32:
                    cpy('dve', CAs[:], Cm, ['CAs'], ['CA'])
                    Cs, csk = CAs[:], 'CAs'
                else:
                    Cs, csk = Cm, 'CA'
                for j in range(3):
                    act(SZ[j][:, 0:T], PJ[:, 6 + j, 3:3 + T], AF.Sigmoid, ['SZ%d' % j], ['PJ%d' % (6 + j)])
                return dict(QB=QB, KS=KS, kQB=kQB, kKS=kKS, vsrc=vsrc, AH=AH, NMX=NMX, WL=WL, MT=MT, kAH=kAH, kNMX=kNMX, kWL=kWL, kMT=kMT, YC=YC, Cm=Cm, Cs=Cs, csk=csk)

            def loopC(cc):
                QB, KS, kQB, kKS, vsrc = cc['QB'], cc['KS'], cc['kQB'], cc['kKS'], cc['vsrc']
                AH, NMX, WL, MT, kAH, kNMX, kWL, kMT = cc['AH'], cc['NMX'], cc['WL'], cc['MT'], cc['kAH'], cc['kNMX'], cc['kWL'], cc['kMT']
                YC, Cm, Cs, csk = cc['YC'], cc['Cm'], cc['Cs'], cc['csk']
                for n in range(NCH):
                    cs = slice(n * L, (n + 1) * L)
                    yield
                    for j in range(3):
                        vs_, vk_ = vsrc(j, cs)
                        tp(PTB[0:L, j * 128:(j + 1) * 128], vs_, ident_s, [kPTB], [vk_, 'CT', 'ident_st'])
                    yield
                    cpy('act', VA[0:L, :, 0:64], PTB[0:L, 0:384].rearrange("p (h v) -> p h v", h=6), ['VA'], [kPTB])
                    yield
                    for j in range(3):
                        tp(PTB[0:L, j * 128:(j + 1) * 128], KS[j][:, cs], ident_s, [kPTB], [kKS[j], 'CT', 'ident_st'])
                    yield
                    cpy('dve', KTm[0:L, :, :], PTB[0:L, 0:384].rearrange("p (h v) -> p h v", h=6), ['KTm'], [kPTB])
                    yield
                    tp(PS[1][0:L, 400:406], WL[:, cs], ident[0:6, 0:6], ['PS1'], [kWL, 'CT'])
                    yield
                    tp(PS[1][0:L, 406:412], MT[:, cs], ident[0:6, 0:6], ['PS1'], [kMT, 'CT'])
                    yield
                    act(WE[0:L, :, :], PS[1][0:L, 400:412].rearrange("p (a h) -> p a h", a=2), AF.Exp, ['WE'], ['PS1'], scale=-1.0)
                    yield
                    yield
                    mm(PS[0][0:L, 0:384].rearrange("p (h l) -> p h l", h=6)[:, :, 0:L], ident[0:L, 0:L], negm6[0:L, :, 0:L],
                       True, False, ['PS0'], ['CT'])
                    yield
                    for h in range(6):
                        o_ = PS[0][0:L, h * 64:h * 64 + L]
                        mm(o_, sel6[:, h, 0:L], NMX[:, cs], False, False, ['PS0'], ['CT', kNMX])
                        mm(o_, AH[:, cs], sel6[:, h, 0:L], False, h == 5, ['PS0'], ['CT', kAH])
                    yield
                    act(DD[0:L, :, 0:L], PS[0][0:L, 0:384].rearrange("p (h l) -> p h l", h=6)[:, :, 0:L], AF.Exp, ['DD'], ['PS0'])
                    yield
                    yield
                    for h in range(6):
                        j, hh = h // 2, h % 2
                        rw = slice(64 * hh, 64 * hh + 64)
                        mm(PS[1][0:L, h * 64:h * 64 + L], KS[j][rw, cs], QB[j][rw, cs], True, True, ['PS1'], [kKS[j], kQB[j]])
                    yield
                    tt('dve', STt[0:L, :, 0:L], PS[1][0:L, 0:384].rearrange("p (h l) -> p h l", h=6)[:, :, 0:L],
                       DD[0:L, :, 0:L], ALU.mult, ['STt'], ['PS1', 'DD'])
                    yield
                    yield
                    for h in range(6):
                        j, hh = h // 2, h % 2
                        rw = slice(64 * hh, 64 * hh + 64)
                        mm(PS[2][0:L, h * 65:h * 65 + 65], STt[0:L, h, 0:L], VA[0:L, h, :], True, True, ['PS2'], ['STt', 'VA'])
                        mm(PS[3][0:L, h * 65:h * 65 + 65], QB[j][rw, cs], Cs[rw, j, :], True, True, ['PS3'], [kQB[j], csk])
                    yield
                    tt('dve', T1[0:L], PS[3][0:L, 0:390].rearrange("p (h v) -> p h v", h=6),
                       WE[0:L, 0, :].rearrange("p (h o) -> p h o", o=1).to_broadcast([L, 6, 65]), ALU.mult, ['T1'], ['PS3', 'WE'])
                    yield
                    tt('dve', NUM[0:L], PS[2][0:L, 0:390].rearrange("p (h v) -> p h v", h=6), T1[0:L], ALU.add, ['NUM'], ['PS2', 'T1'])
                    yield
                    ts('dve', DEN[0:L, :], NUM[0:L, :, 64], -1.0, None, ALU.mult, None, ['DEN'], ['NUM'])
                    yield
                    tt('dve', DEN[0:L, :], DEN[0:L, :], NUM[0:L, :, 64], ALU.max, ['DEN'], ['DEN', 'NUM'])
                    yield
                    tt('dve', DEN[0:L, :], DEN[0:L, :], WE[0:L, 1, :], ALU.max, ['DEN'], ['DEN', 'WE'])
                    yield
                    recip(DEN[0:L, :], DEN[0:L, :], ['DEN'], ['DEN'])
                    yield
                    yield
                    tt('dve', HTt[0:L], NUM[0:L, :, 0:64], DEN[0:L, :].rearrange("p (h o) -> p h o", o=1).to_broadcast([L, 6, 64]),
                       ALU.mult, ['HTt'], ['NUM', 'DEN'])
                    yield
                    red(MEAN[0:L, :], HTt[0:L], ['MEAN'], ['HTt'])
                    yield
                    stt(HC[0:L], MEAN[0:L, :].rearrange("p (h o) -> p h o", o=1).to_broadcast([L, 6, 64]), -1.0 / 64, HTt[0:L],
                        ALU.mult, ALU.add, ['HC'], ['MEAN', 'HTt'])
                    yield
                    tt('dve', SQ[0:L], HC[0:L], HC[0:L], ALU.mult, ['SQ'], ['HC'])
                    yield
                    red(VAR[0:L, :], SQ[0:L], ['VAR'], ['SQ'])
                    yield
                    yield
                    act(VAR[0:L, :], VAR[0:L, :], AF.Ln, ['VAR'], ['VAR'], scale=1.0 / 64, bias=GN_EPS_C)
                    yield
                    act(VAR[0:L, :], VAR[0:L, :], AF.Exp, ['VAR'], ['VAR'], scale=-0.5)
                    yield
                    tt('dve', HC[0:L], HC[0:L], VAR[0:L, :].rearrange("p (h o) -> p h o", o=1).to_broadcast([L, 6, 64]),
                       ALU.mult, ['HC'], ['HC', 'VAR'])
                    yield
                    yield
                    for j in range(3):
                        tp(PS[1][:, j * 64:j * 64 + L], HC[0:L, 2 * j:2 * j + 2, :].rearrange("p h v -> p (h v)"), ident[0:L, 0:L],
                           ['PS1'], ['HC', 'CT'])
                    yield
                    for j in range(3):
                        act(YCt[:, j, 0:L], PS[1][:, j * 64:j * 64 + L], AF.Identity, ['YCt'], ['PS1', 'PV'], scale=pv(l, 81 + j))
                        tt('dve', YM[:, 5 + j, cs], YCt[:, j, 0:L], SZ[j][:, cs], ALU.mult, ['YM%d' % (5 + j)], ['YCt', 'SZ%d' % j])
                    yield
                    yield
                    tt('dve', VW[0:L], VA[0:L], DD[0:L, :, L - 1:L].to_broadcast([L, 6, 65]), ALU.mult, ['VW'], ['VA', 'DD'])
                    yield
                    for h in range(6):
                        j, hh = h // 2, h % 2
                        rw = slice(64 * hh, 64 * hh + 64)
                        mm(PS[0][rw, j * 65:j * 65 + 65], KTm[0:L, h, :], VW[0:L, h, :], True, True, ['PS0'], ['KTm', 'VW'])
                    yield
                    ts('dve', WLE[:, :], pairsel, WL[:, n * L + L - 1:n * L + L], None, ALU.mult, None, ['WLE'], ['CT', kWL])
                    yield
                    mm(PS[0][:, 400:403], selb, WLE[:, :], True, True, ['PS0'], ['CT', 'WLE'])
                    yield
                    act(W0[:, :], PS[0][:, 400:403], AF.Exp, ['W0'], ['PS0'], scale=-1.0)
                    yield
                    tt('dve', CT2[:], Cm, W0[:, :].rearrange("p (h o) -> p h o", o=1).to_broadcast([128, 3, 65]), ALU.mult, ['CT2'], ['CA', 'W0'])
                    yield
                    tt('dve', Cm, PS[0][:, 0:195].rearrange("p (h v) -> p h v", h=3), CT2[:], ALU.add, ['CA'], ['PS0', 'CT2'])
                    if SMDT != F32:
                        cpy('act', CAs[:], Cm, ['CAs'], ['CA'])

                    yield

            def postC():
                for j in range(3):
                    pass

            def lockstep(gens):
                alive = [True] * len(gens)
                while any(alive):
                    for gi, g in enumerate(gens):
                        if alive[gi]:
                            try:
                                next(g)
                            except StopIteration:
                                alive[gi] = False

            if OVERLAP:
                cc = prepC()
                preB()
                pp0 = prepB(0, SL[0])
                pp1 = prepB(1, SL[1])
                lockstep([loopB(0, SL[0], pp0), loopB(1, SL[1], pp1)])
                postB(0, SL[0])
                postB(1, SL[1])
                pp2 = prepB(2, SL[0])
                lockstep([loopB(2, SL[0], pp2), loopC(cc)])
                postB(2, SL[0])
            else:
                preB()
                for j in range(3):
                    ppj = prepB(j, SL[0])
                    for _ in loopB(j, SL[0], ppj):
                        pass
                    postB(j, SL[0])
                cc = prepC()
                for _ in loopC(cc):
                    pass
            postC()

            for half in range(2):
                wt, wkey = wload([(0, 8, 512, WSC['w_out'].ap()[l][:, half * 512:(half + 1) * 512].rearrange("(k p) n -> p k n", p=128), 'DW_w_out%d' % l)])
                for mi in range(4):
                    m = half * 4 + mi
                    bank = mi % 4
                    for k in range(8):
                        mm(PS[bank][:, 0:T], wt[:, k * 512 + mi * 128:k * 512 + mi * 128 + 128], YM[:, k, 0:T], k == 0, k == 7,
                           ['PS%d' % bank], [wkey, 'YM%d' % k])
                    tt('dve', X[:, m, 0:T], PS[bank][:, 0:T], X[:, m, 0:T], ALU.add, ['X%d' % m], ['PS%d' % bank, 'X%d' % m])
            def o2(k):
                stt(HN[:, k, 0:T], X[:, k, 0:T], pv(l, 8 + k), RS[:, 0:T], ALU.mult, ALU.mult, ['HN%d' % k], ['X%d' % k, 'PV', 'TF13'])
            rmsnorm_to(T, 8, l, o2)
            for i in range(11):
                wf = WSC['w_ffn_in'].ap()[l]
                wt, wkey = wload([(0, 8, 256, wf[:, 256 * i:256 * i + 256].rearrange("(k p) n -> p k n", p=128), 'DW_w_ffn_in%d' % l),
                                  (2048, 8, 256, wf[:, DFF + 256 * i:DFF + 256 * i + 256].rearrange("(k p) n -> p k n", p=128), 'DW_w_ffn_in%d' % l)])
                for q in range(2):
                    f = 2 * i + q
                    bg, bu = (0, 1) if q == 0 else (2, 3)
                    for k in range(8):
                        mm(PS[bg][:, 0:T], wt[:, k * 256 + q * 128:k * 256 + q * 128 + 128], HN[:, k, 0:T], k == 0, k == 7,
                           ['PS%d' % bg], [wkey, 'HN%d' % k])
                    for k in range(8):
                        mm(PS[bu][:, 0:T], wt[:, 2048 + k * 256 + q * 128:2048 + k * 256 + q * 128 + 128], HN[:, k, 0:T], k == 0, k == 7,
                           ['PS%d' % bu], [wkey, 'HN%d' % k])
                    tk_ = TF[q]
                    act(tk_[:, 0:T], PS[bg][:, 0:T], AF.Silu, ['TF%d' % q], ['PS%d' % bg])
                    tt('dve', ACTT[:, f, 0:T], tk_[:, 0:T], PS[bu][:, 0:T], ALU.mult, actk(f), ['TF%d' % q, 'PS%d' % bu])
            wo = WSC['w_ffn_out'].ap()[l]
            for half in range(2):
                for (k0, kc_) in ((0, 8), (8, 8), (16, 6)):
                    wt, wkey = wload([(0, kc_, 512, wo[k0 * 128:(k0 + kc_) * 128, half * 512:(half + 1) * 512].rearrange("(k p) n -> p k n", p=128),
                                       'DW_w_ffn_out%d' % l)])
                    for mi in range(4):
                        for kk_ in range(kc_):
                            k = k0 + kk_
                            mm(PS[mi][:, 0:T], wt[:, kk_ * 512 + mi * 128:kk_ * 512 + mi * 128 + 128], ACTT[:, k, 0:T], k == 0, k == 21,
                               ['PS%d' % mi], [wkey] + actk(k))
                for mi in range(4):
                    m = half * 4 + mi
                    tt('dve', X[:, m, 0:T], PS[mi][:, 0:T], X[:, m, 0:T], ALU.add, ['X%d' % m], ['PS%d' % mi, 'X%d' % m])
            if last_layer:
                def o3(k):
                    stt(TF[k][:, 0:T], X[:, k, 0:T], NF[:, k:k + 1], RS[:, 0:T], ALU.mult, ALU.mult, ['TF%d' % k], ['X%d' % k, 'NF', 'TF13'])
                rmsnorm_to(T, 0, l, o3)
                nblk = (T + 127) // 128
                for tb in range(nblk):
                    n = min(128, T - tb * 128)
                    for half in range(2):
                        xt, xk = STG[half]
                        bank = 4 + half
                        for mmi in range(4):
                            m = half * 4 + mmi
                            tp(PS[bank][0:n, mmi * 128:(mmi + 1) * 128], TF[m][:, tb * 128:tb * 128 + n], ident,
                               ['PS%d' % bank], ['TF%d' % m, 'CT'])
                        cpy('act' if half else 'dve', xt[0:n, 0:512], PS[bank][0:n, 0:512], [xk], ['PS%d' % bank])
                        dma('sp', yout[tok0 + tb * 128:tok0 + tb * 128 + n, half * 512:(half + 1) * 512], xt[0:n, 0:512], r=[xk], is_output=True)

        for job in jobs:
            if job == 'p':
                T, L, nseg, xin, yout = TS, 64, TP // TS, I['xp'], O['yp']
                for t_, k_ in ((HISTA, 'HISTA'), (HISTB, 'HISTB'), (HISTC, 'HISTC'), (HL, 'HL'), (HS, 'HS'), (CA, 'CA'), (MS, 'MS')):
                    memset('dve', t_[:], 0.0, [k_])
            else:
                T, L, nseg, xin, yout = TSMP, 32, 1, I['xs'], O['ys']
                for l in range(DEPTH):
                    for j_ in range(3):
                        dma('sp', HISTA[:, l, :, j_], I['s_conv_a'][l, j_].rearrange("(c p) -> p c", p=128), w=['HISTA'], slow=True)
                        dma('sp', HISTC[:, l, :, j_], I['s_conv_c'][l, j_].rearrange("(c p) -> p c", p=128), w=['HISTC'], slow=True)
                    dma('sp', HISTB[:, l, :, 0], I['s_shift'][l].rearrange("(c p) -> p c", p=128), w=['HISTB'], slow=True)
                    dma('sp', HL[:, l, :], I['s_lru'][l].rearrange("(c p) -> p c", p=128), w=['HL'], slow=True)
                    dma('sp', CA[:, l, :, 0:64], I['s_mem_c'][l].rearrange("(jp hh) n v -> (hh n) jp v", hh=2), w=['CA'])
                    dma('sp', CA[:, l, :, 64], I['s_mem_n'][l].rearrange("(jp hh) n -> (hh n) jp", hh=2), w=['CA'], slow=True)
                    dma('sp', WS[:], I['s_wkv'][l].rearrange("h v k -> v h k"), w=['WS'])
                    for j in range(3):
                        tp(PS[4][:, j * 64:j * 64 + 64], WS[:, 2 * j:2 * j + 2, :].rearrange("p h k -> p (h k)"), ident[0:64, 0:64],
                           ['PS4'], ['WS', 'CT'])
                    cpy('dve', HS[:, l, :, :], PS[4][:, 0:192].rearrange("p (j v) -> p j v", j=3), ['HS'], ['PS4'])
                dma('sp', MS[:], I['s_mem_m'].rearrange("l h -> h l"), w=['MS'], slow=True)
            for seg in range(nseg):
                for l in range(DEPTH):
                    task(job, l, T, L, seg * T, seg == 0, seg == nseg - 1, xin, yout, l == DEPTH - 1)
            pre = job + '_'
            for l in range(DEPTH):
                for j_ in range(3):
                    dma('sp', O[pre + 'conv_a'][l, j_].rearrange("(c p) -> p c", p=128), HISTA[:, l, :, j_], r=['HISTA'], slow=True, is_output=True)
                    dma('sp', O[pre + 'conv_c'][l, j_].rearrange("(c p) -> p c", p=128), HISTC[:, l, :, j_], r=['HISTC'], slow=True, is_output=True)
                dma('sp', O[pre + 'shift'][l].rearrange("(c p) -> p c", p=128), HISTB[:, l, :, 0], r=['HISTB'], slow=True, is_output=True)
                dma('sp', O[pre + 'lru'][l].rearrange("(c p) -> p c", p=128), HL[:, l, :], r=['HL'], slow=True, is_output=True)
                dma('sp', O[pre + 'mem_c'][l].rearrange("(jp hh) n v -> (hh n) jp v", hh=2), CA[:, l, :, 0:64], r=['CA'], is_output=True)
                dma('sp', O[pre + 'mem_n'][l].rearrange("(jp hh) n -> (hh n) jp", hh=2), CA[:, l, :, 64], r=['CA'], slow=True, is_output=True)
                for j in range(3):
                    tp(PS[4][0:64, j * 128:(j + 1) * 128], HS[:, l, j, :], ident, ['PS4'], ['HS', 'CT'])
                cpy('dve', WS[:], PS[4][0:64, 0:384].rearrange("p (h k) -> p h k", h=6), ['WS'], ['PS4'])
                dma('sp', O[pre + 'wkv'][l].rearrange("h v k -> v h k"), WS[:], r=['WS'], is_output=True)
            dma('sp', O[pre + 'mem_m'].rearrange("l h -> h l"), MS[:], r=['MS'], slow=True, is_output=True)
        P.finish()
        P.emit(nc, st)
    return nc, cnp


_CACHE = {}
TS_DEFAULT = 512
SMALL_DT = BF16


def kernel(**inputs):
    inputs = {k: np.asarray(v) for k, v in inputs.items()}
    xp = inputs['x_prompt']
    xs = inputs['x_sample']
    DEPTH = inputs['norm1'].shape[0]
    B, TP, _ = xp.shape
    TS = min(TS_DEFAULT, TP)
    key = (DEPTH, TP, TS)
    if key not in _CACHE:
        _CACHE[key] = build(DEPTH, TP, TS, SMDT=SMALL_DT)
    nc, cnp = _CACHE[key]
    f32 = lambda a: np.ascontiguousarray(a, dtype=np.float32)
    in_maps = []
    for c in range(NCORES):
        m = {'xp': f32(xp[c % B]), 'xs': f32(xs[c]), 'consts': cnp}
        m['s_conv_a'] = f32(inputs['state_conv_a'][:, c])
        m['s_lru'] = f32(inputs['state_lru'][:, c])
        m['s_shift'] = f32(inputs['state_shift_b'][:, c, 0])
        m['s_wkv'] = f32(inputs['state_wkv'][:, c])
        m['s_conv_c'] = f32(inputs['state_conv_c'][:, c])
        m['s_mem_c'] = f32(inputs['state_mem_c'][:, c])
        m['s_mem_n'] = f32(inputs['state_mem_n'][:, c])
        m['s_mem_m'] = f32(inputs['state_mem_m'][:, c])
        for k in WNAMES:
            m[k] = f32(inputs[k])
        in_maps.append(m)
    res = run_bass_kernel_spmd(nc, in_maps, core_ids=list(range(NCORES)))
    R = res.results
    y_prompt = np.stack([R[b]['yp'] for b in range(B)], 0)
    y_sample = np.stack([R[c]['ys'] for c in range(NCORES)], 0)
    outs = [y_prompt, y_sample]
    names = ['conv_a', 'lru', 'shift', 'wkv', 'conv_c', 'mem_c', 'mem_n', 'mem_m']
    for jb, n in (('p', B), ('s', NCORES)):
        for nm in names:
            a = np.stack([R[c]['o' + jb + '_' + nm] for c in range(n)], 1)
            if nm == 'shift':
                a = a[:, :, None, :]
            outs.append(np.ascontiguousarray(a.astype(np.float32)))
    return tuple(outs)
```
